# Optimizing a Trainium2 kernel written in Bass

```python
import math
import jax, jax.numpy as jnp
from jax import lax
import numpy as np

D_MODEL = 2048
BATCH = 8
SEQ = 4096
DEPTH = 1
DEC_BATCH = 8
DEC_SEQ = 16
PAST_LEN = 4096

CHUNK = 64
Q_BLOCK = 128
DA_WIDTH = D_MODEL // 2
DA_HEAD_DIM = 64
DA_HEADS = DA_WIDTH // (2 * DA_HEAD_DIM)
ML_WIDTH = D_MODEL - DA_WIDTH
ML_HEADS = 4
ML_HEAD_DIM = ML_WIDTH // ML_HEADS
MEM_LEN = 256
MEM_HEADS = 4
MEM_HEAD_DIM = D_MODEL // MEM_HEADS
D_FF = 5504
CONV_W = 3
EPS = 1e-6
NEG = -1e30
IN_SIZES = (DA_WIDTH, DA_WIDTH, DA_WIDTH, ML_WIDTH, ML_WIDTH, ML_WIDTH, ML_WIDTH, ML_HEADS, ML_HEADS)
D_IN = sum(IN_SIZES)
IN_SPLITS = tuple(sum(IN_SIZES[:i + 1]) for i in range(len(IN_SIZES) - 1))

kernel_name = "hymba_diffattn_mlstm_streaming_step"

F32 = jnp.float32


def _rmsnorm(x, g):
    xf = x.astype(F32)
    y = xf * lax.rsqrt(jnp.mean(xf * xf, axis=-1, keepdims=True) + EPS)
    return (y * g.astype(F32)).astype(x.dtype)


def _diff_attn_core(q, k, v, q_pos, k_pos, lam):
    s = jnp.einsum('bqhcd,bkhcd->bhcqk', q, k).astype(F32) * (DA_HEAD_DIM ** -0.5)
    mask = (k_pos[None, :] // CHUNK) <= (q_pos[:, None] // CHUNK)
    s = jnp.where(mask, s, NEG)
    p = jax.nn.softmax(s, axis=-1)
    a = p[:, :, 0] - lam * p[:, :, 1]
    return jnp.einsum('bhqk,bkhe->bqhe', a.astype(v.dtype), v)


def _diff_attn_prompt(q, k, v, lam):
    B, T = q.shape[0], q.shape[1]
    nb = T // Q_BLOCK
    qb = jnp.moveaxis(q.reshape(B, nb, Q_BLOCK, DA_HEADS, 2, DA_HEAD_DIM), 1, 0)
    k_pos = jnp.arange(T)

    def one(args):
        qi, bi = args
        q_pos = bi * Q_BLOCK + jnp.arange(Q_BLOCK)
        return _diff_attn_core(qi, k, v, q_pos, k_pos, lam)

    ob = lax.map(one, (qb, jnp.arange(nb)))
    return jnp.moveaxis(ob, 0, 1).reshape(B, T, DA_HEADS, 2 * DA_HEAD_DIM)


def _mlstm_chunkwise(q, k, v, ig, lf, c0, n0, m0, chunk):
    B, H, T, d = q.shape
    nc = T // chunk

    def to_chunks(a):
        return jnp.moveaxis(a.reshape((B, H, nc, chunk) + a.shape[3:]), 2, 0)

    causal = jnp.tril(jnp.ones((chunk, chunk), dtype=bool))

    def step(carry, inp):
        c, n, m = carry
        qc, kc, vc, ic, fc = inp
        b = jnp.cumsum(fc, axis=-1)
        dmat = b[..., :, None] - b[..., None, :] + ic[..., None, :]
        dmat = jnp.where(causal, dmat, -jnp.inf)
        inter = b + m[..., None]
        m_t = jnp.maximum(inter, jnp.max(dmat, axis=-1))
        w = jnp.exp(dmat - m_t[..., None])
        g = jnp.exp(inter - m_t)
        s = jnp.einsum('bhtd,bhsd->bhts', qc, kc) * w
        num = jnp.einsum('bhts,bhse->bhte', s, vc) + g[..., None] * jnp.einsum('bhed,bhtd->bhte', c, qc)
        den = jnp.sum(s, axis=-1) + g * jnp.einsum('bhd,bhtd->bht', n, qc)
        h = num / jnp.maximum(jnp.abs(den), jnp.exp(-m_t))[..., None]
        m_new = m_t[..., -1]
        wl = jnp.exp(b[..., -1:] - b + ic - m_new[..., None])
        gl = jnp.exp(inter[..., -1] - m_new)
        c_new = gl[..., None, None] * c + jnp.einsum('bhs,bhse,bhsd->bhed', wl, vc, kc)
        n_new = gl[..., None] * n + jnp.einsum('bhs,bhsd->bhd', wl, kc)
        return (c_new, n_new, m_new), h

    (c, n, m), hs = lax.scan(step, (c0, n0, m0),
                             (to_chunks(q), to_chunks(k), to_chunks(v), to_chunks(ig), to_chunks(lf)))
    h = jnp.moveaxis(hs, 0, 2).reshape(B, H, T, d)
    return h, c, n, m


def _token_mixer(h, p, past_k, past_v, ml_state, lam_init, ml_chunk):
    B, T, _ = h.shape
    dq, dk, dv, mq, mk, mv, mo, mi, mf = jnp.split(h @ p['w_in'], IN_SPLITS, axis=-1)
    lam = (jnp.exp(jnp.sum((p['lq1'] * p['lk1']).astype(F32)))
           - jnp.exp(jnp.sum((p['lq2'] * p['lk2']).astype(F32))) + lam_init)
    q = dq.reshape(B, T, DA_HEADS, 2, DA_HEAD_DIM)
    k_rows = dk.reshape(B, T, DA_HEADS, 2 * DA_HEAD_DIM)
    v_rows = dv.reshape(B, T, DA_HEADS, 2 * DA_HEAD_DIM)
    if past_k is None:
        o = _diff_attn_prompt(q, k_rows.reshape(B, T, DA_HEADS, 2, DA_HEAD_DIM), v_rows, lam)
    else:
        P = past_k.shape[1]
        keys = jnp.concatenate([past_k.astype(k_rows.dtype), k_rows], axis=1)
        vals = jnp.concatenate([past_v.astype(v_rows.dtype), v_rows], axis=1)
        o = _diff_attn_core(q, keys.reshape(B, P + T, DA_HEADS, 2, DA_HEAD_DIM), vals,
                            P + jnp.arange(T), jnp.arange(P + T), lam)
    o = _rmsnorm(o, p['g_da']) * (1.0 - lam_init)
    da_out = o.reshape(B, T, DA_WIDTH)
    def heads(a):
        return a.reshape(B, T, ML_HEADS, ML_HEAD_DIM).transpose(0, 2, 1, 3).astype(F32)
    qm = heads(mq)
    km = heads(mk) * (ML_HEAD_DIM ** -0.5)
    vm = heads(mv)
    ig = (mi + p['b_ig']).astype(F32).transpose(0, 2, 1)
    lf = jax.nn.log_sigmoid((mf + p['b_fg']).astype(F32)).transpose(0, 2, 1)
    c0, n0, m0 = ml_state
    hm, c, n, m = _mlstm_chunkwise(qm, km, vm, ig, lf, c0.astype(F32), n0.astype(F32),
                                   m0.astype(F32), ml_chunk)
    hm = hm.transpose(0, 2, 1, 3).astype(h.dtype)
    hm = _rmsnorm(hm, p['g_ml'].reshape(ML_HEADS, ML_HEAD_DIM))
    ml_out = hm.reshape(B, T, ML_WIDTH) * jax.nn.sigmoid(mo)
    y = jnp.concatenate([da_out, ml_out], axis=-1) @ p['w_out']
    return y, k_rows, v_rows, (c, n, m)


def _mem_kv(mem, g_mem, w_mk, w_mv):
    B, M, _ = mem.shape
    mn = _rmsnorm(mem, g_mem)
    return ((mn @ w_mk).reshape(B, M, MEM_HEADS, MEM_HEAD_DIM),
            (mn @ w_mv).reshape(B, M, MEM_HEADS, MEM_HEAD_DIM))


def _mem_attn(h, mem_k, mem_v, w_mq, w_mo):
    B, T, _ = h.shape
    q = (h @ w_mq).reshape(B, T, MEM_HEADS, MEM_HEAD_DIM)
    s = jnp.einsum('bqhd,bkhd->bhqk', q, mem_k.astype(q.dtype)).astype(F32) * (MEM_HEAD_DIM ** -0.5)
    a = jax.nn.softmax(s, axis=-1).astype(q.dtype)
    o = jnp.einsum('bhqk,bkhd->bqhd', a, mem_v.astype(q.dtype)).reshape(B, T, D_MODEL)
    return o @ w_mo


def _conv_ffn(h, p, conv_past):
    B, T, _ = h.shape
    g = h @ p['w_gate']
    u = h @ p['w_up']
    if conv_past is None:
        past = jnp.zeros((B, CONV_W - 1, D_FF), g.dtype)
    else:
        past = conv_past.astype(g.dtype)
    gp = jnp.concatenate([past, g], axis=1)
    c = p['conv_b']
    for j in range(CONV_W):
        c = c + p['conv_w'][j] * gp[:, j:j + T]
    out = (jax.nn.silu(c) * u) @ p['w_down']
    return out, gp[:, T:]


def _layer(x, mem_k, mem_v, p, past_k, past_v, ml_state, conv_past, lam_init, ml_chunk):
    y, k_rows, v_rows, ml_new = _token_mixer(_rmsnorm(x, p['g_mix']), p, past_k, past_v,
                                             ml_state, lam_init, ml_chunk)
    x = x + y
    x = x + _mem_attn(_rmsnorm(x, p['g_xattn']), mem_k, mem_v, p['w_mq'], p['w_mo'])
    f, conv_new = _conv_ffn(_rmsnorm(x, p['g_ffn']), p, conv_past)
    x = x + f
    return x, k_rows, v_rows, ml_new, conv_new


def setup_inputs(seed: int = 0) -> dict:
    key = jax.random.key(seed)
    ks = iter(jax.random.split(key, 40))

    def nrm(shape, scale=1.0):
        return jax.random.normal(next(ks), shape, F32) * scale

    def gain(shape):
        return 1.0 + nrm(shape, 0.05)

    return {
        'x_prompt': nrm((BATCH, SEQ, D_MODEL)),
        'x_sample': nrm((DEC_BATCH, DEC_SEQ, D_MODEL)),
        'cache_da_k': nrm((DEPTH, DEC_BATCH, PAST_LEN, DA_HEADS, 2 * DA_HEAD_DIM)),
        'cache_da_v': nrm((DEPTH, DEC_BATCH, PAST_LEN, DA_HEADS, 2 * DA_HEAD_DIM)),
        'state_ml_c': nrm((DEPTH, DEC_BATCH, ML_HEADS, ML_HEAD_DIM, ML_HEAD_DIM), 0.1),
        'state_ml_n': nrm((DEPTH, DEC_BATCH, ML_HEADS, ML_HEAD_DIM), 0.1),
        'state_ml_m': nrm((DEPTH, DEC_BATCH, ML_HEADS)),
        'state_ffn_conv': nrm((DEPTH, DEC_BATCH, CONV_W - 1, D_FF)),
        'cache_mem_k': nrm((DEPTH, DEC_BATCH, MEM_LEN, MEM_HEADS, MEM_HEAD_DIM)),
        'cache_mem_v': nrm((DEPTH, DEC_BATCH, MEM_LEN, MEM_HEADS, MEM_HEAD_DIM)),
        'mem_prompt': nrm((BATCH, MEM_LEN, D_MODEL)),
        'w_in': nrm((DEPTH, D_MODEL, D_IN), D_MODEL ** -0.5),
        'g_mix': gain((DEPTH, D_MODEL)),
        'lambda_q1': nrm((DEPTH, DA_HEAD_DIM), 0.1),
        'lambda_k1': nrm((DEPTH, DA_HEAD_DIM), 0.1),
        'lambda_q2': nrm((DEPTH, DA_HEAD_DIM), 0.1),
        'lambda_k2': nrm((DEPTH, DA_HEAD_DIM), 0.1),
        'g_da_sub': gain((DEPTH, 2 * DA_HEAD_DIM)),
        'b_ig': nrm((DEPTH, ML_HEADS), 0.1),
        'b_fg': jnp.linspace(3.0, 6.0, ML_HEADS, dtype=F32)[None, :] + nrm((DEPTH, ML_HEADS), 0.1),
        'g_ml': gain((DEPTH, ML_WIDTH)),
        'w_out': nrm((DEPTH, D_MODEL, D_MODEL), D_MODEL ** -0.5),
        'g_xattn': gain((DEPTH, D_MODEL)),
        'g_mem': gain((DEPTH, D_MODEL)),
        'w_mq': nrm((DEPTH, D_MODEL, D_MODEL), D_MODEL ** -0.5),
        'w_mk': nrm((DEPTH, D_MODEL, D_MODEL), D_MODEL ** -0.5),
        'w_mv': nrm((DEPTH, D_MODEL, D_MODEL), D_MODEL ** -0.5),
        'w_mo': nrm((DEPTH, D_MODEL, D_MODEL), D_MODEL ** -0.5),
        'g_ffn': gain((DEPTH, D_MODEL)),
        'w_gate': nrm((DEPTH, D_MODEL, D_FF), D_MODEL ** -0.5),
        'w_up': nrm((DEPTH, D_MODEL, D_FF), D_MODEL ** -0.5),
        'conv_w': nrm((DEPTH, CONV_W, D_FF), CONV_W ** -0.5),
        'conv_b': nrm((DEPTH, D_FF), 0.02),
        'w_down': nrm((DEPTH, D_FF, D_MODEL), D_FF ** -0.5),
        'g_final': gain((D_MODEL,)),
    }


def reference(x_prompt, x_sample, cache_da_k, cache_da_v, state_ml_c, state_ml_n, state_ml_m,
              state_ffn_conv, cache_mem_k, cache_mem_v, mem_prompt, w_in, g_mix, lambda_q1,
              lambda_k1, lambda_q2, lambda_k2, g_da_sub, b_ig, b_fg, g_ml, w_out, g_xattn, g_mem,
              w_mq, w_mk, w_mv, w_mo, g_ffn, w_gate, w_up, conv_w, conv_b, w_down, g_final):
    xp = x_prompt
    xs = x_sample
    B = xp.shape[0]
    p_k, p_v, p_c, p_n, p_m, p_conv, p_mk, p_mv = [], [], [], [], [], [], [], []
    s_k, s_v, s_c, s_n, s_m, s_conv = [], [], [], [], [], []
    for l in range(DEPTH):
        p = dict(w_in=w_in[l], g_mix=g_mix[l], lq1=lambda_q1[l], lk1=lambda_k1[l],
                 lq2=lambda_q2[l], lk2=lambda_k2[l], g_da=g_da_sub[l], b_ig=b_ig[l],
                 b_fg=b_fg[l], g_ml=g_ml[l], w_out=w_out[l], g_xattn=g_xattn[l],
                 w_mq=w_mq[l], w_mo=w_mo[l], g_ffn=g_ffn[l], w_gate=w_gate[l], w_up=w_up[l],
                 conv_w=conv_w[l], conv_b=conv_b[l], w_down=w_down[l])
        lam_init = 0.8 - 0.6 * math.exp(-0.3 * l)
        mk, mv = _mem_kv(mem_prompt, g_mem[l], w_mk[l], w_mv[l])
        zero_state = (jnp.zeros((B, ML_HEADS, ML_HEAD_DIM, ML_HEAD_DIM), F32),
                      jnp.zeros((B, ML_HEADS, ML_HEAD_DIM), F32),
                      jnp.zeros((B, ML_HEADS), F32))
        xp, kr, vr, (c, n, m), cv = _layer(xp, mk, mv, p, None, None, zero_state, None,
                                           lam_init, CHUNK)
        p_k.append(kr); p_v.append(vr); p_c.append(c); p_n.append(n); p_m.append(m)
        p_conv.append(cv); p_mk.append(mk); p_mv.append(mv)
        xs, kr2, vr2, (c2, n2, m2), cv2 = _layer(
            xs, cache_mem_k[l], cache_mem_v[l], p, cache_da_k[l], cache_da_v[l],
            (state_ml_c[l], state_ml_n[l], state_ml_m[l]), state_ffn_conv[l], lam_init,
            xs.shape[1])
        s_k.append(kr2); s_v.append(vr2); s_c.append(c2); s_n.append(n2); s_m.append(m2)
        s_conv.append(cv2)
    y_prompt = _rmsnorm(xp, g_final)
    y_sample = _rmsnorm(xs, g_final)
    return (y_prompt, y_sample,
            jnp.stack(p_k), jnp.stack(p_v), jnp.stack(p_c), jnp.stack(p_n), jnp.stack(p_m),
            jnp.stack(p_conv), jnp.stack(p_mk), jnp.stack(p_mv),
            jnp.stack(s_k), jnp.stack(s_v), jnp.stack(s_c), jnp.stack(s_n), jnp.stack(s_m),
            jnp.stack(s_conv))
```

```python
import contextlib
import numpy as np
import concourse.bass as bass
import concourse.mybir as mybir
from concourse.bass_utils import run_bass_kernel_spmd

F32 = mybir.dt.float32
BF16 = mybir.dt.bfloat16
AF = mybir.ActivationFunctionType
ALU = mybir.AluOpType
AX = mybir.AxisListType
ENGS = ("pe", "act", "dve", "pool", "sp")

D = 2048
T = 4096
TS = 16
DIN = 7176
DFF = 5504
NH = 8
MEM = 256
EPS = 1e-6
LAM_INIT = 0.2
ATTACH_WAIT = True


class Reg:
    __slots__ = ("name", "w", "r")

    def __init__(self, name=""):
        self.name = name
        self.w = None
        self.r = []


class DSem:
    __slots__ = ("sem", "count", "name")

    def __init__(self, name):
        self.name = name
        self.sem = None
        self.count = 0


class Op:
    __slots__ = ("eng", "fn", "cwaits", "dwaits", "sig", "sigidx", "dsem", "dval")

    def __init__(self, eng, fn):
        self.eng = eng
        self.fn = fn
        self.cwaits = []
        self.dwaits = []
        self.sig = False
        self.sigidx = 0
        self.dsem = None
        self.dval = 0


class FW:
    def __init__(self, nc):
        self.nc = nc
        self.ops = {e: [] for e in ENGS}
        self.dsems = []
        self.nops = 0

    def dsem(self, name):
        d = DSem(name)
        self.dsems.append(d)
        return d

    def _deps(self, o, reads, writes):
        deps = []
        seen = set()
        for r in reads:
            if r.w is not None and id(r.w) not in seen:
                seen.add(id(r.w)); deps.append(r.w)
        for w in writes:
            if w.w is not None and id(w.w) not in seen:
                seen.add(id(w.w)); deps.append(w.w)
            for x in w.r:
                if id(x) not in seen:
                    seen.add(id(x)); deps.append(x)
        for d in deps:
            if d is o:
                continue
            if d.dsem is not None:
                o.dwaits.append((d.dsem, d.dsem.count))
            else:
                if o.eng == "pe" and d.eng == "pe":
                    continue
                d.sig = True
                o.cwaits.append(d)
        for r in reads:
            r.r.append(o)
        for w in writes:
            w.w = o
            w.r = []

    def op(self, eng, fn, reads=(), writes=()):
        o = Op(eng, fn)
        self._deps(o, reads, writes)
        self.ops[eng].append(o)
        self.nops += 1
        return o

    def dma(self, eng, dsem, out_ap, in_ap, reads=(), writes=(), slow=False):
        if slow:
            def fn(e):
                return e.dma_start(out=out_ap, in_=in_ap, allow_slow_non_contiguous=True)
        else:
            def fn(e):
                return e.dma_start(out=out_ap, in_=in_ap)
        o = Op(eng, fn)
        self._deps(o, reads, writes)
        dsem.count += 16
        o.dsem = dsem
        o.dval = dsem.count
        self.ops[eng].append(o)
        self.nops += 1
        return o

    def emit(self):
        nc = self.nc
        with contextlib.ExitStack() as es:
            csem = {}
            for e in ENGS:
                csem[e] = es.enter_context(nc.semaphore("c_" + e))
            for d in self.dsems:
                if d.count > 0:
                    d.sem = es.enter_context(nc.semaphore("d_" + d.name))
            for e in ENGS:
                c = 0
                for o in self.ops[e]:
                    if o.sig and o.dsem is None:
                        c += 1
                        o.sigidx = c
            block = es.enter_context(nc.Block())
            final_d = [(d.sem, d.count) for d in self.dsems if d.count > 0]

            def run(e, engobj, last=False):
                seen = {}
                for o in self.ops[e]:
                    need = {}
                    for p in o.cwaits:
                        s = csem[p.eng]
                        v = p.sigidx
                        if seen.get(id(s), 0) < v:
                            need[id(s)] = (s, max(v, need.get(id(s), (s, 0))[1]))
                            seen[id(s)] = v
                    for (d, v) in o.dwaits:
                        if seen.get(id(d), 0) < v:
                            need[id(d)] = (d.sem, max(v, need.get(id(d), (d.sem, 0))[1]))
                            seen[id(d)] = v
                    need = list(need.values())
                    attach = need.pop() if (need and ATTACH_WAIT and o.dsem is None and e != "pe") else None
                    for (s, v) in need:
                        engobj.wait_ge(s, v)
                    ins = o.fn(engobj)
                    if attach is not None:
                        ins._wait_ge(attach[0], attach[1])
                    if o.dsem is not None:
                        ins.then_inc(o.dsem.sem, 16)
                    elif o.sig:
                        ins.then_inc(csem[e], 1)
                if last:
                    for (s, v) in final_d:
                        engobj.wait_ge(s, v)

            @block.tensor
            def _(pe):
                run("pe", pe)

            @block.scalar
            def _(act):
                run("act", act)

            @block.vector
            def _(dve):
                run("dve", dve)

            @block.gpsimd
            def _(pool):
                run("pool", pool)

            @block.sync
            def _(sp):
                run("sp", sp, last=True)


IN_NAMES = ["x", "xs", "ck", "cv", "c0", "n0", "m0", "conv0", "cmk", "cmv", "mem",
            "w_in", "w_out", "w_mq", "w_mk", "w_mv", "w_mo", "w_gate", "w_up", "w_down",
            "gpk", "lamv", "gda", "bgate", "gml", "convw", "gfinal", "identf", "masks"]


def build(nblk=8, sample=True, phases=("inproj",), stop=99, do_mem=True, debug=False):
    nc = bass.Bass("TRN2", target_bir_lowering=False)

    def din(name, shape):
        return nc.dram_tensor(name, shape, F32, kind="ExternalInput").ap()

    def dout(name, shape):
        return nc.dram_tensor(name, shape, F32, kind="ExternalOutput").ap()

    x = din("x", [T, D]); xs = din("xs", [TS, D])
    ck = din("ck", [T, 1024]); cv = din("cv", [T, 1024])
    c0 = din("c0", [4, 256, 256]); n0 = din("n0", [4, 256]); m0 = din("m0", [4, 1])
    conv0 = din("conv0", [2, DFF])
    cmk = din("cmk", [MEM, D]); cmv = din("cmv", [MEM, D]); mem = din("mem", [MEM, D])
    w_in = din("w_in", [D, DIN]); w_out = din("w_out", [D, D]); w_mq = din("w_mq", [D, D])
    w_mk = din("w_mk", [D, D]); w_mv = din("w_mv", [D, D]); w_mo = din("w_mo", [D, D])
    w_gate = din("w_gate", [D, DFF]); w_up = din("w_up", [D, DFF]); w_down = din("w_down", [DFF, D])
    gpk = din("gpk", [128, 4, 16])
    lamv = din("lamv", [1, 256])
    gda = din("gda", [128, 1])
    bgate = din("bgate", [4, 2])
    gml = din("gml", [1024])
    convw = din("convw", [128, 4, 43])
    gfinal = din("gfinal", [D])
    identf = din("identf", [128, 128])
    masks = din("masks", [128, 4, 512])
    esel = din("esel", [4, 512])
    maskml = din("maskml", [128, 64])

    y = dout("y", [T, D]); ys = dout("ys", [TS, D])
    pk = dout("pk", [T, 1024]); pv = dout("pv", [T, 1024])
    pc = dout("pc", [4, 256, 256]); pn = dout("pn", [4, 256]); pm = dout("pm", [4, 1])
    pconv = dout("pconv", [2, DFF])
    pmk = dout("pmk", [MEM, D]); pmv = dout("pmv", [MEM, D])
    sk = dout("sk", [TS, 1024]); sv = dout("sv", [TS, 1024])
    sc = dout("sc", [4, 256, 256]); sn = dout("sn", [4, 256]); sm = dout("sm", [4, 1])
    sconv = dout("sconv", [2, DFF])

    NKT = 33
    ktS = [nc.dram_tensor("ktS%d" % i, [NH, 128, NKT * 128], BF16, kind="Internal").ap() for i in range(2)]
    vS = [nc.dram_tensor("vS%d" % i, [NH, 128, NKT, 128], BF16, kind="Internal").ap() for i in range(2)]
    mkS = nc.dram_tensor("mkS", [2, 128, 16 * 256], BF16, kind="Internal").ap()
    mvS = nc.dram_tensor("mvS", [2, 128, 2 * 2048], BF16, kind="Internal").ap()

    dbg = nc.dram_tensor("dbg", [16, 128, T + TS], F32, kind="ExternalOutput").ap() if debug else None
    WSPEC = {"w_in": (w_in, 16, 15), "w_out": (w_out, 16, 4), "w_mq": (w_mq, 16, 4), "w_mk": (w_mk, 16, 4), "w_mv": (w_mv, 16, 4),
             "w_mo": (w_mo, 16, 4), "w_gate": (w_gate, 16, 11), "w_up": (w_up, 16, 11), "w_down": (w_down, 43, 12)}
    WB = {k: nc.dram_tensor("wb_" + k, [v[2], 128, 8192], BF16, kind="Internal").ap() for k, v in WSPEC.items()}
    fw = FW(nc)
    es = contextlib.ExitStack()
    with es:
        def sb(name, shape, dt):
            return es.enter_context(nc.sbuf_tensor(name, shape, dt))

        xres = sb("xres", [128, 4, D], F32); R_x = [Reg("x%d" % i) for i in range(4)]
        xn = sb("xn", [128, D], BF16); R_xn = Reg("xn")
        hT = sb("hT", [128, 16, 512], BF16); R_hT = [Reg("hT%d" % i) for i in range(16)]
        NSLOT = 2
        wsl = [sb("wsl%d" % i, [128, 8192], BF16) for i in range(NSLOT)]
        R_w = [Reg("w%d" % i) for i in range(NSLOT)]
        D_w = [fw.dsem("w%d" % i) for i in range(NSLOT)]
        NU = 64
        AR = sb("AR", [128, NU * 512], BF16); R_u = [Reg("u%d" % i) for i in range(NU)]
        NF = 8
        ARF = sb("ARF", [128, NF, 512], F32); R_f = [Reg("f%d" % i) for i in range(NF)]
        D_f = [fw.dsem("f%d" % i) for i in range(NF)]
        identF = sb("identF", [128, 128], F32); identB = sb("identB", [128, 128], BF16)
        R_c = Reg("consts")
        gpk_t = sb("gpk_t", [128, 4, 16], F32)
        stat = sb("stat", [128, 64], F32); R_stat = Reg("stat")
        D_x = [fw.dsem("x%d" % i) for i in range(4)]
        _du = {}

        def D_unit(i, q="sp"):
            if (i, q) not in _du:
                _du[(i, q)] = fw.dsem("u%d%s" % (i, q))
            return _du[(i, q)]

        PS = [es.enter_context(nc.psum_tensor("ps%d" % i, [128, 512], F32)) for i in range(6)]
        R_ps = [Reg("ps%d" % i) for i in range(6)]
        PTB = [es.enter_context(nc.psum_tensor("pt%d" % i, [128, 1024], BF16)) for i in range(2)]
        R_pt = [Reg("pt0"), Reg("pt1")]

        def U(i, n=1):
            return AR[:, i * 512:(i + n) * 512]

        D_c = fw.dsem("consts")
        fw.dma("sp", D_c, identF[:], identf[:, :], writes=[R_c])
        D_c2 = fw.dsem("consts2")
        fw.dma("pool", D_c2, identB[:], identf[:, :], writes=[R_c])
        fw.dma("sp", D_c, gpk_t[:], gpk[:, :, :], writes=[R_c])

        maskB = sb("maskB", [128, 4, 512], BF16)
        onesB = sb("onesB", [128, 128], BF16)
        lam_t = sb("lam_t", [128, 256], F32)
        lam_s = sb("lam_s", [128, 8], F32)
        gda_t = sb("gda_t", [128, 2], F32)
        D_c3 = fw.dsem("consts3")
        for j in range(4):
            fw.dma("pool", D_c3, maskB[:, j, :], masks[:, j, :], writes=[R_c])
        fw.dma("sp", D_c, lam_t[:], lamv[0:1, :].partition_broadcast(128) if False else lamv.partition_broadcast(128), writes=[R_c])
        fw.dma("sp", D_c, gda_t[:, 0:1], gda[:, :], writes=[R_c])
        fw.op("dve", lambda e: e.memset(onesB[:], 1.0), writes=[R_c])
        onesF = sb("onesF", [128, 128], F32)
        fw.op("dve", lambda e: e.memset(onesF[:], 1.0), writes=[R_c])
        fw.op("dve", lambda e: e.tensor_tensor(out=lam_t[:, 0:64], in0=lam_t[:, 0:64], in1=lam_t[:, 64:128], op=ALU.mult), reads=[R_c], writes=[R_c])
        fw.op("dve", lambda e: e.tensor_tensor(out=lam_t[:, 128:192], in0=lam_t[:, 128:192], in1=lam_t[:, 192:256], op=ALU.mult), reads=[R_c], writes=[R_c])
        fw.op("dve", lambda e: e.reduce_sum(out=lam_s[:, 0:1], in_=lam_t[:, 0:64], axis=AX.X), reads=[R_c], writes=[R_c])
        fw.op("dve", lambda e: e.reduce_sum(out=lam_s[:, 1:2], in_=lam_t[:, 128:192], axis=AX.X), reads=[R_c], writes=[R_c])
        fw.op("act", lambda e: e.activation(out=lam_s[:, 2:4], in_=lam_s[:, 0:2], func=AF.Exp), reads=[R_c], writes=[R_c])
        fw.op("dve", lambda e: e.tensor_tensor(out=lam_s[:, 4:5], in0=lam_s[:, 3:4], in1=lam_s[:, 2:3], op=ALU.subtract), reads=[R_c], writes=[R_c])
        fw.op("dve", lambda e: e.tensor_scalar(out=lam_s[:, 5:6], in0=lam_s[:, 4:5], scalar1=-LAM_INIT, scalar2=None, op0=ALU.add), reads=[R_c], writes=[R_c])
        fw.op("dve", lambda e: e.tensor_scalar(out=gda_t[:, 1:2], in0=gda_t[:, 0:1], scalar1=1.0 - LAM_INIT, scalar2=None, op0=ALU.mult), reads=[R_c], writes=[R_c])
        neglam = lam_s[:, 5:6]
        gda_s = gda_t[:, 1:2]
        R_ktS = [[Reg("ktS%d_%d" % (i, h)) for h in range(NH)] for i in range(2)]
        R_vS = [[Reg("vS%d_%d" % (i, h)) for h in range(NH)] for i in range(2)]
        D_h = fw.dsem("hist")
        uKTH = 24; uVH = 33; uPT = 42; uSQ = 46; uCAT = 48; uQT_ = 0
        D_dbg = fw.dsem("dbg")
        D_cp = [fw.dsem("cp%d" % i) for i in range(4)]

        esel_t = sb("esel_t", [4, 512], F32)
        maskml_t = sb("maskml_t", [128, 64], F32)
        bg_t = sb("bg_t", [4, 4], F32)
        gml_bc = sb("gml_bc", [128, 1024], F32)
        CT = sb("CT", [128, 8, 257], F32); R_CT = [Reg("CT%d" % j) for j in range(8)]
        CTb = sb("CTb", [128, 8, 257], BF16); R_CTb = [Reg("CTb%d" % j) for j in range(8)]
        carry = sb("carry", [4, 16], F32); R_carry = Reg("carry")
        gcol = sb("gcol", [128, 64], F32); R_gcol = Reg("gcol")
        GL = sb("GL", [128, 32], F32); R_GL = Reg("GL")
        mlt = sb("mlt", [128, 1024], F32)
        R_wT = Reg("wT"); R_HN = Reg("HN"); R_tmpN = Reg("tmpN"); R_dd = Reg("dd"); R_ssml = Reg("ssml")
        fw.dma("sp", D_c, esel_t[:], esel[:, :], writes=[R_c])
        fw.dma("sp", D_c, maskml_t[:], maskml[:, :], writes=[R_c])
        fw.dma("sp", D_c, bg_t[:, 0:2], bgate[:, :], writes=[R_c])
        fw.dma("sp", D_c, gml_bc[:], gml.partition_broadcast(128), writes=[R_c])
        fw.op("dve", lambda e: e.tensor_scalar(out=bg_t[:, 2:3], in0=bg_t[:, 1:2], scalar1=-1.0, scalar2=None, op0=ALU.mult), reads=[R_c], writes=[R_c])
        D_st = fw.dsem("mlstate")
        gfinal_bc = sb("gfinal_bc", [128, D], F32)
        convw_t = sb("convw_t", [128, 4, 43], F32)
        halo = sb("halo", [128, 2, 43], F32); R_halo = Reg("halo")
        fw.dma("sp", D_c, gfinal_bc[:], gfinal.partition_broadcast(128), writes=[R_c])
        fw.dma("sp", D_c, convw_t[:], convw[:, :, :], writes=[R_c])
        R_mkS = [Reg("mkS0"), Reg("mkS1")]; R_mvS = [Reg("mvS0"), Reg("mvS1")]
        D_mem = fw.dsem("memload"); D_memp = fw.dsem("memloadp")

        state = {"slot": 0, "ps": 0, "f": 0}

        def next_slot():
            s = state["slot"]; state["slot"] = (s + 1) % NSLOT
            return s

        def next_ps():
            p = state["ps"]; state["ps"] = (p + 1) % 4
            return p

        def next_f():
            f = state["f"]; state["f"] = (f + 1) % (NF - 4)
            return f

        R_wb = {k: Reg("wb_" + k) for k in WSPEC}
        D_wb = {k: fw.dsem("wb_" + k) for k in WSPEC}
        wname = {id(v[0].tensor): k for k, v in WSPEC.items()}

        def slab_index(name, c0_, r0):
            if name == "w_down":
                return (c0_ // 512) * 3 + (r0 // 2048)
            return c0_ // 512

        cvq = {"q": 0}

        def convert_weight(name):
            w, K, nsl = WSPEC[name]
            ncol_tot = w.shape[1]
            if name == "w_down":
                parts = [(cg * 512, 512, k0 * 128, kc) for cg in range(4) for (k0, kc) in ((0, 16), (16, 16), (32, 11))]
            else:
                parts = [(c, min(512, ncol_tot - c), 0, 16) for c in range(0, ncol_tot, 512)]
            for (c0_, ncols, r0, kc) in parts:
                si = slab_index(name, c0_, r0)
                for k0 in range(0, kc, 4):
                    kn = min(4, kc - k0)
                    q = cvq["q"]; cvq["q"] += 1
                    buf = q % 4
                    src = w[r0 + k0 * 128:r0 + (k0 + kn) * 128, c0_:c0_ + ncols].rearrange("(k p) c -> p k c", p=128)
                    stg = xres[:, buf, 0:kn * ncols]
                    fw.dma("sp", D_x[buf], stg.rearrange("p (k c) -> p k c", k=kn), src, writes=[R_x[buf]])
                    dst = AR[:, buf * 2048: buf * 2048 + kn * ncols]
                    ru = R_u[buf * 4: buf * 4 + 4]
                    if q % 2 == 0:
                        fw.op("dve", lambda e, dst=dst, stg=stg: e.tensor_copy(out=dst, in_=stg), reads=[R_x[buf]], writes=ru)
                    else:
                        fw.op("act", lambda e, dst=dst, stg=stg: e.activation(out=dst, in_=stg, func=AF.Copy), reads=[R_x[buf]], writes=ru)
                    fw.dma("pool", D_unit(buf * 4, "pool"), WB[name][si, :, k0 * ncols:(k0 + kn) * ncols], dst, reads=ru, writes=[R_wb[name]])

        def load_slab(w, kc, c0_, ncols, r0=0):
            name = wname[id(w.tensor)]
            si = slab_index(name, c0_, r0)
            s = next_slot()
            dst = wsl[s][:, 0:kc * ncols].rearrange("p (k c) -> p k c", k=kc)
            fw.dma("pool", D_w[s], wsl[s][:, 0:kc * ncols], WB[name][si, :, 0:kc * ncols], reads=[R_wb[name]], writes=[R_w[s]])
            return s, dst

        def norm_to_hT(which, ntok, src=None):
            nt = (ntok + 127) // 128
            for tt in range(nt):
                rows = min(128, ntok - tt * 128)
                sc_ = 4 * tt
                if stop < -2:
                    continue
                fw.op("dve", lambda e, c=sc_: e.memset(stat[:, c:c + 2], 0.0), writes=[R_stat])
                fw.op("act", lambda e, tt=tt, rows=rows, c=sc_: e.activation(
                    out=xn[:rows, :], in_=xres[:rows, tt, :], func=AF.Square, accum_out=stat[:rows, c:c + 1]),
                    reads=[R_x[tt]], writes=[R_xn, R_stat])
                fw.op("act", lambda e, rows=rows, c=sc_: e.activation(
                    out=stat[:rows, c + 1:c + 2], in_=stat[:rows, c:c + 1], func=AF.Sqrt, bias=EPS, scale=1.0 / D),
                    reads=[R_stat], writes=[R_stat])
                fw.op("dve", lambda e, rows=rows, c=sc_: e.reciprocal(out=stat[:rows, c + 2:c + 3], in_=stat[:rows, c + 1:c + 2]),
                      reads=[R_stat], writes=[R_stat])
                fw.op("act", lambda e, tt=tt, rows=rows, c=sc_: e.activation(
                    out=xn[:rows, :], in_=xres[:rows, tt, :], func=AF.Copy, scale=stat[:rows, c + 2:c + 3]),
                    reads=[R_x[tt], R_stat], writes=[R_xn])
                for g4 in range(4):
                    if stop < -1:
                        continue
                    half = g4 % 2
                    for j in range(4):
                        kc = g4 * 4 + j
                        fw.op("pe", lambda e, kc=kc, j=j, half=half, rows=rows: e.transpose(
                            out=PTB[half][:, j * 128: j * 128 + rows],
                            in_=xn[:rows, kc * 128:(kc + 1) * 128], identity=identB[:rows, :rows]),
                            reads=[R_xn, R_c], writes=[R_pt[half]])
                    for j in range(4):
                        if stop < 0:
                            continue
                        kc = g4 * 4 + j
                        eng = "dve" if half == 0 else "act"
                        o_ap = hT[:, kc, tt * 128: tt * 128 + rows]
                        i_ap = PTB[half][:, j * 128: j * 128 + rows]
                        g_ap = gpk_t[:, which, kc:kc + 1]
                        if eng == "dve":
                            fw.op("dve", lambda e, o_ap=o_ap, i_ap=i_ap, g_ap=g_ap: e.tensor_scalar(
                                out=o_ap, in0=i_ap, scalar1=g_ap, scalar2=None, op0=ALU.mult),
                                reads=[R_pt[half], R_c], writes=[R_hT[kc]])
                        else:
                            fw.op("act", lambda e, o_ap=o_ap, i_ap=i_ap, g_ap=g_ap: e.activation(
                                out=o_ap, in_=i_ap, func=AF.Copy, scale=g_ap),
                                reads=[R_pt[half], R_c], writes=[R_hT[kc]])

        def proj_TM(slab, kc_n, ncols, actT, actR, ntok, consume, k0=0, acc=None, first=True, last=True):
            s, wv = slab
            nt = (ntok + 127) // 128
            for tt in range(nt):
                rows = min(128, ntok - tt * 128)
                pb = acc[tt] if acc is not None else next_ps()
                for k in range(kc_n):
                    fw.op("pe", lambda e, tt=tt, rows=rows, pb=pb, k=k: e.matmul(
                        PS[pb][:rows, 0:ncols], lhsT=actT(k0 + k)[:, tt * 128: tt * 128 + rows], rhs=wv[:, k, :],
                        start=(first and k == 0), stop=(last and k == kc_n - 1)),
                        reads=[actR[k0 + k], R_w[s]], writes=[R_ps[pb]])
                if last:
                    consume(tt, rows, pb)

        def proj_FM(slab, kc_n, ncols, actT, actR, ntok, consume):
            s, wv = slab
            for ch in range((ncols + 127) // 128):
                m = min(128, ncols - ch * 128)
                pb = next_ps()
                for k in range(kc_n):
                    fw.op("pe", lambda e, ch=ch, m=m, pb=pb, k=k: e.matmul(
                        PS[pb][:m, 0:ntok], lhsT=wv[:, k, ch * 128: ch * 128 + m], rhs=actT(k)[:, 0:ntok],
                        start=(k == 0), stop=(k == kc_n - 1)),
                        reads=[actR[k], R_w[s]], writes=[R_ps[pb]])
                consume(ch, m, pb)

        hT_get = lambda k: hT[:, k, :]

        def attention(seq, pos0, ntok, diag):
            nkeys = pos0 + ntok
            nkt = (nkeys + 127) // 128
            Ru_kth = R_u[uKTH:uKTH + 9]
            Ru_vh = R_u[uVH:uVH + 9]
            KTH = AR[:, uKTH * 512: uKTH * 512 + NKT * 128]
            VH = AR[:, uVH * 512: uVH * 512 + NKT * 128].rearrange("p (k e) -> p k e", e=128)
            A = ARF[:, 6, :]; Bf = ARF[:, 7, :]
            for h in range(NH):
                fw.dma("sp", D_h, KTH[:, 0:nkeys], ktS[seq][h, :, 0:nkeys], reads=[R_ktS[seq][h]], writes=Ru_kth)
                nfull = nkeys // 128
                fw.dma("sp", D_h, VH[:, 0:nfull, :], vS[seq][h, :, 0:nfull, :], reads=[R_vS[seq][h]], writes=Ru_vh)
                if nkeys % 128:
                    fw.dma("sp", D_h, VH[0:nkeys % 128, nfull, :], vS[seq][h, 0:nkeys % 128, nfull, :], reads=[R_vS[seq][h]], writes=Ru_vh)
                steps = [(c, kt) for kt in range(nkt) for c in range(2)]
                SB = (0, 1, 4, 5)
                ACC = (ARF[:, 4, :], ARF[:, 5, :]); R_acc = (R_f[4], R_f[5])

                def s_step(i):
                    c, kt = steps[i]
                    kw = min(128, nkeys - kt * 128)
                    sbk = SB[i % 4]
                    pu = uPT + (i % 4)
                    fw.op("pe", lambda e, c=c, kt=kt, kw=kw, sbk=sbk, h=h: e.matmul(
                        PS[sbk][:kw, 0:ntok], lhsT=KTH[c * 64:(c + 1) * 64, kt * 128: kt * 128 + kw],
                        rhs=U(uQT_ + h)[c * 64:(c + 1) * 64, 0:ntok], start=True, stop=True),
                        reads=Ru_kth + [R_u[uQT_ + h]], writes=[R_ps[sbk]])
                    fw.op("act", lambda e, kw=kw, sbk=sbk, pu=pu: e.activation(
                        out=U(pu)[:kw, 0:ntok], in_=PS[sbk][:kw, 0:ntok], func=AF.Exp, scale=0.125),
                        reads=[R_ps[sbk]], writes=[R_u[pu]])
                    j = kt - (nkt - 4)
                    if diag and j >= 0:
                        fw.op("dve", lambda e, pu=pu, j=j: e.tensor_tensor(
                            out=U(pu)[:, 0:ntok], in0=U(pu)[:, 0:ntok], in1=maskB[:, j, 0:ntok], op=ALU.mult),
                            reads=[R_u[pu], R_c], writes=[R_u[pu]])
                    if kt == 0:
                        if kw < 128:
                            fw.op("dve", lambda e, c=c: e.memset(ACC[c][:, 0:ntok], 0.0), writes=[R_acc[c]])
                        fw.op("dve", lambda e, c=c, kw=kw, pu=pu: e.tensor_copy(out=ACC[c][:kw, 0:ntok], in_=U(pu)[:kw, 0:ntok]),
                              reads=[R_u[pu]], writes=[R_acc[c]])
                    else:
                        fw.op("dve", lambda e, c=c, kw=kw, pu=pu: e.tensor_tensor(
                            out=ACC[c][:kw, 0:ntok], in0=ACC[c][:kw, 0:ntok], in1=U(pu)[:kw, 0:ntok], op=ALU.add),
                            reads=[R_u[pu], R_acc[c]], writes=[R_acc[c]])

                def av_step(i):
                    c, kt = steps[i]
                    kw = min(128, nkeys - kt * 128)
                    pu = uPT + (i % 4)
                    fw.op("pe", lambda e, c=c, kt=kt, kw=kw, pu=pu: e.matmul(
                        PS[2 + c][:, 0:ntok], lhsT=VH[:kw, kt, :], rhs=U(pu)[:kw, 0:ntok],
                        start=(kt == 0), stop=(kt == nkt - 1)),
                        reads=Ru_vh + [R_u[pu]], writes=[R_ps[2 + c]])

                LA = 3
                for i in range(len(steps) + LA):
                    if i < len(steps):
                        s_step(i)
                    if i - LA >= 0:
                        av_step(i - LA)
                n = ntok
                for c in range(2):
                    fw.op("pe", lambda e, c=c: e.matmul(PS[SB[c]][:, 0:n], lhsT=onesF[:, :], rhs=ACC[c][:, 0:n], start=True, stop=True),
                          reads=[R_c, R_acc[c]], writes=[R_ps[SB[c]]])
                fw.op("act", lambda e: e.activation(out=A[:, 0:n], in_=PS[SB[0]][:, 0:n], func=AF.Ln), reads=[R_ps[SB[0]]], writes=[R_f[6]])
                fw.op("act", lambda e: e.activation(out=A[:, 0:n], in_=A[:, 0:n], func=AF.Exp, scale=-1.0), reads=[R_f[6]], writes=[R_f[6]])
                fw.op("act", lambda e: e.activation(out=Bf[:, 0:n], in_=PS[SB[1]][:, 0:n], func=AF.Ln), reads=[R_ps[SB[1]]], writes=[R_f[7]])
                fw.op("act", lambda e: e.activation(out=Bf[:, 0:n], in_=Bf[:, 0:n], func=AF.Exp, scale=-1.0), reads=[R_f[7]], writes=[R_f[7]])
                fw.op("dve", lambda e: e.tensor_tensor(out=A[:, 0:n], in0=PS[2][:, 0:n], in1=A[:, 0:n], op=ALU.mult),
                      reads=[R_ps[2], R_f[6]], writes=[R_f[6]])
                fw.op("dve", lambda e: e.tensor_tensor(out=Bf[:, 0:n], in0=PS[3][:, 0:n], in1=Bf[:, 0:n], op=ALU.mult),
                      reads=[R_ps[3], R_f[7]], writes=[R_f[7]])
                fw.op("dve", lambda e: e.scalar_tensor_tensor(out=A[:, 0:n], in0=Bf[:, 0:n], scalar=neglam, in1=A[:, 0:n],
                                                              op0=ALU.mult, op1=ALU.add),
                      reads=[R_f[6], R_f[7], R_c], writes=[R_f[6]])
                fw.op("dve", lambda e: e.tensor_tensor(out=U(uSQ)[:, 0:n], in0=A[:, 0:n], in1=A[:, 0:n], op=ALU.mult),
                      reads=[R_f[6]], writes=[R_u[uSQ]])
                fw.op("pe", lambda e: e.matmul(PS[4][:, 0:n], lhsT=onesB[:, :], rhs=U(uSQ)[:, 0:n], start=True, stop=True),
                      reads=[R_c, R_u[uSQ]], writes=[R_ps[4]])
                fw.op("act", lambda e: e.activation(out=Bf[:, 0:n], in_=PS[4][:, 0:n], func=AF.Ln, bias=EPS, scale=1.0 / 128),
                      reads=[R_ps[4]], writes=[R_f[7]])
                fw.op("act", lambda e: e.activation(out=Bf[:, 0:n], in_=Bf[:, 0:n], func=AF.Exp, scale=-0.5), reads=[R_f[7]], writes=[R_f[7]])
                fw.op("dve", lambda e, h=h: e.scalar_tensor_tensor(out=U(uCAT + h)[:, 0:n], in0=A[:, 0:n], scalar=gda_s, in1=Bf[:, 0:n],
                                                                   op0=ALU.mult, op1=ALU.mult),
                      reads=[R_f[6], R_f[7], R_c], writes=[R_u[uCAT + h]])
                if dbg is not None:
                    fw.dma("pool", D_dbg, dbg[h, :, pos0:pos0 + n], U(uCAT + h)[:, 0:n], reads=[R_u[uCAT + h]])

        def cache_prologue():
            for kt in range(T // 128):
                uk = 0 + (kt % 2) * 2
                uv = 4 + (kt % 2) * 2
                ut = 8 + (kt % 2) * 2
                fw.dma("pool", D_unit(uk, "pool"), AR[:, uk * 512:(uk + 2) * 512], ck[kt * 128:(kt + 1) * 128, :], writes=R_u[uk:uk + 2])
                fw.dma("pool", D_unit(uv, "pool"), AR[:, uv * 512:(uv + 2) * 512], cv[kt * 128:(kt + 1) * 128, :], writes=R_u[uv:uv + 2])
                for hh in range(NH):
                    fw.dma("sp", D_unit(uv), vS[1][hh, :, kt, :], AR[:, uv * 512 + hh * 128: uv * 512 + (hh + 1) * 128],
                           reads=R_u[uv:uv + 2], writes=[R_vS[1][hh]])
                for g in range(2):
                    for j in range(4):
                        hh = g * 4 + j
                        fw.op("pe", lambda e, g=g, j=j, hh=hh, uk=uk: e.transpose(
                            out=PTB[g][:, j * 128:(j + 1) * 128], in_=AR[:, uk * 512 + hh * 128: uk * 512 + (hh + 1) * 128],
                            identity=identB[:, :]), reads=R_u[uk:uk + 2] + [R_c], writes=[R_pt[g]])
                    eng = "dve" if g == 0 else "act"
                    o_ap = AR[:, (ut + g) * 512:(ut + g + 1) * 512]
                    if g == 0:
                        fw.op("dve", lambda e, o_ap=o_ap, g=g: e.tensor_copy(out=o_ap, in_=PTB[g][:, 0:512]),
                              reads=[R_pt[g]], writes=[R_u[ut + g]])
                    else:
                        fw.op("act", lambda e, o_ap=o_ap, g=g: e.activation(out=o_ap, in_=PTB[g][:, 0:512], func=AF.Copy),
                              reads=[R_pt[g]], writes=[R_u[ut + g]])
                    for j in range(4):
                        hh = g * 4 + j
                        fw.dma("sp", D_unit(ut + g), ktS[1][hh, :, kt * 128:(kt + 1) * 128], AR[:, (ut + g) * 512 + j * 128:(ut + g) * 512 + (j + 1) * 128],
                               reads=[R_u[ut + g]], writes=[R_ktS[1][hh]])

        uMQ = 0; uMK = 8; uMKT = 16; uMV = 24; uMO = 33; uST = 41; uVW = 42
        mvA = AR[:, uMV * 512: uMV * 512 + 16 * 257].rearrange("p (a e) -> p a e", e=257)
        Ru_mv = R_u[uMV:uMV + 9]

        def ml_init_zero():
            fw.op("dve", lambda e: e.memset(CT[:], 0.0), writes=R_CT)
            fw.op("dve", lambda e: e.memset(CTb[:], 0.0), writes=R_CTb)
            fw.op("dve", lambda e: e.memset(carry[:], 0.0), writes=[R_carry])

        def ml_init_state():
            for hh in range(4):
                for ec in range(2):
                    f = next_f()
                    fw.dma("sp", D_f[f], ARF[:, f, 0:256], c0[hh, ec * 128:(ec + 1) * 128, :], writes=[R_f[f]])
                    for dc in range(2):
                        pb = next_ps()
                        fw.op("pe", lambda e, f=f, dc=dc, pb=pb: e.transpose(out=PS[pb][:, 0:128], in_=ARF[:, f, dc * 128:(dc + 1) * 128],
                                                                        identity=identF[:, :]), reads=[R_f[f], R_c], writes=[R_ps[pb]])
                        j = hh * 2 + dc
                        fw.op("dve", lambda e, j=j, ec=ec, pb=pb: e.tensor_copy(out=CT[:, j, ec * 128:(ec + 1) * 128], in_=PS[pb][:, 0:128]),
                              reads=[R_ps[pb]], writes=[R_CT[j]])
            fw.dma("sp", D_st, CT[:, :, 256], n0.rearrange("h (dc p) -> p (h dc)", p=128), writes=R_CT, slow=True)
            fw.dma("sp", D_st, carry[:, 0:1], m0[:, :], writes=[R_carry])
            fw.op("act", lambda e: e.activation(out=CTb[:], in_=CT[:], func=AF.Copy), reads=R_CT, writes=R_CTb)

        def ml_out_state(oc, on, om):
            for hh in range(4):
                for ec in range(2):
                    pb = next_ps()
                    for dc in range(2):
                        j = hh * 2 + dc
                        fw.op("pe", lambda e, j=j, ec=ec, dc=dc, pb=pb: e.transpose(
                            out=PS[pb][:, dc * 128:(dc + 1) * 128], in_=CT[:, j, ec * 128:(ec + 1) * 128], identity=identF[:, :]),
                            reads=[R_CT[j], R_c], writes=[R_ps[pb]])
                    f = next_f()
                    fw.op("dve", lambda e, f=f, pb=pb: e.tensor_copy(out=ARF[:, f, 0:256], in_=PS[pb][:, 0:256]), reads=[R_ps[pb]], writes=[R_f[f]])
                    fw.dma("sp", D_f[f], oc[hh, ec * 128:(ec + 1) * 128, :], ARF[:, f, 0:256], reads=[R_f[f]])
            fw.dma("sp", D_st, on.rearrange("h (dc p) -> p (h dc)", p=128), CT[:, :, 256], reads=R_CT, slow=True)
            fw.dma("sp", D_st, om[:, :], carry[:, 0:1], reads=[R_carry])

        def mlstm(seq, tok0, ntok, L):
            nt = (ntok + 127) // 128
            nch = ntok // L
            for half in range(2):
                slab = load_slab(w_in, 16, 3072 + half * 512, 512)

                def c_mq(ch, m, pb, half=half):
                    u = uMQ + half * 4 + ch
                    fw.op("act", lambda e, u=u, pb=pb: e.activation(out=U(u)[:, 0:ntok], in_=PS[pb][:, 0:ntok], func=AF.Copy),
                          reads=[R_ps[pb]], writes=[R_u[u]])
                proj_FM(slab, 16, 512, hT_get, R_hT, ntok, c_mq)
            for half in range(2):
                slab = load_slab(w_in, 16, 4096 + half * 512, 512)

                def c_mkT(ch, m, pb, half=half):
                    u = uMK + half * 4 + ch
                    fw.op("act", lambda e, u=u, pb=pb: e.activation(out=U(u)[:, 0:ntok], in_=PS[pb][:, 0:ntok], func=AF.Copy, scale=1.0 / 16),
                          reads=[R_ps[pb]], writes=[R_u[u]])
                proj_FM(slab, 16, 512, hT_get, R_hT, ntok, c_mkT)

                def c_mk(tt, rows, pb, half=half):
                    u = uMKT + tt * 2 + half
                    fw.op("act", lambda e, u=u, pb=pb, rows=rows: e.activation(out=U(u)[:rows, :], in_=PS[pb][:rows, :], func=AF.Copy, scale=1.0 / 16),
                          reads=[R_ps[pb]], writes=[R_u[u]])
                proj_TM(slab, 16, 512, hT_get, R_hT, ntok, c_mk)
            fw.op("dve", lambda e: e.memset(mvA[:, :, 256:257], 1.0), writes=Ru_mv)
            for half in range(2):
                slab = load_slab(w_in, 16, 5120 + half * 512, 512)

                def c_mv(tt, rows, pb, half=half):
                    fw.op("act", lambda e, tt=tt, pb=pb, rows=rows, half=half: e.activation(
                        out=mvA[:rows, tt * 4 + half * 2: tt * 4 + half * 2 + 2, 0:256],
                        in_=PS[pb][:rows, :].rearrange("p (a e) -> p a e", e=256), func=AF.Copy),
                        reads=[R_ps[pb]], writes=Ru_mv)
                proj_TM(slab, 16, 512, hT_get, R_hT, ntok, c_mv)
            for half in range(2):
                slab = load_slab(w_in, 16, 6144 + half * 512, 512)

                def c_mo(tt, rows, pb, half=half):
                    u = uMO + tt * 2 + half
                    fw.op("act", lambda e, u=u, pb=pb, rows=rows: e.activation(out=U(u)[:rows, :], in_=PS[pb][:rows, :], func=AF.Sigmoid),
                          reads=[R_ps[pb]], writes=[R_u[u]])
                proj_TM(slab, 16, 512, hT_get, R_hT, ntok, c_mo)
            slab = load_slab(w_in, 16, 7168, 8)
            s_, wv = slab
            n = ntok
            row = lambda f: ARF[0:4, f, 0:n]
            for gi in range(2):
                pb = next_ps()
                for k in range(16):
                    fw.op("pe", lambda e, gi=gi, pb=pb, k=k: e.matmul(PS[pb][0:4, 0:n], lhsT=wv[:, k, gi * 4:(gi + 1) * 4], rhs=hT[:, k, 0:n],
                                                                     start=(k == 0), stop=(k == 15)),
                          reads=[R_hT[k], R_w[s_]], writes=[R_ps[pb]])
                if gi == 0:
                    fw.op("dve", lambda e, pb=pb: e.tensor_scalar(out=row(0), in0=PS[pb][0:4, 0:n], scalar1=bg_t[:, 0:1], scalar2=None, op0=ALU.add),
                          reads=[R_ps[pb], R_c], writes=[R_f[0]])
                else:
                    fw.op("act", lambda e, pb=pb: e.activation(out=row(1), in_=PS[pb][0:4, 0:n], func=AF.Exp, scale=-1.0, bias=bg_t[:, 2:3]),
                          reads=[R_ps[pb], R_c], writes=[R_f[1]])
            fw.op("act", lambda e: e.activation(out=row(1), in_=row(1), func=AF.Ln, bias=1.0), reads=[R_f[1]], writes=[R_f[1]])
            fw.op("dve", lambda e: e.tensor_scalar(out=row(1), in0=row(1), scalar1=-1.0, scalar2=None, op0=ALU.mult), reads=[R_f[1]], writes=[R_f[1]])
            fw.op("dve", lambda e: e.memset(row(7), 0.0), writes=[R_f[7]])
            fw.op("dve", lambda e: e.tensor_tensor_scan(out=row(2), data0=row(1), data1=row(0), initial=carry[:, 0:1], op0=ALU.add, op1=ALU.max),
                  reads=[R_f[1], R_f[0], R_carry], writes=[R_f[2]])
            fw.op("dve", lambda e: e.tensor_tensor_scan(out=row(3), data0=row(1), data1=row(7), initial=0.0, op0=ALU.add, op1=ALU.add),
                  reads=[R_f[1], R_f[7]], writes=[R_f[3]])
            fw.op("dve", lambda e: e.tensor_tensor(out=row(4), in0=row(3), in1=row(2), op=ALU.subtract), reads=[R_f[3], R_f[2]], writes=[R_f[4]])
            fw.op("dve", lambda e: e.tensor_tensor(out=row(0), in0=row(0), in1=row(3), op=ALU.subtract), reads=[R_f[0], R_f[3]], writes=[R_f[0]])
            fw.op("act", lambda e: e.activation(out=row(6), in_=row(2), func=AF.Exp, scale=-1.0), reads=[R_f[2]], writes=[R_f[6]])
            fw.op("dve", lambda e: e.tensor_copy(out=carry[:, 4:5], in_=carry[:, 0:1]), reads=[R_carry], writes=[R_carry])
            for c in range(1, nch):
                fw.op("dve", lambda e, c=c: e.tensor_scalar(out=carry[:, 4 + c:5 + c], in0=ARF[0:4, 4, c * L - 1:c * L], scalar1=-1.0, scalar2=None, op0=ALU.mult),
                      reads=[R_f[4], R_carry], writes=[R_carry])
            for c in range(nch):
                cs = slice(c * L, (c + 1) * L)
                fw.op("act", lambda e, c=c, cs=cs: e.activation(out=ARF[0:4, 5, cs], in_=ARF[0:4, 4, cs], func=AF.Exp, bias=carry[:, 4 + c:5 + c]),
                      reads=[R_f[4], R_carry], writes=[R_f[5]])
                fw.op("act", lambda e, c=c, cs=cs: e.activation(out=ARF[0:4, 7, cs], in_=ARF[0:4, 0, cs], func=AF.Exp, bias=ARF[0:4, 4, (c + 1) * L - 1:(c + 1) * L]),
                      reads=[R_f[0], R_f[4]], writes=[R_f[7]])
            fw.op("dve", lambda e: e.tensor_copy(out=carry[:, 0:1], in_=ARF[0:4, 2, n - 1:n]), reads=[R_f[2]], writes=[R_carry])
            pbT = next_ps()
            for tt in range(nt):
                rows = min(128, ntok - tt * 128)
                for qi, f in enumerate((0, 7, 5, 6)):
                    o = (tt * 4 + qi) * 4
                    fw.op("pe", lambda e, tt=tt, rows=rows, f=f, o=o: e.transpose(
                        out=PS[pbT][:rows, o:o + 4], in_=ARF[0:4, f, tt * 128: tt * 128 + rows], identity=identF[0:4, 0:4]),
                        reads=[R_f[f], R_c], writes=[R_ps[pbT]])
            rows_all = 128 if ntok >= 128 else ntok
            fw.op("dve", lambda e: e.tensor_copy(out=gcol[:rows_all, 0:nt * 16], in_=PS[pbT][:rows_all, 0:nt * 16]), reads=[R_ps[pbT]], writes=[R_gcol])
            pbG = next_ps()
            for hh in range(4):
                for c in range(nch):
                    e_c = (c + 1) * L - 1
                    fw.op("pe", lambda e, hh=hh, c=c, e_c=e_c: e.matmul(PS[pbG][:, hh * 8 + c: hh * 8 + c + 1], lhsT=esel_t[0:4, hh * 128:(hh + 1) * 128],
                                                                        rhs=ARF[0:4, 5, e_c:e_c + 1], start=True, stop=True),
                          reads=[R_f[5], R_c], writes=[R_ps[pbG]])
            fw.op("dve", lambda e: e.tensor_copy(out=GL[:, :], in_=PS[pbG][:, 0:32]), reads=[R_ps[pbG]], writes=[R_GL])
            wT = mlt[:, 0:64]; HN = mlt[:, 64:321]; tmpN = mlt[:, 384:641]; ddv = mlt[:, 700:704]; ssml = mlt[:, 704:712]
            for c in range(nch):
                tt = (c * L) // 128
                po = (c * L) % 128
                P = slice(po, po + L)
                cs = slice(c * L, (c + 1) * L)
                hmf = (tt % 2) * 2
                HM = ARF[:, hmf:hmf + 2, :].rearrange("p a b -> p (a b)")
                R_hm = [R_f[hmf], R_f[hmf + 1]]
                for hh in range(4):
                    gc = lambda qi, tt=tt, hh=hh: gcol[P, (tt * 4 + qi) * 4 + hh:(tt * 4 + qi) * 4 + hh + 1]
                    for dc in range(2):
                        j = hh * 2 + dc
                        fw.op("pe", lambda e, j=j, dc=dc, P=P, cs=cs, po=po: e.matmul(
                            PS[0][P, 0:L], lhsT=U(uMK + j)[:, cs], rhs=U(uMQ + j)[:, cs], start=(dc == 0), stop=(dc == 1),
                            tile_position=(0, po)),
                            reads=[R_u[uMK + j], R_u[uMQ + j]], writes=[R_ps[0]])
                    fw.op("pe", lambda e, hh=hh, P=P, cs=cs, po=po: e.matmul(
                        PS[1][P, 0:L], lhsT=esel_t[0:4, hh * 128: hh * 128 + L], rhs=ARF[0:4, 4, cs], start=True, stop=False,
                        tile_position=(0, po)),
                        reads=[R_f[4], R_c], writes=[R_ps[1]])
                    fw.op("pe", lambda e, P=P, po=po: e.matmul(
                        PS[1][P, 0:L], lhsT=identF[P, P], rhs=maskml_t[P, 0:L], start=False, stop=True, tile_position=(po, po)),
                        reads=[R_c], writes=[R_ps[1]])
                    fw.op("act", lambda e, P=P, g0=gc(0): e.activation(out=wT[P, 0:L], in_=PS[1][P, 0:L], func=AF.Exp, bias=g0),
                          reads=[R_ps[1], R_gcol], writes=[R_wT])
                    fw.op("dve", lambda e, P=P: e.tensor_tensor(out=U(uST)[P, 0:L], in0=PS[0][P, 0:L], in1=wT[P, 0:L], op=ALU.mult),
                          reads=[R_ps[0], R_wT], writes=[R_u[uST]])
                    fw.op("pe", lambda e, P=P, tt=tt, hh=hh, po=po: e.matmul(
                        PS[2][P, 0:257], lhsT=U(uST)[P, 0:L], rhs=mvA[P, tt * 4 + hh, :], start=True, stop=True, tile_position=(po, po)),
                        reads=[R_u[uST]] + Ru_mv, writes=[R_ps[2]])
                    for dc in range(2):
                        j = hh * 2 + dc
                        fw.op("pe", lambda e, j=j, dc=dc, P=P, cs=cs, po=po: e.matmul(
                            PS[5][P, 0:257], lhsT=U(uMQ + j)[:, cs], rhs=CTb[:, j, :], start=(dc == 0), stop=(dc == 1),
                            tile_position=(0, po)),
                            reads=[R_u[uMQ + j], R_CTb[j]], writes=[R_ps[5]])
                    fw.op("act", lambda e, P=P, g2=gc(2): e.activation(out=tmpN[P, :], in_=PS[5][P, 0:257], func=AF.Copy, scale=g2),
                          reads=[R_ps[5], R_gcol], writes=[R_tmpN])
                    fw.op("dve", lambda e, P=P: e.tensor_tensor(out=HN[P, :], in0=tmpN[P, :], in1=PS[2][P, 0:257], op=ALU.add),
                          reads=[R_tmpN, R_ps[2]], writes=[R_HN])
                    fw.op("dve", lambda e, P=P: e.tensor_scalar(out=ddv[P, 2:3], in0=HN[P, 256:257], scalar1=-1.0, scalar2=None, op0=ALU.mult),
                          reads=[R_HN], writes=[R_dd])
                    fw.op("dve", lambda e, P=P: e.tensor_tensor(out=ddv[P, 2:3], in0=ddv[P, 2:3], in1=HN[P, 256:257], op=ALU.max),
                          reads=[R_HN, R_dd], writes=[R_dd])
                    fw.op("dve", lambda e, P=P, g3=gc(3): e.tensor_scalar(out=ddv[P, 0:1], in0=ddv[P, 2:3], scalar1=g3, scalar2=None, op0=ALU.max),
                          reads=[R_dd, R_gcol], writes=[R_dd])
                    fw.op("dve", lambda e, P=P: e.reciprocal(out=ddv[P, 1:2], in_=ddv[P, 0:1]), reads=[R_dd], writes=[R_dd])
                    fw.op("dve", lambda e, P=P, hh=hh, HM=HM: e.tensor_scalar(out=HM[P, hh * 256:(hh + 1) * 256], in0=HN[P, 0:256], scalar1=ddv[P, 1:2],
                                                                             scalar2=None, op0=ALU.mult),
                          reads=[R_HN, R_dd], writes=R_hm)
                    fw.op("dve", lambda e, P=P, tt=tt, hh=hh, g1=gc(1): e.tensor_scalar(out=U(uVW)[P, 0:257], in0=mvA[P, tt * 4 + hh, :], scalar1=g1,
                                                                                      scalar2=None, op0=ALU.mult),
                          reads=Ru_mv + [R_gcol], writes=[R_u[uVW]])
                    for dc in range(2):
                        j = hh * 2 + dc
                        um = uMKT + tt * 2 + (hh // 2)
                        co = (hh % 2) * 256 + dc * 128
                        fw.op("pe", lambda e, dc=dc, um=um, co=co, P=P, po=po: e.matmul(
                            PS[3 + dc][:, 0:257], lhsT=U(um)[P, co:co + 128], rhs=U(uVW)[P, 0:257], start=True, stop=True,
                            tile_position=(po, 0)),
                            reads=[R_u[um], R_u[uVW]], writes=[R_ps[3 + dc]])
                        fw.op("dve", lambda e, j=j, dc=dc, hh=hh, c=c: e.scalar_tensor_tensor(
                            out=CT[:, j, :], in0=CT[:, j, :], scalar=GL[:, hh * 8 + c: hh * 8 + c + 1], in1=PS[3 + dc][:, 0:257],
                            op0=ALU.mult, op1=ALU.add),
                            reads=[R_CT[j], R_GL, R_ps[3 + dc]], writes=[R_CT[j]])
                        fw.op("act", lambda e, j=j: e.activation(out=CTb[:, j, :], in_=CT[:, j, :], func=AF.Copy),
                              reads=[R_CT[j]], writes=[R_CTb[j]])
                if (c + 1) * L % 128 == 0 or c == nch - 1:
                    rows = min(128, ntok - tt * 128)
                    for hh in range(4):
                        fw.op("act", lambda e, hh=hh, rows=rows, HM=HM: e.activation(out=mlt[:rows, 768:1024], in_=HM[:rows, hh * 256:(hh + 1) * 256],
                                                                                    func=AF.Square, accum_out=ssml[:rows, hh:hh + 1]),
                              reads=R_hm, writes=[R_ssml])
                    fw.op("act", lambda e, rows=rows: e.activation(out=ssml[:rows, 4:8], in_=ssml[:rows, 0:4], func=AF.Sqrt, bias=EPS, scale=1.0 / 256),
                          reads=[R_ssml], writes=[R_ssml])
                    fw.op("dve", lambda e, rows=rows: e.reciprocal(out=ssml[:rows, 4:8], in_=ssml[:rows, 4:8]), reads=[R_ssml], writes=[R_ssml])
                    for hh in range(4):
                        fw.op("dve", lambda e, hh=hh, rows=rows, HM=HM: e.scalar_tensor_tensor(
                            out=HM[:rows, hh * 256:(hh + 1) * 256], in0=HM[:rows, hh * 256:(hh + 1) * 256], scalar=ssml[:rows, 4 + hh:5 + hh],
                            in1=gml_bc[:rows, hh * 256:(hh + 1) * 256], op0=ALU.mult, op1=ALU.mult),
                            reads=R_hm + [R_ssml, R_c], writes=R_hm)
                    for half in range(2):
                        u = uMO + tt * 2 + half
                        fw.op("dve", lambda e, u=u, half=half, rows=rows, HM=HM: e.tensor_tensor(
                            out=U(u)[:rows, :], in0=HM[:rows, half * 512:(half + 1) * 512], in1=U(u)[:rows, :], op=ALU.mult),
                            reads=R_hm + [R_u[u]], writes=[R_u[u]])
                    for g in range(2):
                        u = uMO + tt * 2 + g
                        for j in range(4):
                            fw.op("pe", lambda e, g=g, j=j, u=u, rows=rows: e.transpose(
                                out=PTB[g][:, j * 128: j * 128 + rows], in_=U(u)[:rows, j * 128:(j + 1) * 128], identity=identB[:rows, :rows]),
                                reads=[R_u[u], R_c], writes=[R_pt[g]])
                        for j in range(4):
                            k = 8 + g * 4 + j
                            o_ap = U(uCAT + k)[:, tt * 128: tt * 128 + rows]
                            i_ap = PTB[g][:, j * 128: j * 128 + rows]
                            if g == 0:
                                fw.op("dve", lambda e, o_ap=o_ap, i_ap=i_ap: e.tensor_copy(out=o_ap, in_=i_ap), reads=[R_pt[g]], writes=[R_u[uCAT + k]])
                            else:
                                fw.op("act", lambda e, o_ap=o_ap, i_ap=i_ap: e.activation(out=o_ap, in_=i_ap, func=AF.Copy), reads=[R_pt[g]], writes=[R_u[uCAT + k]])
            if dbg is not None:
                for k in range(8, 16):
                    fw.dma("pool", D_dbg, dbg[k, :, tok0:tok0 + ntok], U(uCAT + k)[:, 0:ntok], reads=[R_u[uCAT + k]])

        def block(seq, tok0, src0, ntok, xsrc, ysink, kout, vout):
            nt = (ntok + 127) // 128
            for tt in range(nt):
                rows = min(128, ntok - tt * 128)
                fw.dma("sp", D_x[tt], xres[:rows, tt, :], xsrc[src0 + tt * 128: src0 + tt * 128 + rows, :], writes=[R_x[tt]])
            norm_to_hT(0, ntok)
            if stop < 1:
                return
            uQT = 0; uKT = 8; uV = 16
            for half in range(2):
                slab = load_slab(w_in, 16, half * 512, 512)

                def cq(ch, m, pb, half=half):
                    h = half * 4 + ch
                    fw.op("act", lambda e, h=h, pb=pb: e.activation(out=U(uQT + h)[:, 0:ntok], in_=PS[pb][:, 0:ntok], func=AF.Copy),
                          reads=[R_ps[pb]], writes=[R_u[uQT + h]])
                proj_FM(slab, 16, 512, hT_get, R_hT, ntok, cq)
            if stop < 2:
                return
            for half in range(2):
                slab = load_slab(w_in, 16, 1024 + half * 512, 512)

                def ck_tm(tt, rows, pb, half=half):
                    f = next_f()
                    fw.op("act", lambda e, f=f, pb=pb, rows=rows: e.activation(out=ARF[:rows, f, :], in_=PS[pb][:rows, :], func=AF.Copy),
                          reads=[R_ps[pb]], writes=[R_f[f]])
                    fw.dma("sp", D_f[f], kout[src0 + tt * 128: src0 + tt * 128 + rows, half * 512:(half + 1) * 512], ARF[:rows, f, :],
                           reads=[R_f[f]])
                proj_TM(slab, 16, 512, hT_get, R_hT, ntok, ck_tm)

                def ck_fm(ch, m, pb, half=half):
                    h = half * 4 + ch
                    fw.op("dve", lambda e, h=h, pb=pb: e.tensor_copy(out=U(uKT + h)[:, 0:ntok], in_=PS[pb][:, 0:ntok]),
                          reads=[R_ps[pb]], writes=[R_u[uKT + h]])
                    fw.dma("sp", D_unit(uKT + h), ktS[seq][h, :, tok0:tok0 + ntok], U(uKT + h)[:, 0:ntok], reads=[R_u[uKT + h]], writes=[R_ktS[seq][h]])
                proj_FM(slab, 16, 512, hT_get, R_hT, ntok, ck_fm)
            if stop < 3:
                return
            for half in range(2):
                slab = load_slab(w_in, 16, 2048 + half * 512, 512)

                def cv_tm(tt, rows, pb, half=half):
                    f = next_f()
                    fw.op("act", lambda e, f=f, pb=pb, rows=rows: e.activation(out=ARF[:rows, f, :], in_=PS[pb][:rows, :], func=AF.Copy),
                          reads=[R_ps[pb]], writes=[R_f[f]])
                    fw.dma("sp", D_f[f], vout[src0 + tt * 128: src0 + tt * 128 + rows, half * 512:(half + 1) * 512], ARF[:rows, f, :],
                           reads=[R_f[f]])
                    u = uV + tt * 2 + half
                    fw.op("dve", lambda e, u=u, f=f, rows=rows: e.tensor_copy(out=U(u)[:rows, :], in_=ARF[:rows, f, :]),
                          reads=[R_f[f]], writes=[R_u[u]])
                    kt = (tok0 + tt * 128) // 128
                    for hh in range(4):
                        fw.dma("sp", D_unit(u), vS[seq][half * 4 + hh, 0:rows, kt, :], U(u)[:rows, hh * 128:(hh + 1) * 128], reads=[R_u[u]],
                               writes=[R_vS[seq][half * 4 + hh]])
                proj_TM(slab, 16, 512, hT_get, R_hT, ntok, cv_tm)
            if stop < 4:
                return
            attention(seq, tok0, ntok, diag=(seq == 0))
            if stop < 5:
                return
            mlstm(seq, tok0, ntok, 64 if seq == 0 else TS)
            if stop < 6:
                return
            proj_residual(w_out, lambda k: U(uCAT + k), R_u[uCAT:uCAT + 16], ntok, [(0, 16)])
            if stop < 7:
                dump_x(ntok, ysink, src0)
                return
            xattn(seq, ntok)
            if stop < 8:
                dump_x(ntok, ysink, src0)
                return
            ffn(ntok)
            final_norm(ntok, ysink, src0)

        def proj_residual(w, actT, actR, ntok, kparts):
            for cg in range(4):
                for pi, (k0, kc_n) in enumerate(kparts):
                    slab = load_slab(w, kc_n, cg * 512, 512, r0=k0 * 128)

                    def cons(tt, rows, pb, cg=cg):
                        fw.op("dve", lambda e, tt=tt, rows=rows, pb=pb, cg=cg: e.tensor_tensor(
                            out=xres[:rows, tt, cg * 512:(cg + 1) * 512], in0=xres[:rows, tt, cg * 512:(cg + 1) * 512],
                            in1=PS[pb][:rows, :], op=ALU.add), reads=[R_ps[pb], R_x[tt]], writes=[R_x[tt]])
                    proj_TM(slab, kc_n, 512, actT, actR, ntok, cons, k0=k0, acc=[0, 1, 2, 3],
                            first=(pi == 0), last=(pi == len(kparts) - 1))

        uXQ = 0; uMKT_ = 16; uMV_ = 24; uOT = 32; uXP = 48
        memKT = AR[:, uMKT_ * 512:(uMKT_ + 8) * 512].rearrange("p (j k) -> p j k", k=256)
        memV = AR[:, uMV_ * 512:(uMV_ + 8) * 512].rearrange("p (t c) -> p t c", c=2048)
        Ru_mkt = R_u[uMKT_:uMKT_ + 8]; Ru_mvv = R_u[uMV_:uMV_ + 8]

        def load_mem(seq):
            if seq == 0:
                fw.dma("sp", D_mem, AR[:, uMKT_ * 512:(uMKT_ + 8) * 512], mkS[0][:, :], reads=[R_mkS[0]], writes=Ru_mkt)
                fw.dma("sp", D_mem, AR[:, uMV_ * 512:(uMV_ + 8) * 512], mvS[0][:, :], reads=[R_mvS[0]], writes=Ru_mvv)
            else:
                for tt in range(2):
                    fw.dma("pool", D_memp, memV[:, tt, :], cmv[tt * 128:(tt + 1) * 128, :], writes=Ru_mvv)
                    ust = uOT + tt * 4
                    fw.dma("pool", D_memp, AR[:, ust * 512:(ust + 4) * 512], cmk[tt * 128:(tt + 1) * 128, :], writes=R_u[ust:ust + 4])
                    for g4 in range(4):
                        g = g4 % 2
                        for j in range(4):
                            jj = g4 * 4 + j
                            fw.op("pe", lambda e, g=g, j=j, jj=jj, ust=ust: e.transpose(
                                out=PTB[g][:, j * 128:(j + 1) * 128], in_=AR[:, ust * 512 + jj * 128: ust * 512 + (jj + 1) * 128],
                                identity=identB[:, :]), reads=R_u[ust:ust + 4] + [R_c], writes=[R_pt[g]])
                        for j in range(4):
                            jj = g4 * 4 + j
                            o_ap = memKT[:, jj, tt * 128:(tt + 1) * 128]
                            i_ap = PTB[g][:, j * 128:(j + 1) * 128]
                            if g == 0:
                                fw.op("dve", lambda e, o_ap=o_ap, i_ap=i_ap: e.tensor_copy(out=o_ap, in_=i_ap), reads=[R_pt[g]], writes=Ru_mkt)
                            else:
                                fw.op("act", lambda e, o_ap=o_ap, i_ap=i_ap: e.activation(out=o_ap, in_=i_ap, func=AF.Copy), reads=[R_pt[g]], writes=Ru_mkt)

        def xattn(seq, ntok):
            n = ntok
            norm_to_hT(1, ntok)
            load_mem(seq)
            for cg in range(4):
                slab = load_slab(w_mq, 16, cg * 512, 512)

                def c_q(ch, m, pb, cg=cg):
                    u = uXQ + cg * 4 + ch
                    fw.op("act", lambda e, u=u, pb=pb: e.activation(out=U(u)[:, 0:n], in_=PS[pb][:, 0:n], func=AF.Copy),
                          reads=[R_ps[pb]], writes=[R_u[u]])
                proj_FM(slab, 16, 512, hT_get, R_hT, ntok, c_q)
            rinv = ARF[:, 6, :]
            for hh in range(4):
                for kt in range(2):
                    sbk = kt
                    for dc in range(4):
                        j = hh * 4 + dc
                        fw.op("pe", lambda e, j=j, dc=dc, kt=kt, sbk=sbk: e.matmul(
                            PS[sbk][:, 0:n], lhsT=memKT[:, j, kt * 128:(kt + 1) * 128], rhs=U(uXQ + j)[:, 0:n],
                            start=(dc == 0), stop=(dc == 3)), reads=Ru_mkt + [R_u[uXQ + j]], writes=[R_ps[sbk]])
                    fw.op("act", lambda e, kt=kt, sbk=sbk: e.activation(out=U(uXP + kt)[:, 0:n], in_=PS[sbk][:, 0:n], func=AF.Exp,
                                                                       scale=512.0 ** -0.5), reads=[R_ps[sbk]], writes=[R_u[uXP + kt]])
                for kt in range(2):
                    fw.op("pe", lambda e, kt=kt: e.matmul(PS[4][:, 0:n], lhsT=onesB[:, :], rhs=U(uXP + kt)[:, 0:n], start=(kt == 0), stop=(kt == 1)),
                          reads=[R_c, R_u[uXP + kt]], writes=[R_ps[4]])
                fw.op("act", lambda e: e.activation(out=rinv[:, 0:n], in_=PS[4][:, 0:n], func=AF.Ln), reads=[R_ps[4]], writes=[R_f[6]])
                fw.op("act", lambda e: e.activation(out=rinv[:, 0:n], in_=rinv[:, 0:n], func=AF.Exp, scale=-1.0), reads=[R_f[6]], writes=[R_f[6]])
                for jv in range(4):
                    ob = 2 + (jv % 2)
                    for kt in range(2):
                        fw.op("pe", lambda e, jv=jv, kt=kt, ob=ob, hh=hh: e.matmul(
                            PS[ob][:, 0:n], lhsT=memV[:, kt, hh * 512 + jv * 128: hh * 512 + (jv + 1) * 128], rhs=U(uXP + kt)[:, 0:n],
                            start=(kt == 0), stop=(kt == 1)), reads=Ru_mvv + [R_u[uXP + kt]], writes=[R_ps[ob]])
                    u = uOT + hh * 4 + jv
                    fw.op("dve", lambda e, u=u, ob=ob: e.tensor_tensor(out=U(u)[:, 0:n], in0=PS[ob][:, 0:n], in1=rinv[:, 0:n], op=ALU.mult),
                          reads=[R_ps[ob], R_f[6]], writes=[R_u[u]])
            proj_residual(w_mo, lambda k: U(uOT + k), R_u[uOT:uOT + 16], ntok, [(0, 16)])

        def ffn(ntok):
            n = ntok
            norm_to_hT(2, ntok)
            nslab = 11
            for si in range(nslab):
                ncols = 512 if si < 10 else DFF - 5120
                slab_g = load_slab(w_gate, 16, si * 512, ncols)
                slab_u = load_slab(w_up, 16, si * 512, ncols)
                nchk = ncols // 128
                for ch in range(nchk):
                    s_, wv = slab_g
                    pg = next_ps()
                    for k in range(16):
                        fw.op("pe", lambda e, wv=wv, pg=pg, k=k, ch=ch: e.matmul(
                            PS[pg][:, 0:n], lhsT=wv[:, k, ch * 128:(ch + 1) * 128], rhs=hT[:, k, 0:n], start=(k == 0), stop=(k == 15)),
                            reads=[R_hT[k], R_w[s_]], writes=[R_ps[pg]])
                    fw.op("act", lambda e, ch=ch, pg=pg: e.activation(out=ARF[:, 4 + ch, 0:n], in_=PS[pg][:, 0:n], func=AF.Copy),
                          reads=[R_ps[pg]], writes=[R_f[4 + ch]])
                for ch in range(nchk):
                    s_, wv = slab_u
                    f_ = si * 4 + ch
                    pu = next_ps()
                    for k in range(16):
                        fw.op("pe", lambda e, wv=wv, pu=pu, k=k, ch=ch: e.matmul(
                            PS[pu][:, 0:n], lhsT=wv[:, k, ch * 128:(ch + 1) * 128], rhs=hT[:, k, 0:n], start=(k == 0), stop=(k == 15)),
                            reads=[R_hT[k], R_w[s_]], writes=[R_ps[pu]])
                    G = ARF[:, 4 + ch, :]; RG = [R_f[4 + ch]]
                    fa = next_f()
                    acc = ARF[:, fa, :]
                    cw = lambda j, f_=f_: convw_t[:, j, f_:f_ + 1]
                    fw.op("dve", lambda e, G=G, acc=acc, w2=cw(2), b=cw(3): e.tensor_scalar(out=acc[:, 0:n], in0=G[:, 0:n], scalar1=w2, scalar2=b,
                                                                                       op0=ALU.mult, op1=ALU.add),
                          reads=RG + [R_c], writes=[R_f[fa]])
                    fw.op("dve", lambda e, G=G, acc=acc, w1=cw(1): e.scalar_tensor_tensor(
                        out=acc[:, 1:n], in0=G[:, 0:n - 1], scalar=w1, in1=acc[:, 1:n], op0=ALU.mult, op1=ALU.add),
                        reads=RG + [R_c, R_f[fa]], writes=[R_f[fa]])
                    fw.op("dve", lambda e, G=G, acc=acc, w0=cw(0): e.scalar_tensor_tensor(
                        out=acc[:, 2:n], in0=G[:, 0:n - 2], scalar=w0, in1=acc[:, 2:n], op0=ALU.mult, op1=ALU.add),
                        reads=RG + [R_c, R_f[fa]], writes=[R_f[fa]])
                    fw.op("dve", lambda e, acc=acc, w1=cw(1), f_=f_: e.scalar_tensor_tensor(
                        out=acc[:, 0:1], in0=halo[:, 1, f_:f_ + 1], scalar=w1, in1=acc[:, 0:1], op0=ALU.mult, op1=ALU.add),
                        reads=[R_halo, R_c, R_f[fa]], writes=[R_f[fa]])
                    fw.op("dve", lambda e, acc=acc, w0=cw(0), f_=f_: e.scalar_tensor_tensor(
                        out=acc[:, 0:2], in0=halo[:, :, f_], scalar=w0, in1=acc[:, 0:2], op0=ALU.mult, op1=ALU.add),
                        reads=[R_halo, R_c, R_f[fa]], writes=[R_f[fa]])
                    fw.op("dve", lambda e, G=G, f_=f_: e.tensor_copy(out=halo[:, :, f_], in_=G[:, n - 2:n]), reads=RG, writes=[R_halo])
                    fw.op("act", lambda e, acc=acc: e.activation(out=acc[:, 0:n], in_=acc[:, 0:n], func=AF.Silu), reads=[R_f[fa]], writes=[R_f[fa]])
                    fw.op("dve", lambda e, acc=acc, pu=pu, f_=f_: e.tensor_tensor(out=U(f_)[:, 0:n], in0=acc[:, 0:n], in1=PS[pu][:, 0:n], op=ALU.mult),
                          reads=[R_f[fa], R_ps[pu]], writes=[R_u[f_]])
            proj_residual(w_down, lambda k: U(k), R_u[0:43], ntok, [(0, 16), (16, 16), (32, 11)])

        def final_norm(ntok, ysink, src0):
            nt = (ntok + 127) // 128
            for tt in range(nt):
                rows = min(128, ntok - tt * 128)
                c = 32 + 4 * tt
                fw.op("dve", lambda e, c=c: e.memset(stat[:, c:c + 2], 0.0), writes=[R_stat])
                fw.op("act", lambda e, tt=tt, rows=rows, c=c: e.activation(
                    out=xn[:rows, :], in_=xres[:rows, tt, :], func=AF.Square, accum_out=stat[:rows, c:c + 1]),
                    reads=[R_x[tt]], writes=[R_xn, R_stat])
                fw.op("act", lambda e, rows=rows, c=c: e.activation(
                    out=stat[:rows, c + 1:c + 2], in_=stat[:rows, c:c + 1], func=AF.Sqrt, bias=EPS, scale=1.0 / D),
                    reads=[R_stat], writes=[R_stat])
                fw.op("dve", lambda e, rows=rows, c=c: e.reciprocal(out=stat[:rows, c + 2:c + 3], in_=stat[:rows, c + 1:c + 2]),
                      reads=[R_stat], writes=[R_stat])
                fw.op("dve", lambda e, tt=tt, rows=rows, c=c: e.scalar_tensor_tensor(
                    out=xres[:rows, tt, :], in0=xres[:rows, tt, :], scalar=stat[:rows, c + 2:c + 3], in1=gfinal_bc[:rows, :],
                    op0=ALU.mult, op1=ALU.mult), reads=[R_x[tt], R_stat, R_c], writes=[R_x[tt]])
                fw.dma("sp", D_x[tt], ysink[src0 + tt * 128: src0 + tt * 128 + rows, :], xres[:rows, tt, :], reads=[R_x[tt]])

        def dump_x(ntok, ysink, src0):
            for tt in range((ntok + 127) // 128):
                rows = min(128, ntok - tt * 128)
                fw.dma("sp", D_x[tt], ysink[src0 + tt * 128: src0 + tt * 128 + rows, :], xres[:rows, tt, :], reads=[R_x[tt]])

        def conv_state_out(outp):
            pb = next_ps()
            fw.op("pe", lambda e, pb=pb: e.transpose(out=PS[pb][0:86, 0:128], in_=halo[:].rearrange("p j f -> p (j f)"), identity=identF[:, :]),
                  reads=[R_halo, R_c], writes=[R_ps[pb]])
            f = next_f()
            fw.op("dve", lambda e, pb=pb, f=f: e.tensor_copy(out=ARF[0:86, f, 0:128], in_=PS[pb][0:86, 0:128]), reads=[R_ps[pb]], writes=[R_f[f]])
            fw.dma("sp", D_f[f], outp.rearrange("j (f p) -> (j f) p", p=128), ARF[0:86, f, 0:128], reads=[R_f[f]])

        def conv_state_in(inp_):
            f = next_f()
            fw.dma("sp", D_f[f], ARF[0:86, f, 0:128], inp_.rearrange("j (f p) -> (j f) p", p=128), writes=[R_f[f]])
            pb = next_ps()
            fw.op("pe", lambda e, pb=pb, f=f: e.transpose(out=PS[pb][:, 0:86], in_=ARF[0:86, f, 0:128], identity=identF[0:86, 0:86]),
                  reads=[R_f[f], R_c], writes=[R_ps[pb]])
            fw.op("dve", lambda e, pb=pb: e.tensor_copy(out=halo[:].rearrange("p j f -> p (j f)"), in_=PS[pb][:, 0:86]), reads=[R_ps[pb]], writes=[R_halo])

        def mem_block():
            for tt in range(2):
                fw.dma("sp", D_x[tt], xres[:, tt, :], mem[tt * 128:(tt + 1) * 128, :], writes=[R_x[tt]])
            norm_to_hT(3, MEM)
            for (w, outp) in ((w_mk, pmk), (w_mv, pmv)):
                for cg in range(4):
                    slab = load_slab(w, 16, cg * 512, 512)

                    def cm(tt, rows, pb, cg=cg, outp=outp):
                        f = next_f()
                        fw.op("act", lambda e, f=f, pb=pb, rows=rows: e.activation(out=ARF[:rows, f, :], in_=PS[pb][:rows, :], func=AF.Copy),
                              reads=[R_ps[pb]], writes=[R_f[f]])
                        fw.dma("sp", D_f[f], outp[tt * 128: tt * 128 + rows, cg * 512:(cg + 1) * 512], ARF[:rows, f, :],
                               reads=[R_f[f]])
                        if outp is pmv:
                            u = 36 + (tt * 4 + cg) % 4
                            fw.op("dve", lambda e, u=u, f=f: e.tensor_copy(out=U(u)[:, :], in_=ARF[:, f, :]), reads=[R_f[f]], writes=[R_u[u]])
                            fw.dma("sp", D_unit(u), mvS[0][:, tt * 2048 + cg * 512: tt * 2048 + (cg + 1) * 512], U(u)[:, :], reads=[R_u[u]],
                                   writes=[R_mvS[0]])
                    proj_TM(slab, 16, 512, hT_get, R_hT, MEM, cm)
                    if w is w_mk:
                        def cmkT(ch, m, pb, cg=cg):
                            j = cg * 4 + ch
                            u = 32 + (j % 4)
                            fw.op("dve", lambda e, u=u, pb=pb: e.tensor_copy(out=U(u)[:, 0:MEM], in_=PS[pb][:, 0:MEM]), reads=[R_ps[pb]], writes=[R_u[u]])
                            fw.dma("sp", D_unit(u), mkS[0][:, j * 256:(j + 1) * 256], U(u)[:, 0:MEM], reads=[R_u[u]], writes=[R_mkS[0]])
                        proj_FM(slab, 16, 512, hT_get, R_hT, MEM, cmkT)

        for nm_ in ("w_mk", "w_mv", "w_in", "w_out", "w_mq", "w_mo", "w_gate", "w_up", "w_down"):
            convert_weight(nm_)
        if do_mem:
            mem_block()
        ml_init_zero()
        fw.op("dve", lambda e: e.memset(halo[:], 0.0), writes=[R_halo])
        for b in range(nblk):
            block(0, b * 512, b * 512, 512, x, y, pk, pv)
        if stop >= 5:
            ml_out_state(pc, pn, pm)
        if stop >= 8:
            conv_state_out(pconv)
        if sample:
            if stop >= 8:
                conv_state_in(conv0)
            cache_prologue()
            if stop >= 5:
                ml_init_state()
            block(1, T, 0, TS, xs, ys, sk, sv)
            if stop >= 5:
                ml_out_state(sc, sn, sm)
            if stop >= 8:
                conv_state_out(sconv)
        fw.emit()
    return nc


def _prep_inputs(inp, b):
    f = np.float32
    g = lambda k: np.asarray(inp[k], dtype=f)
    d = {}
    d["x"] = np.ascontiguousarray(g("x_prompt")[b])
    d["xs"] = np.ascontiguousarray(g("x_sample")[b])
    d["ck"] = np.ascontiguousarray(g("cache_da_k")[0, b].reshape(T, 1024))
    d["cv"] = np.ascontiguousarray(g("cache_da_v")[0, b].reshape(T, 1024))
    d["c0"] = np.ascontiguousarray(g("state_ml_c")[0, b])
    d["n0"] = np.ascontiguousarray(g("state_ml_n")[0, b])
    d["m0"] = np.ascontiguousarray(g("state_ml_m")[0, b].reshape(4, 1))
    d["conv0"] = np.ascontiguousarray(g("state_ffn_conv")[0, b])
    d["cmk"] = np.ascontiguousarray(g("cache_mem_k")[0, b].reshape(MEM, D))
    d["cmv"] = np.ascontiguousarray(g("cache_mem_v")[0, b].reshape(MEM, D))
    d["mem"] = np.ascontiguousarray(g("mem_prompt")[b])
    for k in ("w_in", "w_out", "w_mq", "w_mk", "w_mv", "w_mo", "w_gate", "w_up", "w_down"):
        d[k] = np.ascontiguousarray(g(k)[0])
    gs = np.stack([g("g_mix")[0], g("g_xattn")[0], g("g_ffn")[0], g("g_mem")[0]], 0)
    d["gpk"] = np.ascontiguousarray(gs.reshape(4, 16, 128).transpose(2, 0, 1))
    d["lamv"] = np.concatenate([g("lambda_q1")[0], g("lambda_k1")[0], g("lambda_q2")[0], g("lambda_k2")[0]])[None, :].copy()
    d["gda"] = np.ascontiguousarray(g("g_da_sub")[0].reshape(128, 1))
    d["bgate"] = np.ascontiguousarray(np.stack([g("b_ig")[0], g("b_fg")[0]], 1))
    d["gml"] = np.ascontiguousarray(g("g_ml")[0])
    cw = np.concatenate([g("conv_w")[0], g("conv_b")], 0)
    d["convw"] = np.ascontiguousarray(cw.reshape(4, 43, 128).transpose(2, 0, 1))
    d["gfinal"] = np.ascontiguousarray(g("g_final"))
    d["identf"] = np.eye(128, dtype=f)
    kk = np.arange(128)[:, None, None] + 128 * np.arange(4)[None, :, None]
    qq = np.arange(512)[None, None, :]
    d["masks"] = np.ascontiguousarray(((kk // 64) <= (qq // 64)).astype(f))
    es_ = np.zeros((4, 4, 128), f)
    for hh in range(4):
        es_[hh, hh, :] = 1.0
    d["esel"] = es_.reshape(4, 512)
    pp = np.arange(128)[:, None] % 64
    tt_ = np.arange(64)[None, :]
    d["maskml"] = np.where(pp <= tt_, 0.0, -1e30).astype(f)
    return d


_NC_CACHE = {}


def kernel(**inp):
    cfg = ("full",)
    if cfg not in _NC_CACHE:
        _NC_CACHE[cfg] = build()
    nc = _NC_CACHE[cfg]
    in_maps = [_prep_inputs(inp, b) for b in range(8)]
    res = run_bass_kernel_spmd(nc, in_maps, core_ids=list(range(8)))
    r = res.results
    st = lambda k: np.stack([np.asarray(r[b][k], dtype=np.float32) for b in range(8)], 0)
    y_prompt = st("y")
    y_sample = st("ys")
    p_k = st("pk").reshape(1, 8, T, 8, 128)
    p_v = st("pv").reshape(1, 8, T, 8, 128)
    p_c = st("pc")[None]
    p_n = st("pn")[None]
    p_m = st("pm").reshape(1, 8, 4)
    p_conv = st("pconv")[None]
    p_mk = st("pmk").reshape(1, 8, MEM, 4, 512)
    p_mv = st("pmv").reshape(1, 8, MEM, 4, 512)
    s_k = st("sk").reshape(1, 8, TS, 8, 128)
    s_v = st("sv").reshape(1, 8, TS, 8, 128)
    s_c = st("sc")[None]
    s_n = st("sn")[None]
    s_m = st("sm").reshape(1, 8, 4)
    s_conv = st("sconv")[None]
    return (y_prompt, y_sample, p_k, p_v, p_c, p_n, p_m, p_conv, p_mk, p_mv,
            s_k, s_v, s_c, s_n, s_m, s_conv)
```

```python
import contextlib
import numpy as np
import concourse.bass as bass
import concourse.mybir as mybir
from concourse.bass_utils import run_bass_kernel_spmd

F32 = mybir.dt.float32
BF16 = mybir.dt.bfloat16
AF = mybir.ActivationFunctionType
ALU = mybir.AluOpType
AX = mybir.AxisListType
ENGS = ("pe", "act", "dve", "pool", "sp")

D = 2048
T = 4096
TS = 16
DIN = 7176
DFF = 5504
NH = 8
MEM = 256
EPS = 1e-6
LAM_INIT = 0.2
ATTACH_WAIT = True


class Reg:
    __slots__ = ("name", "w", "r")

    def __init__(self, name=""):
        self.name = name
        self.w = None
        self.r = []


class DSem:
    __slots__ = ("sem", "count", "name")

    def __init__(self, name):
        self.name = name
        self.sem = None
        self.count = 0


class Op:
    __slots__ = ("eng", "fn", "cwaits", "dwaits", "sig", "sigidx", "dsem", "dval")

    def __init__(self, eng, fn):
        self.eng = eng
        self.fn = fn
        self.cwaits = []
        self.dwaits = []
        self.sig = False
        self.sigidx = 0
        self.dsem = None
        self.dval = 0


class FW:
    def __init__(self, nc):
        self.nc = nc
        self.ops = {e: [] for e in ENGS}
        self.dsems = []
        self.nops = 0

    def dsem(self, name):
        d = DSem(name)
        self.dsems.append(d)
        return d

    def _deps(self, o, reads, writes):
        deps = []
        seen = set()
        for r in reads:
            if r.w is not None and id(r.w) not in seen:
                seen.add(id(r.w)); deps.append(r.w)
        for w in writes:
            if w.w is not None and id(w.w) not in seen:
                seen.add(id(w.w)); deps.append(w.w)
            for x in w.r:
                if id(x) not in seen:
                    seen.add(id(x)); deps.append(x)
        for d in deps:
            if d is o:
                continue
            if d.dsem is not None:
                o.dwaits.append((d.dsem, d.dsem.count))
            else:
                if o.eng == "pe" and d.eng == "pe":
                    continue
                d.sig = True
                o.cwaits.append(d)
        for r in reads:
            r.r.append(o)
        for w in writes:
            w.w = o
            w.r = []

    def op(self, eng, fn, reads=(), writes=()):
        o = Op(eng, fn)
        self._deps(o, reads, writes)
        self.ops[eng].append(o)
        self.nops += 1
        return o

    def dma(self, eng, dsem, out_ap, in_ap, reads=(), writes=(), slow=False):
        if slow:
            def fn(e):
                return e.dma_start(out=out_ap, in_=in_ap, allow_slow_non_contiguous=True)
        else:
            def fn(e):
                return e.dma_start(out=out_ap, in_=in_ap)
        o = Op(eng, fn)
        self._deps(o, reads, writes)
        dsem.count += 16
        o.dsem = dsem
        o.dval = dsem.count
        self.ops[eng].append(o)
        self.nops += 1
        return o

    def emit(self):
        nc = self.nc
        with contextlib.ExitStack() as es:
            csem = {}
            for e in ENGS:
                csem[e] = es.enter_context(nc.semaphore("c_" + e))
            for d in self.dsems:
                if d.count > 0:
                    d.sem = es.enter_context(nc.semaphore("d_" + d.name))
            for e in ENGS:
                c = 0
                for o in self.ops[e]:
                    if o.sig and o.dsem is None:
                        c += 1
                        o.sigidx = c
            block = es.enter_context(nc.Block())
            final_d = [(d.sem, d.count) for d in self.dsems if d.count > 0]

            def run(e, engobj, last=False):
                seen = {}
                for o in self.ops[e]:
                    need = {}
                    for p in o.cwaits:
                        s = csem[p.eng]
                        v = p.sigidx
                        if seen.get(id(s), 0) < v:
                            need[id(s)] = (s, max(v, need.get(id(s), (s, 0))[1]))
                            seen[id(s)] = v
                    for (d, v) in o.dwaits:
                        if seen.get(id(d), 0) < v:
                            need[id(d)] = (d.sem, max(v, need.get(id(d), (d.sem, 0))[1]))
                            seen[id(d)] = v
                    need = list(need.values())
                    attach = need.pop() if (need and ATTACH_WAIT and o.dsem is None and e != "pe") else None
                    for (s, v) in need:
                        engobj.wait_ge(s, v)
                    ins = o.fn(engobj)
                    if attach is not None:
                        ins._wait_ge(attach[0], attach[1])
                    if o.dsem is not None:
                        ins.then_inc(o.dsem.sem, 16)
                    elif o.sig:
                        ins.then_inc(csem[e], 1)
                if last:
                    for (s, v) in final_d:
                        engobj.wait_ge(s, v)

            @block.tensor
            def _(pe):
                run("pe", pe)

            @block.scalar
            def _(act):
                run("act", act)

            @block.vector
            def _(dve):
                run("dve", dve)

            @block.gpsimd
            def _(pool):
                run("pool", pool)

            @block.sync
            def _(sp):
                run("sp", sp, last=True)


IN_NAMES = ["x", "xs", "ck", "cv", "c0", "n0", "m0", "conv0", "cmk", "cmv", "mem",
            "w_in", "w_out", "w_mq", "w_mk", "w_mv", "w_mo", "w_gate", "w_up", "w_down",
            "gpk", "lamv", "gda", "bgate", "gml", "convw", "gfinal", "identf", "masks"]


def build(nblk=8, sample=True, phases=("inproj",), stop=99, do_mem=True, debug=False):
    nc = bass.Bass("TRN2", target_bir_lowering=False)

    def din(name, shape):
        return nc.dram_tensor(name, shape, F32, kind="ExternalInput").ap()

    def dout(name, shape):
        return nc.dram_tensor(name, shape, F32, kind="ExternalOutput").ap()

    x = din("x", [T, D]); xs = din("xs", [TS, D])
    ck = din("ck", [T, 1024]); cv = din("cv", [T, 1024])
    c0 = din("c0", [4, 256, 256]); n0 = din("n0", [4, 256]); m0 = din("m0", [4, 1])
    conv0 = din("conv0", [2, DFF])
    cmk = din("cmk", [MEM, D]); cmv = din("cmv", [MEM, D]); mem = din("mem", [MEM, D])
    w_in = din("w_in", [D, DIN]); w_out = din("w_out", [D, D]); w_mq = din("w_mq", [D, D])
    w_mk = din("w_mk", [D, D]); w_mv = din("w_mv", [D, D]); w_mo = din("w_mo", [D, D])
    w_gate = din("w_gate", [D, DFF]); w_up = din("w_up", [D, DFF]); w_down = din("w_down", [DFF, D])
    gpk = din("gpk", [128, 4, 16])
    lamv = din("lamv", [1, 256])
    gda = din("gda", [128, 1])
    bgate = din("bgate", [4, 2])
    gml = din("gml", [1024])
    convw = din("convw", [128, 4, 43])
    gfinal = din("gfinal", [D])
    identf = din("identf", [128, 128])
    masks = din("masks", [128, 4, 512])
    esel = din("esel", [4, 512])
    maskml = din("maskml", [128, 64])

    y = dout("y", [T, D]); ys = dout("ys", [TS, D])
    pk = dout("pk", [T, 1024]); pv = dout("pv", [T, 1024])
    pc = dout("pc", [4, 256, 256]); pn = dout("pn", [4, 256]); pm = dout("pm", [4, 1])
    pconv = dout("pconv", [2, DFF])
    pmk = dout("pmk", [MEM, D]); pmv = dout("pmv", [MEM, D])
    sk = dout("sk", [TS, 1024]); sv = dout("sv", [TS, 1024])
    sc = dout("sc", [4, 256, 256]); sn = dout("sn", [4, 256]); sm = dout("sm", [4, 1])
    sconv = dout("sconv", [2, DFF])

    NKT = 33
    ktS = [nc.dram_tensor("ktS%d" % i, [NH, 128, NKT * 128], BF16, kind="Internal").ap() for i in range(2)]
    vS = [nc.dram_tensor("vS%d" % i, [NH, 128, NKT, 128], BF16, kind="Internal").ap() for i in range(2)]
    mkS = nc.dram_tensor("mkS", [2, 128, 16 * 256], BF16, kind="Internal").ap()
    mvS = nc.dram_tensor("mvS", [2, 128, 2 * 2048], BF16, kind="Internal").ap()

    dbg = nc.dram_tensor("dbg", [16, 128, T + TS], F32, kind="ExternalOutput").ap() if debug else None
    WSPEC = {"w_in": (w_in, 16, 15), "w_out": (w_out, 16, 4), "w_mq": (w_mq, 16, 4), "w_mk": (w_mk, 16, 4), "w_mv": (w_mv, 16, 4),
             "w_mo": (w_mo, 16, 4), "w_gate": (w_gate, 16, 11), "w_up": (w_up, 16, 11), "w_down": (w_down, 43, 12)}
    WB = {k: nc.dram_tensor("wb_" + k, [v[2], 128, 8192], BF16, kind="Internal").ap() for k, v in WSPEC.items()}
    fw = FW(nc)
    es = contextlib.ExitStack()
    with es:
        def sb(name, shape, dt):
            return es.enter_context(nc.sbuf_tensor(name, shape, dt))

        xres = sb("xres", [128, 4, D], F32); R_x = [Reg("x%d" % i) for i in range(4)]
        xn = sb("xn", [128, D], BF16); R_xn = Reg("xn")
        hT = sb("hT", [128, 16, 512], BF16); R_hT = [Reg("hT%d" % i) for i in range(16)]
        NSLOT = 2
        wsl = [sb("wsl%d" % i, [128, 8192], BF16) for i in range(NSLOT)]
        R_w = [Reg("w%d" % i) for i in range(NSLOT)]
        D_w = [fw.dsem("w%d" % i) for i in range(NSLOT)]
        NU = 64
        AR = sb("AR", [128, NU * 512], BF16); R_u = [Reg("u%d" % i) for i in range(NU)]
        NF = 8
        ARF = sb("ARF", [128, NF, 512], F32); R_f = [Reg("f%d" % i) for i in range(NF)]
        D_f = [fw.dsem("f%d" % i) for i in range(NF)]
        identF = sb("identF", [128, 128], F32); identB = sb("identB", [128, 128], BF16)
        R_c = Reg("consts")
        gpk_t = sb("gpk_t", [128, 4, 16], F32)
        stat = sb("stat", [128, 64], F32); R_stat = Reg("stat")
        D_x = [fw.dsem("x%d" % i) for i in range(4)]
        _du = {}

        def D_unit(i, q="sp"):
            if (i, q) not in _du:
                _du[(i, q)] = fw.dsem("u%d%s" % (i, q))
            return _du[(i, q)]

        PS = [es.enter_context(nc.psum_tensor("ps%d" % i, [128, 512], F32)) for i in range(6)]
        R_ps = [Reg("ps%d" % i) for i in range(6)]
        PTB = [es.enter_context(nc.psum_tensor("pt%d" % i, [128, 1024], BF16)) for i in range(2)]
        R_pt = [Reg("pt0"), Reg("pt1")]

        def U(i, n=1):
            return AR[:, i * 512:(i + n) * 512]

        D_c = fw.dsem("consts")
        fw.dma("sp", D_c, identF[:], identf[:, :], writes=[R_c])
        D_c2 = fw.dsem("consts2")
        fw.dma("pool", D_c2, identB[:], identf[:, :], writes=[R_c])
        fw.dma("sp", D_c, gpk_t[:], gpk[:, :, :], writes=[R_c])

        maskB = sb("maskB", [128, 4, 512], BF16)
        onesB = sb("onesB", [128, 128], BF16)
        lam_t = sb("lam_t", [128, 256], F32)
        lam_s = sb("lam_s", [128, 8], F32)
        gda_t = sb("gda_t", [128, 2], F32)
        D_c3 = fw.dsem("consts3")
        for j in range(4):
            fw.dma("pool", D_c3, maskB[:, j, :], masks[:, j, :], writes=[R_c])
        fw.dma("sp", D_c, lam_t[:], lamv[0:1, :].partition_broadcast(128) if False else lamv.partition_broadcast(128), writes=[R_c])
        fw.dma("sp", D_c, gda_t[:, 0:1], gda[:, :], writes=[R_c])
        fw.op("dve", lambda e: e.memset(onesB[:], 1.0), writes=[R_c])
        onesF = sb("onesF", [128, 128], F32)
        fw.op("dve", lambda e: e.memset(onesF[:], 1.0), writes=[R_c])
        fw.op("dve", lambda e: e.tensor_tensor(out=lam_t[:, 0:64], in0=lam_t[:, 0:64], in1=lam_t[:, 64:128], op=ALU.mult), reads=[R_c], writes=[R_c])
        fw.op("dve", lambda e: e.tensor_tensor(out=lam_t[:, 128:192], in0=lam_t[:, 128:192], in1=lam_t[:, 192:256], op=ALU.mult), reads=[R_c], writes=[R_c])
        fw.op("dve", lambda e: e.reduce_sum(out=lam_s[:, 0:1], in_=lam_t[:, 0:64], axis=AX.X), reads=[R_c], writes=[R_c])
        fw.op("dve", lambda e: e.reduce_sum(out=lam_s[:, 1:2], in_=lam_t[:, 128:192], axis=AX.X), reads=[R_c], writes=[R_c])
        fw.op("act", lambda e: e.activation(out=lam_s[:, 2:4], in_=lam_s[:, 0:2], func=AF.Exp), reads=[R_c], writes=[R_c])
        fw.op("dve", lambda e: e.tensor_tensor(out=lam_s[:, 4:5], in0=lam_s[:, 3:4], in1=lam_s[:, 2:3], op=ALU.subtract), reads=[R_c], writes=[R_c])
        fw.op("dve", lambda e: e.tensor_scalar(out=lam_s[:, 5:6], in0=lam_s[:, 4:5], scalar1=-LAM_INIT, scalar2=None, op0=ALU.add), reads=[R_c], writes=[R_c])
        fw.op("dve", lambda e: e.tensor_scalar(out=gda_t[:, 1:2], in0=gda_t[:, 0:1], scalar1=1.0 - LAM_INIT, scalar2=None, op0=ALU.mult), reads=[R_c], writes=[R_c])
        neglam = lam_s[:, 5:6]
        gda_s = gda_t[:, 1:2]
        R_ktS = [[Reg("ktS%d_%d" % (i, h)) for h in range(NH)] for i in range(2)]
        R_vS = [[Reg("vS%d_%d" % (i, h)) for h in range(NH)] for i in range(2)]
        D_h = fw.dsem("hist")
        uKTH = 24; uVH = 33; uPT = 42; uSQ = 46; uCAT = 48; uQT_ = 0
        D_dbg = fw.dsem("dbg")
        D_cp = [fw.dsem("cp%d" % i) for i in range(4)]

        esel_t = sb("esel_t", [4, 512], F32)
        maskml_t = sb("maskml_t", [128, 64], F32)
        bg_t = sb("bg_t", [4, 4], F32)
        gml_bc = sb("gml_bc", [128, 1024], F32)
        CT = sb("CT", [128, 8, 257], F32); R_CT = [Reg("CT%d" % j) for j in range(8)]
        CTb = sb("CTb", [128, 8, 257], BF16); R_CTb = [Reg("CTb%d" % j) for j in range(8)]
        carry = sb("carry", [4, 16], F32); R_carry = Reg("carry")
        gcol = sb("gcol", [128, 64], F32); R_gcol = Reg("gcol")
        GL = sb("GL", [128, 32], F32); R_GL = Reg("GL")
        mlt = sb("mlt", [128, 1024], F32)
        R_wT = Reg("wT"); R_HN = Reg("HN"); R_tmpN = Reg("tmpN"); R_dd = Reg("dd"); R_ssml = Reg("ssml")
        fw.dma("sp", D_c, esel_t[:], esel[:, :], writes=[R_c])
        fw.dma("sp", D_c, maskml_t[:], maskml[:, :], writes=[R_c])
        fw.dma("sp", D_c, bg_t[:, 0:2], bgate[:, :], writes=[R_c])
        fw.dma("sp", D_c, gml_bc[:], gml.partition_broadcast(128), writes=[R_c])
        fw.op("dve", lambda e: e.tensor_scalar(out=bg_t[:, 2:3], in0=bg_t[:, 1:2], scalar1=-1.0, scalar2=None, op0=ALU.mult), reads=[R_c], writes=[R_c])
        D_st = fw.dsem("mlstate")
        gfinal_bc = sb("gfinal_bc", [128, D], F32)
        convw_t = sb("convw_t", [128, 4, 43], F32)
        halo = sb("halo", [128, 2, 43], F32); R_halo = Reg("halo")
        fw.dma("sp", D_c, gfinal_bc[:], gfinal.partition_broadcast(128), writes=[R_c])
        fw.dma("sp", D_c, convw_t[:], convw[:, :, :], writes=[R_c])
        R_mkS = [Reg("mkS0"), Reg("mkS1")]; R_mvS = [Reg("mvS0"), Reg("mvS1")]
        D_mem = fw.dsem("memload"); D_memp = fw.dsem("memloadp")

        state = {"slot": 0, "ps": 0, "f": 0}

        def next_slot():
            s = state["slot"]; state["slot"] = (s + 1) % NSLOT
            return s

        def next_ps():
            p = state["ps"]; state["ps"] = (p + 1) % 4
            return p

        def next_f():
            f = state["f"]; state["f"] = (f + 1) % (NF - 4)
            return f

        R_wb = {k: Reg("wb_" + k) for k in WSPEC}
        D_wb = {k: fw.dsem("wb_" + k) for k in WSPEC}
        wname = {id(v[0].tensor): k for k, v in WSPEC.items()}

        def slab_index(name, c0_, r0):
            if name == "w_down":
                return (c0_ // 512) * 3 + (r0 // 2048)
            return c0_ // 512

        cvq = {"q": 0}

        def convert_weight(name):
            w, K, nsl = WSPEC[name]
            ncol_tot = w.shape[1]
            if name == "w_down":
                parts = [(cg * 512, 512, k0 * 128, kc) for cg in range(4) for (k0, kc) in ((0, 16), (16, 16), (32, 11))]
            else:
                parts = [(c, min(512, ncol_tot - c), 0, 16) for c in range(0, ncol_tot, 512)]
            for (c0_, ncols, r0, kc) in parts:
                si = slab_index(name, c0_, r0)
                for k0 in range(0, kc, 4):
                    kn = min(4, kc - k0)
                    q = cvq["q"]; cvq["q"] += 1
                    buf = q % 4
                    src = w[r0 + k0 * 128:r0 + (k0 + kn) * 128, c0_:c0_ + ncols].rearrange("(k p) c -> p k c", p=128)
                    stg = xres[:, buf, 0:kn * ncols]
                    fw.dma("sp", D_x[buf], stg.rearrange("p (k c) -> p k c", k=kn), src, writes=[R_x[buf]])
                    dst = AR[:, buf * 2048: buf * 2048 + kn * ncols]
                    ru = R_u[buf * 4: buf * 4 + 4]
                    if q % 2 == 0:
                        fw.op("dve", lambda e, dst=dst, stg=stg: e.tensor_copy(out=dst, in_=stg), reads=[R_x[buf]], writes=ru)
                    else:
                        fw.op("act", lambda e, dst=dst, stg=stg: e.activation(out=dst, in_=stg, func=AF.Copy), reads=[R_x[buf]], writes=ru)
                    fw.dma("pool", D_unit(buf * 4, "pool"), WB[name][si, :, k0 * ncols:(k0 + kn) * ncols], dst, reads=ru, writes=[R_wb[name]])

        def load_slab(w, kc, c0_, ncols, r0=0):
            name = wname[id(w.tensor)]
            si = slab_index(name, c0_, r0)
            s = next_slot()
            dst = wsl[s][:, 0:kc * ncols].rearrange("p (k c) -> p k c", k=kc)
            fw.dma("pool", D_w[s], wsl[s][:, 0:kc * ncols], WB[name][si, :, 0:kc * ncols], reads=[R_wb[name]], writes=[R_w[s]])
            return s, dst

        def norm_to_hT(which, ntok, src=None):
            nt = (ntok + 127) // 128
            for tt in range(nt):
                rows = min(128, ntok - tt * 128)
                sc_ = 4 * tt
                if stop < -2:
                    continue
                fw.op("dve", lambda e, c=sc_: e.memset(stat[:, c:c + 2], 0.0), writes=[R_stat])
                fw.op("act", lambda e, tt=tt, rows=rows, c=sc_: e.activation(
                    out=xn[:rows, :], in_=xres[:rows, tt, :], func=AF.Square, accum_out=stat[:rows, c:c + 1]),
                    reads=[R_x[tt]], writes=[R_xn, R_stat])
                fw.op("act", lambda e, rows=rows, c=sc_: e.activation(
                    out=stat[:rows, c + 1:c + 2], in_=stat[:rows, c:c + 1], func=AF.Sqrt, bias=EPS, scale=1.0 / D),
                    reads=[R_stat], writes=[R_stat])
                fw.op("dve", lambda e, rows=rows, c=sc_: e.reciprocal(out=stat[:rows, c + 2:c + 3], in_=stat[:rows, c + 1:c + 2]),
                      reads=[R_stat], writes=[R_stat])
                fw.op("act", lambda e, tt=tt, rows=rows, c=sc_: e.activation(
                    out=xn[:rows, :], in_=xres[:rows, tt, :], func=AF.Copy, scale=stat[:rows, c + 2:c + 3]),
                    reads=[R_x[tt], R_stat], writes=[R_xn])
                for g4 in range(4):
                    if stop < -1:
                        continue
                    half = g4 % 2
                    for j in range(4):
                        kc = g4 * 4 + j
                        fw.op("pe", lambda e, kc=kc, j=j, half=half, rows=rows: e.transpose(
                            out=PTB[half][:, j * 128: j * 128 + rows],
                            in_=xn[:rows, kc * 128:(kc + 1) * 128], identity=identB[:rows, :rows]),
                            reads=[R_xn, R_c], writes=[R_pt[half]])
                    for j in range(4):
                        if stop < 0:
                            continue
                        kc = g4 * 4 + j
                        eng = "dve" if half == 0 else "act"
                        o_ap = hT[:, kc, tt * 128: tt * 128 + rows]
                        i_ap = PTB[half][:, j * 128: j * 128 + rows]
                        g_ap = gpk_t[:, which, kc:kc + 1]
                        if eng == "dve":
                            fw.op("dve", lambda e, o_ap=o_ap, i_ap=i_ap, g_ap=g_ap: e.tensor_scalar(
                                out=o_ap, in0=i_ap, scalar1=g_ap, scalar2=None, op0=ALU.mult),
                                reads=[R_pt[half], R_c], writes=[R_hT[kc]])
                        else:
                            fw.op("act", lambda e, o_ap=o_ap, i_ap=i_ap, g_ap=g_ap: e.activation(
                                out=o_ap, in_=i_ap, func=AF.Copy, scale=g_ap),
                                reads=[R_pt[half], R_c], writes=[R_hT[kc]])

        def proj_TM(slab, kc_n, ncols, actT, actR, ntok, consume, k0=0, acc=None, first=True, last=True):
            s, wv = slab
            nt = (ntok + 127) // 128
            for tt in range(nt):
                rows = min(128, ntok - tt * 128)
                pb = acc[tt] if acc is not None else next_ps()
                for k in range(kc_n):
                    fw.op("pe", lambda e, tt=tt, rows=rows, pb=pb, k=k: e.matmul(
                        PS[pb][:rows, 0:ncols], lhsT=actT(k0 + k)[:, tt * 128: tt * 128 + rows], rhs=wv[:, k, :],
                        start=(first and k == 0), stop=(last and k == kc_n - 1)),
                        reads=[actR[k0 + k], R_w[s]], writes=[R_ps[pb]])
                if last:
                    consume(tt, rows, pb)

        def proj_FM(slab, kc_n, ncols, actT, actR, ntok, consume):
            s, wv = slab
            for ch in range((ncols + 127) // 128):
                m = min(128, ncols - ch * 128)
                pb = next_ps()
                for k in range(kc_n):
                    fw.op("pe", lambda e, ch=ch, m=m, pb=pb, k=k: e.matmul(
                        PS[pb][:m, 0:ntok], lhsT=wv[:, k, ch * 128: ch * 128 + m], rhs=actT(k)[:, 0:ntok],
                        start=(k == 0), stop=(k == kc_n - 1)),
                        reads=[actR[k], R_w[s]], writes=[R_ps[pb]])
                consume(ch, m, pb)

        hT_get = lambda k: hT[:, k, :]

        def attention(seq, pos0, ntok, diag):
            nkeys = pos0 + ntok
            nkt = (nkeys + 127) // 128
            Ru_kth = R_u[uKTH:uKTH + 9] + R_u[8:17]
            Ru_vh = R_u[uVH:uVH + 9]
            KTHc = (AR[:, uKTH * 512: uKTH * 512 + NKT * 128], AR[:, 8 * 512: 8 * 512 + NKT * 128])
            fw.op("pool", lambda e: e.memset(KTHc[0][64:128, 0:nkeys], 0.0), writes=R_u[uKTH:uKTH + 9])
            fw.op("pool", lambda e: e.memset(KTHc[1][0:64, 0:nkeys], 0.0), writes=R_u[8:17])
            VH = AR[:, uVH * 512: uVH * 512 + NKT * 128].rearrange("p (k e) -> p k e", e=128)
            A = ARF[:, 6, :]; Bf = ARF[:, 7, :]
            for h in range(NH):
                fw.dma("sp", D_h, KTHc[0][0:64, 0:nkeys], ktS[seq][h, 0:64, 0:nkeys], reads=[R_ktS[seq][h]], writes=Ru_kth)
                fw.dma("sp", D_h, KTHc[1][64:128, 0:nkeys], ktS[seq][h, 64:128, 0:nkeys], reads=[R_ktS[seq][h]], writes=Ru_kth)
                nfull = nkeys // 128
                fw.dma("sp", D_h, VH[:, 0:nfull, :], vS[seq][h, :, 0:nfull, :], reads=[R_vS[seq][h]], writes=Ru_vh)
                if nkeys % 128:
                    fw.dma("sp", D_h, VH[0:nkeys % 128, nfull, :], vS[seq][h, 0:nkeys % 128, nfull, :], reads=[R_vS[seq][h]], writes=Ru_vh)
                steps = [(c, kt) for kt in range(nkt) for c in range(2)]
                SB = (0, 1, 4, 5)
                ACC = (ARF[:, 4, :], ARF[:, 5, :]); R_acc = (R_f[4], R_f[5])

                def s_step(i):
                    c, kt = steps[i]
                    kw = min(128, nkeys - kt * 128)
                    sbk = SB[i % 4]
                    pu = uPT + (i % 4)
                    fw.op("pe", lambda e, c=c, kt=kt, kw=kw, sbk=sbk, h=h: e.matmul(
                        PS[sbk][:kw, 0:ntok], lhsT=KTHc[c][:, kt * 128: kt * 128 + kw],
                        rhs=U(uQT_ + h)[:, 0:ntok], start=True, stop=True),
                        reads=Ru_kth + [R_u[uQT_ + h]], writes=[R_ps[sbk]])
                    fw.op("act", lambda e, kw=kw, sbk=sbk, pu=pu: e.activation(
                        out=U(pu)[:kw, 0:ntok], in_=PS[sbk][:kw, 0:ntok], func=AF.Exp, scale=0.125),
                        reads=[R_ps[sbk]], writes=[R_u[pu]])
                    j = kt - (nkt - 4)
                    if diag and j >= 0:
                        fw.op("dve", lambda e, pu=pu, j=j: e.tensor_tensor(
                            out=U(pu)[:, 0:ntok], in0=U(pu)[:, 0:ntok], in1=maskB[:, j, 0:ntok], op=ALU.mult),
                            reads=[R_u[pu], R_c], writes=[R_u[pu]])
                    if kt == 0:
                        if kw < 128:
                            fw.op("dve", lambda e, c=c: e.memset(ACC[c][:, 0:ntok], 0.0), writes=[R_acc[c]])
                        fw.op("dve", lambda e, c=c, kw=kw, pu=pu: e.tensor_copy(out=ACC[c][:kw, 0:ntok], in_=U(pu)[:kw, 0:ntok]),
                              reads=[R_u[pu]], writes=[R_acc[c]])
                    else:
                        fw.op("dve", lambda e, c=c, kw=kw, pu=pu: e.tensor_tensor(
                            out=ACC[c][:kw, 0:ntok], in0=ACC[c][:kw, 0:ntok], in1=U(pu)[:kw, 0:ntok], op=ALU.add),
                            reads=[R_u[pu], R_acc[c]], writes=[R_acc[c]])

                def av_step(i):
                    c, kt = steps[i]
                    kw = min(128, nkeys - kt * 128)
                    pu = uPT + (i % 4)
                    fw.op("pe", lambda e, c=c, kt=kt, kw=kw, pu=pu: e.matmul(
                        PS[2 + c][:, 0:ntok], lhsT=VH[:kw, kt, :], rhs=U(pu)[:kw, 0:ntok],
                        start=(kt == 0), stop=(kt == nkt - 1)),
                        reads=Ru_vh + [R_u[pu]], writes=[R_ps[2 + c]])

                LA = 3
                for i in range(len(steps) + LA):
                    if i < len(steps):
                        s_step(i)
                    if i - LA >= 0:
                        av_step(i - LA)
                n = ntok
                for c in range(2):
                    fw.op("pe", lambda e, c=c: e.matmul(PS[SB[c]][:, 0:n], lhsT=onesF[:, :], rhs=ACC[c][:, 0:n], start=True, stop=True),
                          reads=[R_c, R_acc[c]], writes=[R_ps[SB[c]]])
                fw.op("act", lambda e: e.activation(out=A[:, 0:n], in_=PS[SB[0]][:, 0:n], func=AF.Ln), reads=[R_ps[SB[0]]], writes=[R_f[6]])
                fw.op("act", lambda e: e.activation(out=A[:, 0:n], in_=A[:, 0:n], func=AF.Exp, scale=-1.0), reads=[R_f[6]], writes=[R_f[6]])
                fw.op("act", lambda e: e.activation(out=Bf[:, 0:n], in_=PS[SB[1]][:, 0:n], func=AF.Ln), reads=[R_ps[SB[1]]], writes=[R_f[7]])
                fw.op("act", lambda e: e.activation(out=Bf[:, 0:n], in_=Bf[:, 0:n], func=AF.Exp, scale=-1.0), reads=[R_f[7]], writes=[R_f[7]])
                fw.op("dve", lambda e: e.tensor_tensor(out=A[:, 0:n], in0=PS[2][:, 0:n], in1=A[:, 0:n], op=ALU.mult),
                      reads=[R_ps[2], R_f[6]], writes=[R_f[6]])
                fw.op("dve", lambda e: e.tensor_tensor(out=Bf[:, 0:n], in0=PS[3][:, 0:n], in1=Bf[:, 0:n], op=ALU.mult),
                      reads=[R_ps[3], R_f[7]], writes=[R_f[7]])
                fw.op("dve", lambda e: e.scalar_tensor_tensor(out=A[:, 0:n], in0=Bf[:, 0:n], scalar=neglam, in1=A[:, 0:n],
                                                              op0=ALU.mult, op1=ALU.add),
                      reads=[R_f[6], R_f[7], R_c], writes=[R_f[6]])
                fw.op("dve", lambda e: e.tensor_tensor(out=U(uSQ)[:, 0:n], in0=A[:, 0:n], in1=A[:, 0:n], op=ALU.mult),
                      reads=[R_f[6]], writes=[R_u[uSQ]])
                fw.op("pe", lambda e: e.matmul(PS[4][:, 0:n], lhsT=onesB[:, :], rhs=U(uSQ)[:, 0:n], start=True, stop=True),
                      reads=[R_c, R_u[uSQ]], writes=[R_ps[4]])
                fw.op("act", lambda e: e.activation(out=Bf[:, 0:n], in_=PS[4][:, 0:n], func=AF.Ln, bias=EPS, scale=1.0 / 128),
                      reads=[R_ps[4]], writes=[R_f[7]])
                fw.op("act", lambda e: e.activation(out=Bf[:, 0:n], in_=Bf[:, 0:n], func=AF.Exp, scale=-0.5), reads=[R_f[7]], writes=[R_f[7]])
                fw.op("dve", lambda e, h=h: e.scalar_tensor_tensor(out=U(uCAT + h)[:, 0:n], in0=A[:, 0:n], scalar=gda_s, in1=Bf[:, 0:n],
                                                                   op0=ALU.mult, op1=ALU.mult),
                      reads=[R_f[6], R_f[7], R_c], writes=[R_u[uCAT + h]])
                if dbg is not None:
                    fw.dma("pool", D_dbg, dbg[h, :, pos0:pos0 + n], U(uCAT + h)[:, 0:n], reads=[R_u[uCAT + h]])

        def cache_prologue():
            for kt in range(T // 128):
                uk = 0 + (kt % 2) * 2
                uv = 4 + (kt % 2) * 2
                ut = 8 + (kt % 2) * 2
                fw.dma("pool", D_unit(uk, "pool"), AR[:, uk * 512:(uk + 2) * 512], ck[kt * 128:(kt + 1) * 128, :], writes=R_u[uk:uk + 2])
                fw.dma("pool", D_unit(uv, "pool"), AR[:, uv * 512:(uv + 2) * 512], cv[kt * 128:(kt + 1) * 128, :], writes=R_u[uv:uv + 2])
                for hh in range(NH):
                    fw.dma("sp", D_unit(uv), vS[1][hh, :, kt, :], AR[:, uv * 512 + hh * 128: uv * 512 + (hh + 1) * 128],
                           reads=R_u[uv:uv + 2], writes=[R_vS[1][hh]])
                for g in range(2):
                    for j in range(4):
                        hh = g * 4 + j
                        fw.op("pe", lambda e, g=g, j=j, hh=hh, uk=uk: e.transpose(
                            out=PTB[g][:, j * 128:(j + 1) * 128], in_=AR[:, uk * 512 + hh * 128: uk * 512 + (hh + 1) * 128],
                            identity=identB[:, :]), reads=R_u[uk:uk + 2] + [R_c], writes=[R_pt[g]])
                    eng = "dve" if g == 0 else "act"
                    o_ap = AR[:, (ut + g) * 512:(ut + g + 1) * 512]
                    if g == 0:
                        fw.op("dve", lambda e, o_ap=o_ap, g=g: e.tensor_copy(out=o_ap, in_=PTB[g][:, 0:512]),
                              reads=[R_pt[g]], writes=[R_u[ut + g]])
                    else:
                        fw.op("act", lambda e, o_ap=o_ap, g=g: e.activation(out=o_ap, in_=PTB[g][:, 0:512], func=AF.Copy),
                              reads=[R_pt[g]], writes=[R_u[ut + g]])
                    for j in range(4):
                        hh = g * 4 + j
                        fw.dma("sp", D_unit(ut + g), ktS[1][hh, :, kt * 128:(kt + 1) * 128], AR[:, (ut + g) * 512 + j * 128:(ut + g) * 512 + (j + 1) * 128],
                               reads=[R_u[ut + g]], writes=[R_ktS[1][hh]])

        uMQ = 0; uMK = 8; uMKT = 16; uMV = 24; uMO = 33; uST = 41; uVW = 42
        mvA = AR[:, uMV * 512: uMV * 512 + 16 * 257].rearrange("p (a e) -> p a e", e=257)
        Ru_mv = R_u[uMV:uMV + 9]

        def ml_init_zero():
            fw.op("dve", lambda e: e.memset(CT[:], 0.0), writes=R_CT)
            fw.op("dve", lambda e: e.memset(CTb[:], 0.0), writes=R_CTb)
            fw.op("dve", lambda e: e.memset(carry[:], 0.0), writes=[R_carry])

        def ml_init_state():
            for hh in range(4):
                for ec in range(2):
                    f = next_f()
                    fw.dma("sp", D_f[f], ARF[:, f, 0:256], c0[hh, ec * 128:(ec + 1) * 128, :], writes=[R_f[f]])
                    for dc in range(2):
                        pb = next_ps()
                        fw.op("pe", lambda e, f=f, dc=dc, pb=pb: e.transpose(out=PS[pb][:, 0:128], in_=ARF[:, f, dc * 128:(dc + 1) * 128],
                                                                        identity=identF[:, :]), reads=[R_f[f], R_c], writes=[R_ps[pb]])
                        j = hh * 2 + dc
                        fw.op("dve", lambda e, j=j, ec=ec, pb=pb: e.tensor_copy(out=CT[:, j, ec * 128:(ec + 1) * 128], in_=PS[pb][:, 0:128]),
                              reads=[R_ps[pb]], writes=[R_CT[j]])
            fw.dma("sp", D_st, CT[:, :, 256], n0.rearrange("h (dc p) -> p (h dc)", p=128), writes=R_CT, slow=True)
            fw.dma("sp", D_st, carry[:, 0:1], m0[:, :], writes=[R_carry])
            fw.op("act", lambda e: e.activation(out=CTb[:], in_=CT[:], func=AF.Copy), reads=R_CT, writes=R_CTb)

        def ml_out_state(oc, on, om):
            for hh in range(4):
                for ec in range(2):
                    pb = next_ps()
                    for dc in range(2):
                        j = hh * 2 + dc
                        fw.op("pe", lambda e, j=j, ec=ec, dc=dc, pb=pb: e.transpose(
                            out=PS[pb][:, dc * 128:(dc + 1) * 128], in_=CT[:, j, ec * 128:(ec + 1) * 128], identity=identF[:, :]),
                            reads=[R_CT[j], R_c], writes=[R_ps[pb]])
                    f = next_f()
                    fw.op("dve", lambda e, f=f, pb=pb: e.tensor_copy(out=ARF[:, f, 0:256], in_=PS[pb][:, 0:256]), reads=[R_ps[pb]], writes=[R_f[f]])
                    fw.dma("sp", D_f[f], oc[hh, ec * 128:(ec + 1) * 128, :], ARF[:, f, 0:256], reads=[R_f[f]])
            fw.dma("sp", D_st, on.rearrange("h (dc p) -> p (h dc)", p=128), CT[:, :, 256], reads=R_CT, slow=True)
            fw.dma("sp", D_st, om[:, :], carry[:, 0:1], reads=[R_carry])

        def mlstm(seq, tok0, ntok, L):
            nt = (ntok + 127) // 128
            nch = ntok // L
            for half in range(2):
                slab = load_slab(w_in, 16, 3072 + half * 512, 512)

                def c_mq(ch, m, pb, half=half):
                    u = uMQ + half * 4 + ch
                    fw.op("act", lambda e, u=u, pb=pb: e.activation(out=U(u)[:, 0:ntok], in_=PS[pb][:, 0:ntok], func=AF.Copy),
                          reads=[R_ps[pb]], writes=[R_u[u]])
                proj_FM(slab, 16, 512, hT_get, R_hT, ntok, c_mq)
            for half in range(2):
                slab = load_slab(w_in, 16, 4096 + half * 512, 512)

                def c_mkT(ch, m, pb, half=half):
                    u = uMK + half * 4 + ch
                    fw.op("act", lambda e, u=u, pb=pb: e.activation(out=U(u)[:, 0:ntok], in_=PS[pb][:, 0:ntok], func=AF.Copy, scale=1.0 / 16),
                          reads=[R_ps[pb]], writes=[R_u[u]])
                proj_FM(slab, 16, 512, hT_get, R_hT, ntok, c_mkT)

                def c_mk(tt, rows, pb, half=half):
                    u = uMKT + tt * 2 + half
                    fw.op("act", lambda e, u=u, pb=pb, rows=rows: e.activation(out=U(u)[:rows, :], in_=PS[pb][:rows, :], func=AF.Copy, scale=1.0 / 16),
                          reads=[R_ps[pb]], writes=[R_u[u]])
                proj_TM(slab, 16, 512, hT_get, R_hT, ntok, c_mk)
            fw.op("dve", lambda e: e.memset(mvA[:, :, 256:257], 1.0), writes=Ru_mv)
            for half in range(2):
                slab = load_slab(w_in, 16, 5120 + half * 512, 512)

                def c_mv(tt, rows, pb, half=half):
                    fw.op("act", lambda e, tt=tt, pb=pb, rows=rows, half=half: e.activation(
                        out=mvA[:rows, tt * 4 + half * 2: tt * 4 + half * 2 + 2, 0:256],
                        in_=PS[pb][:rows, :].rearrange("p (a e) -> p a e", e=256), func=AF.Copy),
                        reads=[R_ps[pb]], writes=Ru_mv)
                proj_TM(slab, 16, 512, hT_get, R_hT, ntok, c_mv)
            for half in range(2):
                slab = load_slab(w_in, 16, 6144 + half * 512, 512)

                def c_mo(tt, rows, pb, half=half):
                    u = uMO + tt * 2 + half
                    fw.op("act", lambda e, u=u, pb=pb, rows=rows: e.activation(out=U(u)[:rows, :], in_=PS[pb][:rows, :], func=AF.Sigmoid),
                          reads=[R_ps[pb]], writes=[R_u[u]])
                proj_TM(slab, 16, 512, hT_get, R_hT, ntok, c_mo)
            slab = load_slab(w_in, 16, 7168, 8)
            s_, wv = slab
            n = ntok
            row = lambda f: ARF[0:4, f, 0:n]
            for gi in range(2):
                pb = next_ps()
                for k in range(16):
                    fw.op("pe", lambda e, gi=gi, pb=pb, k=k: e.matmul(PS[pb][0:4, 0:n], lhsT=wv[:, k, gi * 4:(gi + 1) * 4], rhs=hT[:, k, 0:n],
                                                                     start=(k == 0), stop=(k == 15)),
                          reads=[R_hT[k], R_w[s_]], writes=[R_ps[pb]])
                if gi == 0:
                    fw.op("dve", lambda e, pb=pb: e.tensor_scalar(out=row(0), in0=PS[pb][0:4, 0:n], scalar1=bg_t[:, 0:1], scalar2=None, op0=ALU.add),
                          reads=[R_ps[pb], R_c], writes=[R_f[0]])
                else:
                    fw.op("act", lambda e, pb=pb: e.activation(out=row(1), in_=PS[pb][0:4, 0:n], func=AF.Exp, scale=-1.0, bias=bg_t[:, 2:3]),
                          reads=[R_ps[pb], R_c], writes=[R_f[1]])
            fw.op("act", lambda e: e.activation(out=row(1), in_=row(1), func=AF.Ln, bias=1.0), reads=[R_f[1]], writes=[R_f[1]])
            fw.op("dve", lambda e: e.tensor_scalar(out=row(1), in0=row(1), scalar1=-1.0, scalar2=None, op0=ALU.mult), reads=[R_f[1]], writes=[R_f[1]])
            fw.op("dve", lambda e: e.memset(row(7), 0.0), writes=[R_f[7]])
            fw.op("dve", lambda e: e.tensor_tensor_scan(out=row(2), data0=row(1), data1=row(0), initial=carry[:, 0:1], op0=ALU.add, op1=ALU.max),
                  reads=[R_f[1], R_f[0], R_carry], writes=[R_f[2]])
            fw.op("dve", lambda e: e.tensor_tensor_scan(out=row(3), data0=row(1), data1=row(7), initial=0.0, op0=ALU.add, op1=ALU.add),
                  reads=[R_f[1], R_f[7]], writes=[R_f[3]])
            fw.op("dve", lambda e: e.tensor_tensor(out=row(4), in0=row(3), in1=row(2), op=ALU.subtract), reads=[R_f[3], R_f[2]], writes=[R_f[4]])
            fw.op("dve", lambda e: e.tensor_tensor(out=row(0), in0=row(0), in1=row(3), op=ALU.subtract), reads=[R_f[0], R_f[3]], writes=[R_f[0]])
            fw.op("act", lambda e: e.activation(out=row(6), in_=row(2), func=AF.Exp, scale=-1.0), reads=[R_f[2]], writes=[R_f[6]])
            fw.op("dve", lambda e: e.tensor_copy(out=carry[:, 4:5], in_=carry[:, 0:1]), reads=[R_carry], writes=[R_carry])
            for c in range(1, nch):
                fw.op("dve", lambda e, c=c: e.tensor_scalar(out=carry[:, 4 + c:5 + c], in0=ARF[0:4, 4, c * L - 1:c * L], scalar1=-1.0, scalar2=None, op0=ALU.mult),
                      reads=[R_f[4], R_carry], writes=[R_carry])
            for c in range(nch):
                cs = slice(c * L, (c + 1) * L)
                fw.op("act", lambda e, c=c, cs=cs: e.activation(out=ARF[0:4, 5, cs], in_=ARF[0:4, 4, cs], func=AF.Exp, bias=carry[:, 4 + c:5 + c]),
                      reads=[R_f[4], R_carry], writes=[R_f[5]])
                fw.op("act", lambda e, c=c, cs=cs: e.activation(out=ARF[0:4, 7, cs], in_=ARF[0:4, 0, cs], func=AF.Exp, bias=ARF[0:4, 4, (c + 1) * L - 1:(c + 1) * L]),
                      reads=[R_f[0], R_f[4]], writes=[R_f[7]])
            fw.op("dve", lambda e: e.tensor_copy(out=carry[:, 0:1], in_=ARF[0:4, 2, n - 1:n]), reads=[R_f[2]], writes=[R_carry])
            pbT = next_ps()
            for tt in range(nt):
                rows = min(128, ntok - tt * 128)
                for qi, f in enumerate((0, 7, 5, 6)):
                    o = (tt * 4 + qi) * 4
                    fw.op("pe", lambda e, tt=tt, rows=rows, f=f, o=o: e.transpose(
                        out=PS[pbT][:rows, o:o + 4], in_=ARF[0:4, f, tt * 128: tt * 128 + rows], identity=identF[0:4, 0:4]),
                        reads=[R_f[f], R_c], writes=[R_ps[pbT]])
            rows_all = 128 if ntok >= 128 else ntok
            fw.op("dve", lambda e: e.tensor_copy(out=gcol[:rows_all, 0:nt * 16], in_=PS[pbT][:rows_all, 0:nt * 16]), reads=[R_ps[pbT]], writes=[R_gcol])
            pbG = next_ps()
            for hh in range(4):
                for c in range(nch):
                    e_c = (c + 1) * L - 1
                    fw.op("pe", lambda e, hh=hh, c=c, e_c=e_c: e.matmul(PS[pbG][:, hh * 8 + c: hh * 8 + c + 1], lhsT=esel_t[0:4, hh * 128:(hh + 1) * 128],
                                                                        rhs=ARF[0:4, 5, e_c:e_c + 1], start=True, stop=True),
                          reads=[R_f[5], R_c], writes=[R_ps[pbG]])
            fw.op("dve", lambda e: e.tensor_copy(out=GL[:, :], in_=PS[pbG][:, 0:32]), reads=[R_ps[pbG]], writes=[R_GL])
            wT = mlt[:, 0:64]; HN = mlt[:, 64:321]; tmpN = mlt[:, 384:641]; ddv = mlt[:, 700:704]; ssml = mlt[:, 704:712]
            for c in range(nch):
                tt = (c * L) // 128
                po = (c * L) % 128
                P = slice(po, po + L)
                cs = slice(c * L, (c + 1) * L)
                hmf = (tt % 2) * 2
                HM = ARF[:, hmf:hmf + 2, :].rearrange("p a b -> p (a b)")
                R_hm = [R_f[hmf], R_f[hmf + 1]]
                for hh in range(4):
                    gc = lambda qi, tt=tt, hh=hh: gcol[P, (tt * 4 + qi) * 4 + hh:(tt * 4 + qi) * 4 + hh + 1]
                    for dc in range(2):
                        j = hh * 2 + dc
                        fw.op("pe", lambda e, j=j, dc=dc, P=P, cs=cs, po=po: e.matmul(
                            PS[0][P, 0:L], lhsT=U(uMK + j)[:, cs], rhs=U(uMQ + j)[:, cs], start=(dc == 0), stop=(dc == 1),
                            tile_position=(0, po)),
                            reads=[R_u[uMK + j], R_u[uMQ + j]], writes=[R_ps[0]])
                    fw.op("pe", lambda e, hh=hh, P=P, cs=cs, po=po: e.matmul(
                        PS[1][P, 0:L], lhsT=esel_t[0:4, hh * 128: hh * 128 + L], rhs=ARF[0:4, 4, cs], start=True, stop=False,
                        tile_position=(0, po)),
                        reads=[R_f[4], R_c], writes=[R_ps[1]])
                    fw.op("pe", lambda e, P=P, po=po: e.matmul(
                        PS[1][P, 0:L], lhsT=identF[P, P], rhs=maskml_t[P, 0:L], start=False, stop=True, tile_position=(po, po)),
                        reads=[R_c], writes=[R_ps[1]])
                    fw.op("act", lambda e, P=P, g0=gc(0): e.activation(out=wT[P, 0:L], in_=PS[1][P, 0:L], func=AF.Exp, bias=g0),
                          reads=[R_ps[1], R_gcol], writes=[R_wT])
                    fw.op("dve", lambda e, P=P: e.tensor_tensor(out=U(uST)[P, 0:L], in0=PS[0][P, 0:L], in1=wT[P, 0:L], op=ALU.mult),
                          reads=[R_ps[0], R_wT], writes=[R_u[uST]])
                    fw.op("pe", lambda e, P=P, tt=tt, hh=hh, po=po: e.matmul(
                        PS[2][P, 0:257], lhsT=U(uST)[P, 0:L], rhs=mvA[P, tt * 4 + hh, :], start=True, stop=True, tile_position=(po, po)),
                        reads=[R_u[uST]] + Ru_mv, writes=[R_ps[2]])
                    for dc in range(2):
                        j = hh * 2 + dc
                        fw.op("pe", lambda e, j=j, dc=dc, P=P, cs=cs, po=po: e.matmul(
                            PS[5][P, 0:257], lhsT=U(uMQ + j)[:, cs], rhs=CTb[:, j, :], start=(dc == 0), stop=(dc == 1),
                            tile_position=(0, po)),
                            reads=[R_u[uMQ + j], R_CTb[j]], writes=[R_ps[5]])
                    fw.op("act", lambda e, P=P, g2=gc(2): e.activation(out=tmpN[P, :], in_=PS[5][P, 0:257], func=AF.Copy, scale=g2),
                          reads=[R_ps[5], R_gcol], writes=[R_tmpN])
                    fw.op("dve", lambda e, P=P: e.tensor_tensor(out=HN[P, :], in0=tmpN[P, :], in1=PS[2][P, 0:257], op=ALU.add),
                          reads=[R_tmpN, R_ps[2]], writes=[R_HN])
                    fw.op("dve", lambda e, P=P: e.tensor_scalar(out=ddv[P, 2:3], in0=HN[P, 256:257], scalar1=-1.0, scalar2=None, op0=ALU.mult),
                          reads=[R_HN], writes=[R_dd])
                    fw.op("dve", lambda e, P=P: e.tensor_tensor(out=ddv[P, 2:3], in0=ddv[P, 2:3], in1=HN[P, 256:257], op=ALU.max),
                          reads=[R_HN, R_dd], writes=[R_dd])
                    fw.op("dve", lambda e, P=P, g3=gc(3): e.tensor_scalar(out=ddv[P, 0:1], in0=ddv[P, 2:3], scalar1=g3, scalar2=None, op0=ALU.max),
                          reads=[R_dd, R_gcol], writes=[R_dd])
                    fw.op("dve", lambda e, P=P: e.reciprocal(out=ddv[P, 1:2], in_=ddv[P, 0:1]), reads=[R_dd], writes=[R_dd])
                    fw.op("dve", lambda e, P=P, hh=hh, HM=HM: e.tensor_scalar(out=HM[P, hh * 256:(hh + 1) * 256], in0=HN[P, 0:256], scalar1=ddv[P, 1:2],
                                                                             scalar2=None, op0=ALU.mult),
                          reads=[R_HN, R_dd], writes=R_hm)
                    fw.op("dve", lambda e, P=P, tt=tt, hh=hh, g1=gc(1): e.tensor_scalar(out=U(uVW)[P, 0:257], in0=mvA[P, tt * 4 + hh, :], scalar1=g1,
                                                                                      scalar2=None, op0=ALU.mult),
                          reads=Ru_mv + [R_gcol], writes=[R_u[uVW]])
                    for dc in range(2):
                        j = hh * 2 + dc
                        um = uMKT + tt * 2 + (hh // 2)
                        co = (hh % 2) * 256 + dc * 128
                        fw.op("pe", lambda e, dc=dc, um=um, co=co, P=P, po=po: e.matmul(
                            PS[3 + dc][:, 0:257], lhsT=U(um)[P, co:co + 128], rhs=U(uVW)[P, 0:257], start=True, stop=True,
                            tile_position=(po, 0)),
                            reads=[R_u[um], R_u[uVW]], writes=[R_ps[3 + dc]])
                        fw.op("dve", lambda e, j=j, dc=dc, hh=hh, c=c: e.scalar_tensor_tensor(
                            out=CT[:, j, :], in0=CT[:, j, :], scalar=GL[:, hh * 8 + c: hh * 8 + c + 1], in1=PS[3 + dc][:, 0:257],
                            op0=ALU.mult, op1=ALU.add),
                            reads=[R_CT[j], R_GL, R_ps[3 + dc]], writes=[R_CT[j]])
                        fw.op("act", lambda e, j=j: e.activation(out=CTb[:, j, :], in_=CT[:, j, :], func=AF.Copy),
                              reads=[R_CT[j]], writes=[R_CTb[j]])
                if (c + 1) * L % 128 == 0 or c == nch - 1:
                    rows = min(128, ntok - tt * 128)
                    for hh in range(4):
                        fw.op("act", lambda e, hh=hh, rows=rows, HM=HM: e.activation(out=mlt[:rows, 768:1024], in_=HM[:rows, hh * 256:(hh + 1) * 256],
                                                                                    func=AF.Square, accum_out=ssml[:rows, hh:hh + 1]),
                              reads=R_hm, writes=[R_ssml])
                    fw.op("act", lambda e, rows=rows: e.activation(out=ssml[:rows, 4:8], in_=ssml[:rows, 0:4], func=AF.Sqrt, bias=EPS, scale=1.0 / 256),
                          reads=[R_ssml], writes=[R_ssml])
                    fw.op("dve", lambda e, rows=rows: e.reciprocal(out=ssml[:rows, 4:8], in_=ssml[:rows, 4:8]), reads=[R_ssml], writes=[R_ssml])
                    for hh in range(4):
                        fw.op("dve", lambda e, hh=hh, rows=rows, HM=HM: e.scalar_tensor_tensor(
                            out=HM[:rows, hh * 256:(hh + 1) * 256], in0=HM[:rows, hh * 256:(hh + 1) * 256], scalar=ssml[:rows, 4 + hh:5 + hh],
                            in1=gml_bc[:rows, hh * 256:(hh + 1) * 256], op0=ALU.mult, op1=ALU.mult),
                            reads=R_hm + [R_ssml, R_c], writes=R_hm)
                    for half in range(2):
                        u = uMO + tt * 2 + half
                        fw.op("dve", lambda e, u=u, half=half, rows=rows, HM=HM: e.tensor_tensor(
                            out=U(u)[:rows, :], in0=HM[:rows, half * 512:(half + 1) * 512], in1=U(u)[:rows, :], op=ALU.mult),
                            reads=R_hm + [R_u[u]], writes=[R_u[u]])
                    for g in range(2):
                        u = uMO + tt * 2 + g
                        for j in range(4):
                            fw.op("pe", lambda e, g=g, j=j, u=u, rows=rows: e.transpose(
                                out=PTB[g][:, j * 128: j * 128 + rows], in_=U(u)[:rows, j * 128:(j + 1) * 128], identity=identB[:rows, :rows]),
                                reads=[R_u[u], R_c], writes=[R_pt[g]])
                        for j in range(4):
                            k = 8 + g * 4 + j
                            o_ap = U(uCAT + k)[:, tt * 128: tt * 128 + rows]
                            i_ap = PTB[g][:, j * 128: j * 128 + rows]
                            if g == 0:
                                fw.op("dve", lambda e, o_ap=o_ap, i_ap=i_ap: e.tensor_copy(out=o_ap, in_=i_ap), reads=[R_pt[g]], writes=[R_u[uCAT + k]])
                            else:
                                fw.op("act", lambda e, o_ap=o_ap, i_ap=i_ap: e.activation(out=o_ap, in_=i_ap, func=AF.Copy), reads=[R_pt[g]], writes=[R_u[uCAT + k]])
            if dbg is not None:
                for k in range(8, 16):
                    fw.dma("pool", D_dbg, dbg[k, :, tok0:tok0 + ntok], U(uCAT + k)[:, 0:ntok], reads=[R_u[uCAT + k]])

        def block(seq, tok0, src0, ntok, xsrc, ysink, kout, vout):
            nt = (ntok + 127) // 128
            for tt in range(nt):
                rows = min(128, ntok - tt * 128)
                fw.dma("sp", D_x[tt], xres[:rows, tt, :], xsrc[src0 + tt * 128: src0 + tt * 128 + rows, :], writes=[R_x[tt]])
            norm_to_hT(0, ntok)
            if stop < 1:
                return
            uQT = 0; uKT = 8; uV = 16
            for half in range(2):
                slab = load_slab(w_in, 16, half * 512, 512)

                def cq(ch, m, pb, half=half):
                    h = half * 4 + ch
                    fw.op("act", lambda e, h=h, pb=pb: e.activation(out=U(uQT + h)[:, 0:ntok], in_=PS[pb][:, 0:ntok], func=AF.Copy),
                          reads=[R_ps[pb]], writes=[R_u[uQT + h]])
                proj_FM(slab, 16, 512, hT_get, R_hT, ntok, cq)
            if stop < 2:
                return
            for half in range(2):
                slab = load_slab(w_in, 16, 1024 + half * 512, 512)

                def ck_tm(tt, rows, pb, half=half):
                    f = next_f()
                    fw.op("act", lambda e, f=f, pb=pb, rows=rows: e.activation(out=ARF[:rows, f, :], in_=PS[pb][:rows, :], func=AF.Copy),
                          reads=[R_ps[pb]], writes=[R_f[f]])
                    fw.dma("sp", D_f[f], kout[src0 + tt * 128: src0 + tt * 128 + rows, half * 512:(half + 1) * 512], ARF[:rows, f, :],
                           reads=[R_f[f]])
                proj_TM(slab, 16, 512, hT_get, R_hT, ntok, ck_tm)

                def ck_fm(ch, m, pb, half=half):
                    h = half * 4 + ch
                    fw.op("dve", lambda e, h=h, pb=pb: e.tensor_copy(out=U(uKT + h)[:, 0:ntok], in_=PS[pb][:, 0:ntok]),
                          reads=[R_ps[pb]], writes=[R_u[uKT + h]])
                    fw.dma("sp", D_unit(uKT + h), ktS[seq][h, :, tok0:tok0 + ntok], U(uKT + h)[:, 0:ntok], reads=[R_u[uKT + h]], writes=[R_ktS[seq][h]])
                proj_FM(slab, 16, 512, hT_get, R_hT, ntok, ck_fm)
            if stop < 3:
                return
            for half in range(2):
                slab = load_slab(w_in, 16, 2048 + half * 512, 512)

                def cv_tm(tt, rows, pb, half=half):
                    f = next_f()
                    fw.op("act", lambda e, f=f, pb=pb, rows=rows: e.activation(out=ARF[:rows, f, :], in_=PS[pb][:rows, :], func=AF.Copy),
                          reads=[R_ps[pb]], writes=[R_f[f]])
                    fw.dma("sp", D_f[f], vout[src0 + tt * 128: src0 + tt * 128 + rows, half * 512:(half + 1) * 512], ARF[:rows, f, :],
                           reads=[R_f[f]])
                    u = uV + tt * 2 + half
                    fw.op("dve", lambda e, u=u, f=f, rows=rows: e.tensor_copy(out=U(u)[:rows, :], in_=ARF[:rows, f, :]),
                          reads=[R_f[f]], writes=[R_u[u]])
                    kt = (tok0 + tt * 128) // 128
                    for hh in range(4):
                        fw.dma("sp", D_unit(u), vS[seq][half * 4 + hh, 0:rows, kt, :], U(u)[:rows, hh * 128:(hh + 1) * 128], reads=[R_u[u]],
                               writes=[R_vS[seq][half * 4 + hh]])
                proj_TM(slab, 16, 512, hT_get, R_hT, ntok, cv_tm)
            if stop < 4:
                return
            attention(seq, tok0, ntok, diag=(seq == 0))
            if stop < 5:
                return
            mlstm(seq, tok0, ntok, 64 if seq == 0 else TS)
            if stop < 6:
                return
            proj_residual(w_out, lambda k: U(uCAT + k), R_u[uCAT:uCAT + 16], ntok, [(0, 16)])
            if stop < 7:
                dump_x(ntok, ysink, src0)
                return
            xattn(seq, ntok)
            if stop < 8:
                dump_x(ntok, ysink, src0)
                return
            ffn(ntok)
            final_norm(ntok, ysink, src0)

        def proj_residual(w, actT, actR, ntok, kparts):
            for cg in range(4):
                for pi, (k0, kc_n) in enumerate(kparts):
                    slab = load_slab(w, kc_n, cg * 512, 512, r0=k0 * 128)

                    def cons(tt, rows, pb, cg=cg):
                        fw.op("dve", lambda e, tt=tt, rows=rows, pb=pb, cg=cg: e.tensor_tensor(
                            out=xres[:rows, tt, cg * 512:(cg + 1) * 512], in0=xres[:rows, tt, cg * 512:(cg + 1) * 512],
                            in1=PS[pb][:rows, :], op=ALU.add), reads=[R_ps[pb], R_x[tt]], writes=[R_x[tt]])
                    proj_TM(slab, kc_n, 512, actT, actR, ntok, cons, k0=k0, acc=[0, 1, 2, 3],
                            first=(pi == 0), last=(pi == len(kparts) - 1))

        uXQ = 0; uMKT_ = 16; uMV_ = 24; uOT = 32; uXP = 48
        memKT = AR[:, uMKT_ * 512:(uMKT_ + 8) * 512].rearrange("p (j k) -> p j k", k=256)
        memV = AR[:, uMV_ * 512:(uMV_ + 8) * 512].rearrange("p (t c) -> p t c", c=2048)
        Ru_mkt = R_u[uMKT_:uMKT_ + 8]; Ru_mvv = R_u[uMV_:uMV_ + 8]

        def load_mem(seq):
            if seq == 0:
                fw.dma("sp", D_mem, AR[:, uMKT_ * 512:(uMKT_ + 8) * 512], mkS[0][:, :], reads=[R_mkS[0]], writes=Ru_mkt)
                fw.dma("sp", D_mem, AR[:, uMV_ * 512:(uMV_ + 8) * 512], mvS[0][:, :], reads=[R_mvS[0]], writes=Ru_mvv)
            else:
                for tt in range(2):
                    fw.dma("pool", D_memp, memV[:, tt, :], cmv[tt * 128:(tt + 1) * 128, :], writes=Ru_mvv)
                    ust = uOT + tt * 4
                    fw.dma("pool", D_memp, AR[:, ust * 512:(ust + 4) * 512], cmk[tt * 128:(tt + 1) * 128, :], writes=R_u[ust:ust + 4])
                    for g4 in range(4):
                        g = g4 % 2
                        for j in range(4):
                            jj = g4 * 4 + j
                            fw.op("pe", lambda e, g=g, j=j, jj=jj, ust=ust: e.transpose(
                                out=PTB[g][:, j * 128:(j + 1) * 128], in_=AR[:, ust * 512 + jj * 128: ust * 512 + (jj + 1) * 128],
                                identity=identB[:, :]), reads=R_u[ust:ust + 4] + [R_c], writes=[R_pt[g]])
                        for j in range(4):
                            jj = g4 * 4 + j
                            o_ap = memKT[:, jj, tt * 128:(tt + 1) * 128]
                            i_ap = PTB[g][:, j * 128:(j + 1) * 128]
                            if g == 0:
                                fw.op("dve", lambda e, o_ap=o_ap, i_ap=i_ap: e.tensor_copy(out=o_ap, in_=i_ap), reads=[R_pt[g]], writes=Ru_mkt)
                            else:
                                fw.op("act", lambda e, o_ap=o_ap, i_ap=i_ap: e.activation(out=o_ap, in_=i_ap, func=AF.Copy), reads=[R_pt[g]], writes=Ru_mkt)

        def xattn(seq, ntok):
            n = ntok
            norm_to_hT(1, ntok)
            load_mem(seq)
            for cg in range(4):
                slab = load_slab(w_mq, 16, cg * 512, 512)

                def c_q(ch, m, pb, cg=cg):
                    u = uXQ + cg * 4 + ch
                    fw.op("act", lambda e, u=u, pb=pb: e.activation(out=U(u)[:, 0:n], in_=PS[pb][:, 0:n], func=AF.Copy),
                          reads=[R_ps[pb]], writes=[R_u[u]])
                proj_FM(slab, 16, 512, hT_get, R_hT, ntok, c_q)
            rinv = ARF[:, 6, :]
            for hh in range(4):
                for kt in range(2):
                    sbk = kt
                    for dc in range(4):
                        j = hh * 4 + dc
                        fw.op("pe", lambda e, j=j, dc=dc, kt=kt, sbk=sbk: e.matmul(
                            PS[sbk][:, 0:n], lhsT=memKT[:, j, kt * 128:(kt + 1) * 128], rhs=U(uXQ + j)[:, 0:n],
                            start=(dc == 0), stop=(dc == 3)), reads=Ru_mkt + [R_u[uXQ + j]], writes=[R_ps[sbk]])
                    fw.op("act", lambda e, kt=kt, sbk=sbk: e.activation(out=U(uXP + kt)[:, 0:n], in_=PS[sbk][:, 0:n], func=AF.Exp,
                                                                       scale=512.0 ** -0.5), reads=[R_ps[sbk]], writes=[R_u[uXP + kt]])
                for kt in range(2):
                    fw.op("pe", lambda e, kt=kt: e.matmul(PS[4][:, 0:n], lhsT=onesB[:, :], rhs=U(uXP + kt)[:, 0:n], start=(kt == 0), stop=(kt == 1)),
                          reads=[R_c, R_u[uXP + kt]], writes=[R_ps[4]])
                fw.op("act", lambda e: e.activation(out=rinv[:, 0:n], in_=PS[4][:, 0:n], func=AF.Ln), reads=[R_ps[4]], writes=[R_f[6]])
                fw.op("act", lambda e: e.activation(out=rinv[:, 0:n], in_=rinv[:, 0:n], func=AF.Exp, scale=-1.0), reads=[R_f[6]], writes=[R_f[6]])
                for jv in range(4):
                    ob = 2 + (jv % 2)
                    for kt in range(2):
                        fw.op("pe", lambda e, jv=jv, kt=kt, ob=ob, hh=hh: e.matmul(
                            PS[ob][:, 0:n], lhsT=memV[:, kt, hh * 512 + jv * 128: hh * 512 + (jv + 1) * 128], rhs=U(uXP + kt)[:, 0:n],
                            start=(kt == 0), stop=(kt == 1)), reads=Ru_mvv + [R_u[uXP + kt]], writes=[R_ps[ob]])
                    u = uOT + hh * 4 + jv
                    fw.op("dve", lambda e, u=u, ob=ob: e.tensor_tensor(out=U(u)[:, 0:n], in0=PS[ob][:, 0:n], in1=rinv[:, 0:n], op=ALU.mult),
                          reads=[R_ps[ob], R_f[6]], writes=[R_u[u]])
            proj_residual(w_mo, lambda k: U(uOT + k), R_u[uOT:uOT + 16], ntok, [(0, 16)])

        def ffn(ntok):
            n = ntok
            norm_to_hT(2, ntok)
            nslab = 11
            for si in range(nslab):
                ncols = 512 if si < 10 else DFF - 5120
                slab_g = load_slab(w_gate, 16, si * 512, ncols)
                slab_u = load_slab(w_up, 16, si * 512, ncols)
                nchk = ncols // 128
                for ch in range(nchk):
                    s_, wv = slab_g
                    pg = next_ps()
                    for k in range(16):
                        fw.op("pe", lambda e, wv=wv, pg=pg, k=k, ch=ch: e.matmul(
                            PS[pg][:, 0:n], lhsT=wv[:, k, ch * 128:(ch + 1) * 128], rhs=hT[:, k, 0:n], start=(k == 0), stop=(k == 15)),
                            reads=[R_hT[k], R_w[s_]], writes=[R_ps[pg]])
                    fw.op("act", lambda e, ch=ch, pg=pg: e.activation(out=ARF[:, 4 + ch, 0:n], in_=PS[pg][:, 0:n], func=AF.Copy),
                          reads=[R_ps[pg]], writes=[R_f[4 + ch]])
                for ch in range(nchk):
                    s_, wv = slab_u
                    f_ = si * 4 + ch
                    pu = next_ps()
                    for k in range(16):
                        fw.op("pe", lambda e, wv=wv, pu=pu, k=k, ch=ch: e.matmul(
                            PS[pu][:, 0:n], lhsT=wv[:, k, ch * 128:(ch + 1) * 128], rhs=hT[:, k, 0:n], start=(k == 0), stop=(k == 15)),
                            reads=[R_hT[k], R_w[s_]], writes=[R_ps[pu]])
                    G = ARF[:, 4 + ch, :]; RG = [R_f[4 + ch]]
                    fa = next_f()
                    acc = ARF[:, fa, :]
                    cw = lambda j, f_=f_: convw_t[:, j, f_:f_ + 1]
                    fw.op("dve", lambda e, G=G, acc=acc, w2=cw(2), b=cw(3): e.tensor_scalar(out=acc[:, 0:n], in0=G[:, 0:n], scalar1=w2, scalar2=b,
                                                                                       op0=ALU.mult, op1=ALU.add),
                          reads=RG + [R_c], writes=[R_f[fa]])
                    fw.op("dve", lambda e, G=G, acc=acc, w1=cw(1): e.scalar_tensor_tensor(
                        out=acc[:, 1:n], in0=G[:, 0:n - 1], scalar=w1, in1=acc[:, 1:n], op0=ALU.mult, op1=ALU.add),
                        reads=RG + [R_c, R_f[fa]], writes=[R_f[fa]])
                    fw.op("dve", lambda e, G=G, acc=acc, w0=cw(0): e.scalar_tensor_tensor(
                        out=acc[:, 2:n], in0=G[:, 0:n - 2], scalar=w0, in1=acc[:, 2:n], op0=ALU.mult, op1=ALU.add),
                        reads=RG + [R_c, R_f[fa]], writes=[R_f[fa]])
                    fw.op("dve", lambda e, acc=acc, w1=cw(1), f_=f_: e.scalar_tensor_tensor(
                        out=acc[:, 0:1], in0=halo[:, 1, f_:f_ + 1], scalar=w1, in1=acc[:, 0:1], op0=ALU.mult, op1=ALU.add),
                        reads=[R_halo, R_c, R_f[fa]], writes=[R_f[fa]])
                    fw.op("dve", lambda e, acc=acc, w0=cw(0), f_=f_: e.scalar_tensor_tensor(
                        out=acc[:, 0:2], in0=halo[:, :, f_], scalar=w0, in1=acc[:, 0:2], op0=ALU.mult, op1=ALU.add),
                        reads=[R_halo, R_c, R_f[fa]], writes=[R_f[fa]])
                    fw.op("dve", lambda e, G=G, f_=f_: e.tensor_copy(out=halo[:, :, f_], in_=G[:, n - 2:n]), reads=RG, writes=[R_halo])
                    fw.op("act", lambda e, acc=acc: e.activation(out=acc[:, 0:n], in_=acc[:, 0:n], func=AF.Silu), reads=[R_f[fa]], writes=[R_f[fa]])
                    fw.op("dve", lambda e, acc=acc, pu=pu, f_=f_: e.tensor_tensor(out=U(f_)[:, 0:n], in0=acc[:, 0:n], in1=PS[pu][:, 0:n], op=ALU.mult),
                          reads=[R_f[fa], R_ps[pu]], writes=[R_u[f_]])
            proj_residual(w_down, lambda k: U(k), R_u[0:43], ntok, [(0, 16), (16, 16), (32, 11)])

        def final_norm(ntok, ysink, src0):
            nt = (ntok + 127) // 128
            for tt in range(nt):
                rows = min(128, ntok - tt * 128)
                c = 32 + 4 * tt
                fw.op("dve", lambda e, c=c: e.memset(stat[:, c:c + 2], 0.0), writes=[R_stat])
                fw.op("act", lambda e, tt=tt, rows=rows, c=c: e.activation(
                    out=xn[:rows, :], in_=xres[:rows, tt, :], func=AF.Square, accum_out=stat[:rows, c:c + 1]),
                    reads=[R_x[tt]], writes=[R_xn, R_stat])
                fw.op("act", lambda e, rows=rows, c=c: e.activation(
                    out=stat[:rows, c + 1:c + 2], in_=stat[:rows, c:c + 1], func=AF.Sqrt, bias=EPS, scale=1.0 / D),
                    reads=[R_stat], writes=[R_stat])
                fw.op("dve", lambda e, rows=rows, c=c: e.reciprocal(out=stat[:rows, c + 2:c + 3], in_=stat[:rows, c + 1:c + 2]),
                      reads=[R_stat], writes=[R_stat])
                fw.op("dve", lambda e, tt=tt, rows=rows, c=c: e.scalar_tensor_tensor(
                    out=xres[:rows, tt, :], in0=xres[:rows, tt, :], scalar=stat[:rows, c + 2:c + 3], in1=gfinal_bc[:rows, :],
                    op0=ALU.mult, op1=ALU.mult), reads=[R_x[tt], R_stat, R_c], writes=[R_x[tt]])
                fw.dma("sp", D_x[tt], ysink[src0 + tt * 128: src0 + tt * 128 + rows, :], xres[:rows, tt, :], reads=[R_x[tt]])

        def dump_x(ntok, ysink, src0):
            for tt in range((ntok + 127) // 128):
                rows = min(128, ntok - tt * 128)
                fw.dma("sp", D_x[tt], ysink[src0 + tt * 128: src0 + tt * 128 + rows, :], xres[:rows, tt, :], reads=[R_x[tt]])

        def conv_state_out(outp):
            pb = next_ps()
            fw.op("pe", lambda e, pb=pb: e.transpose(out=PS[pb][0:86, 0:128], in_=halo[:].rearrange("p j f -> p (j f)"), identity=identF[:, :]),
                  reads=[R_halo, R_c], writes=[R_ps[pb]])
            f = next_f()
            fw.op("dve", lambda e, pb=pb, f=f: e.tensor_copy(out=ARF[0:86, f, 0:128], in_=PS[pb][0:86, 0:128]), reads=[R_ps[pb]], writes=[R_f[f]])
            fw.dma("sp", D_f[f], outp.rearrange("j (f p) -> (j f) p", p=128), ARF[0:86, f, 0:128], reads=[R_f[f]])

        def conv_state_in(inp_):
            f = next_f()
            fw.dma("sp", D_f[f], ARF[0:86, f, 0:128], inp_.rearrange("j (f p) -> (j f) p", p=128), writes=[R_f[f]])
            pb = next_ps()
            fw.op("pe", lambda e, pb=pb, f=f: e.transpose(out=PS[pb][:, 0:86], in_=ARF[0:86, f, 0:128], identity=identF[0:86, 0:86]),
                  reads=[R_f[f], R_c], writes=[R_ps[pb]])
            fw.op("dve", lambda e, pb=pb: e.tensor_copy(out=halo[:].rearrange("p j f -> p (j f)"), in_=PS[pb][:, 0:86]), reads=[R_ps[pb]], writes=[R_halo])

        def mem_block():
            for tt in range(2):
                fw.dma("sp", D_x[tt], xres[:, tt, :], mem[tt * 128:(tt + 1) * 128, :], writes=[R_x[tt]])
            norm_to_hT(3, MEM)
            for (w, outp) in ((w_mk, pmk), (w_mv, pmv)):
                for cg in range(4):
                    slab = load_slab(w, 16, cg * 512, 512)

                    def cm(tt, rows, pb, cg=cg, outp=outp):
                        f = next_f()
                        fw.op("act", lambda e, f=f, pb=pb, rows=rows: e.activation(out=ARF[:rows, f, :], in_=PS[pb][:rows, :], func=AF.Copy),
                              reads=[R_ps[pb]], writes=[R_f[f]])
                        fw.dma("sp", D_f[f], outp[tt * 128: tt * 128 + rows, cg * 512:(cg + 1) * 512], ARF[:rows, f, :],
                               reads=[R_f[f]])
                        if outp is pmv:
                            u = 36 + (tt * 4 + cg) % 4
                            fw.op("dve", lambda e, u=u, f=f: e.tensor_copy(out=U(u)[:, :], in_=ARF[:, f, :]), reads=[R_f[f]], writes=[R_u[u]])
                            fw.dma("sp", D_unit(u), mvS[0][:, tt * 2048 + cg * 512: tt * 2048 + (cg + 1) * 512], U(u)[:, :], reads=[R_u[u]],
                                   writes=[R_mvS[0]])
                    proj_TM(slab, 16, 512, hT_get, R_hT, MEM, cm)
                    if w is w_mk:
                        def cmkT(ch, m, pb, cg=cg):
                            j = cg * 4 + ch
                            u = 32 + (j % 4)
                            fw.op("dve", lambda e, u=u, pb=pb: e.tensor_copy(out=U(u)[:, 0:MEM], in_=PS[pb][:, 0:MEM]), reads=[R_ps[pb]], writes=[R_u[u]])
                            fw.dma("sp", D_unit(u), mkS[0][:, j * 256:(j + 1) * 256], U(u)[:, 0:MEM], reads=[R_u[u]], writes=[R_mkS[0]])
                        proj_FM(slab, 16, 512, hT_get, R_hT, MEM, cmkT)

        for nm_ in ("w_mk", "w_mv", "w_in", "w_out", "w_mq", "w_mo", "w_gate", "w_up", "w_down"):
            convert_weight(nm_)
        if do_mem:
            mem_block()
        ml_init_zero()
        fw.op("dve", lambda e: e.memset(halo[:], 0.0), writes=[R_halo])
        for b in range(nblk):
            block(0, b * 512, b * 512, 512, x, y, pk, pv)
        if stop >= 5:
            ml_out_state(pc, pn, pm)
        if stop >= 8:
            conv_state_out(pconv)
        if sample:
            if stop >= 8:
                conv_state_in(conv0)
            cache_prologue()
            if stop >= 5:
                ml_init_state()
            block(1, T, 0, TS, xs, ys, sk, sv)
            if stop >= 5:
                ml_out_state(sc, sn, sm)
            if stop >= 8:
                conv_state_out(sconv)
        fw.emit()
    return nc


def _prep_inputs(inp, b):
    f = np.float32
    g = lambda k: np.asarray(inp[k], dtype=f)
    d = {}
    d["x"] = np.ascontiguousarray(g("x_prompt")[b])
    d["xs"] = np.ascontiguousarray(g("x_sample")[b])
    d["ck"] = np.ascontiguousarray(g("cache_da_k")[0, b].reshape(T, 1024))
    d["cv"] = np.ascontiguousarray(g("cache_da_v")[0, b].reshape(T, 1024))
    d["c0"] = np.ascontiguousarray(g("state_ml_c")[0, b])
    d["n0"] = np.ascontiguousarray(g("state_ml_n")[0, b])
    d["m0"] = np.ascontiguousarray(g("state_ml_m")[0, b].reshape(4, 1))
    d["conv0"] = np.ascontiguousarray(g("state_ffn_conv")[0, b])
    d["cmk"] = np.ascontiguousarray(g("cache_mem_k")[0, b].reshape(MEM, D))
    d["cmv"] = np.ascontiguousarray(g("cache_mem_v")[0, b].reshape(MEM, D))
    d["mem"] = np.ascontiguousarray(g("mem_prompt")[b])
    for k in ("w_in", "w_out", "w_mq", "w_mk", "w_mv", "w_mo", "w_gate", "w_up", "w_down"):
        d[k] = np.ascontiguousarray(g(k)[0])
    gs = np.stack([g("g_mix")[0], g("g_xattn")[0], g("g_ffn")[0], g("g_mem")[0]], 0)
    d["gpk"] = np.ascontiguousarray(gs.reshape(4, 16, 128).transpose(2, 0, 1))
    d["lamv"] = np.concatenate([g("lambda_q1")[0], g("lambda_k1")[0], g("lambda_q2")[0], g("lambda_k2")[0]])[None, :].copy()
    d["gda"] = np.ascontiguousarray(g("g_da_sub")[0].reshape(128, 1))
    d["bgate"] = np.ascontiguousarray(np.stack([g("b_ig")[0], g("b_fg")[0]], 1))
    d["gml"] = np.ascontiguousarray(g("g_ml")[0])
    cw = np.concatenate([g("conv_w")[0], g("conv_b")], 0)
    d["convw"] = np.ascontiguousarray(cw.reshape(4, 43, 128).transpose(2, 0, 1))
    d["gfinal"] = np.ascontiguousarray(g("g_final"))
    d["identf"] = np.eye(128, dtype=f)
    kk = np.arange(128)[:, None, None] + 128 * np.arange(4)[None, :, None]
    qq = np.arange(512)[None, None, :]
    d["masks"] = np.ascontiguousarray(((kk // 64) <= (qq // 64)).astype(f))
    es_ = np.zeros((4, 4, 128), f)
    for hh in range(4):
        es_[hh, hh, :] = 1.0
    d["esel"] = es_.reshape(4, 512)
    pp = np.arange(128)[:, None] % 64
    tt_ = np.arange(64)[None, :]
    d["maskml"] = np.where(pp <= tt_, 0.0, -1e30).astype(f)
    return d


_NC_CACHE = {}


def kernel(**inp):
    cfg = ("full",)
    if cfg not in _NC_CACHE:
        _NC_CACHE[cfg] = build()
    nc = _NC_CACHE[cfg]
    in_maps = [_prep_inputs(inp, b) for b in range(8)]
    res = run_bass_kernel_spmd(nc, in_maps, core_ids=list(range(8)))
    r = res.results
    st = lambda k: np.stack([np.asarray(r[b][k], dtype=np.float32) for b in range(8)], 0)
    y_prompt = st("y")
    y_sample = st("ys")
    p_k = st("pk").reshape(1, 8, T, 8, 128)
    p_v = st("pv").reshape(1, 8, T, 8, 128)
    p_c = st("pc")[None]
    p_n = st("pn")[None]
    p_m = st("pm").reshape(1, 8, 4)
    p_conv = st("pconv")[None]
    p_mk = st("pmk").reshape(1, 8, MEM, 4, 512)
    p_mv = st("pmv").reshape(1, 8, MEM, 4, 512)
    s_k = st("sk").reshape(1, 8, TS, 8, 128)
    s_v = st("sv").reshape(1, 8, TS, 8, 128)
    s_c = st("sc")[None]
    s_n = st("sn")[None]
    s_m = st("sm").reshape(1, 8, 4)
    s_conv = st("sconv")[None]
    return (y_prompt, y_sample, p_k, p_v, p_c, p_n, p_m, p_conv, p_mk, p_mv,
            s_k, s_v, s_c, s_n, s_m, s_conv)
```

```python
import contextlib
import numpy as np
import concourse.bass as bass
import concourse.mybir as mybir
from concourse.bass_utils import run_bass_kernel_spmd

F32 = mybir.dt.float32
BF16 = mybir.dt.bfloat16
AF = mybir.ActivationFunctionType
ALU = mybir.AluOpType
AX = mybir.AxisListType
ENGS = ("pe", "act", "dve", "pool", "sp")

D = 2048
T = 4096
TS = 16
DIN = 7176
DFF = 5504
NH = 8
MEM = 256
EPS = 1e-6
LAM_INIT = 0.2
ATTACH_WAIT = True


class Reg:
    __slots__ = ("name", "w", "r")

    def __init__(self, name=""):
        self.name = name
        self.w = None
        self.r = []


class DSem:
    __slots__ = ("sem", "count", "name")

    def __init__(self, name):
        self.name = name
        self.sem = None
        self.count = 0


class Op:
    __slots__ = ("eng", "fn", "cwaits", "dwaits", "sig", "sigidx", "dsem", "dval")

    def __init__(self, eng, fn):
        self.eng = eng
        self.fn = fn
        self.cwaits = []
        self.dwaits = []
        self.sig = False
        self.sigidx = 0
        self.dsem = None
        self.dval = 0


class FW:
    def __init__(self, nc):
        self.nc = nc
        self.ops = {e: [] for e in ENGS}
        self.dsems = []
        self.nops = 0

    def dsem(self, name):
        d = DSem(name)
        self.dsems.append(d)
        return d

    def _deps(self, o, reads, writes):
        deps = []
        seen = set()
        for r in reads:
            if r.w is not None and id(r.w) not in seen:
                seen.add(id(r.w)); deps.append(r.w)
        for w in writes:
            if w.w is not None and id(w.w) not in seen:
                seen.add(id(w.w)); deps.append(w.w)
            for x in w.r:
                if id(x) not in seen:
                    seen.add(id(x)); deps.append(x)
        for d in deps:
            if d is o:
                continue
            if d.dsem is not None:
                o.dwaits.append((d.dsem, d.dsem.count))
            else:
                if o.eng == "pe" and d.eng == "pe":
                    continue
                d.sig = True
                o.cwaits.append(d)
        for r in reads:
            r.r.append(o)
        for w in writes:
            w.w = o
            w.r = []

    def op(self, eng, fn, reads=(), writes=()):
        o = Op(eng, fn)
        self._deps(o, reads, writes)
        self.ops[eng].append(o)
        self.nops += 1
        return o

    def dma(self, eng, dsem, out_ap, in_ap, reads=(), writes=(), slow=False):
        if slow:
            def fn(e):
                return e.dma_start(out=out_ap, in_=in_ap, allow_slow_non_contiguous=True)
        else:
            def fn(e):
                return e.dma_start(out=out_ap, in_=in_ap)
        o = Op(eng, fn)
        self._deps(o, reads, writes)
        dsem.count += 16
        o.dsem = dsem
        o.dval = dsem.count
        self.ops[eng].append(o)
        self.nops += 1
        return o

    def emit(self):
        nc = self.nc
        with contextlib.ExitStack() as es:
            csem = {}
            for e in ENGS:
                csem[e] = es.enter_context(nc.semaphore("c_" + e))
            for d in self.dsems:
                if d.count > 0:
                    d.sem = es.enter_context(nc.semaphore("d_" + d.name))
            for e in ENGS:
                c = 0
                for o in self.ops[e]:
                    if o.sig and o.dsem is None:
                        c += 1
                        o.sigidx = c
            block = es.enter_context(nc.Block())
            final_d = [(d.sem, d.count) for d in self.dsems if d.count > 0]

            def run(e, engobj, last=False):
                seen = {}
                for o in self.ops[e]:
                    need = {}
                    for p in o.cwaits:
                        s = csem[p.eng]
                        v = p.sigidx
                        if seen.get(id(s), 0) < v:
                            need[id(s)] = (s, max(v, need.get(id(s), (s, 0))[1]))
                            seen[id(s)] = v
                    for (d, v) in o.dwaits:
                        if seen.get(id(d), 0) < v:
                            need[id(d)] = (d.sem, max(v, need.get(id(d), (d.sem, 0))[1]))
                            seen[id(d)] = v
                    need = list(need.values())
                    attach = need.pop() if (need and ATTACH_WAIT and o.dsem is None and e != "pe") else None
                    for (s, v) in need:
                        engobj.wait_ge(s, v)
                    ins = o.fn(engobj)
                    if attach is not None:
                        ins._wait_ge(attach[0], attach[1])
                    if o.dsem is not None:
                        ins.then_inc(o.dsem.sem, 16)
                    elif o.sig:
                        ins.then_inc(csem[e], 1)
                if last:
                    for (s, v) in final_d:
                        engobj.wait_ge(s, v)

            @block.tensor
            def _(pe):
                run("pe", pe)

            @block.scalar
            def _(act):
                run("act", act)

            @block.vector
            def _(dve):
                run("dve", dve)

            @block.gpsimd
            def _(pool):
                run("pool", pool)

            @block.sync
            def _(sp):
                run("sp", sp, last=True)


IN_NAMES = ["x", "xs", "ck", "cv", "c0", "n0", "m0", "conv0", "cmk", "cmv", "mem",
            "w_in", "w_out", "w_mq", "w_mk", "w_mv", "w_mo", "w_gate", "w_up", "w_down",
            "gpk", "lamv", "gda", "bgate", "gml", "convw", "gfinal", "identf", "masks"]


def build(nblk=8, sample=True, phases=("inproj",), stop=99, do_mem=True, debug=False):
    nc = bass.Bass("TRN2", target_bir_lowering=False)

    def din(name, shape):
        return nc.dram_tensor(name, shape, F32, kind="ExternalInput").ap()

    def dout(name, shape):
        return nc.dram_tensor(name, shape, F32, kind="ExternalOutput").ap()

    x = din("x", [T, D]); xs = din("xs", [TS, D])
    ck = din("ck", [T, 1024]); cv = din("cv", [T, 1024])
    c0 = din("c0", [4, 256, 256]); n0 = din("n0", [4, 256]); m0 = din("m0", [4, 1])
    conv0 = din("conv0", [2, DFF])
    cmk = din("cmk", [MEM, D]); cmv = din("cmv", [MEM, D]); mem = din("mem", [MEM, D])
    w_in = din("w_in", [D, DIN]); w_out = din("w_out", [D, D]); w_mq = din("w_mq", [D, D])
    w_mk = din("w_mk", [D, D]); w_mv = din("w_mv", [D, D]); w_mo = din("w_mo", [D, D])
    w_gate = din("w_gate", [D, DFF]); w_up = din("w_up", [D, DFF]); w_down = din("w_down", [DFF, D])
    gpk = din("gpk", [128, 4, 16])
    lamv = din("lamv", [1, 256])
    gda = din("gda", [128, 1])
    bgate = din("bgate", [4, 2])
    gml = din("gml", [1024])
    convw = din("convw", [128, 4, 43])
    gfinal = din("gfinal", [D])
    identf = din("identf", [128, 128])
    masks = din("masks", [128, 4, 512])
    esel = din("esel", [4, 512])
    maskml = din("maskml", [128, 64])

    y = dout("y", [T, D]); ys = dout("ys", [TS, D])
    pk = dout("pk", [T, 1024]); pv = dout("pv", [T, 1024])
    pc = dout("pc", [4, 256, 256]); pn = dout("pn", [4, 256]); pm = dout("pm", [4, 1])
    pconv = dout("pconv", [2, DFF])
    pmk = dout("pmk", [MEM, D]); pmv = dout("pmv", [MEM, D])
    sk = dout("sk", [TS, 1024]); sv = dout("sv", [TS, 1024])
    sc = dout("sc", [4, 256, 256]); sn = dout("sn", [4, 256]); sm = dout("sm", [4, 1])
    sconv = dout("sconv", [2, DFF])

    NKT = 33
    ktS = [nc.dram_tensor("ktS%d" % i, [NH, 128, NKT * 128], BF16, kind="Internal").ap() for i in range(2)]
    vS = [nc.dram_tensor("vS%d" % i, [NH, 128, NKT, 128], BF16, kind="Internal").ap() for i in range(2)]
    mkS = nc.dram_tensor("mkS", [2, 128, 16 * 256], BF16, kind="Internal").ap()
    mvS = nc.dram_tensor("mvS", [2, 128, 2 * 2048], BF16, kind="Internal").ap()

    dbg = nc.dram_tensor("dbg", [16, 128, T + TS], F32, kind="ExternalOutput").ap() if debug else None
    WSPEC = {"w_in": (w_in, 16, 15), "w_out": (w_out, 16, 4), "w_mq": (w_mq, 16, 4), "w_mk": (w_mk, 16, 4), "w_mv": (w_mv, 16, 4),
             "w_mo": (w_mo, 16, 4), "w_gate": (w_gate, 16, 11), "w_up": (w_up, 16, 11), "w_down": (w_down, 43, 12)}
    WB = {k: nc.dram_tensor("wb_" + k, [v[2], 128, 8192], BF16, kind="Internal").ap() for k, v in WSPEC.items()}
    fw = FW(nc)
    es = contextlib.ExitStack()
    with es:
        def sb(name, shape, dt):
            return es.enter_context(nc.sbuf_tensor(name, shape, dt))

        xres = sb("xres", [128, 4, D], F32); R_x = [Reg("x%d" % i) for i in range(4)]
        xn = sb("xn", [128, D], BF16); R_xn = Reg("xn")
        hT = sb("hT", [128, 16, 512], BF16); R_hT = [Reg("hT%d" % i) for i in range(16)]
        NSLOT = 2
        wsl = [sb("wsl%d" % i, [128, 8192], BF16) for i in range(NSLOT)]
        R_w = [Reg("w%d" % i) for i in range(NSLOT)]
        D_w = [fw.dsem("w%d" % i) for i in range(NSLOT)]
        NU = 64
        AR = sb("AR", [128, NU * 512], BF16); R_u = [Reg("u%d" % i) for i in range(NU)]
        NF = 8
        ARF = sb("ARF", [128, NF, 512], F32); R_f = [Reg("f%d" % i) for i in range(NF)]
        D_f = [fw.dsem("f%d" % i) for i in range(NF)]
        identF = sb("identF", [128, 128], F32); identB = sb("identB", [128, 128], BF16)
        R_c = Reg("consts")
        gpk_t = sb("gpk_t", [128, 4, 16], F32)
        stat = sb("stat", [128, 64], F32); R_stat = Reg("stat")
        D_x = [fw.dsem("x%d" % i) for i in range(4)]
        _du = {}

        def D_unit(i, q="sp"):
            if (i, q) not in _du:
                _du[(i, q)] = fw.dsem("u%d%s" % (i, q))
            return _du[(i, q)]

        PS = [es.enter_context(nc.psum_tensor("ps%d" % i, [128, 512], F32)) for i in range(6)]
        R_ps = [Reg("ps%d" % i) for i in range(6)]
        PTB = [es.enter_context(nc.psum_tensor("pt%d" % i, [128, 1024], BF16)) for i in range(2)]
        R_pt = [Reg("pt0"), Reg("pt1")]

        def U(i, n=1):
            return AR[:, i * 512:(i + n) * 512]

        D_c = fw.dsem("consts")
        fw.dma("sp", D_c, identF[:], identf[:, :], writes=[R_c])
        D_c2 = fw.dsem("consts2")
        fw.dma("pool", D_c2, identB[:], identf[:, :], writes=[R_c])
        fw.dma("sp", D_c, gpk_t[:], gpk[:, :, :], writes=[R_c])

        maskB = sb("maskB", [128, 4, 512], BF16)
        onesB = sb("onesB", [128, 128], BF16)
        lam_t = sb("lam_t", [128, 256], F32)
        lam_s = sb("lam_s", [128, 8], F32)
        gda_t = sb("gda_t", [128, 2], F32)
        D_c3 = fw.dsem("consts3")
        for j in range(4):
            fw.dma("pool", D_c3, maskB[:, j, :], masks[:, j, :], writes=[R_c])
        fw.dma("sp", D_c, lam_t[:], lamv[0:1, :].partition_broadcast(128) if False else lamv.partition_broadcast(128), writes=[R_c])
        fw.dma("sp", D_c, gda_t[:, 0:1], gda[:, :], writes=[R_c])
        fw.op("dve", lambda e: e.memset(onesB[:], 1.0), writes=[R_c])
        onesF = sb("onesF", [128, 128], F32)
        fw.op("dve", lambda e: e.memset(onesF[:], 1.0), writes=[R_c])
        fw.op("dve", lambda e: e.tensor_tensor(out=lam_t[:, 0:64], in0=lam_t[:, 0:64], in1=lam_t[:, 64:128], op=ALU.mult), reads=[R_c], writes=[R_c])
        fw.op("dve", lambda e: e.tensor_tensor(out=lam_t[:, 128:192], in0=lam_t[:, 128:192], in1=lam_t[:, 192:256], op=ALU.mult), reads=[R_c], writes=[R_c])
        fw.op("dve", lambda e: e.reduce_sum(out=lam_s[:, 0:1], in_=lam_t[:, 0:64], axis=AX.X), reads=[R_c], writes=[R_c])
        fw.op("dve", lambda e: e.reduce_sum(out=lam_s[:, 1:2], in_=lam_t[:, 128:192], axis=AX.X), reads=[R_c], writes=[R_c])
        fw.op("act", lambda e: e.activation(out=lam_s[:, 2:4], in_=lam_s[:, 0:2], func=AF.Exp), reads=[R_c], writes=[R_c])
        fw.op("dve", lambda e: e.tensor_tensor(out=lam_s[:, 4:5], in0=lam_s[:, 3:4], in1=lam_s[:, 2:3], op=ALU.subtract), reads=[R_c], writes=[R_c])
        fw.op("dve", lambda e: e.tensor_scalar(out=lam_s[:, 5:6], in0=lam_s[:, 4:5], scalar1=-LAM_INIT, scalar2=None, op0=ALU.add), reads=[R_c], writes=[R_c])
        fw.op("dve", lambda e: e.tensor_scalar(out=gda_t[:, 1:2], in0=gda_t[:, 0:1], scalar1=1.0 - LAM_INIT, scalar2=None, op0=ALU.mult), reads=[R_c], writes=[R_c])
        neglam = lam_s[:, 5:6]
        gda_s = gda_t[:, 1:2]
        R_ktS = [[Reg("ktS%d_%d" % (i, h)) for h in range(NH)] for i in range(2)]
        R_vS = [[Reg("vS%d_%d" % (i, h)) for h in range(NH)] for i in range(2)]
        D_h = fw.dsem("hist")
        uKTH = 24; uVH = 33; uPT = 42; uSQ = 46; uCAT = 48; uQT_ = 0
        D_dbg = fw.dsem("dbg")
        D_cp = [fw.dsem("cp%d" % i) for i in range(4)]

        esel_t = sb("esel_t", [4, 512], F32)
        maskml_t = sb("maskml_t", [128, 64], F32)
        bg_t = sb("bg_t", [4, 4], F32)
        gml_bc = sb("gml_bc", [128, 1024], F32)
        CT = sb("CT", [128, 8, 257], F32); R_CT = [Reg("CT%d" % j) for j in range(8)]
        CTb = sb("CTb", [128, 8, 257], BF16); R_CTb = [Reg("CTb%d" % j) for j in range(8)]
        carry = sb("carry", [4, 16], F32); R_carry = Reg("carry")
        gcol = sb("gcol", [128, 64], F32); R_gcol = Reg("gcol")
        GL = sb("GL", [128, 32], F32); R_GL = Reg("GL")
        mlt = sb("mlt", [128, 1024], F32)
        R_wT = Reg("wT"); R_HN = Reg("HN"); R_tmpN = Reg("tmpN"); R_dd = Reg("dd"); R_ssml = Reg("ssml")
        fw.dma("sp", D_c, esel_t[:], esel[:, :], writes=[R_c])
        fw.dma("sp", D_c, maskml_t[:], maskml[:, :], writes=[R_c])
        fw.dma("sp", D_c, bg_t[:, 0:2], bgate[:, :], writes=[R_c])
        fw.dma("sp", D_c, gml_bc[:], gml.partition_broadcast(128), writes=[R_c])
        fw.op("dve", lambda e: e.tensor_scalar(out=bg_t[:, 2:3], in0=bg_t[:, 1:2], scalar1=-1.0, scalar2=None, op0=ALU.mult), reads=[R_c], writes=[R_c])
        D_st = fw.dsem("mlstate")
        gfinal_bc = sb("gfinal_bc", [128, D], F32)
        convw_t = sb("convw_t", [128, 4, 43], F32)
        halo = sb("halo", [128, 2, 43], F32); R_halo = Reg("halo")
        fw.dma("sp", D_c, gfinal_bc[:], gfinal.partition_broadcast(128), writes=[R_c])
        fw.dma("sp", D_c, convw_t[:], convw[:, :, :], writes=[R_c])
        R_mkS = [Reg("mkS0"), Reg("mkS1")]; R_mvS = [Reg("mvS0"), Reg("mvS1")]
        D_mem = fw.dsem("memload"); D_memp = fw.dsem("memloadp")

        state = {"slot": 0, "ps": 0, "f": 0}

        def next_slot():
            s = state["slot"]; state["slot"] = (s + 1) % NSLOT
            return s

        def next_ps():
            p = state["ps"]; state["ps"] = (p + 1) % 4
            return p

        def next_f():
            f = state["f"]; state["f"] = (f + 1) % (NF - 4)
            return f

        R_wb = {k: Reg("wb_" + k) for k in WSPEC}
        D_wb = {k: fw.dsem("wb_" + k) for k in WSPEC}
        wname = {id(v[0].tensor): k for k, v in WSPEC.items()}

        def slab_index(name, c0_, r0):
            if name == "w_down":
                return (c0_ // 512) * 3 + (r0 // 2048)
            return c0_ // 512

        cvq = {"q": 0}

        def convert_weight(name):
            w, K, nsl = WSPEC[name]
            ncol_tot = w.shape[1]
            if name == "w_down":
                parts = [(cg * 512, 512, k0 * 128, kc) for cg in range(4) for (k0, kc) in ((0, 16), (16, 16), (32, 11))]
            else:
                parts = [(c, min(512, ncol_tot - c), 0, 16) for c in range(0, ncol_tot, 512)]
            for (c0_, ncols, r0, kc) in parts:
                si = slab_index(name, c0_, r0)
                for k0 in range(0, kc, 4):
                    kn = min(4, kc - k0)
                    q = cvq["q"]; cvq["q"] += 1
                    buf = q % 4
                    src = w[r0 + k0 * 128:r0 + (k0 + kn) * 128, c0_:c0_ + ncols].rearrange("(k p) c -> p k c", p=128)
                    stg = xres[:, buf, 0:kn * ncols]
                    fw.dma("sp", D_x[buf], stg.rearrange("p (k c) -> p k c", k=kn), src, writes=[R_x[buf]])
                    dst = AR[:, buf * 2048: buf * 2048 + kn * ncols]
                    ru = R_u[buf * 4: buf * 4 + 4]
                    if q % 2 == 0:
                        fw.op("dve", lambda e, dst=dst, stg=stg: e.tensor_copy(out=dst, in_=stg), reads=[R_x[buf]], writes=ru)
                    else:
                        fw.op("act", lambda e, dst=dst, stg=stg: e.activation(out=dst, in_=stg, func=AF.Copy), reads=[R_x[buf]], writes=ru)
                    fw.dma("pool", D_unit(buf * 4, "pool"), WB[name][si, :, k0 * ncols:(k0 + kn) * ncols], dst, reads=ru, writes=[R_wb[name]])

        def load_slab(w, kc, c0_, ncols, r0=0):
            name = wname[id(w.tensor)]
            si = slab_index(name, c0_, r0)
            s = next_slot()
            dst = wsl[s][:, 0:kc * ncols].rearrange("p (k c) -> p k c", k=kc)
            fw.dma("pool", D_w[s], wsl[s][:, 0:kc * ncols], WB[name][si, :, 0:kc * ncols], reads=[R_wb[name]], writes=[R_w[s]])
            return s, dst

        def norm_to_hT(which, ntok, src=None):
            nt = (ntok + 127) // 128
            for tt in range(nt):
                rows = min(128, ntok - tt * 128)
                sc_ = 4 * tt
                if stop < -2:
                    continue
                fw.op("dve", lambda e, c=sc_: e.memset(stat[:, c:c + 2], 0.0), writes=[R_stat])
                fw.op("act", lambda e, tt=tt, rows=rows, c=sc_: e.activation(
                    out=xn[:rows, :], in_=xres[:rows, tt, :], func=AF.Square, accum_out=stat[:rows, c:c + 1]),
                    reads=[R_x[tt]], writes=[R_xn, R_stat])
                fw.op("act", lambda e, rows=rows, c=sc_: e.activation(
                    out=stat[:rows, c + 1:c + 2], in_=stat[:rows, c:c + 1], func=AF.Sqrt, bias=EPS, scale=1.0 / D),
                    reads=[R_stat], writes=[R_stat])
                fw.op("dve", lambda e, rows=rows, c=sc_: e.reciprocal(out=stat[:rows, c + 2:c + 3], in_=stat[:rows, c + 1:c + 2]),
                      reads=[R_stat], writes=[R_stat])
                fw.op("act", lambda e, tt=tt, rows=rows, c=sc_: e.activation(
                    out=xn[:rows, :], in_=xres[:rows, tt, :], func=AF.Copy, scale=stat[:rows, c + 2:c + 3]),
                    reads=[R_x[tt], R_stat], writes=[R_xn])
                for g4 in range(4):
                    if stop < -1:
                        continue
                    half = g4 % 2
                    for j in range(4):
                        kc = g4 * 4 + j
                        fw.op("pe", lambda e, kc=kc, j=j, half=half, rows=rows: e.transpose(
                            out=PTB[half][:, j * 128: j * 128 + rows],
                            in_=xn[:rows, kc * 128:(kc + 1) * 128], identity=identB[:rows, :rows]),
                            reads=[R_xn, R_c], writes=[R_pt[half]])
                    for j in range(4):
                        if stop < 0:
                            continue
                        kc = g4 * 4 + j
                        eng = "dve" if half == 0 else "act"
                        o_ap = hT[:, kc, tt * 128: tt * 128 + rows]
                        i_ap = PTB[half][:, j * 128: j * 128 + rows]
                        g_ap = gpk_t[:, which, kc:kc + 1]
                        if eng == "dve":
                            fw.op("dve", lambda e, o_ap=o_ap, i_ap=i_ap, g_ap=g_ap: e.tensor_scalar(
                                out=o_ap, in0=i_ap, scalar1=g_ap, scalar2=None, op0=ALU.mult),
                                reads=[R_pt[half], R_c], writes=[R_hT[kc]])
                        else:
                            fw.op("act", lambda e, o_ap=o_ap, i_ap=i_ap, g_ap=g_ap: e.activation(
                                out=o_ap, in_=i_ap, func=AF.Copy, scale=g_ap),
                                reads=[R_pt[half], R_c], writes=[R_hT[kc]])

        def proj_TM(slab, kc_n, ncols, actT, actR, ntok, consume, k0=0, acc=None, first=True, last=True):
            s, wv = slab
            nt = (ntok + 127) // 128
            for tt in range(nt):
                rows = min(128, ntok - tt * 128)
                pb = acc[tt] if acc is not None else next_ps()
                for k in range(kc_n):
                    fw.op("pe", lambda e, tt=tt, rows=rows, pb=pb, k=k: e.matmul(
                        PS[pb][:rows, 0:ncols], lhsT=actT(k0 + k)[:, tt * 128: tt * 128 + rows], rhs=wv[:, k, :],
                        start=(first and k == 0), stop=(last and k == kc_n - 1)),
                        reads=[actR[k0 + k], R_w[s]], writes=[R_ps[pb]])
                if last:
                    consume(tt, rows, pb)

        def proj_FM(slab, kc_n, ncols, actT, actR, ntok, consume):
            s, wv = slab
            for ch in range((ncols + 127) // 128):
                m = min(128, ncols - ch * 128)
                pb = next_ps()
                for k in range(kc_n):
                    fw.op("pe", lambda e, ch=ch, m=m, pb=pb, k=k: e.matmul(
                        PS[pb][:m, 0:ntok], lhsT=wv[:, k, ch * 128: ch * 128 + m], rhs=actT(k)[:, 0:ntok],
                        start=(k == 0), stop=(k == kc_n - 1)),
                        reads=[actR[k], R_w[s]], writes=[R_ps[pb]])
                consume(ch, m, pb)

        def tm_to_fm(src, src_regs, dst_u0, tt, rows, g):
            for j in range(4):
                fw.op("pe", lambda e, j=j: e.transpose(out=PTB[g][:, j * 128: j * 128 + rows], in_=src[:rows, j * 128:(j + 1) * 128],
                                                       identity=identB[:rows, :rows]), reads=list(src_regs) + [R_c], writes=[R_pt[g]])
            dst = AR[:, dst_u0 * 512:(dst_u0 + 4) * 512].rearrange("p (h t) -> p h t", t=512)[:, :, tt * 128: tt * 128 + rows]
            srcp = PTB[g][:, 0:512].rearrange("p (h t) -> p h t", t=128)[:, :, 0:rows]
            if g == 0:
                fw.op("dve", lambda e: e.tensor_copy(out=dst, in_=srcp), reads=[R_pt[g]], writes=R_u[dst_u0:dst_u0 + 4])
            else:
                fw.op("act", lambda e: e.activation(out=dst, in_=srcp, func=AF.Copy), reads=[R_pt[g]], writes=R_u[dst_u0:dst_u0 + 4])

        hT_get = lambda k: hT[:, k, :]

        def attention(seq, pos0, ntok, diag):
            nkeys = pos0 + ntok
            nkt = (nkeys + 127) // 128
            Ru_kth = R_u[uKTH:uKTH + 9] + R_u[8:17]
            Ru_vh = R_u[uVH:uVH + 9]
            KTHc = (AR[:, uKTH * 512: uKTH * 512 + NKT * 128], AR[:, 8 * 512: 8 * 512 + NKT * 128])
            fw.op("pool", lambda e: e.memset(KTHc[0][64:128, 0:nkeys], 0.0), writes=R_u[uKTH:uKTH + 9])
            fw.op("pool", lambda e: e.memset(KTHc[1][0:64, 0:nkeys], 0.0), writes=R_u[8:17])
            VH = AR[:, uVH * 512: uVH * 512 + NKT * 128].rearrange("p (k e) -> p k e", e=128)
            A = ARF[:, 6, :]; Bf = ARF[:, 7, :]
            for h in range(NH):
                fw.dma("sp", D_h, KTHc[0][0:64, 0:nkeys], ktS[seq][h, 0:64, 0:nkeys], reads=[R_ktS[seq][h]], writes=Ru_kth)
                fw.dma("sp", D_h, KTHc[1][64:128, 0:nkeys], ktS[seq][h, 64:128, 0:nkeys], reads=[R_ktS[seq][h]], writes=Ru_kth)
                nfull = nkeys // 128
                fw.dma("sp", D_h, VH[:, 0:nfull, :], vS[seq][h, :, 0:nfull, :], reads=[R_vS[seq][h]], writes=Ru_vh)
                if nkeys % 128:
                    fw.dma("sp", D_h, VH[0:nkeys % 128, nfull, :], vS[seq][h, 0:nkeys % 128, nfull, :], reads=[R_vS[seq][h]], writes=Ru_vh)
                steps = [(c, kt) for kt in range(nkt) for c in range(2)]
                SB = (0, 1, 4, 5)
                ACC = (ARF[:, 4, :], ARF[:, 5, :]); R_acc = (R_f[4], R_f[5])

                def s_step(i):
                    c, kt = steps[i]
                    kw = min(128, nkeys - kt * 128)
                    sbk = SB[i % 4]
                    pu = uPT + (i % 4)
                    fw.op("pe", lambda e, c=c, kt=kt, kw=kw, sbk=sbk, h=h: e.matmul(
                        PS[sbk][:kw, 0:ntok], lhsT=KTHc[c][:, kt * 128: kt * 128 + kw],
                        rhs=U(uQT_ + h)[:, 0:ntok], start=True, stop=True),
                        reads=Ru_kth + [R_u[uQT_ + h]], writes=[R_ps[sbk]])
                    fw.op("act", lambda e, kw=kw, sbk=sbk, pu=pu: e.activation(
                        out=U(pu)[:kw, 0:ntok], in_=PS[sbk][:kw, 0:ntok], func=AF.Exp, scale=0.125),
                        reads=[R_ps[sbk]], writes=[R_u[pu]])
                    j = kt - (nkt - 4)
                    if diag and j >= 0:
                        fw.op("dve", lambda e, pu=pu, j=j: e.tensor_tensor(
                            out=U(pu)[:, 0:ntok], in0=U(pu)[:, 0:ntok], in1=maskB[:, j, 0:ntok], op=ALU.mult),
                            reads=[R_u[pu], R_c], writes=[R_u[pu]])
                    if kt == 0:
                        if kw < 128:
                            fw.op("dve", lambda e, c=c: e.memset(ACC[c][:, 0:ntok], 0.0), writes=[R_acc[c]])
                        fw.op("dve", lambda e, c=c, kw=kw, pu=pu: e.tensor_copy(out=ACC[c][:kw, 0:ntok], in_=U(pu)[:kw, 0:ntok]),
                              reads=[R_u[pu]], writes=[R_acc[c]])
                    else:
                        fw.op("dve", lambda e, c=c, kw=kw, pu=pu: e.tensor_tensor(
                            out=ACC[c][:kw, 0:ntok], in0=ACC[c][:kw, 0:ntok], in1=U(pu)[:kw, 0:ntok], op=ALU.add),
                            reads=[R_u[pu], R_acc[c]], writes=[R_acc[c]])

                def av_step(i):
                    c, kt = steps[i]
                    kw = min(128, nkeys - kt * 128)
                    pu = uPT + (i % 4)
                    fw.op("pe", lambda e, c=c, kt=kt, kw=kw, pu=pu: e.matmul(
                        PS[2 + c][:, 0:ntok], lhsT=VH[:kw, kt, :], rhs=U(pu)[:kw, 0:ntok],
                        start=(kt == 0), stop=(kt == nkt - 1)),
                        reads=Ru_vh + [R_u[pu]], writes=[R_ps[2 + c]])

                LA = 3
                for i in range(len(steps) + LA):
                    if i < len(steps):
                        s_step(i)
                    if i - LA >= 0:
                        av_step(i - LA)
                n = ntok
                for c in range(2):
                    fw.op("pe", lambda e, c=c: e.matmul(PS[SB[c]][:, 0:n], lhsT=onesF[:, :], rhs=ACC[c][:, 0:n], start=True, stop=True),
                          reads=[R_c, R_acc[c]], writes=[R_ps[SB[c]]])
                fw.op("act", lambda e: e.activation(out=A[:, 0:n], in_=PS[SB[0]][:, 0:n], func=AF.Ln), reads=[R_ps[SB[0]]], writes=[R_f[6]])
                fw.op("act", lambda e: e.activation(out=A[:, 0:n], in_=A[:, 0:n], func=AF.Exp, scale=-1.0), reads=[R_f[6]], writes=[R_f[6]])
                fw.op("act", lambda e: e.activation(out=Bf[:, 0:n], in_=PS[SB[1]][:, 0:n], func=AF.Ln), reads=[R_ps[SB[1]]], writes=[R_f[7]])
                fw.op("act", lambda e: e.activation(out=Bf[:, 0:n], in_=Bf[:, 0:n], func=AF.Exp, scale=-1.0), reads=[R_f[7]], writes=[R_f[7]])
                fw.op("dve", lambda e: e.tensor_tensor(out=A[:, 0:n], in0=PS[2][:, 0:n], in1=A[:, 0:n], op=ALU.mult),
                      reads=[R_ps[2], R_f[6]], writes=[R_f[6]])
                fw.op("dve", lambda e: e.tensor_tensor(out=Bf[:, 0:n], in0=PS[3][:, 0:n], in1=Bf[:, 0:n], op=ALU.mult),
                      reads=[R_ps[3], R_f[7]], writes=[R_f[7]])
                fw.op("dve", lambda e: e.scalar_tensor_tensor(out=A[:, 0:n], in0=Bf[:, 0:n], scalar=neglam, in1=A[:, 0:n],
                                                              op0=ALU.mult, op1=ALU.add),
                      reads=[R_f[6], R_f[7], R_c], writes=[R_f[6]])
                fw.op("dve", lambda e: e.tensor_tensor(out=U(uSQ)[:, 0:n], in0=A[:, 0:n], in1=A[:, 0:n], op=ALU.mult),
                      reads=[R_f[6]], writes=[R_u[uSQ]])
                fw.op("pe", lambda e: e.matmul(PS[4][:, 0:n], lhsT=onesB[:, :], rhs=U(uSQ)[:, 0:n], start=True, stop=True),
                      reads=[R_c, R_u[uSQ]], writes=[R_ps[4]])
                fw.op("act", lambda e: e.activation(out=Bf[:, 0:n], in_=PS[4][:, 0:n], func=AF.Ln, bias=EPS, scale=1.0 / 128),
                      reads=[R_ps[4]], writes=[R_f[7]])
                fw.op("act", lambda e: e.activation(out=Bf[:, 0:n], in_=Bf[:, 0:n], func=AF.Exp, scale=-0.5), reads=[R_f[7]], writes=[R_f[7]])
                fw.op("dve", lambda e, h=h: e.scalar_tensor_tensor(out=U(uCAT + h)[:, 0:n], in0=A[:, 0:n], scalar=gda_s, in1=Bf[:, 0:n],
                                                                   op0=ALU.mult, op1=ALU.mult),
                      reads=[R_f[6], R_f[7], R_c], writes=[R_u[uCAT + h]])
                if dbg is not None:
                    fw.dma("pool", D_dbg, dbg[h, :, pos0:pos0 + n], U(uCAT + h)[:, 0:n], reads=[R_u[uCAT + h]])

        def cache_prologue():
            for kt in range(T // 128):
                uk = 0 + (kt % 2) * 2
                uv = 4 + (kt % 2) * 2
                ut = 8 + (kt % 2) * 2
                fw.dma("pool", D_unit(uk, "pool"), AR[:, uk * 512:(uk + 2) * 512], ck[kt * 128:(kt + 1) * 128, :], writes=R_u[uk:uk + 2])
                fw.dma("pool", D_unit(uv, "pool"), AR[:, uv * 512:(uv + 2) * 512], cv[kt * 128:(kt + 1) * 128, :], writes=R_u[uv:uv + 2])
                for hh in range(NH):
                    fw.dma("sp", D_unit(uv), vS[1][hh, :, kt, :], AR[:, uv * 512 + hh * 128: uv * 512 + (hh + 1) * 128],
                           reads=R_u[uv:uv + 2], writes=[R_vS[1][hh]])
                for g in range(2):
                    for j in range(4):
                        hh = g * 4 + j
                        fw.op("pe", lambda e, g=g, j=j, hh=hh, uk=uk: e.transpose(
                            out=PTB[g][:, j * 128:(j + 1) * 128], in_=AR[:, uk * 512 + hh * 128: uk * 512 + (hh + 1) * 128],
                            identity=identB[:, :]), reads=R_u[uk:uk + 2] + [R_c], writes=[R_pt[g]])
                    eng = "dve" if g == 0 else "act"
                    o_ap = AR[:, (ut + g) * 512:(ut + g + 1) * 512]
                    if g == 0:
                        fw.op("dve", lambda e, o_ap=o_ap, g=g: e.tensor_copy(out=o_ap, in_=PTB[g][:, 0:512]),
                              reads=[R_pt[g]], writes=[R_u[ut + g]])
                    else:
                        fw.op("act", lambda e, o_ap=o_ap, g=g: e.activation(out=o_ap, in_=PTB[g][:, 0:512], func=AF.Copy),
                              reads=[R_pt[g]], writes=[R_u[ut + g]])
                    for j in range(4):
                        hh = g * 4 + j
                        fw.dma("sp", D_unit(ut + g), ktS[1][hh, :, kt * 128:(kt + 1) * 128], AR[:, (ut + g) * 512 + j * 128:(ut + g) * 512 + (j + 1) * 128],
                               reads=[R_u[ut + g]], writes=[R_ktS[1][hh]])

        uMQ = 0; uMK = 8; uMKT = 16; uMV = 24; uMO = 33; uST = 41; uVW = 42
        mvA = AR[:, uMV * 512: uMV * 512 + 16 * 257].rearrange("p (a e) -> p a e", e=257)
        Ru_mv = R_u[uMV:uMV + 9]

        def ml_init_zero():
            fw.op("dve", lambda e: e.memset(CT[:], 0.0), writes=R_CT)
            fw.op("dve", lambda e: e.memset(CTb[:], 0.0), writes=R_CTb)
            fw.op("dve", lambda e: e.memset(carry[:], 0.0), writes=[R_carry])

        def ml_init_state():
            for hh in range(4):
                for ec in range(2):
                    f = next_f()
                    fw.dma("sp", D_f[f], ARF[:, f, 0:256], c0[hh, ec * 128:(ec + 1) * 128, :], writes=[R_f[f]])
                    for dc in range(2):
                        pb = next_ps()
                        fw.op("pe", lambda e, f=f, dc=dc, pb=pb: e.transpose(out=PS[pb][:, 0:128], in_=ARF[:, f, dc * 128:(dc + 1) * 128],
                                                                        identity=identF[:, :]), reads=[R_f[f], R_c], writes=[R_ps[pb]])
                        j = hh * 2 + dc
                        fw.op("dve", lambda e, j=j, ec=ec, pb=pb: e.tensor_copy(out=CT[:, j, ec * 128:(ec + 1) * 128], in_=PS[pb][:, 0:128]),
                              reads=[R_ps[pb]], writes=[R_CT[j]])
            fw.dma("sp", D_st, CT[:, :, 256], n0.rearrange("h (dc p) -> p (h dc)", p=128), writes=R_CT, slow=True)
            fw.dma("sp", D_st, carry[:, 0:1], m0[:, :], writes=[R_carry])
            fw.op("act", lambda e: e.activation(out=CTb[:], in_=CT[:], func=AF.Copy), reads=R_CT, writes=R_CTb)

        def ml_out_state(oc, on, om):
            for hh in range(4):
                for ec in range(2):
                    pb = next_ps()
                    for dc in range(2):
                        j = hh * 2 + dc
                        fw.op("pe", lambda e, j=j, ec=ec, dc=dc, pb=pb: e.transpose(
                            out=PS[pb][:, dc * 128:(dc + 1) * 128], in_=CT[:, j, ec * 128:(ec + 1) * 128], identity=identF[:, :]),
                            reads=[R_CT[j], R_c], writes=[R_ps[pb]])
                    f = next_f()
                    fw.op("dve", lambda e, f=f, pb=pb: e.tensor_copy(out=ARF[:, f, 0:256], in_=PS[pb][:, 0:256]), reads=[R_ps[pb]], writes=[R_f[f]])
                    fw.dma("sp", D_f[f], oc[hh, ec * 128:(ec + 1) * 128, :], ARF[:, f, 0:256], reads=[R_f[f]])
            fw.dma("sp", D_st, on.rearrange("h (dc p) -> p (h dc)", p=128), CT[:, :, 256], reads=R_CT, slow=True)
            fw.dma("sp", D_st, om[:, :], carry[:, 0:1], reads=[R_carry])

        def mlstm(seq, tok0, ntok, L):
            nt = (ntok + 127) // 128
            nch = ntok // L
            for half in range(2):
                slab = load_slab(w_in, 16, 3072 + half * 512, 512)

                def c_mq(ch, m, pb, half=half):
                    u = uMQ + half * 4 + ch
                    fw.op("act", lambda e, u=u, pb=pb: e.activation(out=U(u)[:, 0:ntok], in_=PS[pb][:, 0:ntok], func=AF.Copy),
                          reads=[R_ps[pb]], writes=[R_u[u]])
                proj_FM(slab, 16, 512, hT_get, R_hT, ntok, c_mq)
            for half in range(2):
                slab = load_slab(w_in, 16, 4096 + half * 512, 512)

                def c_mk(tt, rows, pb, half=half):
                    u = uMKT + tt * 2 + half
                    fw.op("act", lambda e, u=u, pb=pb, rows=rows: e.activation(out=U(u)[:rows, :], in_=PS[pb][:rows, :], func=AF.Copy, scale=1.0 / 16),
                          reads=[R_ps[pb]], writes=[R_u[u]])
                    tm_to_fm(U(u), [R_u[u]], uMK + half * 4, tt, rows, tt % 2)
                proj_TM(slab, 16, 512, hT_get, R_hT, ntok, c_mk)
            fw.op("dve", lambda e: e.memset(mvA[:, :, 256:257], 1.0), writes=Ru_mv)
            for half in range(2):
                slab = load_slab(w_in, 16, 5120 + half * 512, 512)

                def c_mv(tt, rows, pb, half=half):
                    fw.op("act", lambda e, tt=tt, pb=pb, rows=rows, half=half: e.activation(
                        out=mvA[:rows, tt * 4 + half * 2: tt * 4 + half * 2 + 2, 0:256],
                        in_=PS[pb][:rows, :].rearrange("p (a e) -> p a e", e=256), func=AF.Copy),
                        reads=[R_ps[pb]], writes=Ru_mv)
                proj_TM(slab, 16, 512, hT_get, R_hT, ntok, c_mv)
            for half in range(2):
                slab = load_slab(w_in, 16, 6144 + half * 512, 512)

                def c_mo(tt, rows, pb, half=half):
                    u = uMO + tt * 2 + half
                    fw.op("act", lambda e, u=u, pb=pb, rows=rows: e.activation(out=U(u)[:rows, :], in_=PS[pb][:rows, :], func=AF.Sigmoid),
                          reads=[R_ps[pb]], writes=[R_u[u]])
                proj_TM(slab, 16, 512, hT_get, R_hT, ntok, c_mo)
            slab = load_slab(w_in, 16, 7168, 8)
            s_, wv = slab
            n = ntok
            row = lambda f: ARF[0:4, f, 0:n]
            for gi in range(2):
                pb = next_ps()
                for k in range(16):
                    fw.op("pe", lambda e, gi=gi, pb=pb, k=k: e.matmul(PS[pb][0:4, 0:n], lhsT=wv[:, k, gi * 4:(gi + 1) * 4], rhs=hT[:, k, 0:n],
                                                                     start=(k == 0), stop=(k == 15)),
                          reads=[R_hT[k], R_w[s_]], writes=[R_ps[pb]])
                if gi == 0:
                    fw.op("dve", lambda e, pb=pb: e.tensor_scalar(out=row(0), in0=PS[pb][0:4, 0:n], scalar1=bg_t[:, 0:1], scalar2=None, op0=ALU.add),
                          reads=[R_ps[pb], R_c], writes=[R_f[0]])
                else:
                    fw.op("act", lambda e, pb=pb: e.activation(out=row(1), in_=PS[pb][0:4, 0:n], func=AF.Exp, scale=-1.0, bias=bg_t[:, 2:3]),
                          reads=[R_ps[pb], R_c], writes=[R_f[1]])
            fw.op("act", lambda e: e.activation(out=row(1), in_=row(1), func=AF.Ln, bias=1.0), reads=[R_f[1]], writes=[R_f[1]])
            fw.op("dve", lambda e: e.tensor_scalar(out=row(1), in0=row(1), scalar1=-1.0, scalar2=None, op0=ALU.mult), reads=[R_f[1]], writes=[R_f[1]])
            fw.op("dve", lambda e: e.memset(row(7), 0.0), writes=[R_f[7]])
            fw.op("dve", lambda e: e.tensor_tensor_scan(out=row(2), data0=row(1), data1=row(0), initial=carry[:, 0:1], op0=ALU.add, op1=ALU.max),
                  reads=[R_f[1], R_f[0], R_carry], writes=[R_f[2]])
            fw.op("dve", lambda e: e.tensor_tensor_scan(out=row(3), data0=row(1), data1=row(7), initial=0.0, op0=ALU.add, op1=ALU.add),
                  reads=[R_f[1], R_f[7]], writes=[R_f[3]])
            fw.op("dve", lambda e: e.tensor_tensor(out=row(4), in0=row(3), in1=row(2), op=ALU.subtract), reads=[R_f[3], R_f[2]], writes=[R_f[4]])
            fw.op("dve", lambda e: e.tensor_tensor(out=row(0), in0=row(0), in1=row(3), op=ALU.subtract), reads=[R_f[0], R_f[3]], writes=[R_f[0]])
            fw.op("act", lambda e: e.activation(out=row(6), in_=row(2), func=AF.Exp, scale=-1.0), reads=[R_f[2]], writes=[R_f[6]])
            fw.op("dve", lambda e: e.tensor_copy(out=carry[:, 4:5], in_=carry[:, 0:1]), reads=[R_carry], writes=[R_carry])
            for c in range(1, nch):
                fw.op("dve", lambda e, c=c: e.tensor_scalar(out=carry[:, 4 + c:5 + c], in0=ARF[0:4, 4, c * L - 1:c * L], scalar1=-1.0, scalar2=None, op0=ALU.mult),
                      reads=[R_f[4], R_carry], writes=[R_carry])
            for c in range(nch):
                cs = slice(c * L, (c + 1) * L)
                fw.op("act", lambda e, c=c, cs=cs: e.activation(out=ARF[0:4, 5, cs], in_=ARF[0:4, 4, cs], func=AF.Exp, bias=carry[:, 4 + c:5 + c]),
                      reads=[R_f[4], R_carry], writes=[R_f[5]])
                fw.op("act", lambda e, c=c, cs=cs: e.activation(out=ARF[0:4, 7, cs], in_=ARF[0:4, 0, cs], func=AF.Exp, bias=ARF[0:4, 4, (c + 1) * L - 1:(c + 1) * L]),
                      reads=[R_f[0], R_f[4]], writes=[R_f[7]])
            fw.op("dve", lambda e: e.tensor_copy(out=carry[:, 0:1], in_=ARF[0:4, 2, n - 1:n]), reads=[R_f[2]], writes=[R_carry])
            pbT = next_ps()
            for tt in range(nt):
                rows = min(128, ntok - tt * 128)
                for qi, f in enumerate((0, 7, 5, 6)):
                    o = (tt * 4 + qi) * 4
                    fw.op("pe", lambda e, tt=tt, rows=rows, f=f, o=o: e.transpose(
                        out=PS[pbT][:rows, o:o + 4], in_=ARF[0:4, f, tt * 128: tt * 128 + rows], identity=identF[0:4, 0:4]),
                        reads=[R_f[f], R_c], writes=[R_ps[pbT]])
            rows_all = 128 if ntok >= 128 else ntok
            fw.op("dve", lambda e: e.tensor_copy(out=gcol[:rows_all, 0:nt * 16], in_=PS[pbT][:rows_all, 0:nt * 16]), reads=[R_ps[pbT]], writes=[R_gcol])
            pbG = next_ps()
            for hh in range(4):
                for c in range(nch):
                    e_c = (c + 1) * L - 1
                    fw.op("pe", lambda e, hh=hh, c=c, e_c=e_c: e.matmul(PS[pbG][:, hh * 8 + c: hh * 8 + c + 1], lhsT=esel_t[0:4, hh * 128:(hh + 1) * 128],
                                                                        rhs=ARF[0:4, 5, e_c:e_c + 1], start=True, stop=True),
                          reads=[R_f[5], R_c], writes=[R_ps[pbG]])
            fw.op("dve", lambda e: e.tensor_copy(out=GL[:, :], in_=PS[pbG][:, 0:32]), reads=[R_ps[pbG]], writes=[R_GL])
            wT = mlt[:, 0:64]; HN = mlt[:, 64:321]; tmpN = mlt[:, 384:641]; ddv = mlt[:, 700:704]; ssml = mlt[:, 704:712]
            for c in range(nch):
                tt = (c * L) // 128
                po = (c * L) % 128
                P = slice(po, po + L)
                cs = slice(c * L, (c + 1) * L)
                hmf = (tt % 2) * 2
                HM = ARF[:, hmf:hmf + 2, :].rearrange("p a b -> p (a b)")
                R_hm = [R_f[hmf], R_f[hmf + 1]]
                for hh in range(4):
                    gc = lambda qi, tt=tt, hh=hh: gcol[P, (tt * 4 + qi) * 4 + hh:(tt * 4 + qi) * 4 + hh + 1]
                    for dc in range(2):
                        j = hh * 2 + dc
                        fw.op("pe", lambda e, j=j, dc=dc, P=P, cs=cs, po=po: e.matmul(
                            PS[0][P, 0:L], lhsT=U(uMK + j)[:, cs], rhs=U(uMQ + j)[:, cs], start=(dc == 0), stop=(dc == 1),
                            tile_position=(0, po)),
                            reads=[R_u[uMK + j], R_u[uMQ + j]], writes=[R_ps[0]])
                    fw.op("pe", lambda e, hh=hh, P=P, cs=cs, po=po: e.matmul(
                        PS[1][P, 0:L], lhsT=esel_t[0:4, hh * 128: hh * 128 + L], rhs=ARF[0:4, 4, cs], start=True, stop=False,
                        tile_position=(0, po)),
                        reads=[R_f[4], R_c], writes=[R_ps[1]])
                    fw.op("pe", lambda e, P=P, po=po: e.matmul(
                        PS[1][P, 0:L], lhsT=identF[P, P], rhs=maskml_t[P, 0:L], start=False, stop=True, tile_position=(po, po)),
                        reads=[R_c], writes=[R_ps[1]])
                    fw.op("act", lambda e, P=P, g0=gc(0): e.activation(out=wT[P, 0:L], in_=PS[1][P, 0:L], func=AF.Exp, bias=g0),
                          reads=[R_ps[1], R_gcol], writes=[R_wT])
                    fw.op("dve", lambda e, P=P: e.tensor_tensor(out=U(uST)[P, 0:L], in0=PS[0][P, 0:L], in1=wT[P, 0:L], op=ALU.mult),
                          reads=[R_ps[0], R_wT], writes=[R_u[uST]])
                    fw.op("pe", lambda e, P=P, tt=tt, hh=hh, po=po: e.matmul(
                        PS[2][P, 0:257], lhsT=U(uST)[P, 0:L], rhs=mvA[P, tt * 4 + hh, :], start=True, stop=True, tile_position=(po, po)),
                        reads=[R_u[uST]] + Ru_mv, writes=[R_ps[2]])
                    for dc in range(2):
                        j = hh * 2 + dc
                        fw.op("pe", lambda e, j=j, dc=dc, P=P, cs=cs, po=po: e.matmul(
                            PS[5][P, 0:257], lhsT=U(uMQ + j)[:, cs], rhs=CTb[:, j, :], start=(dc == 0), stop=(dc == 1),
                            tile_position=(0, po)),
                            reads=[R_u[uMQ + j], R_CTb[j]], writes=[R_ps[5]])
                    fw.op("act", lambda e, P=P, g2=gc(2): e.activation(out=tmpN[P, :], in_=PS[5][P, 0:257], func=AF.Copy, scale=g2),
                          reads=[R_ps[5], R_gcol], writes=[R_tmpN])
                    fw.op("dve", lambda e, P=P: e.tensor_tensor(out=HN[P, :], in0=tmpN[P, :], in1=PS[2][P, 0:257], op=ALU.add),
                          reads=[R_tmpN, R_ps[2]], writes=[R_HN])
                    fw.op("dve", lambda e, P=P: e.tensor_scalar(out=ddv[P, 2:3], in0=HN[P, 256:257], scalar1=-1.0, scalar2=None, op0=ALU.mult),
                          reads=[R_HN], writes=[R_dd])
                    fw.op("dve", lambda e, P=P: e.tensor_tensor(out=ddv[P, 2:3], in0=ddv[P, 2:3], in1=HN[P, 256:257], op=ALU.max),
                          reads=[R_HN, R_dd], writes=[R_dd])
                    fw.op("dve", lambda e, P=P, g3=gc(3): e.tensor_scalar(out=ddv[P, 0:1], in0=ddv[P, 2:3], scalar1=g3, scalar2=None, op0=ALU.max),
                          reads=[R_dd, R_gcol], writes=[R_dd])
                    fw.op("dve", lambda e, P=P: e.reciprocal(out=ddv[P, 1:2], in_=ddv[P, 0:1]), reads=[R_dd], writes=[R_dd])
                    fw.op("dve", lambda e, P=P, hh=hh, HM=HM: e.tensor_scalar(out=HM[P, hh * 256:(hh + 1) * 256], in0=HN[P, 0:256], scalar1=ddv[P, 1:2],
                                                                             scalar2=None, op0=ALU.mult),
                          reads=[R_HN, R_dd], writes=R_hm)
                    fw.op("dve", lambda e, P=P, tt=tt, hh=hh, g1=gc(1): e.tensor_scalar(out=U(uVW)[P, 0:257], in0=mvA[P, tt * 4 + hh, :], scalar1=g1,
                                                                                      scalar2=None, op0=ALU.mult),
                          reads=Ru_mv + [R_gcol], writes=[R_u[uVW]])
                    for dc in range(2):
                        j = hh * 2 + dc
                        um = uMKT + tt * 2 + (hh // 2)
                        co = (hh % 2) * 256 + dc * 128
                        fw.op("pe", lambda e, dc=dc, um=um, co=co, P=P, po=po: e.matmul(
                            PS[3 + dc][:, 0:257], lhsT=U(um)[P, co:co + 128], rhs=U(uVW)[P, 0:257], start=True, stop=True,
                            tile_position=(po, 0)),
                            reads=[R_u[um], R_u[uVW]], writes=[R_ps[3 + dc]])
                        fw.op("dve", lambda e, j=j, dc=dc, hh=hh, c=c: e.scalar_tensor_tensor(
                            out=CT[:, j, :], in0=CT[:, j, :], scalar=GL[:, hh * 8 + c: hh * 8 + c + 1], in1=PS[3 + dc][:, 0:257],
                            op0=ALU.mult, op1=ALU.add),
                            reads=[R_CT[j], R_GL, R_ps[3 + dc]], writes=[R_CT[j]])
                        fw.op("act", lambda e, j=j: e.activation(out=CTb[:, j, :], in_=CT[:, j, :], func=AF.Copy),
                              reads=[R_CT[j]], writes=[R_CTb[j]])
                if (c + 1) * L % 128 == 0 or c == nch - 1:
                    rows = min(128, ntok - tt * 128)
                    for hh in range(4):
                        fw.op("act", lambda e, hh=hh, rows=rows, HM=HM: e.activation(out=mlt[:rows, 768:1024], in_=HM[:rows, hh * 256:(hh + 1) * 256],
                                                                                    func=AF.Square, accum_out=ssml[:rows, hh:hh + 1]),
                              reads=R_hm, writes=[R_ssml])
                    fw.op("act", lambda e, rows=rows: e.activation(out=ssml[:rows, 4:8], in_=ssml[:rows, 0:4], func=AF.Sqrt, bias=EPS, scale=1.0 / 256),
                          reads=[R_ssml], writes=[R_ssml])
                    fw.op("dve", lambda e, rows=rows: e.reciprocal(out=ssml[:rows, 4:8], in_=ssml[:rows, 4:8]), reads=[R_ssml], writes=[R_ssml])
                    for hh in range(4):
                        fw.op("dve", lambda e, hh=hh, rows=rows, HM=HM: e.scalar_tensor_tensor(
                            out=HM[:rows, hh * 256:(hh + 1) * 256], in0=HM[:rows, hh * 256:(hh + 1) * 256], scalar=ssml[:rows, 4 + hh:5 + hh],
                            in1=gml_bc[:rows, hh * 256:(hh + 1) * 256], op0=ALU.mult, op1=ALU.mult),
                            reads=R_hm + [R_ssml, R_c], writes=R_hm)
                    for half in range(2):
                        u = uMO + tt * 2 + half
                        fw.op("dve", lambda e, u=u, half=half, rows=rows, HM=HM: e.tensor_tensor(
                            out=U(u)[:rows, :], in0=HM[:rows, half * 512:(half + 1) * 512], in1=U(u)[:rows, :], op=ALU.mult),
                            reads=R_hm + [R_u[u]], writes=[R_u[u]])
                    for g in range(2):
                        u = uMO + tt * 2 + g
                        for j in range(4):
                            fw.op("pe", lambda e, g=g, j=j, u=u, rows=rows: e.transpose(
                                out=PTB[g][:, j * 128: j * 128 + rows], in_=U(u)[:rows, j * 128:(j + 1) * 128], identity=identB[:rows, :rows]),
                                reads=[R_u[u], R_c], writes=[R_pt[g]])
                        for j in range(4):
                            k = 8 + g * 4 + j
                            o_ap = U(uCAT + k)[:, tt * 128: tt * 128 + rows]
                            i_ap = PTB[g][:, j * 128: j * 128 + rows]
                            if g == 0:
                                fw.op("dve", lambda e, o_ap=o_ap, i_ap=i_ap: e.tensor_copy(out=o_ap, in_=i_ap), reads=[R_pt[g]], writes=[R_u[uCAT + k]])
                            else:
                                fw.op("act", lambda e, o_ap=o_ap, i_ap=i_ap: e.activation(out=o_ap, in_=i_ap, func=AF.Copy), reads=[R_pt[g]], writes=[R_u[uCAT + k]])
            if dbg is not None:
                for k in range(8, 16):
                    fw.dma("pool", D_dbg, dbg[k, :, tok0:tok0 + ntok], U(uCAT + k)[:, 0:ntok], reads=[R_u[uCAT + k]])

        def block(seq, tok0, src0, ntok, xsrc, ysink, kout, vout):
            nt = (ntok + 127) // 128
            for tt in range(nt):
                rows = min(128, ntok - tt * 128)
                fw.dma("sp", D_x[tt], xres[:rows, tt, :], xsrc[src0 + tt * 128: src0 + tt * 128 + rows, :], writes=[R_x[tt]])
            norm_to_hT(0, ntok)
            if stop < 1:
                return
            uQT = 0; uKT = 8; uV = 16
            for half in range(2):
                slab = load_slab(w_in, 16, half * 512, 512)

                def cq(ch, m, pb, half=half):
                    h = half * 4 + ch
                    fw.op("act", lambda e, h=h, pb=pb: e.activation(out=U(uQT + h)[:, 0:ntok], in_=PS[pb][:, 0:ntok], func=AF.Copy),
                          reads=[R_ps[pb]], writes=[R_u[uQT + h]])
                proj_FM(slab, 16, 512, hT_get, R_hT, ntok, cq)
            if stop < 2:
                return
            for half in range(2):
                slab = load_slab(w_in, 16, 1024 + half * 512, 512)

                def ck_tm(tt, rows, pb, half=half):
                    f = next_f()
                    fw.op("act", lambda e, f=f, pb=pb, rows=rows: e.activation(out=ARF[:rows, f, :], in_=PS[pb][:rows, :], func=AF.Copy),
                          reads=[R_ps[pb]], writes=[R_f[f]])
                    fw.dma("sp", D_f[f], kout[src0 + tt * 128: src0 + tt * 128 + rows, half * 512:(half + 1) * 512], ARF[:rows, f, :],
                           reads=[R_f[f]])
                    ub = 24 + tt * 2 + half
                    fw.op("dve", lambda e, ub=ub, f=f, rows=rows: e.tensor_copy(out=U(ub)[:rows, :], in_=ARF[:rows, f, :]),
                          reads=[R_f[f]], writes=[R_u[ub]])
                    tm_to_fm(U(ub), [R_u[ub]], uKT + half * 4, tt, rows, tt % 2)
                proj_TM(slab, 16, 512, hT_get, R_hT, ntok, ck_tm)

                for ch in range(4):
                    h = half * 4 + ch
                    fw.dma("sp", D_unit(uKT + h), ktS[seq][h, :, tok0:tok0 + ntok], U(uKT + h)[:, 0:ntok], reads=[R_u[uKT + h]], writes=[R_ktS[seq][h]])
            if stop < 3:
                return
            for half in range(2):
                slab = load_slab(w_in, 16, 2048 + half * 512, 512)

                def cv_tm(tt, rows, pb, half=half):
                    f = next_f()
                    fw.op("act", lambda e, f=f, pb=pb, rows=rows: e.activation(out=ARF[:rows, f, :], in_=PS[pb][:rows, :], func=AF.Copy),
                          reads=[R_ps[pb]], writes=[R_f[f]])
                    fw.dma("sp", D_f[f], vout[src0 + tt * 128: src0 + tt * 128 + rows, half * 512:(half + 1) * 512], ARF[:rows, f, :],
                           reads=[R_f[f]])
                    u = uV + tt * 2 + half
                    fw.op("dve", lambda e, u=u, f=f, rows=rows: e.tensor_copy(out=U(u)[:rows, :], in_=ARF[:rows, f, :]),
                          reads=[R_f[f]], writes=[R_u[u]])
                    kt = (tok0 + tt * 128) // 128
                    for hh in range(4):
                        fw.dma("sp", D_unit(u), vS[seq][half * 4 + hh, 0:rows, kt, :], U(u)[:rows, hh * 128:(hh + 1) * 128], reads=[R_u[u]],
                               writes=[R_vS[seq][half * 4 + hh]])
                proj_TM(slab, 16, 512, hT_get, R_hT, ntok, cv_tm)
            if stop < 4:
                return
            attention(seq, tok0, ntok, diag=(seq == 0))
            if stop < 5:
                return
            mlstm(seq, tok0, ntok, 64 if seq == 0 else TS)
            if stop < 6:
                return
            proj_residual(w_out, lambda k: U(uCAT + k), R_u[uCAT:uCAT + 16], ntok, [(0, 16)])
            if stop < 7:
                dump_x(ntok, ysink, src0)
                return
            xattn(seq, ntok)
            if stop < 8:
                dump_x(ntok, ysink, src0)
                return
            ffn(ntok)
            final_norm(ntok, ysink, src0)

        def proj_residual(w, actT, actR, ntok, kparts):
            for cg in range(4):
                for pi, (k0, kc_n) in enumerate(kparts):
                    slab = load_slab(w, kc_n, cg * 512, 512, r0=k0 * 128)

                    def cons(tt, rows, pb, cg=cg):
                        fw.op("dve", lambda e, tt=tt, rows=rows, pb=pb, cg=cg: e.tensor_tensor(
                            out=xres[:rows, tt, cg * 512:(cg + 1) * 512], in0=xres[:rows, tt, cg * 512:(cg + 1) * 512],
                            in1=PS[pb][:rows, :], op=ALU.add), reads=[R_ps[pb], R_x[tt]], writes=[R_x[tt]])
                    proj_TM(slab, kc_n, 512, actT, actR, ntok, cons, k0=k0, acc=[0, 1, 2, 3],
                            first=(pi == 0), last=(pi == len(kparts) - 1))

        uXQ = 0; uMKT_ = 16; uMV_ = 24; uOT = 32; uXP = 48
        memKT = AR[:, uMKT_ * 512:(uMKT_ + 8) * 512].rearrange("p (j k) -> p j k", k=256)
        memV = AR[:, uMV_ * 512:(uMV_ + 8) * 512].rearrange("p (t c) -> p t c", c=2048)
        Ru_mkt = R_u[uMKT_:uMKT_ + 8]; Ru_mvv = R_u[uMV_:uMV_ + 8]

        def load_mem(seq):
            if seq == 0:
                fw.dma("sp", D_mem, AR[:, uMKT_ * 512:(uMKT_ + 8) * 512], mkS[0][:, :], reads=[R_mkS[0]], writes=Ru_mkt)
                fw.dma("sp", D_mem, AR[:, uMV_ * 512:(uMV_ + 8) * 512], mvS[0][:, :], reads=[R_mvS[0]], writes=Ru_mvv)
            else:
                for tt in range(2):
                    fw.dma("pool", D_memp, memV[:, tt, :], cmv[tt * 128:(tt + 1) * 128, :], writes=Ru_mvv)
                    ust = uOT + tt * 4
                    fw.dma("pool", D_memp, AR[:, ust * 512:(ust + 4) * 512], cmk[tt * 128:(tt + 1) * 128, :], writes=R_u[ust:ust + 4])
                    for g4 in range(4):
                        g = g4 % 2
                        for j in range(4):
                            jj = g4 * 4 + j
                            fw.op("pe", lambda e, g=g, j=j, jj=jj, ust=ust: e.transpose(
                                out=PTB[g][:, j * 128:(j + 1) * 128], in_=AR[:, ust * 512 + jj * 128: ust * 512 + (jj + 1) * 128],
                                identity=identB[:, :]), reads=R_u[ust:ust + 4] + [R_c], writes=[R_pt[g]])
                        for j in range(4):
                            jj = g4 * 4 + j
                            o_ap = memKT[:, jj, tt * 128:(tt + 1) * 128]
                            i_ap = PTB[g][:, j * 128:(j + 1) * 128]
                            if g == 0:
                                fw.op("dve", lambda e, o_ap=o_ap, i_ap=i_ap: e.tensor_copy(out=o_ap, in_=i_ap), reads=[R_pt[g]], writes=Ru_mkt)
                            else:
                                fw.op("act", lambda e, o_ap=o_ap, i_ap=i_ap: e.activation(out=o_ap, in_=i_ap, func=AF.Copy), reads=[R_pt[g]], writes=Ru_mkt)

        def xattn(seq, ntok):
            n = ntok
            norm_to_hT(1, ntok)
            load_mem(seq)
            for cg in range(4):
                slab = load_slab(w_mq, 16, cg * 512, 512)

                def c_q(ch, m, pb, cg=cg):
                    u = uXQ + cg * 4 + ch
                    fw.op("act", lambda e, u=u, pb=pb: e.activation(out=U(u)[:, 0:n], in_=PS[pb][:, 0:n], func=AF.Copy),
                          reads=[R_ps[pb]], writes=[R_u[u]])
                proj_FM(slab, 16, 512, hT_get, R_hT, ntok, c_q)
            rinv = ARF[:, 6, :]
            for hh in range(4):
                for kt in range(2):
                    sbk = kt
                    for dc in range(4):
                        j = hh * 4 + dc
                        fw.op("pe", lambda e, j=j, dc=dc, kt=kt, sbk=sbk: e.matmul(
                            PS[sbk][:, 0:n], lhsT=memKT[:, j, kt * 128:(kt + 1) * 128], rhs=U(uXQ + j)[:, 0:n],
                            start=(dc == 0), stop=(dc == 3)), reads=Ru_mkt + [R_u[uXQ + j]], writes=[R_ps[sbk]])
                    fw.op("act", lambda e, kt=kt, sbk=sbk: e.activation(out=U(uXP + kt)[:, 0:n], in_=PS[sbk][:, 0:n], func=AF.Exp,
                                                                       scale=512.0 ** -0.5), reads=[R_ps[sbk]], writes=[R_u[uXP + kt]])
                for kt in range(2):
                    fw.op("pe", lambda e, kt=kt: e.matmul(PS[4][:, 0:n], lhsT=onesB[:, :], rhs=U(uXP + kt)[:, 0:n], start=(kt == 0), stop=(kt == 1)),
                          reads=[R_c, R_u[uXP + kt]], writes=[R_ps[4]])
                fw.op("act", lambda e: e.activation(out=rinv[:, 0:n], in_=PS[4][:, 0:n], func=AF.Ln), reads=[R_ps[4]], writes=[R_f[6]])
                fw.op("act", lambda e: e.activation(out=rinv[:, 0:n], in_=rinv[:, 0:n], func=AF.Exp, scale=-1.0), reads=[R_f[6]], writes=[R_f[6]])
                for jv in range(4):
                    ob = 2 + (jv % 2)
                    for kt in range(2):
                        fw.op("pe", lambda e, jv=jv, kt=kt, ob=ob, hh=hh: e.matmul(
                            PS[ob][:, 0:n], lhsT=memV[:, kt, hh * 512 + jv * 128: hh * 512 + (jv + 1) * 128], rhs=U(uXP + kt)[:, 0:n],
                            start=(kt == 0), stop=(kt == 1)), reads=Ru_mvv + [R_u[uXP + kt]], writes=[R_ps[ob]])
                    u = uOT + hh * 4 + jv
                    fw.op("dve", lambda e, u=u, ob=ob: e.tensor_tensor(out=U(u)[:, 0:n], in0=PS[ob][:, 0:n], in1=rinv[:, 0:n], op=ALU.mult),
                          reads=[R_ps[ob], R_f[6]], writes=[R_u[u]])
            proj_residual(w_mo, lambda k: U(uOT + k), R_u[uOT:uOT + 16], ntok, [(0, 16)])

        def ffn(ntok):
            n = ntok
            norm_to_hT(2, ntok)
            nslab = 11
            for si in range(nslab):
                ncols = 512 if si < 10 else DFF - 5120
                slab_g = load_slab(w_gate, 16, si * 512, ncols)
                slab_u = load_slab(w_up, 16, si * 512, ncols)
                nchk = ncols // 128
                for ch in range(nchk):
                    s_, wv = slab_g
                    pg = next_ps()
                    for k in range(16):
                        fw.op("pe", lambda e, wv=wv, pg=pg, k=k, ch=ch: e.matmul(
                            PS[pg][:, 0:n], lhsT=wv[:, k, ch * 128:(ch + 1) * 128], rhs=hT[:, k, 0:n], start=(k == 0), stop=(k == 15)),
                            reads=[R_hT[k], R_w[s_]], writes=[R_ps[pg]])
                    fw.op("act", lambda e, ch=ch, pg=pg: e.activation(out=ARF[:, 4 + ch, 0:n], in_=PS[pg][:, 0:n], func=AF.Copy),
                          reads=[R_ps[pg]], writes=[R_f[4 + ch]])
                for ch in range(nchk):
                    s_, wv = slab_u
                    f_ = si * 4 + ch
                    pu = next_ps()
                    for k in range(16):
                        fw.op("pe", lambda e, wv=wv, pu=pu, k=k, ch=ch: e.matmul(
                            PS[pu][:, 0:n], lhsT=wv[:, k, ch * 128:(ch + 1) * 128], rhs=hT[:, k, 0:n], start=(k == 0), stop=(k == 15)),
                            reads=[R_hT[k], R_w[s_]], writes=[R_ps[pu]])
                    G = ARF[:, 4 + ch, :]; RG = [R_f[4 + ch]]
                    fa = next_f()
                    acc = ARF[:, fa, :]
                    cw = lambda j, f_=f_: convw_t[:, j, f_:f_ + 1]
                    fw.op("dve", lambda e, G=G, acc=acc, w2=cw(2), b=cw(3): e.tensor_scalar(out=acc[:, 0:n], in0=G[:, 0:n], scalar1=w2, scalar2=b,
                                                                                       op0=ALU.mult, op1=ALU.add),
                          reads=RG + [R_c], writes=[R_f[fa]])
                    fw.op("dve", lambda e, G=G, acc=acc, w1=cw(1): e.scalar_tensor_tensor(
                        out=acc[:, 1:n], in0=G[:, 0:n - 1], scalar=w1, in1=acc[:, 1:n], op0=ALU.mult, op1=ALU.add),
                        reads=RG + [R_c, R_f[fa]], writes=[R_f[fa]])
                    fw.op("dve", lambda e, G=G, acc=acc, w0=cw(0): e.scalar_tensor_tensor(
                        out=acc[:, 2:n], in0=G[:, 0:n - 2], scalar=w0, in1=acc[:, 2:n], op0=ALU.mult, op1=ALU.add),
                        reads=RG + [R_c, R_f[fa]], writes=[R_f[fa]])
                    fw.op("dve", lambda e, acc=acc, w1=cw(1), f_=f_: e.scalar_tensor_tensor(
                        out=acc[:, 0:1], in0=halo[:, 1, f_:f_ + 1], scalar=w1, in1=acc[:, 0:1], op0=ALU.mult, op1=ALU.add),
                        reads=[R_halo, R_c, R_f[fa]], writes=[R_f[fa]])
                    fw.op("dve", lambda e, acc=acc, w0=cw(0), f_=f_: e.scalar_tensor_tensor(
                        out=acc[:, 0:2], in0=halo[:, :, f_], scalar=w0, in1=acc[:, 0:2], op0=ALU.mult, op1=ALU.add),
                        reads=[R_halo, R_c, R_f[fa]], writes=[R_f[fa]])
                    fw.op("dve", lambda e, G=G, f_=f_: e.tensor_copy(out=halo[:, :, f_], in_=G[:, n - 2:n]), reads=RG, writes=[R_halo])
                    fw.op("act", lambda e, acc=acc: e.activation(out=acc[:, 0:n], in_=acc[:, 0:n], func=AF.Silu), reads=[R_f[fa]], writes=[R_f[fa]])
                    fw.op("dve", lambda e, acc=acc, pu=pu, f_=f_: e.tensor_tensor(out=U(f_)[:, 0:n], in0=acc[:, 0:n], in1=PS[pu][:, 0:n], op=ALU.mult),
                          reads=[R_f[fa], R_ps[pu]], writes=[R_u[f_]])
            proj_residual(w_down, lambda k: U(k), R_u[0:43], ntok, [(0, 16), (16, 16), (32, 11)])

        def final_norm(ntok, ysink, src0):
            nt = (ntok + 127) // 128
            for tt in range(nt):
                rows = min(128, ntok - tt * 128)
                c = 32 + 4 * tt
                fw.op("dve", lambda e, c=c: e.memset(stat[:, c:c + 2], 0.0), writes=[R_stat])
                fw.op("act", lambda e, tt=tt, rows=rows, c=c: e.activation(
                    out=xn[:rows, :], in_=xres[:rows, tt, :], func=AF.Square, accum_out=stat[:rows, c:c + 1]),
                    reads=[R_x[tt]], writes=[R_xn, R_stat])
                fw.op("act", lambda e, rows=rows, c=c: e.activation(
                    out=stat[:rows, c + 1:c + 2], in_=stat[:rows, c:c + 1], func=AF.Sqrt, bias=EPS, scale=1.0 / D),
                    reads=[R_stat], writes=[R_stat])
                fw.op("dve", lambda e, rows=rows, c=c: e.reciprocal(out=stat[:rows, c + 2:c + 3], in_=stat[:rows, c + 1:c + 2]),
                      reads=[R_stat], writes=[R_stat])
                fw.op("dve", lambda e, tt=tt, rows=rows, c=c: e.scalar_tensor_tensor(
                    out=xres[:rows, tt, :], in0=xres[:rows, tt, :], scalar=stat[:rows, c + 2:c + 3], in1=gfinal_bc[:rows, :],
                    op0=ALU.mult, op1=ALU.mult), reads=[R_x[tt], R_stat, R_c], writes=[R_x[tt]])
                fw.dma("sp", D_x[tt], ysink[src0 + tt * 128: src0 + tt * 128 + rows, :], xres[:rows, tt, :], reads=[R_x[tt]])

        def dump_x(ntok, ysink, src0):
            for tt in range((ntok + 127) // 128):
                rows = min(128, ntok - tt * 128)
                fw.dma("sp", D_x[tt], ysink[src0 + tt * 128: src0 + tt * 128 + rows, :], xres[:rows, tt, :], reads=[R_x[tt]])

        def conv_state_out(outp):
            pb = next_ps()
            fw.op("pe", lambda e, pb=pb: e.transpose(out=PS[pb][0:86, 0:128], in_=halo[:].rearrange("p j f -> p (j f)"), identity=identF[:, :]),
                  reads=[R_halo, R_c], writes=[R_ps[pb]])
            f = next_f()
            fw.op("dve", lambda e, pb=pb, f=f: e.tensor_copy(out=ARF[0:86, f, 0:128], in_=PS[pb][0:86, 0:128]), reads=[R_ps[pb]], writes=[R_f[f]])
            fw.dma("sp", D_f[f], outp.rearrange("j (f p) -> (j f) p", p=128), ARF[0:86, f, 0:128], reads=[R_f[f]])

        def conv_state_in(inp_):
            f = next_f()
            fw.dma("sp", D_f[f], ARF[0:86, f, 0:128], inp_.rearrange("j (f p) -> (j f) p", p=128), writes=[R_f[f]])
            pb = next_ps()
            fw.op("pe", lambda e, pb=pb, f=f: e.transpose(out=PS[pb][:, 0:86], in_=ARF[0:86, f, 0:128], identity=identF[0:86, 0:86]),
                  reads=[R_f[f], R_c], writes=[R_ps[pb]])
            fw.op("dve", lambda e, pb=pb: e.tensor_copy(out=halo[:].rearrange("p j f -> p (j f)"), in_=PS[pb][:, 0:86]), reads=[R_ps[pb]], writes=[R_halo])

        def mem_block():
            for tt in range(2):
                fw.dma("sp", D_x[tt], xres[:, tt, :], mem[tt * 128:(tt + 1) * 128, :], writes=[R_x[tt]])
            norm_to_hT(3, MEM)
            for (w, outp) in ((w_mk, pmk), (w_mv, pmv)):
                for cg in range(4):
                    slab = load_slab(w, 16, cg * 512, 512)

                    def cm(tt, rows, pb, cg=cg, outp=outp):
                        f = next_f()
                        fw.op("act", lambda e, f=f, pb=pb, rows=rows: e.activation(out=ARF[:rows, f, :], in_=PS[pb][:rows, :], func=AF.Copy),
                              reads=[R_ps[pb]], writes=[R_f[f]])
                        fw.dma("sp", D_f[f], outp[tt * 128: tt * 128 + rows, cg * 512:(cg + 1) * 512], ARF[:rows, f, :],
                               reads=[R_f[f]])
                        if outp is pmv:
                            u = 36 + (tt * 4 + cg) % 4
                            fw.op("dve", lambda e, u=u, f=f: e.tensor_copy(out=U(u)[:, :], in_=ARF[:, f, :]), reads=[R_f[f]], writes=[R_u[u]])
                            fw.dma("sp", D_unit(u), mvS[0][:, tt * 2048 + cg * 512: tt * 2048 + (cg + 1) * 512], U(u)[:, :], reads=[R_u[u]],
                                   writes=[R_mvS[0]])
                    proj_TM(slab, 16, 512, hT_get, R_hT, MEM, cm)
                    if w is w_mk:
                        def cmkT(ch, m, pb, cg=cg):
                            j = cg * 4 + ch
                            u = 32 + (j % 4)
                            fw.op("dve", lambda e, u=u, pb=pb: e.tensor_copy(out=U(u)[:, 0:MEM], in_=PS[pb][:, 0:MEM]), reads=[R_ps[pb]], writes=[R_u[u]])
                            fw.dma("sp", D_unit(u), mkS[0][:, j * 256:(j + 1) * 256], U(u)[:, 0:MEM], reads=[R_u[u]], writes=[R_mkS[0]])
                        proj_FM(slab, 16, 512, hT_get, R_hT, MEM, cmkT)

        for nm_ in ("w_mk", "w_mv", "w_in", "w_out", "w_mq", "w_mo", "w_gate", "w_up", "w_down"):
            convert_weight(nm_)
        if do_mem:
            mem_block()
        ml_init_zero()
        fw.op("dve", lambda e: e.memset(halo[:], 0.0), writes=[R_halo])
        for b in range(nblk):
            block(0, b * 512, b * 512, 512, x, y, pk, pv)
        if stop >= 5:
            ml_out_state(pc, pn, pm)
        if stop >= 8:
            conv_state_out(pconv)
        if sample:
            if stop >= 8:
                conv_state_in(conv0)
            cache_prologue()
            if stop >= 5:
                ml_init_state()
            block(1, T, 0, TS, xs, ys, sk, sv)
            if stop >= 5:
                ml_out_state(sc, sn, sm)
            if stop >= 8:
                conv_state_out(sconv)
        fw.emit()
    return nc


def _prep_inputs(inp, b):
    f = np.float32
    g = lambda k: np.asarray(inp[k], dtype=f)
    d = {}
    d["x"] = np.ascontiguousarray(g("x_prompt")[b])
    d["xs"] = np.ascontiguousarray(g("x_sample")[b])
    d["ck"] = np.ascontiguousarray(g("cache_da_k")[0, b].reshape(T, 1024))
    d["cv"] = np.ascontiguousarray(g("cache_da_v")[0, b].reshape(T, 1024))
    d["c0"] = np.ascontiguousarray(g("state_ml_c")[0, b])
    d["n0"] = np.ascontiguousarray(g("state_ml_n")[0, b])
    d["m0"] = np.ascontiguousarray(g("state_ml_m")[0, b].reshape(4, 1))
    d["conv0"] = np.ascontiguousarray(g("state_ffn_conv")[0, b])
    d["cmk"] = np.ascontiguousarray(g("cache_mem_k")[0, b].reshape(MEM, D))
    d["cmv"] = np.ascontiguousarray(g("cache_mem_v")[0, b].reshape(MEM, D))
    d["mem"] = np.ascontiguousarray(g("mem_prompt")[b])
    for k in ("w_in", "w_out", "w_mq", "w_mk", "w_mv", "w_mo", "w_gate", "w_up", "w_down"):
        d[k] = np.ascontiguousarray(g(k)[0])
    gs = np.stack([g("g_mix")[0], g("g_xattn")[0], g("g_ffn")[0], g("g_mem")[0]], 0)
    d["gpk"] = np.ascontiguousarray(gs.reshape(4, 16, 128).transpose(2, 0, 1))
    d["lamv"] = np.concatenate([g("lambda_q1")[0], g("lambda_k1")[0], g("lambda_q2")[0], g("lambda_k2")[0]])[None, :].copy()
    d["gda"] = np.ascontiguousarray(g("g_da_sub")[0].reshape(128, 1))
    d["bgate"] = np.ascontiguousarray(np.stack([g("b_ig")[0], g("b_fg")[0]], 1))
    d["gml"] = np.ascontiguousarray(g("g_ml")[0])
    cw = np.concatenate([g("conv_w")[0], g("conv_b")], 0)
    d["convw"] = np.ascontiguousarray(cw.reshape(4, 43, 128).transpose(2, 0, 1))
    d["gfinal"] = np.ascontiguousarray(g("g_final"))
    d["identf"] = np.eye(128, dtype=f)
    kk = np.arange(128)[:, None, None] + 128 * np.arange(4)[None, :, None]
    qq = np.arange(512)[None, None, :]
    d["masks"] = np.ascontiguousarray(((kk // 64) <= (qq // 64)).astype(f))
    es_ = np.zeros((4, 4, 128), f)
    for hh in range(4):
        es_[hh, hh, :] = 1.0
    d["esel"] = es_.reshape(4, 512)
    pp = np.arange(128)[:, None] % 64
    tt_ = np.arange(64)[None, :]
    d["maskml"] = np.where(pp <= tt_, 0.0, -1e30).astype(f)
    return d


_NC_CACHE = {}


def kernel(**inp):
    cfg = ("full",)
    if cfg not in _NC_CACHE:
        _NC_CACHE[cfg] = build()
    nc = _NC_CACHE[cfg]
    in_maps = [_prep_inputs(inp, b) for b in range(8)]
    res = run_bass_kernel_spmd(nc, in_maps, core_ids=list(range(8)))
    r = res.results
    st = lambda k: np.stack([np.asarray(r[b][k], dtype=np.float32) for b in range(8)], 0)
    y_prompt = st("y")
    y_sample = st("ys")
    p_k = st("pk").reshape(1, 8, T, 8, 128)
    p_v = st("pv").reshape(1, 8, T, 8, 128)
    p_c = st("pc")[None]
    p_n = st("pn")[None]
    p_m = st("pm").reshape(1, 8, 4)
    p_conv = st("pconv")[None]
    p_mk = st("pmk").reshape(1, 8, MEM, 4, 512)
    p_mv = st("pmv").reshape(1, 8, MEM, 4, 512)
    s_k = st("sk").reshape(1, 8, TS, 8, 128)
    s_v = st("sv").reshape(1, 8, TS, 8, 128)
    s_c = st("sc")[None]
    s_n = st("sn")[None]
    s_m = st("sm").reshape(1, 8, 4)
    s_conv = st("sconv")[None]
    return (y_prompt, y_sample, p_k, p_v, p_c, p_n, p_m, p_conv, p_mk, p_mv,
            s_k, s_v, s_c, s_n, s_m, s_conv)
```

```python
import contextlib
import numpy as np
import concourse.bass as bass
import concourse.mybir as mybir
from concourse.bass_utils import run_bass_kernel_spmd

F32 = mybir.dt.float32
BF16 = mybir.dt.bfloat16
AF = mybir.ActivationFunctionType
ALU = mybir.AluOpType
AX = mybir.AxisListType
ENGS = ("pe", "act", "dve", "pool", "sp")

D = 2048
T = 4096
TS = 16
DIN = 7176
DFF = 5504
NH = 8
MEM = 256
EPS = 1e-6
LAM_INIT = 0.2
ATTACH_WAIT = True


class Reg:
    __slots__ = ("name", "w", "r")

    def __init__(self, name=""):
        self.name = name
        self.w = None
        self.r = []


class DSem:
    __slots__ = ("sem", "count", "name")

    def __init__(self, name):
        self.name = name
        self.sem = None
        self.count = 0


class Op:
    __slots__ = ("eng", "fn", "cwaits", "dwaits", "sig", "sigidx", "dsem", "dval")

    def __init__(self, eng, fn):
        self.eng = eng
        self.fn = fn
        self.cwaits = []
        self.dwaits = []
        self.sig = False
        self.sigidx = 0
        self.dsem = None
        self.dval = 0


class FW:
    def __init__(self, nc):
        self.nc = nc
        self.ops = {e: [] for e in ENGS}
        self.dsems = []
        self.nops = 0

    def dsem(self, name):
        d = DSem(name)
        self.dsems.append(d)
        return d

    def _deps(self, o, reads, writes):
        deps = []
        seen = set()
        for r in reads:
            if r.w is not None and id(r.w) not in seen:
                seen.add(id(r.w)); deps.append(r.w)
        for w in writes:
            if w.w is not None and id(w.w) not in seen:
                seen.add(id(w.w)); deps.append(w.w)
            for x in w.r:
                if id(x) not in seen:
                    seen.add(id(x)); deps.append(x)
        for d in deps:
            if d is o:
                continue
            if d.dsem is not None:
                o.dwaits.append((d.dsem, d.dsem.count))
            else:
                if o.eng == "pe" and d.eng == "pe":
                    continue
                d.sig = True
                o.cwaits.append(d)
        for r in reads:
            r.r.append(o)
        for w in writes:
            w.w = o
            w.r = []

    def op(self, eng, fn, reads=(), writes=()):
        o = Op(eng, fn)
        self._deps(o, reads, writes)
        self.ops[eng].append(o)
        self.nops += 1
        return o

    def dma(self, eng, dsem, out_ap, in_ap, reads=(), writes=(), slow=False):
        if slow:
            def fn(e):
                return e.dma_start(out=out_ap, in_=in_ap, allow_slow_non_contiguous=True)
        else:
            def fn(e):
                return e.dma_start(out=out_ap, in_=in_ap)
        o = Op(eng, fn)
        self._deps(o, reads, writes)
        dsem.count += 16
        o.dsem = dsem
        o.dval = dsem.count
        self.ops[eng].append(o)
        self.nops += 1
        return o

    def emit(self):
        nc = self.nc
        with contextlib.ExitStack() as es:
            csem = {}
            for e in ENGS:
                csem[e] = es.enter_context(nc.semaphore("c_" + e))
            for d in self.dsems:
                if d.count > 0:
                    d.sem = es.enter_context(nc.semaphore("d_" + d.name))
            for e in ENGS:
                c = 0
                for o in self.ops[e]:
                    if o.sig and o.dsem is None:
                        c += 1
                        o.sigidx = c
            block = es.enter_context(nc.Block())
            final_d = [(d.sem, d.count) for d in self.dsems if d.count > 0]

            def run(e, engobj, last=False):
                seen = {}
                for o in self.ops[e]:
                    need = {}
                    for p in o.cwaits:
                        s = csem[p.eng]
                        v = p.sigidx
                        if seen.get(id(s), 0) < v:
                            need[id(s)] = (s, max(v, need.get(id(s), (s, 0))[1]))
                            seen[id(s)] = v
                    for (d, v) in o.dwaits:
                        if seen.get(id(d), 0) < v:
                            need[id(d)] = (d.sem, max(v, need.get(id(d), (d.sem, 0))[1]))
                            seen[id(d)] = v
                    need = list(need.values())
                    attach = need.pop() if (need and ATTACH_WAIT and o.dsem is None and e != "pe") else None
                    for (s, v) in need:
                        engobj.wait_ge(s, v)
                    ins = o.fn(engobj)
                    if attach is not None:
                        ins._wait_ge(attach[0], attach[1])
                    if o.dsem is not None:
                        ins.then_inc(o.dsem.sem, 16)
                    elif o.sig:
                        ins.then_inc(csem[e], 1)
                if last:
                    for (s, v) in final_d:
                        engobj.wait_ge(s, v)

            @block.tensor
            def _(pe):
                run("pe", pe)

            @block.scalar
            def _(act):
                run("act", act)

            @block.vector
            def _(dve):
                run("dve", dve)

            @block.gpsimd
            def _(pool):
                run("pool", pool)

            @block.sync
            def _(sp):
                run("sp", sp, last=True)


IN_NAMES = ["x", "xs", "ck", "cv", "c0", "n0", "m0", "conv0", "cmk", "cmv", "mem",
            "w_in", "w_out", "w_mq", "w_mk", "w_mv", "w_mo", "w_gate", "w_up", "w_down",
            "gpk", "lamv", "gda", "bgate", "gml", "convw", "gfinal", "identf", "masks"]


def build(nblk=8, sample=True, phases=("inproj",), stop=99, do_mem=True, debug=False):
    nc = bass.Bass("TRN2", target_bir_lowering=False)

    def din(name, shape):
        return nc.dram_tensor(name, shape, F32, kind="ExternalInput").ap()

    def dout(name, shape):
        return nc.dram_tensor(name, shape, F32, kind="ExternalOutput").ap()

    x = din("x", [T, D]); xs = din("xs", [TS, D])
    ck = din("ck", [T, 1024]); cv = din("cv", [T, 1024])
    c0 = din("c0", [4, 256, 256]); n0 = din("n0", [4, 256]); m0 = din("m0", [4, 1])
    conv0 = din("conv0", [2, DFF])
    cmk = din("cmk", [MEM, D]); cmv = din("cmv", [MEM, D]); mem = din("mem", [MEM, D])
    w_in = din("w_in", [D, DIN]); w_out = din("w_out", [D, D]); w_mq = din("w_mq", [D, D])
    w_mk = din("w_mk", [D, D]); w_mv = din("w_mv", [D, D]); w_mo = din("w_mo", [D, D])
    w_gate = din("w_gate", [D, DFF]); w_up = din("w_up", [D, DFF]); w_down = din("w_down", [DFF, D])
    gpk = din("gpk", [128, 4, 16])
    lamv = din("lamv", [1, 256])
    gda = din("gda", [128, 1])
    bgate = din("bgate", [4, 2])
    gml = din("gml", [1024])
    convw = din("convw", [128, 4, 43])
    gfinal = din("gfinal", [D])
    identf = din("identf", [128, 128])
    masks = din("masks", [128, 4, 512])
    esel = din("esel", [4, 512])
    maskml = din("maskml", [128, 64])

    y = dout("y", [T, D]); ys = dout("ys", [TS, D])
    pk = dout("pk", [T, 1024]); pv = dout("pv", [T, 1024])
    pc = dout("pc", [4, 256, 256]); pn = dout("pn", [4, 256]); pm = dout("pm", [4, 1])
    pconv = dout("pconv", [2, DFF])
    pmk = dout("pmk", [MEM, D]); pmv = dout("pmv", [MEM, D])
    sk = dout("sk", [TS, 1024]); sv = dout("sv", [TS, 1024])
    sc = dout("sc", [4, 256, 256]); sn = dout("sn", [4, 256]); sm = dout("sm", [4, 1])
    sconv = dout("sconv", [2, DFF])

    NKT = 33
    ktS = [nc.dram_tensor("ktS%d" % i, [NH, 128, NKT * 128], BF16, kind="Internal").ap() for i in range(2)]
    vS = [nc.dram_tensor("vS%d" % i, [NH, 128, NKT, 128], BF16, kind="Internal").ap() for i in range(2)]
    mkS = nc.dram_tensor("mkS", [2, 128, 16 * 256], BF16, kind="Internal").ap()
    mvS = nc.dram_tensor("mvS", [2, 128, 2 * 2048], BF16, kind="Internal").ap()

    dbg = nc.dram_tensor("dbg", [16, 128, T + TS], F32, kind="ExternalOutput").ap() if debug else None
    WSPEC = {"w_in": (w_in, 16, 15), "w_out": (w_out, 16, 4), "w_mq": (w_mq, 16, 4), "w_mk": (w_mk, 16, 4), "w_mv": (w_mv, 16, 4),
             "w_mo": (w_mo, 16, 4), "w_gate": (w_gate, 16, 11), "w_up": (w_up, 16, 11), "w_down": (w_down, 43, 12)}
    WB = {k: nc.dram_tensor("wb_" + k, [v[2], 128, 8192], BF16, kind="Internal").ap() for k, v in WSPEC.items()}
    fw = FW(nc)
    es = contextlib.ExitStack()
    with es:
        def sb(name, shape, dt):
            return es.enter_context(nc.sbuf_tensor(name, shape, dt))

        xres = sb("xres", [128, 4, D], F32); R_x = [Reg("x%d" % i) for i in range(4)]
        xn = sb("xn", [128, D], BF16); R_xn = Reg("xn")
        hT = sb("hT", [128, 16, 512], BF16); R_hT = [Reg("hT%d" % i) for i in range(16)]
        NSLOT = 2
        wsl = [sb("wsl%d" % i, [128, 8192], BF16) for i in range(NSLOT)]
        R_w = [Reg("w%d" % i) for i in range(NSLOT)]
        D_w = [fw.dsem("w%d" % i) for i in range(NSLOT)]
        NU = 64
        AR = sb("AR", [128, NU * 512], BF16); R_u = [Reg("u%d" % i) for i in range(NU)]
        NF = 8
        ARF = sb("ARF", [128, NF, 512], F32); R_f = [Reg("f%d" % i) for i in range(NF)]
        D_f = [fw.dsem("f%d" % i) for i in range(NF)]
        identF = sb("identF", [128, 128], F32); identB = sb("identB", [128, 128], BF16)
        R_c = Reg("consts")
        gpk_t = sb("gpk_t", [128, 4, 16], F32)
        stat = sb("stat", [128, 64], F32); R_stat = Reg("stat")
        D_x = [fw.dsem("x%d" % i) for i in range(4)]
        _du = {}

        def D_unit(i, q="sp"):
            if (i, q) not in _du:
                _du[(i, q)] = fw.dsem("u%d%s" % (i, q))
            return _du[(i, q)]

        PS = [es.enter_context(nc.psum_tensor("ps%d" % i, [128, 512], F32)) for i in range(6)]
        R_ps = [Reg("ps%d" % i) for i in range(6)]
        PTB = [es.enter_context(nc.psum_tensor("pt%d" % i, [128, 1024], BF16)) for i in range(2)]
        R_pt = [Reg("pt0"), Reg("pt1")]

        def U(i, n=1):
            return AR[:, i * 512:(i + n) * 512]

        D_c = fw.dsem("consts")
        fw.dma("sp", D_c, identF[:], identf[:, :], writes=[R_c])
        D_c2 = fw.dsem("consts2")
        fw.dma("pool", D_c2, identB[:], identf[:, :], writes=[R_c])
        fw.dma("sp", D_c, gpk_t[:], gpk[:, :, :], writes=[R_c])

        maskB = sb("maskB", [128, 4, 512], BF16)
        onesB = sb("onesB", [128, 128], BF16)
        lam_t = sb("lam_t", [128, 256], F32)
        lam_s = sb("lam_s", [128, 8], F32)
        gda_t = sb("gda_t", [128, 2], F32)
        D_c3 = fw.dsem("consts3")
        for j in range(4):
            fw.dma("pool", D_c3, maskB[:, j, :], masks[:, j, :], writes=[R_c])
        fw.dma("sp", D_c, lam_t[:], lamv[0:1, :].partition_broadcast(128) if False else lamv.partition_broadcast(128), writes=[R_c])
        fw.dma("sp", D_c, gda_t[:, 0:1], gda[:, :], writes=[R_c])
        fw.op("dve", lambda e: e.memset(onesB[:], 1.0), writes=[R_c])
        onesF = sb("onesF", [128, 128], F32)
        fw.op("dve", lambda e: e.memset(onesF[:], 1.0), writes=[R_c])
        fw.op("dve", lambda e: e.tensor_tensor(out=lam_t[:, 0:64], in0=lam_t[:, 0:64], in1=lam_t[:, 64:128], op=ALU.mult), reads=[R_c], writes=[R_c])
        fw.op("dve", lambda e: e.tensor_tensor(out=lam_t[:, 128:192], in0=lam_t[:, 128:192], in1=lam_t[:, 192:256], op=ALU.mult), reads=[R_c], writes=[R_c])
        fw.op("dve", lambda e: e.reduce_sum(out=lam_s[:, 0:1], in_=lam_t[:, 0:64], axis=AX.X), reads=[R_c], writes=[R_c])
        fw.op("dve", lambda e: e.reduce_sum(out=lam_s[:, 1:2], in_=lam_t[:, 128:192], axis=AX.X), reads=[R_c], writes=[R_c])
        fw.op("act", lambda e: e.activation(out=lam_s[:, 2:4], in_=lam_s[:, 0:2], func=AF.Exp), reads=[R_c], writes=[R_c])
        fw.op("dve", lambda e: e.tensor_tensor(out=lam_s[:, 4:5], in0=lam_s[:, 3:4], in1=lam_s[:, 2:3], op=ALU.subtract), reads=[R_c], writes=[R_c])
        fw.op("dve", lambda e: e.tensor_scalar(out=lam_s[:, 5:6], in0=lam_s[:, 4:5], scalar1=-LAM_INIT, scalar2=None, op0=ALU.add), reads=[R_c], writes=[R_c])
        fw.op("dve", lambda e: e.tensor_scalar(out=gda_t[:, 1:2], in0=gda_t[:, 0:1], scalar1=1.0 - LAM_INIT, scalar2=None, op0=ALU.mult), reads=[R_c], writes=[R_c])
        neglam = lam_s[:, 5:6]
        gda_s = gda_t[:, 1:2]
        R_ktS = [[Reg("ktS%d_%d" % (i, h)) for h in range(NH)] for i in range(2)]
        R_vS = [[Reg("vS%d_%d" % (i, h)) for h in range(NH)] for i in range(2)]
        D_h = fw.dsem("hist")
        uKTH = 24; uVH = 33; uPT = 42; uSQ = 46; uCAT = 48; uQT_ = 0
        D_dbg = fw.dsem("dbg")
        D_cp = [fw.dsem("cp%d" % i) for i in range(4)]

        esel_t = sb("esel_t", [4, 512], F32)
        maskml_t = sb("maskml_t", [128, 64], F32)
        bg_t = sb("bg_t", [4, 4], F32)
        gml_bc = sb("gml_bc", [128, 1024], F32)
        CT = sb("CT", [128, 8, 257], F32); R_CT = [Reg("CT%d" % j) for j in range(8)]
        CTb = sb("CTb", [128, 8, 257], BF16); R_CTb = [Reg("CTb%d" % j) for j in range(8)]
        carry = sb("carry", [4, 16], F32); R_carry = Reg("carry")
        gcol = sb("gcol", [128, 64], F32); R_gcol = Reg("gcol")
        GL = sb("GL", [128, 32], F32); R_GL = Reg("GL")
        mlt = sb("mlt", [128, 1024], F32)
        R_wT0 = Reg("wT"); R_HN0 = Reg("HN"); R_tmpN0 = Reg("tmpN"); R_dd0 = Reg("dd"); R_ssml = Reg("ssml"); R_dd2 = Reg("dd2")
        fw.dma("sp", D_c, esel_t[:], esel[:, :], writes=[R_c])
        fw.dma("sp", D_c, maskml_t[:], maskml[:, :], writes=[R_c])
        fw.dma("sp", D_c, bg_t[:, 0:2], bgate[:, :], writes=[R_c])
        fw.dma("sp", D_c, gml_bc[:], gml.partition_broadcast(128), writes=[R_c])
        fw.op("dve", lambda e: e.tensor_scalar(out=bg_t[:, 2:3], in0=bg_t[:, 1:2], scalar1=-1.0, scalar2=None, op0=ALU.mult), reads=[R_c], writes=[R_c])
        D_st = fw.dsem("mlstate")
        gfinal_bc = sb("gfinal_bc", [128, D], F32)
        convw_t = sb("convw_t", [128, 4, 43], F32)
        halo = sb("halo", [128, 2, 43], F32); R_halo = Reg("halo")
        fw.dma("sp", D_c, gfinal_bc[:], gfinal.partition_broadcast(128), writes=[R_c])
        fw.dma("sp", D_c, convw_t[:], convw[:, :, :], writes=[R_c])
        R_mkS = [Reg("mkS0"), Reg("mkS1")]; R_mvS = [Reg("mvS0"), Reg("mvS1")]
        D_mem = fw.dsem("memload"); D_memp = fw.dsem("memloadp")

        state = {"slot": 0, "ps": 0, "f": 0}

        def next_slot():
            s = state["slot"]; state["slot"] = (s + 1) % NSLOT
            return s

        def next_ps():
            p = state["ps"]; state["ps"] = (p + 1) % 4
            return p

        def next_f():
            f = state["f"]; state["f"] = (f + 1) % (NF - 4)
            return f

        R_wb = {k: Reg("wb_" + k) for k in WSPEC}
        D_wb = {k: fw.dsem("wb_" + k) for k in WSPEC}
        wname = {id(v[0].tensor): k for k, v in WSPEC.items()}

        def slab_index(name, c0_, r0):
            if name == "w_down":
                return (c0_ // 512) * 3 + (r0 // 2048)
            return c0_ // 512

        cvq = {"q": 0}

        def convert_weight(name):
            w, K, nsl = WSPEC[name]
            ncol_tot = w.shape[1]
            if name == "w_down":
                parts = [(cg * 512, 512, k0 * 128, kc) for cg in range(4) for (k0, kc) in ((0, 16), (16, 16), (32, 11))]
            else:
                parts = [(c, min(512, ncol_tot - c), 0, 16) for c in range(0, ncol_tot, 512)]
            for (c0_, ncols, r0, kc) in parts:
                si = slab_index(name, c0_, r0)
                for k0 in range(0, kc, 4):
                    kn = min(4, kc - k0)
                    q = cvq["q"]; cvq["q"] += 1
                    buf = q % 4
                    src = w[r0 + k0 * 128:r0 + (k0 + kn) * 128, c0_:c0_ + ncols].rearrange("(k p) c -> p k c", p=128)
                    stg = xres[:, buf, 0:kn * ncols]
                    fw.dma("sp", D_x[buf], stg.rearrange("p (k c) -> p k c", k=kn), src, writes=[R_x[buf]])
                    dst = AR[:, buf * 2048: buf * 2048 + kn * ncols]
                    ru = R_u[buf * 4: buf * 4 + 4]
                    if q % 2 == 0:
                        fw.op("dve", lambda e, dst=dst, stg=stg: e.tensor_copy(out=dst, in_=stg), reads=[R_x[buf]], writes=ru)
                    else:
                        fw.op("act", lambda e, dst=dst, stg=stg: e.activation(out=dst, in_=stg, func=AF.Copy), reads=[R_x[buf]], writes=ru)
                    fw.dma("pool", D_unit(buf * 4, "pool"), WB[name][si, :, k0 * ncols:(k0 + kn) * ncols], dst, reads=ru, writes=[R_wb[name]])

        def load_slab(w, kc, c0_, ncols, r0=0):
            name = wname[id(w.tensor)]
            si = slab_index(name, c0_, r0)
            s = next_slot()
            dst = wsl[s][:, 0:kc * ncols].rearrange("p (k c) -> p k c", k=kc)
            fw.dma("pool", D_w[s], wsl[s][:, 0:kc * ncols], WB[name][si, :, 0:kc * ncols], reads=[R_wb[name]], writes=[R_w[s]])
            return s, dst

        def norm_to_hT(which, ntok, src=None):
            nt = (ntok + 127) // 128
            for tt in range(nt):
                rows = min(128, ntok - tt * 128)
                sc_ = 4 * tt
                if stop < -2:
                    continue
                fw.op("dve", lambda e, c=sc_: e.memset(stat[:, c:c + 2], 0.0), writes=[R_stat])
                fw.op("act", lambda e, tt=tt, rows=rows, c=sc_: e.activation(
                    out=xn[:rows, :], in_=xres[:rows, tt, :], func=AF.Square, accum_out=stat[:rows, c:c + 1]),
                    reads=[R_x[tt]], writes=[R_xn, R_stat])
                fw.op("act", lambda e, rows=rows, c=sc_: e.activation(
                    out=stat[:rows, c + 1:c + 2], in_=stat[:rows, c:c + 1], func=AF.Sqrt, bias=EPS, scale=1.0 / D),
                    reads=[R_stat], writes=[R_stat])
                fw.op("dve", lambda e, rows=rows, c=sc_: e.reciprocal(out=stat[:rows, c + 2:c + 3], in_=stat[:rows, c + 1:c + 2]),
                      reads=[R_stat], writes=[R_stat])
                fw.op("act", lambda e, tt=tt, rows=rows, c=sc_: e.activation(
                    out=xn[:rows, :], in_=xres[:rows, tt, :], func=AF.Copy, scale=stat[:rows, c + 2:c + 3]),
                    reads=[R_x[tt], R_stat], writes=[R_xn])
                for g4 in range(4):
                    if stop < -1:
                        continue
                    half = g4 % 2
                    for j in range(4):
                        kc = g4 * 4 + j
                        fw.op("pe", lambda e, kc=kc, j=j, half=half, rows=rows: e.transpose(
                            out=PTB[half][:, j * 128: j * 128 + rows],
                            in_=xn[:rows, kc * 128:(kc + 1) * 128], identity=identB[:rows, :rows]),
                            reads=[R_xn, R_c], writes=[R_pt[half]])
                    for j in range(4):
                        if stop < 0:
                            continue
                        kc = g4 * 4 + j
                        eng = "dve" if half == 0 else "act"
                        o_ap = hT[:, kc, tt * 128: tt * 128 + rows]
                        i_ap = PTB[half][:, j * 128: j * 128 + rows]
                        g_ap = gpk_t[:, which, kc:kc + 1]
                        if eng == "dve":
                            fw.op("dve", lambda e, o_ap=o_ap, i_ap=i_ap, g_ap=g_ap: e.tensor_scalar(
                                out=o_ap, in0=i_ap, scalar1=g_ap, scalar2=None, op0=ALU.mult),
                                reads=[R_pt[half], R_c], writes=[R_hT[kc]])
                        else:
                            fw.op("act", lambda e, o_ap=o_ap, i_ap=i_ap, g_ap=g_ap: e.activation(
                                out=o_ap, in_=i_ap, func=AF.Copy, scale=g_ap),
                                reads=[R_pt[half], R_c], writes=[R_hT[kc]])

        def proj_TM(slab, kc_n, ncols, actT, actR, ntok, consume, k0=0, acc=None, first=True, last=True):
            s, wv = slab
            nt = (ntok + 127) // 128
            for tt in range(nt):
                rows = min(128, ntok - tt * 128)
                pb = acc[tt] if acc is not None else next_ps()
                for k in range(kc_n):
                    fw.op("pe", lambda e, tt=tt, rows=rows, pb=pb, k=k: e.matmul(
                        PS[pb][:rows, 0:ncols], lhsT=actT(k0 + k)[:, tt * 128: tt * 128 + rows], rhs=wv[:, k, :],
                        start=(first and k == 0), stop=(last and k == kc_n - 1)),
                        reads=[actR[k0 + k], R_w[s]], writes=[R_ps[pb]])
                if last:
                    consume(tt, rows, pb)

        def proj_FM(slab, kc_n, ncols, actT, actR, ntok, consume):
            s, wv = slab
            for ch in range((ncols + 127) // 128):
                m = min(128, ncols - ch * 128)
                pb = next_ps()
                for k in range(kc_n):
                    fw.op("pe", lambda e, ch=ch, m=m, pb=pb, k=k: e.matmul(
                        PS[pb][:m, 0:ntok], lhsT=wv[:, k, ch * 128: ch * 128 + m], rhs=actT(k)[:, 0:ntok],
                        start=(k == 0), stop=(k == kc_n - 1)),
                        reads=[actR[k], R_w[s]], writes=[R_ps[pb]])
                consume(ch, m, pb)

        def tm_to_fm(src, src_regs, dst_u0, tt, rows, g):
            for j in range(4):
                fw.op("pe", lambda e, j=j: e.transpose(out=PTB[g][:, j * 128: j * 128 + rows], in_=src[:rows, j * 128:(j + 1) * 128],
                                                       identity=identB[:rows, :rows]), reads=list(src_regs) + [R_c], writes=[R_pt[g]])
            dst = AR[:, dst_u0 * 512:(dst_u0 + 4) * 512].rearrange("p (h t) -> p h t", t=512)[:, :, tt * 128: tt * 128 + rows]
            srcp = PTB[g][:, 0:512].rearrange("p (h t) -> p h t", t=128)[:, :, 0:rows]
            if g == 0:
                fw.op("dve", lambda e: e.tensor_copy(out=dst, in_=srcp), reads=[R_pt[g]], writes=R_u[dst_u0:dst_u0 + 4])
            else:
                fw.op("act", lambda e: e.activation(out=dst, in_=srcp, func=AF.Copy), reads=[R_pt[g]], writes=R_u[dst_u0:dst_u0 + 4])

        hT_get = lambda k: hT[:, k, :]

        def attention(seq, pos0, ntok, diag):
            nkeys = pos0 + ntok
            nkt = (nkeys + 127) // 128
            Ru_kth = R_u[uKTH:uKTH + 9] + R_u[8:17]
            Ru_vh = R_u[uVH:uVH + 9]
            KTHc = (AR[:, uKTH * 512: uKTH * 512 + NKT * 128], AR[:, 8 * 512: 8 * 512 + NKT * 128])
            fw.op("pool", lambda e: e.memset(KTHc[0][64:128, 0:nkeys], 0.0), writes=R_u[uKTH:uKTH + 9])
            fw.op("pool", lambda e: e.memset(KTHc[1][0:64, 0:nkeys], 0.0), writes=R_u[8:17])
            VH = AR[:, uVH * 512: uVH * 512 + NKT * 128].rearrange("p (k e) -> p k e", e=128)
            A = ARF[:, 6, :]; Bf = ARF[:, 7, :]
            for h in range(NH):
                fw.dma("sp", D_h, KTHc[0][0:64, 0:nkeys], ktS[seq][h, 0:64, 0:nkeys], reads=[R_ktS[seq][h]], writes=Ru_kth)
                fw.dma("sp", D_h, KTHc[1][64:128, 0:nkeys], ktS[seq][h, 64:128, 0:nkeys], reads=[R_ktS[seq][h]], writes=Ru_kth)
                nfull = nkeys // 128
                fw.dma("sp", D_h, VH[:, 0:nfull, :], vS[seq][h, :, 0:nfull, :], reads=[R_vS[seq][h]], writes=Ru_vh)
                if nkeys % 128:
                    fw.dma("sp", D_h, VH[0:nkeys % 128, nfull, :], vS[seq][h, 0:nkeys % 128, nfull, :], reads=[R_vS[seq][h]], writes=Ru_vh)
                steps = [(c, kt) for kt in range(nkt) for c in range(2)]
                SB = (0, 1, 4)
                NSB = 3
                ACC = (ARF[:, 4, :], ARF[:, 5, :]); R_acc = (R_f[4], R_f[5])

                def s_step(i):
                    c, kt = steps[i]
                    kw = min(128, nkeys - kt * 128)
                    sbk = SB[i % NSB]
                    pu = uPT + (i % 4)
                    fw.op("pe", lambda e, c=c, kt=kt, kw=kw, sbk=sbk, h=h: e.matmul(
                        PS[sbk][:kw, 0:ntok], lhsT=KTHc[c][:, kt * 128: kt * 128 + kw],
                        rhs=U(uQT_ + h)[:, 0:ntok], start=True, stop=True),
                        reads=Ru_kth + [R_u[uQT_ + h]], writes=[R_ps[sbk]])
                    fw.op("act", lambda e, kw=kw, sbk=sbk, pu=pu: e.activation(
                        out=U(pu)[:kw, 0:ntok], in_=PS[sbk][:kw, 0:ntok], func=AF.Exp, scale=0.125),
                        reads=[R_ps[sbk]], writes=[R_u[pu]])
                    j = kt - (nkt - 4)
                    if diag and j >= 0:
                        fw.op("dve", lambda e, pu=pu, j=j: e.tensor_tensor(
                            out=U(pu)[:, 0:ntok], in0=U(pu)[:, 0:ntok], in1=maskB[:, j, 0:ntok], op=ALU.mult),
                            reads=[R_u[pu], R_c], writes=[R_u[pu]])
                    if c == 1:
                        return
                    if kt == 0:
                        if kw < 128:
                            fw.op("dve", lambda e, c=c: e.memset(ACC[c][:, 0:ntok], 0.0), writes=[R_acc[c]])
                        fw.op("dve", lambda e, c=c, kw=kw, pu=pu: e.tensor_copy(out=ACC[c][:kw, 0:ntok], in_=U(pu)[:kw, 0:ntok]),
                              reads=[R_u[pu]], writes=[R_acc[c]])
                    else:
                        fw.op("dve", lambda e, c=c, kw=kw, pu=pu: e.tensor_tensor(
                            out=ACC[c][:kw, 0:ntok], in0=ACC[c][:kw, 0:ntok], in1=U(pu)[:kw, 0:ntok], op=ALU.add),
                            reads=[R_u[pu], R_acc[c]], writes=[R_acc[c]])

                def av_step(i):
                    c, kt = steps[i]
                    kw = min(128, nkeys - kt * 128)
                    pu = uPT + (i % 4)
                    fw.op("pe", lambda e, c=c, kt=kt, kw=kw, pu=pu: e.matmul(
                        PS[2 + c][:, 0:ntok], lhsT=VH[:kw, kt, :], rhs=U(pu)[:kw, 0:ntok],
                        start=(kt == 0), stop=(kt == nkt - 1)),
                        reads=Ru_vh + [R_u[pu]], writes=[R_ps[2 + c]])
                    if c == 1:
                        fw.op("pe", lambda e, kt=kt, kw=kw, pu=pu: e.matmul(
                            PS[5][:, 0:ntok], lhsT=onesB[:kw, :], rhs=U(pu)[:kw, 0:ntok], start=(kt == 0), stop=(kt == nkt - 1)),
                            reads=[R_c, R_u[pu]], writes=[R_ps[5]])

                LA = 2
                for i in range(len(steps) + LA):
                    if i < len(steps):
                        s_step(i)
                    if i - LA >= 0:
                        av_step(i - LA)
                n = ntok
                fw.op("pe", lambda e: e.matmul(PS[SB[0]][:, 0:n], lhsT=onesF[:, :], rhs=ACC[0][:, 0:n], start=True, stop=True),
                      reads=[R_c, R_acc[0]], writes=[R_ps[SB[0]]])
                fw.op("act", lambda e: e.activation(out=A[:, 0:n], in_=PS[SB[0]][:, 0:n], func=AF.Ln), reads=[R_ps[SB[0]]], writes=[R_f[6]])
                fw.op("act", lambda e: e.activation(out=A[:, 0:n], in_=A[:, 0:n], func=AF.Exp, scale=-1.0), reads=[R_f[6]], writes=[R_f[6]])
                fw.op("act", lambda e: e.activation(out=Bf[:, 0:n], in_=PS[5][:, 0:n], func=AF.Ln), reads=[R_ps[5]], writes=[R_f[7]])
                fw.op("act", lambda e: e.activation(out=Bf[:, 0:n], in_=Bf[:, 0:n], func=AF.Exp, scale=-1.0), reads=[R_f[7]], writes=[R_f[7]])
                fw.op("dve", lambda e: e.tensor_tensor(out=A[:, 0:n], in0=PS[2][:, 0:n], in1=A[:, 0:n], op=ALU.mult),
                      reads=[R_ps[2], R_f[6]], writes=[R_f[6]])
                fw.op("dve", lambda e: e.tensor_tensor(out=Bf[:, 0:n], in0=PS[3][:, 0:n], in1=Bf[:, 0:n], op=ALU.mult),
                      reads=[R_ps[3], R_f[7]], writes=[R_f[7]])
                fw.op("dve", lambda e: e.scalar_tensor_tensor(out=A[:, 0:n], in0=Bf[:, 0:n], scalar=neglam, in1=A[:, 0:n],
                                                              op0=ALU.mult, op1=ALU.add),
                      reads=[R_f[6], R_f[7], R_c], writes=[R_f[6]])
                fw.op("dve", lambda e: e.tensor_tensor(out=U(uSQ)[:, 0:n], in0=A[:, 0:n], in1=A[:, 0:n], op=ALU.mult),
                      reads=[R_f[6]], writes=[R_u[uSQ]])
                fw.op("pe", lambda e: e.matmul(PS[4][:, 0:n], lhsT=onesB[:, :], rhs=U(uSQ)[:, 0:n], start=True, stop=True),
                      reads=[R_c, R_u[uSQ]], writes=[R_ps[4]])
                fw.op("act", lambda e: e.activation(out=Bf[:, 0:n], in_=PS[4][:, 0:n], func=AF.Ln, bias=EPS, scale=1.0 / 128),
                      reads=[R_ps[4]], writes=[R_f[7]])
                fw.op("act", lambda e: e.activation(out=Bf[:, 0:n], in_=Bf[:, 0:n], func=AF.Exp, scale=-0.5), reads=[R_f[7]], writes=[R_f[7]])
                fw.op("dve", lambda e, h=h: e.scalar_tensor_tensor(out=U(uCAT + h)[:, 0:n], in0=A[:, 0:n], scalar=gda_s, in1=Bf[:, 0:n],
                                                                   op0=ALU.mult, op1=ALU.mult),
                      reads=[R_f[6], R_f[7], R_c], writes=[R_u[uCAT + h]])
                if dbg is not None:
                    fw.dma("pool", D_dbg, dbg[h, :, pos0:pos0 + n], U(uCAT + h)[:, 0:n], reads=[R_u[uCAT + h]])

        def cache_prologue():
            for kt in range(T // 128):
                uk = 0 + (kt % 2) * 2
                uv = 4 + (kt % 2) * 2
                ut = 8 + (kt % 2) * 2
                fw.dma("pool", D_unit(uk, "pool"), AR[:, uk * 512:(uk + 2) * 512], ck[kt * 128:(kt + 1) * 128, :], writes=R_u[uk:uk + 2])
                fw.dma("pool", D_unit(uv, "pool"), AR[:, uv * 512:(uv + 2) * 512], cv[kt * 128:(kt + 1) * 128, :], writes=R_u[uv:uv + 2])
                for hh in range(NH):
                    fw.dma("sp", D_unit(uv), vS[1][hh, :, kt, :], AR[:, uv * 512 + hh * 128: uv * 512 + (hh + 1) * 128],
                           reads=R_u[uv:uv + 2], writes=[R_vS[1][hh]])
                for g in range(2):
                    for j in range(4):
                        hh = g * 4 + j
                        fw.op("pe", lambda e, g=g, j=j, hh=hh, uk=uk: e.transpose(
                            out=PTB[g][:, j * 128:(j + 1) * 128], in_=AR[:, uk * 512 + hh * 128: uk * 512 + (hh + 1) * 128],
                            identity=identB[:, :]), reads=R_u[uk:uk + 2] + [R_c], writes=[R_pt[g]])
                    eng = "dve" if g == 0 else "act"
                    o_ap = AR[:, (ut + g) * 512:(ut + g + 1) * 512]
                    if g == 0:
                        fw.op("dve", lambda e, o_ap=o_ap, g=g: e.tensor_copy(out=o_ap, in_=PTB[g][:, 0:512]),
                              reads=[R_pt[g]], writes=[R_u[ut + g]])
                    else:
                        fw.op("act", lambda e, o_ap=o_ap, g=g: e.activation(out=o_ap, in_=PTB[g][:, 0:512], func=AF.Copy),
                              reads=[R_pt[g]], writes=[R_u[ut + g]])
                    for j in range(4):
                        hh = g * 4 + j
                        fw.dma("sp", D_unit(ut + g), ktS[1][hh, :, kt * 128:(kt + 1) * 128], AR[:, (ut + g) * 512 + j * 128:(ut + g) * 512 + (j + 1) * 128],
                               reads=[R_u[ut + g]], writes=[R_ktS[1][hh]])

        uMQ = 0; uMK = 8; uMKT = 16; uMV = 24; uMO = 33; uST = 41; uVW = 42
        mvA = AR[:, uMV * 512: uMV * 512 + 16 * 257].rearrange("p (a e) -> p a e", e=257)
        Ru_mv = R_u[uMV:uMV + 9]

        def ml_init_zero():
            fw.op("dve", lambda e: e.memset(CT[:], 0.0), writes=R_CT)
            fw.op("dve", lambda e: e.memset(CTb[:], 0.0), writes=R_CTb)
            fw.op("dve", lambda e: e.memset(carry[:], 0.0), writes=[R_carry])

        def ml_init_state():
            for hh in range(4):
                for ec in range(2):
                    f = next_f()
                    fw.dma("sp", D_f[f], ARF[:, f, 0:256], c0[hh, ec * 128:(ec + 1) * 128, :], writes=[R_f[f]])
                    for dc in range(2):
                        pb = next_ps()
                        fw.op("pe", lambda e, f=f, dc=dc, pb=pb: e.transpose(out=PS[pb][:, 0:128], in_=ARF[:, f, dc * 128:(dc + 1) * 128],
                                                                        identity=identF[:, :]), reads=[R_f[f], R_c], writes=[R_ps[pb]])
                        j = hh * 2 + dc
                        fw.op("dve", lambda e, j=j, ec=ec, pb=pb: e.tensor_copy(out=CT[:, j, ec * 128:(ec + 1) * 128], in_=PS[pb][:, 0:128]),
                              reads=[R_ps[pb]], writes=[R_CT[j]])
            fw.dma("sp", D_st, CT[:, :, 256], n0.rearrange("h (dc p) -> p (h dc)", p=128), writes=R_CT, slow=True)
            fw.dma("sp", D_st, carry[:, 0:1], m0[:, :], writes=[R_carry])
            fw.op("act", lambda e: e.activation(out=CTb[:], in_=CT[:], func=AF.Copy), reads=R_CT, writes=R_CTb)

        def ml_out_state(oc, on, om):
            for hh in range(4):
                for ec in range(2):
                    pb = next_ps()
                    for dc in range(2):
                        j = hh * 2 + dc
                        fw.op("pe", lambda e, j=j, ec=ec, dc=dc, pb=pb: e.transpose(
                            out=PS[pb][:, dc * 128:(dc + 1) * 128], in_=CT[:, j, ec * 128:(ec + 1) * 128], identity=identF[:, :]),
                            reads=[R_CT[j], R_c], writes=[R_ps[pb]])
                    f = next_f()
                    fw.op("dve", lambda e, f=f, pb=pb: e.tensor_copy(out=ARF[:, f, 0:256], in_=PS[pb][:, 0:256]), reads=[R_ps[pb]], writes=[R_f[f]])
                    fw.dma("sp", D_f[f], oc[hh, ec * 128:(ec + 1) * 128, :], ARF[:, f, 0:256], reads=[R_f[f]])
            fw.dma("sp", D_st, on.rearrange("h (dc p) -> p (h dc)", p=128), CT[:, :, 256], reads=R_CT, slow=True)
            fw.dma("sp", D_st, om[:, :], carry[:, 0:1], reads=[R_carry])

        def mlstm(seq, tok0, ntok, L):
            nt = (ntok + 127) // 128
            nch = ntok // L
            for half in range(2):
                slab = load_slab(w_in, 16, 3072 + half * 512, 512)

                def c_mq(ch, m, pb, half=half):
                    u = uMQ + half * 4 + ch
                    fw.op("act", lambda e, u=u, pb=pb: e.activation(out=U(u)[:, 0:ntok], in_=PS[pb][:, 0:ntok], func=AF.Copy),
                          reads=[R_ps[pb]], writes=[R_u[u]])
                proj_FM(slab, 16, 512, hT_get, R_hT, ntok, c_mq)
            for half in range(2):
                slab = load_slab(w_in, 16, 4096 + half * 512, 512)

                def c_mk(tt, rows, pb, half=half):
                    u = uMKT + tt * 2 + half
                    fw.op("act", lambda e, u=u, pb=pb, rows=rows: e.activation(out=U(u)[:rows, :], in_=PS[pb][:rows, :], func=AF.Copy, scale=1.0 / 16),
                          reads=[R_ps[pb]], writes=[R_u[u]])
                    tm_to_fm(U(u), [R_u[u]], uMK + half * 4, tt, rows, tt % 2)
                proj_TM(slab, 16, 512, hT_get, R_hT, ntok, c_mk)
            fw.op("dve", lambda e: e.memset(mvA[:, :, 256:257], 1.0), writes=Ru_mv)
            for half in range(2):
                slab = load_slab(w_in, 16, 5120 + half * 512, 512)

                def c_mv(tt, rows, pb, half=half):
                    fw.op("act", lambda e, tt=tt, pb=pb, rows=rows, half=half: e.activation(
                        out=mvA[:rows, tt * 4 + half * 2: tt * 4 + half * 2 + 2, 0:256],
                        in_=PS[pb][:rows, :].rearrange("p (a e) -> p a e", e=256), func=AF.Copy),
                        reads=[R_ps[pb]], writes=Ru_mv)
                proj_TM(slab, 16, 512, hT_get, R_hT, ntok, c_mv)
            for half in range(2):
                slab = load_slab(w_in, 16, 6144 + half * 512, 512)

                def c_mo(tt, rows, pb, half=half):
                    u = uMO + tt * 2 + half
                    fw.op("act", lambda e, u=u, pb=pb, rows=rows: e.activation(out=U(u)[:rows, :], in_=PS[pb][:rows, :], func=AF.Sigmoid),
                          reads=[R_ps[pb]], writes=[R_u[u]])
                proj_TM(slab, 16, 512, hT_get, R_hT, ntok, c_mo)
            slab = load_slab(w_in, 16, 7168, 8)
            s_, wv = slab
            n = ntok
            row = lambda f: ARF[0:4, f, 0:n]
            for gi in range(2):
                pb = next_ps()
                for k in range(16):
                    fw.op("pe", lambda e, gi=gi, pb=pb, k=k: e.matmul(PS[pb][0:4, 0:n], lhsT=wv[:, k, gi * 4:(gi + 1) * 4], rhs=hT[:, k, 0:n],
                                                                     start=(k == 0), stop=(k == 15)),
                          reads=[R_hT[k], R_w[s_]], writes=[R_ps[pb]])
                if gi == 0:
                    fw.op("dve", lambda e, pb=pb: e.tensor_scalar(out=row(0), in0=PS[pb][0:4, 0:n], scalar1=bg_t[:, 0:1], scalar2=None, op0=ALU.add),
                          reads=[R_ps[pb], R_c], writes=[R_f[0]])
                else:
                    fw.op("act", lambda e, pb=pb: e.activation(out=row(1), in_=PS[pb][0:4, 0:n], func=AF.Exp, scale=-1.0, bias=bg_t[:, 2:3]),
                          reads=[R_ps[pb], R_c], writes=[R_f[1]])
            fw.op("act", lambda e: e.activation(out=row(1), in_=row(1), func=AF.Ln, bias=1.0), reads=[R_f[1]], writes=[R_f[1]])
            fw.op("dve", lambda e: e.tensor_scalar(out=row(1), in0=row(1), scalar1=-1.0, scalar2=None, op0=ALU.mult), reads=[R_f[1]], writes=[R_f[1]])
            fw.op("dve", lambda e: e.memset(row(7), 0.0), writes=[R_f[7]])
            fw.op("dve", lambda e: e.tensor_tensor_scan(out=row(2), data0=row(1), data1=row(0), initial=carry[:, 0:1], op0=ALU.add, op1=ALU.max),
                  reads=[R_f[1], R_f[0], R_carry], writes=[R_f[2]])
            fw.op("dve", lambda e: e.tensor_tensor_scan(out=row(3), data0=row(1), data1=row(7), initial=0.0, op0=ALU.add, op1=ALU.add),
                  reads=[R_f[1], R_f[7]], writes=[R_f[3]])
            fw.op("dve", lambda e: e.tensor_tensor(out=row(4), in0=row(3), in1=row(2), op=ALU.subtract), reads=[R_f[3], R_f[2]], writes=[R_f[4]])
            fw.op("dve", lambda e: e.tensor_tensor(out=row(0), in0=row(0), in1=row(3), op=ALU.subtract), reads=[R_f[0], R_f[3]], writes=[R_f[0]])
            fw.op("act", lambda e: e.activation(out=row(6), in_=row(2), func=AF.Exp, scale=-1.0), reads=[R_f[2]], writes=[R_f[6]])
            fw.op("dve", lambda e: e.tensor_copy(out=carry[:, 4:5], in_=carry[:, 0:1]), reads=[R_carry], writes=[R_carry])
            for c in range(1, nch):
                fw.op("dve", lambda e, c=c: e.tensor_scalar(out=carry[:, 4 + c:5 + c], in0=ARF[0:4, 4, c * L - 1:c * L], scalar1=-1.0, scalar2=None, op0=ALU.mult),
                      reads=[R_f[4], R_carry], writes=[R_carry])
            for c in range(nch):
                cs = slice(c * L, (c + 1) * L)
                fw.op("act", lambda e, c=c, cs=cs: e.activation(out=ARF[0:4, 5, cs], in_=ARF[0:4, 4, cs], func=AF.Exp, bias=carry[:, 4 + c:5 + c]),
                      reads=[R_f[4], R_carry], writes=[R_f[5]])
                fw.op("act", lambda e, c=c, cs=cs: e.activation(out=ARF[0:4, 7, cs], in_=ARF[0:4, 0, cs], func=AF.Exp, bias=ARF[0:4, 4, (c + 1) * L - 1:(c + 1) * L]),
                      reads=[R_f[0], R_f[4]], writes=[R_f[7]])
            fw.op("dve", lambda e: e.tensor_copy(out=carry[:, 0:1], in_=ARF[0:4, 2, n - 1:n]), reads=[R_f[2]], writes=[R_carry])
            pbT = next_ps()
            for tt in range(nt):
                rows = min(128, ntok - tt * 128)
                for qi, f in enumerate((0, 7, 5, 6)):
                    o = (tt * 4 + qi) * 4
                    fw.op("pe", lambda e, tt=tt, rows=rows, f=f, o=o: e.transpose(
                        out=PS[pbT][:rows, o:o + 4], in_=ARF[0:4, f, tt * 128: tt * 128 + rows], identity=identF[0:4, 0:4]),
                        reads=[R_f[f], R_c], writes=[R_ps[pbT]])
            rows_all = 128 if ntok >= 128 else ntok
            fw.op("dve", lambda e: e.tensor_copy(out=gcol[:rows_all, 0:nt * 16], in_=PS[pbT][:rows_all, 0:nt * 16]), reads=[R_ps[pbT]], writes=[R_gcol])
            pbG = next_ps()
            for hh in range(4):
                for c in range(nch):
                    e_c = (c + 1) * L - 1
                    fw.op("pe", lambda e, hh=hh, c=c, e_c=e_c: e.matmul(PS[pbG][:, hh * 8 + c: hh * 8 + c + 1], lhsT=esel_t[0:4, hh * 128:(hh + 1) * 128],
                                                                        rhs=ARF[0:4, 5, e_c:e_c + 1], start=True, stop=True),
                          reads=[R_f[5], R_c], writes=[R_ps[pbG]])
            fw.op("dve", lambda e: e.tensor_copy(out=GL[:, :], in_=PS[pbG][:, 0:32]), reads=[R_ps[pbG]], writes=[R_GL])
            ssml = mlt[:, 704:712]
            TMP = [dict(wT=mlt[:, 0:64], HN=mlt[:, 64:321], tmpN=mlt[:, 384:641], ddv=mlt[:, 700:704], uST=41, uVW=42,
                        R_wT=R_wT0, R_HN=R_HN0, R_tmpN=R_tmpN0, R_dd=R_dd0),
                   dict(wT=ARF[:, 5, 0:64], HN=ARF[:, 6, 0:257], tmpN=ARF[:, 7, 0:257], ddv=ARF[:, 5, 64:68], uST=43, uVW=44,
                        R_wT=R_f[5], R_HN=R_f[6], R_tmpN=R_f[7], R_dd=R_dd2)]
            for c in range(nch):
                tt = (c * L) // 128
                po = (c * L) % 128
                P = slice(po, po + L)
                cs = slice(c * L, (c + 1) * L)
                hmf = (tt % 2) * 2
                HM = ARF[:, hmf:hmf + 2, :].rearrange("p a b -> p (a b)")
                R_hm = [R_f[hmf], R_f[hmf + 1]]
                for hh in range(4):
                    gc = lambda qi, tt=tt, hh=hh: gcol[P, (tt * 4 + qi) * 4 + hh:(tt * 4 + qi) * 4 + hh + 1]
                    T_ = TMP[hh % 2]
                    wT = T_["wT"]; HN = T_["HN"]; tmpN = T_["tmpN"]; ddv = T_["ddv"]; uSTl = T_["uST"]; uVWl = T_["uVW"]
                    R_wT = T_["R_wT"]; R_HN = T_["R_HN"]; R_tmpN = T_["R_tmpN"]; R_dd = T_["R_dd"]
                    for dc in range(2):
                        j = hh * 2 + dc
                        fw.op("pe", lambda e, wT=wT, HN=HN, tmpN=tmpN, ddv=ddv, uST=uSTl, uVW=uVWl, j=j, dc=dc, P=P, cs=cs, po=po: e.matmul(
                            PS[0][P, 0:L], lhsT=U(uMK + j)[:, cs], rhs=U(uMQ + j)[:, cs], start=(dc == 0), stop=(dc == 1),
                            tile_position=(0, po)),
                            reads=[R_u[uMK + j], R_u[uMQ + j]], writes=[R_ps[0]])
                    fw.op("pe", lambda e, wT=wT, HN=HN, tmpN=tmpN, ddv=ddv, uST=uSTl, uVW=uVWl, hh=hh, P=P, cs=cs, po=po: e.matmul(
                        PS[1][P, 0:L], lhsT=esel_t[0:4, hh * 128: hh * 128 + L], rhs=ARF[0:4, 4, cs], start=True, stop=False,
                        tile_position=(0, po)),
                        reads=[R_f[4], R_c], writes=[R_ps[1]])
                    fw.op("pe", lambda e, wT=wT, HN=HN, tmpN=tmpN, ddv=ddv, uST=uSTl, uVW=uVWl, P=P, po=po: e.matmul(
                        PS[1][P, 0:L], lhsT=identF[P, P], rhs=maskml_t[P, 0:L], start=False, stop=True, tile_position=(po, po)),
                        reads=[R_c], writes=[R_ps[1]])
                    fw.op("act", lambda e, wT=wT, HN=HN, tmpN=tmpN, ddv=ddv, uST=uSTl, uVW=uVWl, P=P, g0=gc(0): e.activation(out=wT[P, 0:L], in_=PS[1][P, 0:L], func=AF.Exp, bias=g0),
                          reads=[R_ps[1], R_gcol], writes=[R_wT])
                    fw.op("dve", lambda e, wT=wT, HN=HN, tmpN=tmpN, ddv=ddv, uST=uSTl, uVW=uVWl, P=P: e.tensor_tensor(out=U(uST)[P, 0:L], in0=PS[0][P, 0:L], in1=wT[P, 0:L], op=ALU.mult),
                          reads=[R_ps[0], R_wT], writes=[R_u[uSTl]])
                    fw.op("pe", lambda e, wT=wT, HN=HN, tmpN=tmpN, ddv=ddv, uST=uSTl, uVW=uVWl, P=P, tt=tt, hh=hh, po=po: e.matmul(
                        PS[2][P, 0:257], lhsT=U(uST)[P, 0:L], rhs=mvA[P, tt * 4 + hh, :], start=True, stop=True, tile_position=(po, po)),
                        reads=[R_u[uSTl]] + Ru_mv, writes=[R_ps[2]])
                    for dc in range(2):
                        j = hh * 2 + dc
                        fw.op("pe", lambda e, wT=wT, HN=HN, tmpN=tmpN, ddv=ddv, uST=uSTl, uVW=uVWl, j=j, dc=dc, P=P, cs=cs, po=po: e.matmul(
                            PS[5][P, 0:257], lhsT=U(uMQ + j)[:, cs], rhs=CTb[:, j, :], start=(dc == 0), stop=(dc == 1),
                            tile_position=(0, po)),
                            reads=[R_u[uMQ + j], R_CTb[j]], writes=[R_ps[5]])
                    fw.op("act", lambda e, wT=wT, HN=HN, tmpN=tmpN, ddv=ddv, uST=uSTl, uVW=uVWl, P=P, g2=gc(2): e.activation(out=tmpN[P, :], in_=PS[5][P, 0:257], func=AF.Copy, scale=g2),
                          reads=[R_ps[5], R_gcol], writes=[R_tmpN])
                    fw.op("dve", lambda e, wT=wT, HN=HN, tmpN=tmpN, ddv=ddv, uST=uSTl, uVW=uVWl, P=P: e.tensor_tensor(out=HN[P, :], in0=tmpN[P, :], in1=PS[2][P, 0:257], op=ALU.add),
                          reads=[R_tmpN, R_ps[2]], writes=[R_HN])
                    fw.op("dve", lambda e, wT=wT, HN=HN, tmpN=tmpN, ddv=ddv, uST=uSTl, uVW=uVWl, P=P: e.tensor_scalar(out=ddv[P, 2:3], in0=HN[P, 256:257], scalar1=-1.0, scalar2=None, op0=ALU.mult),
                          reads=[R_HN], writes=[R_dd])
                    fw.op("dve", lambda e, wT=wT, HN=HN, tmpN=tmpN, ddv=ddv, uST=uSTl, uVW=uVWl, P=P: e.tensor_tensor(out=ddv[P, 2:3], in0=ddv[P, 2:3], in1=HN[P, 256:257], op=ALU.max),
                          reads=[R_HN, R_dd], writes=[R_dd])
                    fw.op("dve", lambda e, wT=wT, HN=HN, tmpN=tmpN, ddv=ddv, uST=uSTl, uVW=uVWl, P=P, g3=gc(3): e.tensor_scalar(out=ddv[P, 0:1], in0=ddv[P, 2:3], scalar1=g3, scalar2=None, op0=ALU.max),
                          reads=[R_dd, R_gcol], writes=[R_dd])
                    fw.op("dve", lambda e, wT=wT, HN=HN, tmpN=tmpN, ddv=ddv, uST=uSTl, uVW=uVWl, P=P: e.reciprocal(out=ddv[P, 1:2], in_=ddv[P, 0:1]), reads=[R_dd], writes=[R_dd])
                    fw.op("dve", lambda e, wT=wT, HN=HN, tmpN=tmpN, ddv=ddv, uST=uSTl, uVW=uVWl, P=P, hh=hh, HM=HM: e.tensor_scalar(out=HM[P, hh * 256:(hh + 1) * 256], in0=HN[P, 0:256], scalar1=ddv[P, 1:2],
                                                                             scalar2=None, op0=ALU.mult),
                          reads=[R_HN, R_dd], writes=R_hm)
                    fw.op("dve", lambda e, wT=wT, HN=HN, tmpN=tmpN, ddv=ddv, uST=uSTl, uVW=uVWl, P=P, tt=tt, hh=hh, g1=gc(1): e.tensor_scalar(out=U(uVW)[P, 0:257], in0=mvA[P, tt * 4 + hh, :], scalar1=g1,
                                                                                      scalar2=None, op0=ALU.mult),
                          reads=Ru_mv + [R_gcol], writes=[R_u[uVWl]])
                    for dc in range(2):
                        j = hh * 2 + dc
                        um = uMKT + tt * 2 + (hh // 2)
                        co = (hh % 2) * 256 + dc * 128
                        fw.op("pe", lambda e, wT=wT, HN=HN, tmpN=tmpN, ddv=ddv, uST=uSTl, uVW=uVWl, dc=dc, um=um, co=co, P=P, po=po: e.matmul(
                            PS[3 + dc][:, 0:257], lhsT=U(um)[P, co:co + 128], rhs=U(uVW)[P, 0:257], start=True, stop=True,
                            tile_position=(po, 0)),
                            reads=[R_u[um], R_u[uVWl]], writes=[R_ps[3 + dc]])
                        fw.op("dve", lambda e, wT=wT, HN=HN, tmpN=tmpN, ddv=ddv, uST=uSTl, uVW=uVWl, j=j, dc=dc, hh=hh, c=c: e.scalar_tensor_tensor(
                            out=CT[:, j, :], in0=CT[:, j, :], scalar=GL[:, hh * 8 + c: hh * 8 + c + 1], in1=PS[3 + dc][:, 0:257],
                            op0=ALU.mult, op1=ALU.add),
                            reads=[R_CT[j], R_GL, R_ps[3 + dc]], writes=[R_CT[j]])
                        fw.op("act", lambda e, wT=wT, HN=HN, tmpN=tmpN, ddv=ddv, uST=uSTl, uVW=uVWl, j=j: e.activation(out=CTb[:, j, :], in_=CT[:, j, :], func=AF.Copy),
                              reads=[R_CT[j]], writes=[R_CTb[j]])
                if (c + 1) * L % 128 == 0 or c == nch - 1:
                    rows = min(128, ntok - tt * 128)
                    for hh in range(4):
                        fw.op("act", lambda e, hh=hh, rows=rows, HM=HM: e.activation(out=mlt[:rows, 768:1024], in_=HM[:rows, hh * 256:(hh + 1) * 256],
                                                                                    func=AF.Square, accum_out=ssml[:rows, hh:hh + 1]),
                              reads=R_hm, writes=[R_ssml])
                    fw.op("act", lambda e, rows=rows: e.activation(out=ssml[:rows, 4:8], in_=ssml[:rows, 0:4], func=AF.Sqrt, bias=EPS, scale=1.0 / 256),
                          reads=[R_ssml], writes=[R_ssml])
                    fw.op("dve", lambda e, rows=rows: e.reciprocal(out=ssml[:rows, 4:8], in_=ssml[:rows, 4:8]), reads=[R_ssml], writes=[R_ssml])
                    for hh in range(4):
                        fw.op("dve", lambda e, hh=hh, rows=rows, HM=HM: e.scalar_tensor_tensor(
                            out=HM[:rows, hh * 256:(hh + 1) * 256], in0=HM[:rows, hh * 256:(hh + 1) * 256], scalar=ssml[:rows, 4 + hh:5 + hh],
                            in1=gml_bc[:rows, hh * 256:(hh + 1) * 256], op0=ALU.mult, op1=ALU.mult),
                            reads=R_hm + [R_ssml, R_c], writes=R_hm)
                    for half in range(2):
                        u = uMO + tt * 2 + half
                        fw.op("dve", lambda e, u=u, half=half, rows=rows, HM=HM: e.tensor_tensor(
                            out=U(u)[:rows, :], in0=HM[:rows, half * 512:(half + 1) * 512], in1=U(u)[:rows, :], op=ALU.mult),
                            reads=R_hm + [R_u[u]], writes=[R_u[u]])
                    for g in range(2):
                        u = uMO + tt * 2 + g
                        for j in range(4):
                            fw.op("pe", lambda e, g=g, j=j, u=u, rows=rows: e.transpose(
                                out=PTB[g][:, j * 128: j * 128 + rows], in_=U(u)[:rows, j * 128:(j + 1) * 128], identity=identB[:rows, :rows]),
                                reads=[R_u[u], R_c], writes=[R_pt[g]])
                        for j in range(4):
                            k = 8 + g * 4 + j
                            o_ap = U(uCAT + k)[:, tt * 128: tt * 128 + rows]
                            i_ap = PTB[g][:, j * 128: j * 128 + rows]
                            if g == 0:
                                fw.op("dve", lambda e, o_ap=o_ap, i_ap=i_ap: e.tensor_copy(out=o_ap, in_=i_ap), reads=[R_pt[g]], writes=[R_u[uCAT + k]])
                            else:
                                fw.op("act", lambda e, o_ap=o_ap, i_ap=i_ap: e.activation(out=o_ap, in_=i_ap, func=AF.Copy), reads=[R_pt[g]], writes=[R_u[uCAT + k]])
            if dbg is not None:
                for k in range(8, 16):
                    fw.dma("pool", D_dbg, dbg[k, :, tok0:tok0 + ntok], U(uCAT + k)[:, 0:ntok], reads=[R_u[uCAT + k]])

        def block(seq, tok0, src0, ntok, xsrc, ysink, kout, vout):
            nt = (ntok + 127) // 128
            for tt in range(nt):
                rows = min(128, ntok - tt * 128)
                fw.dma("sp", D_x[tt], xres[:rows, tt, :], xsrc[src0 + tt * 128: src0 + tt * 128 + rows, :], writes=[R_x[tt]])
            norm_to_hT(0, ntok)
            if stop < 1:
                return
            uQT = 0; uKT = 8; uV = 16
            for half in range(2):
                slab = load_slab(w_in, 16, half * 512, 512)

                def cq(ch, m, pb, half=half):
                    h = half * 4 + ch
                    fw.op("act", lambda e, h=h, pb=pb: e.activation(out=U(uQT + h)[:, 0:ntok], in_=PS[pb][:, 0:ntok], func=AF.Copy),
                          reads=[R_ps[pb]], writes=[R_u[uQT + h]])
                proj_FM(slab, 16, 512, hT_get, R_hT, ntok, cq)
            if stop < 2:
                return
            for half in range(2):
                slab = load_slab(w_in, 16, 1024 + half * 512, 512)

                def ck_tm(tt, rows, pb, half=half):
                    f = next_f()
                    fw.op("act", lambda e, f=f, pb=pb, rows=rows: e.activation(out=ARF[:rows, f, :], in_=PS[pb][:rows, :], func=AF.Copy),
                          reads=[R_ps[pb]], writes=[R_f[f]])
                    fw.dma("sp", D_f[f], kout[src0 + tt * 128: src0 + tt * 128 + rows, half * 512:(half + 1) * 512], ARF[:rows, f, :],
                           reads=[R_f[f]])
                    ub = 24 + tt * 2 + half
                    fw.op("dve", lambda e, ub=ub, f=f, rows=rows: e.tensor_copy(out=U(ub)[:rows, :], in_=ARF[:rows, f, :]),
                          reads=[R_f[f]], writes=[R_u[ub]])
                    tm_to_fm(U(ub), [R_u[ub]], uKT + half * 4, tt, rows, tt % 2)
                proj_TM(slab, 16, 512, hT_get, R_hT, ntok, ck_tm)

                for ch in range(4):
                    h = half * 4 + ch
                    fw.dma("sp", D_unit(uKT + h), ktS[seq][h, :, tok0:tok0 + ntok], U(uKT + h)[:, 0:ntok], reads=[R_u[uKT + h]], writes=[R_ktS[seq][h]])
            if stop < 3:
                return
            for half in range(2):
                slab = load_slab(w_in, 16, 2048 + half * 512, 512)

                def cv_tm(tt, rows, pb, half=half):
                    f = next_f()
                    fw.op("act", lambda e, f=f, pb=pb, rows=rows: e.activation(out=ARF[:rows, f, :], in_=PS[pb][:rows, :], func=AF.Copy),
                          reads=[R_ps[pb]], writes=[R_f[f]])
                    fw.dma("sp", D_f[f], vout[src0 + tt * 128: src0 + tt * 128 + rows, half * 512:(half + 1) * 512], ARF[:rows, f, :],
                           reads=[R_f[f]])
                    u = uV + tt * 2 + half
                    fw.op("dve", lambda e, u=u, f=f, rows=rows: e.tensor_copy(out=U(u)[:rows, :], in_=ARF[:rows, f, :]),
                          reads=[R_f[f]], writes=[R_u[u]])
                    kt = (tok0 + tt * 128) // 128
                    for hh in range(4):
                        fw.dma("sp", D_unit(u), vS[seq][half * 4 + hh, 0:rows, kt, :], U(u)[:rows, hh * 128:(hh + 1) * 128], reads=[R_u[u]],
                               writes=[R_vS[seq][half * 4 + hh]])
                proj_TM(slab, 16, 512, hT_get, R_hT, ntok, cv_tm)
            if stop < 4:
                return
            attention(seq, tok0, ntok, diag=(seq == 0))
            if stop < 5:
                return
            mlstm(seq, tok0, ntok, 64 if seq == 0 else TS)
            if stop < 6:
                return
            proj_residual(w_out, lambda k: U(uCAT + k), R_u[uCAT:uCAT + 16], ntok, [(0, 16)])
            if stop < 7:
                dump_x(ntok, ysink, src0)
                return
            xattn(seq, ntok)
            if stop < 8:
                dump_x(ntok, ysink, src0)
                return
            ffn(ntok)
            final_norm(ntok, ysink, src0)

        def proj_residual(w, actT, actR, ntok, kparts):
            for cg in range(4):
                for pi, (k0, kc_n) in enumerate(kparts):
                    slab = load_slab(w, kc_n, cg * 512, 512, r0=k0 * 128)

                    def cons(tt, rows, pb, cg=cg):
                        fw.op("dve", lambda e, tt=tt, rows=rows, pb=pb, cg=cg: e.tensor_tensor(
                            out=xres[:rows, tt, cg * 512:(cg + 1) * 512], in0=xres[:rows, tt, cg * 512:(cg + 1) * 512],
                            in1=PS[pb][:rows, :], op=ALU.add), reads=[R_ps[pb], R_x[tt]], writes=[R_x[tt]])
                    proj_TM(slab, kc_n, 512, actT, actR, ntok, cons, k0=k0, acc=[0, 1, 2, 3],
                            first=(pi == 0), last=(pi == len(kparts) - 1))

        uXQ = 0; uMKT_ = 16; uMV_ = 24; uOT = 32; uXP = 48
        memKT = AR[:, uMKT_ * 512:(uMKT_ + 8) * 512].rearrange("p (j k) -> p j k", k=256)
        memV = AR[:, uMV_ * 512:(uMV_ + 8) * 512].rearrange("p (t c) -> p t c", c=2048)
        Ru_mkt = R_u[uMKT_:uMKT_ + 8]; Ru_mvv = R_u[uMV_:uMV_ + 8]

        def load_mem(seq):
            if seq == 0:
                fw.dma("sp", D_mem, AR[:, uMKT_ * 512:(uMKT_ + 8) * 512], mkS[0][:, :], reads=[R_mkS[0]], writes=Ru_mkt)
                fw.dma("sp", D_mem, AR[:, uMV_ * 512:(uMV_ + 8) * 512], mvS[0][:, :], reads=[R_mvS[0]], writes=Ru_mvv)
            else:
                for tt in range(2):
                    fw.dma("pool", D_memp, memV[:, tt, :], cmv[tt * 128:(tt + 1) * 128, :], writes=Ru_mvv)
                    ust = uOT + tt * 4
                    fw.dma("pool", D_memp, AR[:, ust * 512:(ust + 4) * 512], cmk[tt * 128:(tt + 1) * 128, :], writes=R_u[ust:ust + 4])
                    for g4 in range(4):
                        g = g4 % 2
                        for j in range(4):
                            jj = g4 * 4 + j
                            fw.op("pe", lambda e, g=g, j=j, jj=jj, ust=ust: e.transpose(
                                out=PTB[g][:, j * 128:(j + 1) * 128], in_=AR[:, ust * 512 + jj * 128: ust * 512 + (jj + 1) * 128],
                                identity=identB[:, :]), reads=R_u[ust:ust + 4] + [R_c], writes=[R_pt[g]])
                        for j in range(4):
                            jj = g4 * 4 + j
                            o_ap = memKT[:, jj, tt * 128:(tt + 1) * 128]
                            i_ap = PTB[g][:, j * 128:(j + 1) * 128]
                            if g == 0:
                                fw.op("dve", lambda e, o_ap=o_ap, i_ap=i_ap: e.tensor_copy(out=o_ap, in_=i_ap), reads=[R_pt[g]], writes=Ru_mkt)
                            else:
                                fw.op("act", lambda e, o_ap=o_ap, i_ap=i_ap: e.activation(out=o_ap, in_=i_ap, func=AF.Copy), reads=[R_pt[g]], writes=Ru_mkt)

        def xattn(seq, ntok):
            n = ntok
            norm_to_hT(1, ntok)
            load_mem(seq)
            for cg in range(4):
                slab = load_slab(w_mq, 16, cg * 512, 512)

                def c_q(ch, m, pb, cg=cg):
                    u = uXQ + cg * 4 + ch
                    fw.op("act", lambda e, u=u, pb=pb: e.activation(out=U(u)[:, 0:n], in_=PS[pb][:, 0:n], func=AF.Copy),
                          reads=[R_ps[pb]], writes=[R_u[u]])
                proj_FM(slab, 16, 512, hT_get, R_hT, ntok, c_q)
            rinv = ARF[:, 6, :]
            for hh in range(4):
                for kt in range(2):
                    sbk = kt
                    for dc in range(4):
                        j = hh * 4 + dc
                        fw.op("pe", lambda e, j=j, dc=dc, kt=kt, sbk=sbk: e.matmul(
                            PS[sbk][:, 0:n], lhsT=memKT[:, j, kt * 128:(kt + 1) * 128], rhs=U(uXQ + j)[:, 0:n],
                            start=(dc == 0), stop=(dc == 3)), reads=Ru_mkt + [R_u[uXQ + j]], writes=[R_ps[sbk]])
                    fw.op("act", lambda e, kt=kt, sbk=sbk: e.activation(out=U(uXP + kt)[:, 0:n], in_=PS[sbk][:, 0:n], func=AF.Exp,
                                                                       scale=512.0 ** -0.5), reads=[R_ps[sbk]], writes=[R_u[uXP + kt]])
                for kt in range(2):
                    fw.op("pe", lambda e, kt=kt: e.matmul(PS[4][:, 0:n], lhsT=onesB[:, :], rhs=U(uXP + kt)[:, 0:n], start=(kt == 0), stop=(kt == 1)),
                          reads=[R_c, R_u[uXP + kt]], writes=[R_ps[4]])
                fw.op("act", lambda e: e.activation(out=rinv[:, 0:n], in_=PS[4][:, 0:n], func=AF.Ln), reads=[R_ps[4]], writes=[R_f[6]])
                fw.op("act", lambda e: e.activation(out=rinv[:, 0:n], in_=rinv[:, 0:n], func=AF.Exp, scale=-1.0), reads=[R_f[6]], writes=[R_f[6]])
                for jv in range(4):
                    ob = 2 + (jv % 2)
                    for kt in range(2):
                        fw.op("pe", lambda e, jv=jv, kt=kt, ob=ob, hh=hh: e.matmul(
                            PS[ob][:, 0:n], lhsT=memV[:, kt, hh * 512 + jv * 128: hh * 512 + (jv + 1) * 128], rhs=U(uXP + kt)[:, 0:n],
                            start=(kt == 0), stop=(kt == 1)), reads=Ru_mvv + [R_u[uXP + kt]], writes=[R_ps[ob]])
                    u = uOT + hh * 4 + jv
                    fw.op("dve", lambda e, u=u, ob=ob: e.tensor_tensor(out=U(u)[:, 0:n], in0=PS[ob][:, 0:n], in1=rinv[:, 0:n], op=ALU.mult),
                          reads=[R_ps[ob], R_f[6]], writes=[R_u[u]])
            proj_residual(w_mo, lambda k: U(uOT + k), R_u[uOT:uOT + 16], ntok, [(0, 16)])

        def ffn(ntok):
            n = ntok
            norm_to_hT(2, ntok)
            nslab = 11
            for si in range(nslab):
                ncols = 512 if si < 10 else DFF - 5120
                slab_g = load_slab(w_gate, 16, si * 512, ncols)
                slab_u = load_slab(w_up, 16, si * 512, ncols)
                nchk = ncols // 128
                for ch in range(nchk):
                    s_, wv = slab_g
                    pg = next_ps()
                    for k in range(16):
                        fw.op("pe", lambda e, wv=wv, pg=pg, k=k, ch=ch: e.matmul(
                            PS[pg][:, 0:n], lhsT=wv[:, k, ch * 128:(ch + 1) * 128], rhs=hT[:, k, 0:n], start=(k == 0), stop=(k == 15)),
                            reads=[R_hT[k], R_w[s_]], writes=[R_ps[pg]])
                    fw.op("act", lambda e, ch=ch, pg=pg: e.activation(out=ARF[:, 4 + ch, 0:n], in_=PS[pg][:, 0:n], func=AF.Copy),
                          reads=[R_ps[pg]], writes=[R_f[4 + ch]])
                for ch in range(nchk):
                    s_, wv = slab_u
                    f_ = si * 4 + ch
                    pu = next_ps()
                    for k in range(16):
                        fw.op("pe", lambda e, wv=wv, pu=pu, k=k, ch=ch: e.matmul(
                            PS[pu][:, 0:n], lhsT=wv[:, k, ch * 128:(ch + 1) * 128], rhs=hT[:, k, 0:n], start=(k == 0), stop=(k == 15)),
                            reads=[R_hT[k], R_w[s_]], writes=[R_ps[pu]])
                    G = ARF[:, 4 + ch, :]; RG = [R_f[4 + ch]]
                    fa = next_f()
                    acc = ARF[:, fa, :]
                    cw = lambda j, f_=f_: convw_t[:, j, f_:f_ + 1]
                    fw.op("dve", lambda e, G=G, acc=acc, w2=cw(2), b=cw(3): e.tensor_scalar(out=acc[:, 0:n], in0=G[:, 0:n], scalar1=w2, scalar2=b,
                                                                                       op0=ALU.mult, op1=ALU.add),
                          reads=RG + [R_c], writes=[R_f[fa]])
                    fw.op("dve", lambda e, G=G, acc=acc, w1=cw(1): e.scalar_tensor_tensor(
                        out=acc[:, 1:n], in0=G[:, 0:n - 1], scalar=w1, in1=acc[:, 1:n], op0=ALU.mult, op1=ALU.add),
                        reads=RG + [R_c, R_f[fa]], writes=[R_f[fa]])
                    fw.op("dve", lambda e, G=G, acc=acc, w0=cw(0): e.scalar_tensor_tensor(
                        out=acc[:, 2:n], in0=G[:, 0:n - 2], scalar=w0, in1=acc[:, 2:n], op0=ALU.mult, op1=ALU.add),
                        reads=RG + [R_c, R_f[fa]], writes=[R_f[fa]])
                    fw.op("dve", lambda e, acc=acc, w1=cw(1), f_=f_: e.scalar_tensor_tensor(
                        out=acc[:, 0:1], in0=halo[:, 1, f_:f_ + 1], scalar=w1, in1=acc[:, 0:1], op0=ALU.mult, op1=ALU.add),
                        reads=[R_halo, R_c, R_f[fa]], writes=[R_f[fa]])
                    fw.op("dve", lambda e, acc=acc, w0=cw(0), f_=f_: e.scalar_tensor_tensor(
                        out=acc[:, 0:2], in0=halo[:, :, f_], scalar=w0, in1=acc[:, 0:2], op0=ALU.mult, op1=ALU.add),
                        reads=[R_halo, R_c, R_f[fa]], writes=[R_f[fa]])
                    fw.op("dve", lambda e, G=G, f_=f_: e.tensor_copy(out=halo[:, :, f_], in_=G[:, n - 2:n]), reads=RG, writes=[R_halo])
                    fw.op("act", lambda e, acc=acc: e.activation(out=acc[:, 0:n], in_=acc[:, 0:n], func=AF.Silu), reads=[R_f[fa]], writes=[R_f[fa]])
                    fw.op("dve", lambda e, acc=acc, pu=pu, f_=f_: e.tensor_tensor(out=U(f_)[:, 0:n], in0=acc[:, 0:n], in1=PS[pu][:, 0:n], op=ALU.mult),
                          reads=[R_f[fa], R_ps[pu]], writes=[R_u[f_]])
            proj_residual(w_down, lambda k: U(k), R_u[0:43], ntok, [(0, 16), (16, 16), (32, 11)])

        def final_norm(ntok, ysink, src0):
            nt = (ntok + 127) // 128
            for tt in range(nt):
                rows = min(128, ntok - tt * 128)
                c = 32 + 4 * tt
                fw.op("dve", lambda e, c=c: e.memset(stat[:, c:c + 2], 0.0), writes=[R_stat])
                fw.op("act", lambda e, tt=tt, rows=rows, c=c: e.activation(
                    out=xn[:rows, :], in_=xres[:rows, tt, :], func=AF.Square, accum_out=stat[:rows, c:c + 1]),
                    reads=[R_x[tt]], writes=[R_xn, R_stat])
                fw.op("act", lambda e, rows=rows, c=c: e.activation(
                    out=stat[:rows, c + 1:c + 2], in_=stat[:rows, c:c + 1], func=AF.Sqrt, bias=EPS, scale=1.0 / D),
                    reads=[R_stat], writes=[R_stat])
                fw.op("dve", lambda e, rows=rows, c=c: e.reciprocal(out=stat[:rows, c + 2:c + 3], in_=stat[:rows, c + 1:c + 2]),
                      reads=[R_stat], writes=[R_stat])
                fw.op("dve", lambda e, tt=tt, rows=rows, c=c: e.scalar_tensor_tensor(
                    out=xres[:rows, tt, :], in0=xres[:rows, tt, :], scalar=stat[:rows, c + 2:c + 3], in1=gfinal_bc[:rows, :],
                    op0=ALU.mult, op1=ALU.mult), reads=[R_x[tt], R_stat, R_c], writes=[R_x[tt]])
                fw.dma("sp", D_x[tt], ysink[src0 + tt * 128: src0 + tt * 128 + rows, :], xres[:rows, tt, :], reads=[R_x[tt]])

        def dump_x(ntok, ysink, src0):
            for tt in range((ntok + 127) // 128):
                rows = min(128, ntok - tt * 128)
                fw.dma("sp", D_x[tt], ysink[src0 + tt * 128: src0 + tt * 128 + rows, :], xres[:rows, tt, :], reads=[R_x[tt]])

        def conv_state_out(outp):
            pb = next_ps()
            fw.op("pe", lambda e, pb=pb: e.transpose(out=PS[pb][0:86, 0:128], in_=halo[:].rearrange("p j f -> p (j f)"), identity=identF[:, :]),
                  reads=[R_halo, R_c], writes=[R_ps[pb]])
            f = next_f()
            fw.op("dve", lambda e, pb=pb, f=f: e.tensor_copy(out=ARF[0:86, f, 0:128], in_=PS[pb][0:86, 0:128]), reads=[R_ps[pb]], writes=[R_f[f]])
            fw.dma("sp", D_f[f], outp.rearrange("j (f p) -> (j f) p", p=128), ARF[0:86, f, 0:128], reads=[R_f[f]])

        def conv_state_in(inp_):
            f = next_f()
            fw.dma("sp", D_f[f], ARF[0:86, f, 0:128], inp_.rearrange("j (f p) -> (j f) p", p=128), writes=[R_f[f]])
            pb = next_ps()
            fw.op("pe", lambda e, pb=pb, f=f: e.transpose(out=PS[pb][:, 0:86], in_=ARF[0:86, f, 0:128], identity=identF[0:86, 0:86]),
                  reads=[R_f[f], R_c], writes=[R_ps[pb]])
            fw.op("dve", lambda e, pb=pb: e.tensor_copy(out=halo[:].rearrange("p j f -> p (j f)"), in_=PS[pb][:, 0:86]), reads=[R_ps[pb]], writes=[R_halo])

        def mem_block():
            for tt in range(2):
                fw.dma("sp", D_x[tt], xres[:, tt, :], mem[tt * 128:(tt + 1) * 128, :], writes=[R_x[tt]])
            norm_to_hT(3, MEM)
            for (w, outp) in ((w_mk, pmk), (w_mv, pmv)):
                for cg in range(4):
                    slab = load_slab(w, 16, cg * 512, 512)

                    def cm(tt, rows, pb, cg=cg, outp=outp):
                        f = next_f()
                        fw.op("act", lambda e, f=f, pb=pb, rows=rows: e.activation(out=ARF[:rows, f, :], in_=PS[pb][:rows, :], func=AF.Copy),
                              reads=[R_ps[pb]], writes=[R_f[f]])
                        fw.dma("sp", D_f[f], outp[tt * 128: tt * 128 + rows, cg * 512:(cg + 1) * 512], ARF[:rows, f, :],
                               reads=[R_f[f]])
                        if outp is pmv:
                            u = 36 + (tt * 4 + cg) % 4
                            fw.op("dve", lambda e, u=u, f=f: e.tensor_copy(out=U(u)[:, :], in_=ARF[:, f, :]), reads=[R_f[f]], writes=[R_u[u]])
                            fw.dma("sp", D_unit(u), mvS[0][:, tt * 2048 + cg * 512: tt * 2048 + (cg + 1) * 512], U(u)[:, :], reads=[R_u[u]],
                                   writes=[R_mvS[0]])
                    proj_TM(slab, 16, 512, hT_get, R_hT, MEM, cm)
                    if w is w_mk:
                        def cmkT(ch, m, pb, cg=cg):
                            j = cg * 4 + ch
                            u = 32 + (j % 4)
                            fw.op("dve", lambda e, u=u, pb=pb: e.tensor_copy(out=U(u)[:, 0:MEM], in_=PS[pb][:, 0:MEM]), reads=[R_ps[pb]], writes=[R_u[u]])
                            fw.dma("sp", D_unit(u), mkS[0][:, j * 256:(j + 1) * 256], U(u)[:, 0:MEM], reads=[R_u[u]], writes=[R_mkS[0]])
                        proj_FM(slab, 16, 512, hT_get, R_hT, MEM, cmkT)

        for nm_ in ("w_mk", "w_mv", "w_in", "w_out", "w_mq", "w_mo", "w_gate", "w_up", "w_down"):
            convert_weight(nm_)
        if do_mem:
            mem_block()
        ml_init_zero()
        fw.op("dve", lambda e: e.memset(halo[:], 0.0), writes=[R_halo])
        for b in range(nblk):
            block(0, b * 512, b * 512, 512, x, y, pk, pv)
        if stop >= 5:
            ml_out_state(pc, pn, pm)
        if stop >= 8:
            conv_state_out(pconv)
        if sample:
            if stop >= 8:
                conv_state_in(conv0)
            cache_prologue()
            if stop >= 5:
                ml_init_state()
            block(1, T, 0, TS, xs, ys, sk, sv)
            if stop >= 5:
                ml_out_state(sc, sn, sm)
            if stop >= 8:
                conv_state_out(sconv)
        fw.emit()
    return nc


def _prep_inputs(inp, b):
    f = np.float32
    g = lambda k: np.asarray(inp[k], dtype=f)
    d = {}
    d["x"] = np.ascontiguousarray(g("x_prompt")[b])
    d["xs"] = np.ascontiguousarray(g("x_sample")[b])
    d["ck"] = np.ascontiguousarray(g("cache_da_k")[0, b].reshape(T, 1024))
    d["cv"] = np.ascontiguousarray(g("cache_da_v")[0, b].reshape(T, 1024))
    d["c0"] = np.ascontiguousarray(g("state_ml_c")[0, b])
    d["n0"] = np.ascontiguousarray(g("state_ml_n")[0, b])
    d["m0"] = np.ascontiguousarray(g("state_ml_m")[0, b].reshape(4, 1))
    d["conv0"] = np.ascontiguousarray(g("state_ffn_conv")[0, b])
    d["cmk"] = np.ascontiguousarray(g("cache_mem_k")[0, b].reshape(MEM, D))
    d["cmv"] = np.ascontiguousarray(g("cache_mem_v")[0, b].reshape(MEM, D))
    d["mem"] = np.ascontiguousarray(g("mem_prompt")[b])
    for k in ("w_in", "w_out", "w_mq", "w_mk", "w_mv", "w_mo", "w_gate", "w_up", "w_down"):
        d[k] = np.ascontiguousarray(g(k)[0])
    gs = np.stack([g("g_mix")[0], g("g_xattn")[0], g("g_ffn")[0], g("g_mem")[0]], 0)
    d["gpk"] = np.ascontiguousarray(gs.reshape(4, 16, 128).transpose(2, 0, 1))
    d["lamv"] = np.concatenate([g("lambda_q1")[0], g("lambda_k1")[0], g("lambda_q2")[0], g("lambda_k2")[0]])[None, :].copy()
    d["gda"] = np.ascontiguousarray(g("g_da_sub")[0].reshape(128, 1))
    d["bgate"] = np.ascontiguousarray(np.stack([g("b_ig")[0], g("b_fg")[0]], 1))
    d["gml"] = np.ascontiguousarray(g("g_ml")[0])
    cw = np.concatenate([g("conv_w")[0], g("conv_b")], 0)
    d["convw"] = np.ascontiguousarray(cw.reshape(4, 43, 128).transpose(2, 0, 1))
    d["gfinal"] = np.ascontiguousarray(g("g_final"))
    d["identf"] = np.eye(128, dtype=f)
    kk = np.arange(128)[:, None, None] + 128 * np.arange(4)[None, :, None]
    qq = np.arange(512)[None, None, :]
    d["masks"] = np.ascontiguousarray(((kk // 64) <= (qq // 64)).astype(f))
    es_ = np.zeros((4, 4, 128), f)
    for hh in range(4):
        es_[hh, hh, :] = 1.0
    d["esel"] = es_.reshape(4, 512)
    pp = np.arange(128)[:, None] % 64
    tt_ = np.arange(64)[None, :]
    d["maskml"] = np.where(pp <= tt_, 0.0, -1e30).astype(f)
    return d


_NC_CACHE = {}


def kernel(**inp):
    cfg = ("full",)
    if cfg not in _NC_CACHE:
        _NC_CACHE[cfg] = build()
    nc = _NC_CACHE[cfg]
    in_maps = [_prep_inputs(inp, b) for b in range(8)]
    res = run_bass_kernel_spmd(nc, in_maps, core_ids=list(range(8)))
    r = res.results
    st = lambda k: np.stack([np.asarray(r[b][k], dtype=np.float32) for b in range(8)], 0)
    y_prompt = st("y")
    y_sample = st("ys")
    p_k = st("pk").reshape(1, 8, T, 8, 128)
    p_v = st("pv").reshape(1, 8, T, 8, 128)
    p_c = st("pc")[None]
    p_n = st("pn")[None]
    p_m = st("pm").reshape(1, 8, 4)
    p_conv = st("pconv")[None]
    p_mk = st("pmk").reshape(1, 8, MEM, 4, 512)
    p_mv = st("pmv").reshape(1, 8, MEM, 4, 512)
    s_k = st("sk").reshape(1, 8, TS, 8, 128)
    s_v = st("sv").reshape(1, 8, TS, 8, 128)
    s_c = st("sc")[None]
    s_n = st("sn")[None]
    s_m = st("sm").reshape(1, 8, 4)
    s_conv = st("sconv")[None]
    return (y_prompt, y_sample, p_k, p_v, p_c, p_n, p_m, p_conv, p_mk, p_mv,
            s_k, s_v, s_c, s_n, s_m, s_conv)
```

```python
import contextlib
import numpy as np
import concourse.bass as bass
import concourse.mybir as mybir
from concourse.bass_utils import run_bass_kernel_spmd

F32 = mybir.dt.float32
BF16 = mybir.dt.bfloat16
AF = mybir.ActivationFunctionType
ALU = mybir.AluOpType
AX = mybir.AxisListType
ENGS = ("pe", "act", "dve", "pool", "sp")

D = 2048
T = 4096
TS = 16
DIN = 7176
DFF = 5504
NH = 8
MEM = 256
EPS = 1e-6
LAM_INIT = 0.2
ATTACH_WAIT = True


class Reg:
    __slots__ = ("name", "w", "r")

    def __init__(self, name=""):
        self.name = name
        self.w = None
        self.r = []


class DSem:
    __slots__ = ("sem", "count", "name")

    def __init__(self, name):
        self.name = name
        self.sem = None
        self.count = 0


class Op:
    __slots__ = ("eng", "fn", "cwaits", "dwaits", "sig", "sigidx", "dsem", "dval")

    def __init__(self, eng, fn):
        self.eng = eng
        self.fn = fn
        self.cwaits = []
        self.dwaits = []
        self.sig = False
        self.sigidx = 0
        self.dsem = None
        self.dval = 0


class FW:
    def __init__(self, nc):
        self.nc = nc
        self.ops = {e: [] for e in ENGS}
        self.dsems = []
        self.nops = 0

    def dsem(self, name):
        d = DSem(name)
        self.dsems.append(d)
        return d

    def _deps(self, o, reads, writes):
        deps = []
        seen = set()
        for r in reads:
            if r.w is not None and id(r.w) not in seen:
                seen.add(id(r.w)); deps.append(r.w)
        for w in writes:
            if w.w is not None and id(w.w) not in seen:
                seen.add(id(w.w)); deps.append(w.w)
            for x in w.r:
                if id(x) not in seen:
                    seen.add(id(x)); deps.append(x)
        for d in deps:
            if d is o:
                continue
            if d.dsem is not None:
                o.dwaits.append((d.dsem, d.dsem.count))
            else:
                if o.eng == "pe" and d.eng == "pe":
                    continue
                d.sig = True
                o.cwaits.append(d)
        for r in reads:
            r.r.append(o)
        for w in writes:
            w.w = o
            w.r = []

    def op(self, eng, fn, reads=(), writes=()):
        o = Op(eng, fn)
        self._deps(o, reads, writes)
        self.ops[eng].append(o)
        self.nops += 1
        return o

    def dma(self, eng, dsem, out_ap, in_ap, reads=(), writes=(), slow=False):
        if slow:
            def fn(e):
                return e.dma_start(out=out_ap, in_=in_ap, allow_slow_non_contiguous=True)
        else:
            def fn(e):
                return e.dma_start(out=out_ap, in_=in_ap)
        o = Op(eng, fn)
        self._deps(o, reads, writes)
        dsem.count += 16
        o.dsem = dsem
        o.dval = dsem.count
        self.ops[eng].append(o)
        self.nops += 1
        return o

    def emit(self):
        nc = self.nc
        with contextlib.ExitStack() as es:
            csem = {}
            for e in ENGS:
                csem[e] = es.enter_context(nc.semaphore("c_" + e))
            for d in self.dsems:
                if d.count > 0:
                    d.sem = es.enter_context(nc.semaphore("d_" + d.name))
            for e in ENGS:
                c = 0
                for o in self.ops[e]:
                    if o.sig and o.dsem is None:
                        c += 1
                        o.sigidx = c
            block = es.enter_context(nc.Block())
            final_d = [(d.sem, d.count) for d in self.dsems if d.count > 0]

            def run(e, engobj, last=False):
                seen = {}
                for o in self.ops[e]:
                    need = {}
                    for p in o.cwaits:
                        s = csem[p.eng]
                        v = p.sigidx
                        if seen.get(id(s), 0) < v:
                            need[id(s)] = (s, max(v, need.get(id(s), (s, 0))[1]))
                            seen[id(s)] = v
                    for (d, v) in o.dwaits:
                        if seen.get(id(d), 0) < v:
                            need[id(d)] = (d.sem, max(v, need.get(id(d), (d.sem, 0))[1]))
                            seen[id(d)] = v
                    need = list(need.values())
                    attach = need.pop() if (need and ATTACH_WAIT and o.dsem is None and e != "pe") else None
                    for (s, v) in need:
                        engobj.wait_ge(s, v)
                    ins = o.fn(engobj)
                    if attach is not None:
                        ins._wait_ge(attach[0], attach[1])
                    if o.dsem is not None:
                        ins.then_inc(o.dsem.sem, 16)
                    elif o.sig:
                        ins.then_inc(csem[e], 1)
                if last:
                    for (s, v) in final_d:
                        engobj.wait_ge(s, v)

            @block.tensor
            def _(pe):
                run("pe", pe)

            @block.scalar
            def _(act):
                run("act", act)

            @block.vector
            def _(dve):
                run("dve", dve)

            @block.gpsimd
            def _(pool):
                run("pool", pool)

            @block.sync
            def _(sp):
                run("sp", sp, last=True)


IN_NAMES = ["x", "xs", "ck", "cv", "c0", "n0", "m0", "conv0", "cmk", "cmv", "mem",
            "w_in", "w_out", "w_mq", "w_mk", "w_mv", "w_mo", "w_gate", "w_up", "w_down",
            "gpk", "lamv", "gda", "bgate", "gml", "convw", "gfinal", "identf", "masks"]


def build(nblk=8, sample=True, phases=("inproj",), stop=99, do_mem=True, debug=False):
    nc = bass.Bass("TRN2", target_bir_lowering=False)

    def din(name, shape):
        return nc.dram_tensor(name, shape, F32, kind="ExternalInput").ap()

    def dout(name, shape):
        return nc.dram_tensor(name, shape, F32, kind="ExternalOutput").ap()

    x = din("x", [T, D]); xs = din("xs", [TS, D])
    ck = din("ck", [T, 1024]); cv = din("cv", [T, 1024])
    c0 = din("c0", [4, 256, 256]); n0 = din("n0", [4, 256]); m0 = din("m0", [4, 1])
    conv0 = din("conv0", [2, DFF])
    cmk = din("cmk", [MEM, D]); cmv = din("cmv", [MEM, D]); mem = din("mem", [MEM, D])
    w_in = din("w_in", [D, DIN]); w_out = din("w_out", [D, D]); w_mq = din("w_mq", [D, D])
    w_mk = din("w_mk", [D, D]); w_mv = din("w_mv", [D, D]); w_mo = din("w_mo", [D, D])
    w_gate = din("w_gate", [D, DFF]); w_up = din("w_up", [D, DFF]); w_down = din("w_down", [DFF, D])
    gpk = din("gpk", [128, 4, 16])
    lamv = din("lamv", [1, 256])
    gda = din("gda", [128, 1])
    bgate = din("bgate", [4, 2])
    gml = din("gml", [1024])
    convw = din("convw", [128, 4, 43])
    gfinal = din("gfinal", [D])
    identf = din("identf", [128, 128])
    masks = din("masks", [128, 4, 512])
    esel = din("esel", [4, 512])
    maskml = din("maskml", [128, 64])

    y = dout("y", [T, D]); ys = dout("ys", [TS, D])
    pk = dout("pk", [T, 1024]); pv = dout("pv", [T, 1024])
    pc = dout("pc", [4, 256, 256]); pn = dout("pn", [4, 256]); pm = dout("pm", [4, 1])
    pconv = dout("pconv", [2, DFF])
    pmk = dout("pmk", [MEM, D]); pmv = dout("pmv", [MEM, D])
    sk = dout("sk", [TS, 1024]); sv = dout("sv", [TS, 1024])
    sc = dout("sc", [4, 256, 256]); sn = dout("sn", [4, 256]); sm = dout("sm", [4, 1])
    sconv = dout("sconv", [2, DFF])

    NKT = 33
    ktS = [nc.dram_tensor("ktS%d" % i, [NH, 128, NKT * 128], BF16, kind="Internal").ap() for i in range(2)]
    vS = [nc.dram_tensor("vS%d" % i, [NH, 128, NKT, 128], BF16, kind="Internal").ap() for i in range(2)]
    mkS = nc.dram_tensor("mkS", [2, 128, 16 * 256], BF16, kind="Internal").ap()
    mvS = nc.dram_tensor("mvS", [2, 128, 2 * 2048], BF16, kind="Internal").ap()

    dbg = nc.dram_tensor("dbg", [16, 128, T + TS], F32, kind="ExternalOutput").ap() if debug else None
    WSPEC = {"w_in": (w_in, 16, 15), "w_out": (w_out, 16, 4), "w_mq": (w_mq, 16, 4), "w_mk": (w_mk, 16, 4), "w_mv": (w_mv, 16, 4),
             "w_mo": (w_mo, 16, 4), "w_gate": (w_gate, 16, 11), "w_up": (w_up, 16, 11), "w_down": (w_down, 43, 12)}
    WB = {k: nc.dram_tensor("wb_" + k, [v[2], 128, 8192], BF16, kind="Internal").ap() for k, v in WSPEC.items()}
    fw = FW(nc)
    es = contextlib.ExitStack()
    with es:
        def sb(name, shape, dt):
            return es.enter_context(nc.sbuf_tensor(name, shape, dt))

        xres = sb("xres", [128, 4, D], F32); R_x = [Reg("x%d" % i) for i in range(4)]
        xn = sb("xn", [128, D], BF16); R_xn = Reg("xn")
        hT = sb("hT", [128, 16, 512], BF16); R_hT = [Reg("hT%d" % i) for i in range(16)]
        NSLOT = 2
        wsl = [sb("wsl%d" % i, [128, 8192], BF16) for i in range(NSLOT)]
        R_w = [Reg("w%d" % i) for i in range(NSLOT)]
        D_w = [fw.dsem("w%d" % i) for i in range(NSLOT)]
        NU = 64
        AR = sb("AR", [128, NU * 512], BF16); R_u = [Reg("u%d" % i) for i in range(NU)]
        NF = 8
        ARF = sb("ARF", [128, NF, 512], F32); R_f = [Reg("f%d" % i) for i in range(NF)]
        D_f = [fw.dsem("f%d" % i) for i in range(NF)]
        identF = sb("identF", [128, 128], F32); identB = sb("identB", [128, 128], BF16)
        R_c = Reg("consts")
        gpk_t = sb("gpk_t", [128, 4, 16], F32)
        stat = sb("stat", [128, 64], F32); R_stat = Reg("stat")
        D_x = [fw.dsem("x%d" % i) for i in range(4)]
        _du = {}

        def D_unit(i, q="sp"):
            if (i, q) not in _du:
                _du[(i, q)] = fw.dsem("u%d%s" % (i, q))
            return _du[(i, q)]

        PS = [es.enter_context(nc.psum_tensor("ps%d" % i, [128, 512], F32)) for i in range(6)]
        R_ps = [Reg("ps%d" % i) for i in range(6)]
        PTB = [es.enter_context(nc.psum_tensor("pt%d" % i, [128, 1024], BF16)) for i in range(2)]
        R_pt = [Reg("pt0"), Reg("pt1")]

        def U(i, n=1):
            return AR[:, i * 512:(i + n) * 512]

        D_c = fw.dsem("consts")
        fw.dma("sp", D_c, identF[:], identf[:, :], writes=[R_c])
        D_c2 = fw.dsem("consts2")
        fw.dma("pool", D_c2, identB[:], identf[:, :], writes=[R_c])
        fw.dma("sp", D_c, gpk_t[:], gpk[:, :, :], writes=[R_c])

        maskB = sb("maskB", [128, 4, 512], BF16)
        onesB = sb("onesB", [128, 128], BF16)
        lam_t = sb("lam_t", [128, 256], F32)
        lam_s = sb("lam_s", [128, 8], F32)
        gda_t = sb("gda_t", [128, 2], F32)
        D_c3 = fw.dsem("consts3")
        for j in range(4):
            fw.dma("pool", D_c3, maskB[:, j, :], masks[:, j, :], writes=[R_c])
        fw.dma("sp", D_c, lam_t[:], lamv[0:1, :].partition_broadcast(128) if False else lamv.partition_broadcast(128), writes=[R_c])
        fw.dma("sp", D_c, gda_t[:, 0:1], gda[:, :], writes=[R_c])
        fw.op("dve", lambda e: e.memset(onesB[:], 1.0), writes=[R_c])
        onesF = sb("onesF", [128, 128], F32)
        fw.op("dve", lambda e: e.memset(onesF[:], 1.0), writes=[R_c])
        fw.op("dve", lambda e: e.tensor_tensor(out=lam_t[:, 0:64], in0=lam_t[:, 0:64], in1=lam_t[:, 64:128], op=ALU.mult), reads=[R_c], writes=[R_c])
        fw.op("dve", lambda e: e.tensor_tensor(out=lam_t[:, 128:192], in0=lam_t[:, 128:192], in1=lam_t[:, 192:256], op=ALU.mult), reads=[R_c], writes=[R_c])
        fw.op("dve", lambda e: e.reduce_sum(out=lam_s[:, 0:1], in_=lam_t[:, 0:64], axis=AX.X), reads=[R_c], writes=[R_c])
        fw.op("dve", lambda e: e.reduce_sum(out=lam_s[:, 1:2], in_=lam_t[:, 128:192], axis=AX.X), reads=[R_c], writes=[R_c])
        fw.op("act", lambda e: e.activation(out=lam_s[:, 2:4], in_=lam_s[:, 0:2], func=AF.Exp), reads=[R_c], writes=[R_c])
        fw.op("dve", lambda e: e.tensor_tensor(out=lam_s[:, 4:5], in0=lam_s[:, 3:4], in1=lam_s[:, 2:3], op=ALU.subtract), reads=[R_c], writes=[R_c])
        fw.op("dve", lambda e: e.tensor_scalar(out=lam_s[:, 5:6], in0=lam_s[:, 4:5], scalar1=-LAM_INIT, scalar2=None, op0=ALU.add), reads=[R_c], writes=[R_c])
        fw.op("dve", lambda e: e.tensor_scalar(out=gda_t[:, 1:2], in0=gda_t[:, 0:1], scalar1=1.0 - LAM_INIT, scalar2=None, op0=ALU.mult), reads=[R_c], writes=[R_c])
        neglam = lam_s[:, 5:6]
        gda_s = gda_t[:, 1:2]
        R_ktS = [[Reg("ktS%d_%d" % (i, h)) for h in range(NH)] for i in range(2)]
        R_vS = [[Reg("vS%d_%d" % (i, h)) for h in range(NH)] for i in range(2)]
        D_h = fw.dsem("hist")
        uKTH = 24; uVH = 33; uPT = 42; uSQ = 46; uCAT = 48; uQT_ = 0
        D_dbg = fw.dsem("dbg")
        D_cp = [fw.dsem("cp%d" % i) for i in range(4)]

        esel_t = sb("esel_t", [4, 512], F32)
        maskml_t = sb("maskml_t", [128, 64], F32)
        bg_t = sb("bg_t", [4, 4], F32)
        gml_bc = sb("gml_bc", [128, 1024], F32)
        CT = sb("CT", [128, 8, 257], F32); R_CT = [Reg("CT%d" % j) for j in range(8)]
        CTb = sb("CTb", [128, 8, 257], BF16); R_CTb = [Reg("CTb%d" % j) for j in range(8)]
        carry = sb("carry", [4, 16], F32); R_carry = Reg("carry")
        gcol = sb("gcol", [128, 64], F32); R_gcol = Reg("gcol")
        GL = sb("GL", [128, 32], F32); R_GL = Reg("GL")
        mlt = sb("mlt", [128, 1024], F32)
        R_wT0 = Reg("wT"); R_HN0 = Reg("HN"); R_tmpN0 = Reg("tmpN"); R_dd0 = Reg("dd"); R_ssml = Reg("ssml"); R_dd2 = Reg("dd2")
        fw.dma("sp", D_c, esel_t[:], esel[:, :], writes=[R_c])
        fw.dma("sp", D_c, maskml_t[:], maskml[:, :], writes=[R_c])
        fw.dma("sp", D_c, bg_t[:, 0:2], bgate[:, :], writes=[R_c])
        fw.dma("sp", D_c, gml_bc[:], gml.partition_broadcast(128), writes=[R_c])
        fw.op("dve", lambda e: e.tensor_scalar(out=bg_t[:, 2:3], in0=bg_t[:, 1:2], scalar1=-1.0, scalar2=None, op0=ALU.mult), reads=[R_c], writes=[R_c])
        D_st = fw.dsem("mlstate")
        gfinal_bc = sb("gfinal_bc", [128, D], F32)
        convw_t = sb("convw_t", [128, 4, 43], F32)
        halo = sb("halo", [128, 2, 43], F32); R_halo = Reg("halo")
        fw.dma("sp", D_c, gfinal_bc[:], gfinal.partition_broadcast(128), writes=[R_c])
        fw.dma("sp", D_c, convw_t[:], convw[:, :, :], writes=[R_c])
        R_mkS = [Reg("mkS0"), Reg("mkS1")]; R_mvS = [Reg("mvS0"), Reg("mvS1")]
        D_mem = fw.dsem("memload"); D_memp = fw.dsem("memloadp")

        state = {"slot": 0, "ps": 0, "f": 0}

        def next_slot():
            s = state["slot"]; state["slot"] = (s + 1) % NSLOT
            return s

        def next_ps():
            p = state["ps"]; state["ps"] = (p + 1) % 4
            return p

        def next_f():
            f = state["f"]; state["f"] = (f + 1) % (NF - 4)
            return f

        R_wb = {k: Reg("wb_" + k) for k in WSPEC}
        D_wb = {k: fw.dsem("wb_" + k) for k in WSPEC}
        wname = {id(v[0].tensor): k for k, v in WSPEC.items()}

        def slab_index(name, c0_, r0):
            if name == "w_down":
                return (c0_ // 512) * 3 + (r0 // 2048)
            return c0_ // 512

        cvq = {"q": 0}

        def convert_weight(name):
            w, K, nsl = WSPEC[name]
            ncol_tot = w.shape[1]
            if name == "w_down":
                parts = [(cg * 512, 512, k0 * 128, kc) for cg in range(4) for (k0, kc) in ((0, 16), (16, 16), (32, 11))]
            else:
                parts = [(c, min(512, ncol_tot - c), 0, 16) for c in range(0, ncol_tot, 512)]
            for (c0_, ncols, r0, kc) in parts:
                si = slab_index(name, c0_, r0)
                for k0 in range(0, kc, 4):
                    kn = min(4, kc - k0)
                    q = cvq["q"]; cvq["q"] += 1
                    buf = q % 4
                    src = w[r0 + k0 * 128:r0 + (k0 + kn) * 128, c0_:c0_ + ncols].rearrange("(k p) c -> p k c", p=128)
                    stg = xres[:, buf, 0:kn * ncols]
                    fw.dma("sp", D_x[buf], stg.rearrange("p (k c) -> p k c", k=kn), src, writes=[R_x[buf]])
                    dst = AR[:, buf * 2048: buf * 2048 + kn * ncols]
                    ru = R_u[buf * 4: buf * 4 + 4]
                    if q % 2 == 0:
                        fw.op("dve", lambda e, dst=dst, stg=stg: e.tensor_copy(out=dst, in_=stg), reads=[R_x[buf]], writes=ru)
                    else:
                        fw.op("act", lambda e, dst=dst, stg=stg: e.activation(out=dst, in_=stg, func=AF.Copy), reads=[R_x[buf]], writes=ru)
                    fw.dma("pool", D_unit(buf * 4, "pool"), WB[name][si, :, k0 * ncols:(k0 + kn) * ncols], dst, reads=ru, writes=[R_wb[name]])

        def load_slab(w, kc, c0_, ncols, r0=0):
            name = wname[id(w.tensor)]
            si = slab_index(name, c0_, r0)
            s = next_slot()
            dst = wsl[s][:, 0:kc * ncols].rearrange("p (k c) -> p k c", k=kc)
            fw.dma("pool", D_w[s], wsl[s][:, 0:kc * ncols], WB[name][si, :, 0:kc * ncols], reads=[R_wb[name]], writes=[R_w[s]])
            return s, dst

        def norm_to_hT(which, ntok, src=None):
            nt = (ntok + 127) // 128
            for tt in range(nt):
                rows = min(128, ntok - tt * 128)
                sc_ = 4 * tt
                if stop < -2:
                    continue
                fw.op("dve", lambda e, c=sc_: e.memset(stat[:, c:c + 2], 0.0), writes=[R_stat])
                fw.op("act", lambda e, tt=tt, rows=rows, c=sc_: e.activation(
                    out=AR[:rows, 60 * 512:64 * 512], in_=xres[:rows, tt, :], func=AF.Square, accum_out=stat[:rows, c:c + 1]),
                    reads=[R_x[tt]], writes=R_u[60:64] + [R_stat])
                fw.op("act", lambda e, rows=rows, c=sc_: e.activation(
                    out=stat[:rows, c + 1:c + 2], in_=stat[:rows, c:c + 1], func=AF.Sqrt, bias=EPS, scale=1.0 / D),
                    reads=[R_stat], writes=[R_stat])
                fw.op("dve", lambda e, rows=rows, c=sc_: e.reciprocal(out=stat[:rows, c + 2:c + 3], in_=stat[:rows, c + 1:c + 2]),
                      reads=[R_stat], writes=[R_stat])
                fw.op("dve", lambda e, tt=tt, rows=rows, c=sc_: e.tensor_scalar(
                    out=xn[:rows, :], in0=xres[:rows, tt, :], scalar1=stat[:rows, c + 2:c + 3], scalar2=None, op0=ALU.mult),
                    reads=[R_x[tt], R_stat], writes=[R_xn])
                for g4 in range(4):
                    if stop < -1:
                        continue
                    half = g4 % 2
                    for j in range(4):
                        kc = g4 * 4 + j
                        fw.op("pe", lambda e, kc=kc, j=j, half=half, rows=rows: e.transpose(
                            out=PTB[half][:, j * 128: j * 128 + rows],
                            in_=xn[:rows, kc * 128:(kc + 1) * 128], identity=identB[:rows, :rows]),
                            reads=[R_xn, R_c], writes=[R_pt[half]])
                    for j in range(4):
                        if stop < 0:
                            continue
                        kc = g4 * 4 + j
                        eng = "dve" if half == 0 else "act"
                        o_ap = hT[:, kc, tt * 128: tt * 128 + rows]
                        i_ap = PTB[half][:, j * 128: j * 128 + rows]
                        g_ap = gpk_t[:, which, kc:kc + 1]
                        if eng == "dve":
                            fw.op("dve", lambda e, o_ap=o_ap, i_ap=i_ap, g_ap=g_ap: e.tensor_scalar(
                                out=o_ap, in0=i_ap, scalar1=g_ap, scalar2=None, op0=ALU.mult),
                                reads=[R_pt[half], R_c], writes=[R_hT[kc]])
                        else:
                            fw.op("act", lambda e, o_ap=o_ap, i_ap=i_ap, g_ap=g_ap: e.activation(
                                out=o_ap, in_=i_ap, func=AF.Copy, scale=g_ap),
                                reads=[R_pt[half], R_c], writes=[R_hT[kc]])

        def proj_TM(slab, kc_n, ncols, actT, actR, ntok, consume, k0=0, acc=None, first=True, last=True):
            s, wv = slab
            nt = (ntok + 127) // 128
            for tt in range(nt):
                rows = min(128, ntok - tt * 128)
                pb = acc[tt] if acc is not None else next_ps()
                for k in range(kc_n):
                    fw.op("pe", lambda e, tt=tt, rows=rows, pb=pb, k=k: e.matmul(
                        PS[pb][:rows, 0:ncols], lhsT=actT(k0 + k)[:, tt * 128: tt * 128 + rows], rhs=wv[:, k, :],
                        start=(first and k == 0), stop=(last and k == kc_n - 1)),
                        reads=[actR[k0 + k], R_w[s]], writes=[R_ps[pb]])
                if last:
                    consume(tt, rows, pb)

        def proj_FM(slab, kc_n, ncols, actT, actR, ntok, consume):
            s, wv = slab
            for ch in range((ncols + 127) // 128):
                m = min(128, ncols - ch * 128)
                pb = next_ps()
                for k in range(kc_n):
                    fw.op("pe", lambda e, ch=ch, m=m, pb=pb, k=k: e.matmul(
                        PS[pb][:m, 0:ntok], lhsT=wv[:, k, ch * 128: ch * 128 + m], rhs=actT(k)[:, 0:ntok],
                        start=(k == 0), stop=(k == kc_n - 1)),
                        reads=[actR[k], R_w[s]], writes=[R_ps[pb]])
                consume(ch, m, pb)

        def tm_to_fm(src, src_regs, dst_u0, tt, rows, g):
            for j in range(4):
                fw.op("pe", lambda e, j=j: e.transpose(out=PTB[g][:, j * 128: j * 128 + rows], in_=src[:rows, j * 128:(j + 1) * 128],
                                                       identity=identB[:rows, :rows]), reads=list(src_regs) + [R_c], writes=[R_pt[g]])
            dst = AR[:, dst_u0 * 512:(dst_u0 + 4) * 512].rearrange("p (h t) -> p h t", t=512)[:, :, tt * 128: tt * 128 + rows]
            srcp = PTB[g][:, 0:512].rearrange("p (h t) -> p h t", t=128)[:, :, 0:rows]
            if g == 0:
                fw.op("dve", lambda e: e.tensor_copy(out=dst, in_=srcp), reads=[R_pt[g]], writes=R_u[dst_u0:dst_u0 + 4])
            else:
                fw.op("act", lambda e: e.activation(out=dst, in_=srcp, func=AF.Copy), reads=[R_pt[g]], writes=R_u[dst_u0:dst_u0 + 4])

        hT_get = lambda k: hT[:, k, :]

        def attention(seq, pos0, ntok, diag):
            nkeys = pos0 + ntok
            nkt = (nkeys + 127) // 128
            Ru_kth = R_u[uKTH:uKTH + 9] + R_u[8:17]
            Ru_vh = R_u[uVH:uVH + 9]
            KTHc = (AR[:, uKTH * 512: uKTH * 512 + NKT * 128], AR[:, 8 * 512: 8 * 512 + NKT * 128])
            fw.op("pool", lambda e: e.memset(KTHc[0][64:128, 0:nkeys], 0.0), writes=R_u[uKTH:uKTH + 9])
            fw.op("pool", lambda e: e.memset(KTHc[1][0:64, 0:nkeys], 0.0), writes=R_u[8:17])
            VH = AR[:, uVH * 512: uVH * 512 + NKT * 128].rearrange("p (k e) -> p k e", e=128)
            A = ARF[:, 6, :]; Bf = ARF[:, 7, :]
            for h in range(NH):
                fw.dma("sp", D_h, KTHc[0][0:64, 0:nkeys], ktS[seq][h, 0:64, 0:nkeys], reads=[R_ktS[seq][h]], writes=Ru_kth)
                fw.dma("sp", D_h, KTHc[1][64:128, 0:nkeys], ktS[seq][h, 64:128, 0:nkeys], reads=[R_ktS[seq][h]], writes=Ru_kth)
                nfull = nkeys // 128
                fw.dma("sp", D_h, VH[:, 0:nfull, :], vS[seq][h, :, 0:nfull, :], reads=[R_vS[seq][h]], writes=Ru_vh)
                if nkeys % 128:
                    fw.dma("sp", D_h, VH[0:nkeys % 128, nfull, :], vS[seq][h, 0:nkeys % 128, nfull, :], reads=[R_vS[seq][h]], writes=Ru_vh)
                steps = [(c, kt) for kt in range(nkt) for c in range(2)]
                SB = (0, 1, 4)
                NSB = 3
                ACC = (ARF[:, 4, :], ARF[:, 5, :]); R_acc = (R_f[4], R_f[5])

                def s_step(i):
                    c, kt = steps[i]
                    kw = min(128, nkeys - kt * 128)
                    sbk = SB[i % NSB]
                    pu = uPT + (i % 4)
                    fw.op("pe", lambda e, c=c, kt=kt, kw=kw, sbk=sbk, h=h: e.matmul(
                        PS[sbk][:kw, 0:ntok], lhsT=KTHc[c][:, kt * 128: kt * 128 + kw],
                        rhs=U(uQT_ + h)[:, 0:ntok], start=True, stop=True),
                        reads=Ru_kth + [R_u[uQT_ + h]], writes=[R_ps[sbk]])
                    fw.op("act", lambda e, kw=kw, sbk=sbk, pu=pu: e.activation(
                        out=U(pu)[:kw, 0:ntok], in_=PS[sbk][:kw, 0:ntok], func=AF.Exp, scale=0.125),
                        reads=[R_ps[sbk]], writes=[R_u[pu]])
                    j = kt - (nkt - 4)
                    if diag and j >= 0:
                        fw.op("dve", lambda e, pu=pu, j=j: e.tensor_tensor(
                            out=U(pu)[:, 0:ntok], in0=U(pu)[:, 0:ntok], in1=maskB[:, j, 0:ntok], op=ALU.mult),
                            reads=[R_u[pu], R_c], writes=[R_u[pu]])
                    if c == 1:
                        return
                    if kt == 0:
                        if kw < 128:
                            fw.op("dve", lambda e, c=c: e.memset(ACC[c][:, 0:ntok], 0.0), writes=[R_acc[c]])
                        fw.op("dve", lambda e, c=c, kw=kw, pu=pu: e.tensor_copy(out=ACC[c][:kw, 0:ntok], in_=U(pu)[:kw, 0:ntok]),
                              reads=[R_u[pu]], writes=[R_acc[c]])
                    else:
                        fw.op("dve", lambda e, c=c, kw=kw, pu=pu: e.tensor_tensor(
                            out=ACC[c][:kw, 0:ntok], in0=ACC[c][:kw, 0:ntok], in1=U(pu)[:kw, 0:ntok], op=ALU.add),
                            reads=[R_u[pu], R_acc[c]], writes=[R_acc[c]])

                def av_step(i):
                    c, kt = steps[i]
                    kw = min(128, nkeys - kt * 128)
                    pu = uPT + (i % 4)
                    fw.op("pe", lambda e, c=c, kt=kt, kw=kw, pu=pu: e.matmul(
                        PS[2 + c][:, 0:ntok], lhsT=VH[:kw, kt, :], rhs=U(pu)[:kw, 0:ntok],
                        start=(kt == 0), stop=(kt == nkt - 1)),
                        reads=Ru_vh + [R_u[pu]], writes=[R_ps[2 + c]])
                    if c == 1:
                        fw.op("pe", lambda e, kt=kt, kw=kw, pu=pu: e.matmul(
                            PS[5][:, 0:ntok], lhsT=onesB[:kw, :], rhs=U(pu)[:kw, 0:ntok], start=(kt == 0), stop=(kt == nkt - 1)),
                            reads=[R_c, R_u[pu]], writes=[R_ps[5]])

                LA = 2
                for i in range(len(steps) + LA):
                    if i < len(steps):
                        s_step(i)
                    if i - LA >= 0:
                        av_step(i - LA)
                n = ntok
                fw.op("pe", lambda e: e.matmul(PS[SB[0]][:, 0:n], lhsT=onesF[:, :], rhs=ACC[0][:, 0:n], start=True, stop=True),
                      reads=[R_c, R_acc[0]], writes=[R_ps[SB[0]]])
                fw.op("act", lambda e: e.activation(out=A[:, 0:n], in_=PS[SB[0]][:, 0:n], func=AF.Ln), reads=[R_ps[SB[0]]], writes=[R_f[6]])
                fw.op("act", lambda e: e.activation(out=A[:, 0:n], in_=A[:, 0:n], func=AF.Exp, scale=-1.0), reads=[R_f[6]], writes=[R_f[6]])
                fw.op("act", lambda e: e.activation(out=Bf[:, 0:n], in_=PS[5][:, 0:n], func=AF.Ln), reads=[R_ps[5]], writes=[R_f[7]])
                fw.op("act", lambda e: e.activation(out=Bf[:, 0:n], in_=Bf[:, 0:n], func=AF.Exp, scale=-1.0), reads=[R_f[7]], writes=[R_f[7]])
                fw.op("dve", lambda e: e.tensor_tensor(out=A[:, 0:n], in0=PS[2][:, 0:n], in1=A[:, 0:n], op=ALU.mult),
                      reads=[R_ps[2], R_f[6]], writes=[R_f[6]])
                fw.op("dve", lambda e: e.tensor_tensor(out=Bf[:, 0:n], in0=PS[3][:, 0:n], in1=Bf[:, 0:n], op=ALU.mult),
                      reads=[R_ps[3], R_f[7]], writes=[R_f[7]])
                fw.op("dve", lambda e: e.scalar_tensor_tensor(out=A[:, 0:n], in0=Bf[:, 0:n], scalar=neglam, in1=A[:, 0:n],
                                                              op0=ALU.mult, op1=ALU.add),
                      reads=[R_f[6], R_f[7], R_c], writes=[R_f[6]])
                fw.op("dve", lambda e: e.tensor_tensor(out=U(uSQ)[:, 0:n], in0=A[:, 0:n], in1=A[:, 0:n], op=ALU.mult),
                      reads=[R_f[6]], writes=[R_u[uSQ]])
                fw.op("pe", lambda e: e.matmul(PS[4][:, 0:n], lhsT=onesB[:, :], rhs=U(uSQ)[:, 0:n], start=True, stop=True),
                      reads=[R_c, R_u[uSQ]], writes=[R_ps[4]])
                fw.op("act", lambda e: e.activation(out=Bf[:, 0:n], in_=PS[4][:, 0:n], func=AF.Ln, bias=EPS, scale=1.0 / 128),
                      reads=[R_ps[4]], writes=[R_f[7]])
                fw.op("act", lambda e: e.activation(out=Bf[:, 0:n], in_=Bf[:, 0:n], func=AF.Exp, scale=-0.5), reads=[R_f[7]], writes=[R_f[7]])
                fw.op("dve", lambda e, h=h: e.scalar_tensor_tensor(out=U(uCAT + h)[:, 0:n], in0=A[:, 0:n], scalar=gda_s, in1=Bf[:, 0:n],
                                                                   op0=ALU.mult, op1=ALU.mult),
                      reads=[R_f[6], R_f[7], R_c], writes=[R_u[uCAT + h]])
                if dbg is not None:
                    fw.dma("pool", D_dbg, dbg[h, :, pos0:pos0 + n], U(uCAT + h)[:, 0:n], reads=[R_u[uCAT + h]])

        def cache_prologue():
            for kt in range(T // 128):
                uk = 0 + (kt % 2) * 2
                uv = 4 + (kt % 2) * 2
                ut = 8 + (kt % 2) * 2
                fw.dma("pool", D_unit(uk, "pool"), AR[:, uk * 512:(uk + 2) * 512], ck[kt * 128:(kt + 1) * 128, :], writes=R_u[uk:uk + 2])
                fw.dma("pool", D_unit(uv, "pool"), AR[:, uv * 512:(uv + 2) * 512], cv[kt * 128:(kt + 1) * 128, :], writes=R_u[uv:uv + 2])
                for hh in range(NH):
                    fw.dma("sp", D_unit(uv), vS[1][hh, :, kt, :], AR[:, uv * 512 + hh * 128: uv * 512 + (hh + 1) * 128],
                           reads=R_u[uv:uv + 2], writes=[R_vS[1][hh]])
                for g in range(2):
                    for j in range(4):
                        hh = g * 4 + j
                        fw.op("pe", lambda e, g=g, j=j, hh=hh, uk=uk: e.transpose(
                            out=PTB[g][:, j * 128:(j + 1) * 128], in_=AR[:, uk * 512 + hh * 128: uk * 512 + (hh + 1) * 128],
                            identity=identB[:, :]), reads=R_u[uk:uk + 2] + [R_c], writes=[R_pt[g]])
                    eng = "dve" if g == 0 else "act"
                    o_ap = AR[:, (ut + g) * 512:(ut + g + 1) * 512]
                    if g == 0:
                        fw.op("dve", lambda e, o_ap=o_ap, g=g: e.tensor_copy(out=o_ap, in_=PTB[g][:, 0:512]),
                              reads=[R_pt[g]], writes=[R_u[ut + g]])
                    else:
                        fw.op("act", lambda e, o_ap=o_ap, g=g: e.activation(out=o_ap, in_=PTB[g][:, 0:512], func=AF.Copy),
                              reads=[R_pt[g]], writes=[R_u[ut + g]])
                    for j in range(4):
                        hh = g * 4 + j
                        fw.dma("sp", D_unit(ut + g), ktS[1][hh, :, kt * 128:(kt + 1) * 128], AR[:, (ut + g) * 512 + j * 128:(ut + g) * 512 + (j + 1) * 128],
                               reads=[R_u[ut + g]], writes=[R_ktS[1][hh]])

        uMQ = 0; uMK = 8; uMKT = 16; uMV = 24; uMO = 33; uST = 41; uVW = 42
        mvA = AR[:, uMV * 512: uMV * 512 + 16 * 257].rearrange("p (a e) -> p a e", e=257)
        Ru_mv = R_u[uMV:uMV + 9]

        def ml_init_zero():
            fw.op("dve", lambda e: e.memset(CT[:], 0.0), writes=R_CT)
            fw.op("dve", lambda e: e.memset(CTb[:], 0.0), writes=R_CTb)
            fw.op("dve", lambda e: e.memset(carry[:], 0.0), writes=[R_carry])

        def ml_init_state():
            for hh in range(4):
                for ec in range(2):
                    f = next_f()
                    fw.dma("sp", D_f[f], ARF[:, f, 0:256], c0[hh, ec * 128:(ec + 1) * 128, :], writes=[R_f[f]])
                    for dc in range(2):
                        pb = next_ps()
                        fw.op("pe", lambda e, f=f, dc=dc, pb=pb: e.transpose(out=PS[pb][:, 0:128], in_=ARF[:, f, dc * 128:(dc + 1) * 128],
                                                                        identity=identF[:, :]), reads=[R_f[f], R_c], writes=[R_ps[pb]])
                        j = hh * 2 + dc
                        fw.op("dve", lambda e, j=j, ec=ec, pb=pb: e.tensor_copy(out=CT[:, j, ec * 128:(ec + 1) * 128], in_=PS[pb][:, 0:128]),
                              reads=[R_ps[pb]], writes=[R_CT[j]])
            fw.dma("sp", D_st, CT[:, :, 256], n0.rearrange("h (dc p) -> p (h dc)", p=128), writes=R_CT, slow=True)
            fw.dma("sp", D_st, carry[:, 0:1], m0[:, :], writes=[R_carry])
            fw.op("act", lambda e: e.activation(out=CTb[:], in_=CT[:], func=AF.Copy), reads=R_CT, writes=R_CTb)

        def ml_out_state(oc, on, om):
            for hh in range(4):
                for ec in range(2):
                    pb = next_ps()
                    for dc in range(2):
                        j = hh * 2 + dc
                        fw.op("pe", lambda e, j=j, ec=ec, dc=dc, pb=pb: e.transpose(
                            out=PS[pb][:, dc * 128:(dc + 1) * 128], in_=CT[:, j, ec * 128:(ec + 1) * 128], identity=identF[:, :]),
                            reads=[R_CT[j], R_c], writes=[R_ps[pb]])
                    f = next_f()
                    fw.op("dve", lambda e, f=f, pb=pb: e.tensor_copy(out=ARF[:, f, 0:256], in_=PS[pb][:, 0:256]), reads=[R_ps[pb]], writes=[R_f[f]])
                    fw.dma("sp", D_f[f], oc[hh, ec * 128:(ec + 1) * 128, :], ARF[:, f, 0:256], reads=[R_f[f]])
            fw.dma("sp", D_st, on.rearrange("h (dc p) -> p (h dc)", p=128), CT[:, :, 256], reads=R_CT, slow=True)
            fw.dma("sp", D_st, om[:, :], carry[:, 0:1], reads=[R_carry])

        def mlstm(seq, tok0, ntok, L):
            nt = (ntok + 127) // 128
            nch = ntok // L
            for half in range(2):
                slab = load_slab(w_in, 16, 3072 + half * 512, 512)

                def c_mq(ch, m, pb, half=half):
                    u = uMQ + half * 4 + ch
                    fw.op("act", lambda e, u=u, pb=pb: e.activation(out=U(u)[:, 0:ntok], in_=PS[pb][:, 0:ntok], func=AF.Copy),
                          reads=[R_ps[pb]], writes=[R_u[u]])
                proj_FM(slab, 16, 512, hT_get, R_hT, ntok, c_mq)
            for half in range(2):
                slab = load_slab(w_in, 16, 4096 + half * 512, 512)

                def c_mk(tt, rows, pb, half=half):
                    u = uMKT + tt * 2 + half
                    fw.op("act", lambda e, u=u, pb=pb, rows=rows: e.activation(out=U(u)[:rows, :], in_=PS[pb][:rows, :], func=AF.Copy, scale=1.0 / 16),
                          reads=[R_ps[pb]], writes=[R_u[u]])
                    tm_to_fm(U(u), [R_u[u]], uMK + half * 4, tt, rows, tt % 2)
                proj_TM(slab, 16, 512, hT_get, R_hT, ntok, c_mk)
            fw.op("dve", lambda e: e.memset(mvA[:, :, 256:257], 1.0), writes=Ru_mv)
            for half in range(2):
                slab = load_slab(w_in, 16, 5120 + half * 512, 512)

                def c_mv(tt, rows, pb, half=half):
                    fw.op("act", lambda e, tt=tt, pb=pb, rows=rows, half=half: e.activation(
                        out=mvA[:rows, tt * 4 + half * 2: tt * 4 + half * 2 + 2, 0:256],
                        in_=PS[pb][:rows, :].rearrange("p (a e) -> p a e", e=256), func=AF.Copy),
                        reads=[R_ps[pb]], writes=Ru_mv)
                proj_TM(slab, 16, 512, hT_get, R_hT, ntok, c_mv)
            for half in range(2):
                slab = load_slab(w_in, 16, 6144 + half * 512, 512)

                def c_mo(tt, rows, pb, half=half):
                    u = uMO + tt * 2 + half
                    fw.op("act", lambda e, u=u, pb=pb, rows=rows: e.activation(out=U(u)[:rows, :], in_=PS[pb][:rows, :], func=AF.Sigmoid),
                          reads=[R_ps[pb]], writes=[R_u[u]])
                proj_TM(slab, 16, 512, hT_get, R_hT, ntok, c_mo)
            slab = load_slab(w_in, 16, 7168, 8)
            s_, wv = slab
            n = ntok
            row = lambda f: ARF[0:4, f, 0:n]
            for gi in range(2):
                pb = next_ps()
                for k in range(16):
                    fw.op("pe", lambda e, gi=gi, pb=pb, k=k: e.matmul(PS[pb][0:4, 0:n], lhsT=wv[:, k, gi * 4:(gi + 1) * 4], rhs=hT[:, k, 0:n],
                                                                     start=(k == 0), stop=(k == 15)),
                          reads=[R_hT[k], R_w[s_]], writes=[R_ps[pb]])
                if gi == 0:
                    fw.op("dve", lambda e, pb=pb: e.tensor_scalar(out=row(0), in0=PS[pb][0:4, 0:n], scalar1=bg_t[:, 0:1], scalar2=None, op0=ALU.add),
                          reads=[R_ps[pb], R_c], writes=[R_f[0]])
                else:
                    fw.op("act", lambda e, pb=pb: e.activation(out=row(1), in_=PS[pb][0:4, 0:n], func=AF.Exp, scale=-1.0, bias=bg_t[:, 2:3]),
                          reads=[R_ps[pb], R_c], writes=[R_f[1]])
            fw.op("act", lambda e: e.activation(out=row(1), in_=row(1), func=AF.Ln, bias=1.0), reads=[R_f[1]], writes=[R_f[1]])
            fw.op("dve", lambda e: e.tensor_scalar(out=row(1), in0=row(1), scalar1=-1.0, scalar2=None, op0=ALU.mult), reads=[R_f[1]], writes=[R_f[1]])
            fw.op("dve", lambda e: e.memset(row(7), 0.0), writes=[R_f[7]])
            fw.op("dve", lambda e: e.tensor_tensor_scan(out=row(2), data0=row(1), data1=row(0), initial=carry[:, 0:1], op0=ALU.add, op1=ALU.max),
                  reads=[R_f[1], R_f[0], R_carry], writes=[R_f[2]])
            fw.op("dve", lambda e: e.tensor_tensor_scan(out=row(3), data0=row(1), data1=row(7), initial=0.0, op0=ALU.add, op1=ALU.add),
                  reads=[R_f[1], R_f[7]], writes=[R_f[3]])
            fw.op("dve", lambda e: e.tensor_tensor(out=row(4), in0=row(3), in1=row(2), op=ALU.subtract), reads=[R_f[3], R_f[2]], writes=[R_f[4]])
            fw.op("dve", lambda e: e.tensor_tensor(out=row(0), in0=row(0), in1=row(3), op=ALU.subtract), reads=[R_f[0], R_f[3]], writes=[R_f[0]])
            fw.op("act", lambda e: e.activation(out=row(6), in_=row(2), func=AF.Exp, scale=-1.0), reads=[R_f[2]], writes=[R_f[6]])
            fw.op("dve", lambda e: e.tensor_copy(out=carry[:, 4:5], in_=carry[:, 0:1]), reads=[R_carry], writes=[R_carry])
            for c in range(1, nch):
                fw.op("dve", lambda e, c=c: e.tensor_scalar(out=carry[:, 4 + c:5 + c], in0=ARF[0:4, 4, c * L - 1:c * L], scalar1=-1.0, scalar2=None, op0=ALU.mult),
                      reads=[R_f[4], R_carry], writes=[R_carry])
            for c in range(nch):
                cs = slice(c * L, (c + 1) * L)
                fw.op("act", lambda e, c=c, cs=cs: e.activation(out=ARF[0:4, 5, cs], in_=ARF[0:4, 4, cs], func=AF.Exp, bias=carry[:, 4 + c:5 + c]),
                      reads=[R_f[4], R_carry], writes=[R_f[5]])
                fw.op("act", lambda e, c=c, cs=cs: e.activation(out=ARF[0:4, 7, cs], in_=ARF[0:4, 0, cs], func=AF.Exp, bias=ARF[0:4, 4, (c + 1) * L - 1:(c + 1) * L]),
                      reads=[R_f[0], R_f[4]], writes=[R_f[7]])
            fw.op("dve", lambda e: e.tensor_copy(out=carry[:, 0:1], in_=ARF[0:4, 2, n - 1:n]), reads=[R_f[2]], writes=[R_carry])
            pbT = next_ps()
            for tt in range(nt):
                rows = min(128, ntok - tt * 128)
                for qi, f in enumerate((0, 7, 5, 6)):
                    o = (tt * 4 + qi) * 4
                    fw.op("pe", lambda e, tt=tt, rows=rows, f=f, o=o: e.transpose(
                        out=PS[pbT][:rows, o:o + 4], in_=ARF[0:4, f, tt * 128: tt * 128 + rows], identity=identF[0:4, 0:4]),
                        reads=[R_f[f], R_c], writes=[R_ps[pbT]])
            rows_all = 128 if ntok >= 128 else ntok
            fw.op("dve", lambda e: e.tensor_copy(out=gcol[:rows_all, 0:nt * 16], in_=PS[pbT][:rows_all, 0:nt * 16]), reads=[R_ps[pbT]], writes=[R_gcol])
            pbG = next_ps()
            for hh in range(4):
                for c in range(nch):
                    e_c = (c + 1) * L - 1
                    fw.op("pe", lambda e, hh=hh, c=c, e_c=e_c: e.matmul(PS[pbG][:, hh * 8 + c: hh * 8 + c + 1], lhsT=esel_t[0:4, hh * 128:(hh + 1) * 128],
                                                                        rhs=ARF[0:4, 5, e_c:e_c + 1], start=True, stop=True),
                          reads=[R_f[5], R_c], writes=[R_ps[pbG]])
            fw.op("dve", lambda e: e.tensor_copy(out=GL[:, :], in_=PS[pbG][:, 0:32]), reads=[R_ps[pbG]], writes=[R_GL])
            ssml = mlt[:, 704:712]
            TMP = [dict(wT=mlt[:, 0:64], HN=mlt[:, 64:321], tmpN=mlt[:, 384:641], ddv=mlt[:, 700:704], uST=41, uVW=42,
                        R_wT=R_wT0, R_HN=R_HN0, R_tmpN=R_tmpN0, R_dd=R_dd0),
                   dict(wT=ARF[:, 5, 0:64], HN=ARF[:, 6, 0:257], tmpN=ARF[:, 7, 0:257], ddv=ARF[:, 5, 64:68], uST=43, uVW=44,
                        R_wT=R_f[5], R_HN=R_f[6], R_tmpN=R_f[7], R_dd=R_dd2)]
            for c in range(nch):
                tt = (c * L) // 128
                po = (c * L) % 128
                P = slice(po, po + L)
                cs = slice(c * L, (c + 1) * L)
                hmf = (tt % 2) * 2
                HM = ARF[:, hmf:hmf + 2, :].rearrange("p a b -> p (a b)")
                R_hm = [R_f[hmf], R_f[hmf + 1]]
                for hh in range(4):
                    gc = lambda qi, tt=tt, hh=hh: gcol[P, (tt * 4 + qi) * 4 + hh:(tt * 4 + qi) * 4 + hh + 1]
                    T_ = TMP[hh % 2]
                    wT = T_["wT"]; HN = T_["HN"]; tmpN = T_["tmpN"]; ddv = T_["ddv"]; uSTl = T_["uST"]; uVWl = T_["uVW"]
                    R_wT = T_["R_wT"]; R_HN = T_["R_HN"]; R_tmpN = T_["R_tmpN"]; R_dd = T_["R_dd"]
                    for dc in range(2):
                        j = hh * 2 + dc
                        fw.op("pe", lambda e, wT=wT, HN=HN, tmpN=tmpN, ddv=ddv, uST=uSTl, uVW=uVWl, j=j, dc=dc, P=P, cs=cs, po=po: e.matmul(
                            PS[0][P, 0:L], lhsT=U(uMK + j)[:, cs], rhs=U(uMQ + j)[:, cs], start=(dc == 0), stop=(dc == 1),
                            tile_position=(0, po)),
                            reads=[R_u[uMK + j], R_u[uMQ + j]], writes=[R_ps[0]])
                    fw.op("pe", lambda e, wT=wT, HN=HN, tmpN=tmpN, ddv=ddv, uST=uSTl, uVW=uVWl, hh=hh, P=P, cs=cs, po=po: e.matmul(
                        PS[1][P, 0:L], lhsT=esel_t[0:4, hh * 128: hh * 128 + L], rhs=ARF[0:4, 4, cs], start=True, stop=False,
                        tile_position=(0, po)),
                        reads=[R_f[4], R_c], writes=[R_ps[1]])
                    fw.op("pe", lambda e, wT=wT, HN=HN, tmpN=tmpN, ddv=ddv, uST=uSTl, uVW=uVWl, P=P, po=po: e.matmul(
                        PS[1][P, 0:L], lhsT=identF[P, P], rhs=maskml_t[P, 0:L], start=False, stop=True, tile_position=(po, po)),
                        reads=[R_c], writes=[R_ps[1]])
                    fw.op("act", lambda e, wT=wT, HN=HN, tmpN=tmpN, ddv=ddv, uST=uSTl, uVW=uVWl, P=P, g0=gc(0): e.activation(out=wT[P, 0:L], in_=PS[1][P, 0:L], func=AF.Exp, bias=g0),
                          reads=[R_ps[1], R_gcol], writes=[R_wT])
                    fw.op("dve", lambda e, wT=wT, HN=HN, tmpN=tmpN, ddv=ddv, uST=uSTl, uVW=uVWl, P=P: e.tensor_tensor(out=U(uST)[P, 0:L], in0=PS[0][P, 0:L], in1=wT[P, 0:L], op=ALU.mult),
                          reads=[R_ps[0], R_wT], writes=[R_u[uSTl]])
                    fw.op("pe", lambda e, wT=wT, HN=HN, tmpN=tmpN, ddv=ddv, uST=uSTl, uVW=uVWl, P=P, tt=tt, hh=hh, po=po: e.matmul(
                        PS[2][P, 0:257], lhsT=U(uST)[P, 0:L], rhs=mvA[P, tt * 4 + hh, :], start=True, stop=True, tile_position=(po, po)),
                        reads=[R_u[uSTl]] + Ru_mv, writes=[R_ps[2]])
                    for dc in range(2):
                        j = hh * 2 + dc
                        fw.op("pe", lambda e, wT=wT, HN=HN, tmpN=tmpN, ddv=ddv, uST=uSTl, uVW=uVWl, j=j, dc=dc, P=P, cs=cs, po=po: e.matmul(
                            PS[5][P, 0:257], lhsT=U(uMQ + j)[:, cs], rhs=CTb[:, j, :], start=(dc == 0), stop=(dc == 1),
                            tile_position=(0, po)),
                            reads=[R_u[uMQ + j], R_CTb[j]], writes=[R_ps[5]])
                    fw.op("act", lambda e, wT=wT, HN=HN, tmpN=tmpN, ddv=ddv, uST=uSTl, uVW=uVWl, P=P, g2=gc(2): e.activation(out=tmpN[P, :], in_=PS[5][P, 0:257], func=AF.Copy, scale=g2),
                          reads=[R_ps[5], R_gcol], writes=[R_tmpN])
                    fw.op("dve", lambda e, wT=wT, HN=HN, tmpN=tmpN, ddv=ddv, uST=uSTl, uVW=uVWl, P=P: e.tensor_tensor(out=HN[P, :], in0=tmpN[P, :], in1=PS[2][P, 0:257], op=ALU.add),
                          reads=[R_tmpN, R_ps[2]], writes=[R_HN])
                    fw.op("dve", lambda e, wT=wT, HN=HN, tmpN=tmpN, ddv=ddv, uST=uSTl, uVW=uVWl, P=P: e.tensor_scalar(out=ddv[P, 2:3], in0=HN[P, 256:257], scalar1=-1.0, scalar2=None, op0=ALU.mult),
                          reads=[R_HN], writes=[R_dd])
                    fw.op("dve", lambda e, wT=wT, HN=HN, tmpN=tmpN, ddv=ddv, uST=uSTl, uVW=uVWl, P=P: e.tensor_tensor(out=ddv[P, 2:3], in0=ddv[P, 2:3], in1=HN[P, 256:257], op=ALU.max),
                          reads=[R_HN, R_dd], writes=[R_dd])
                    fw.op("dve", lambda e, wT=wT, HN=HN, tmpN=tmpN, ddv=ddv, uST=uSTl, uVW=uVWl, P=P, g3=gc(3): e.tensor_scalar(out=ddv[P, 0:1], in0=ddv[P, 2:3], scalar1=g3, scalar2=None, op0=ALU.max),
                          reads=[R_dd, R_gcol], writes=[R_dd])
                    fw.op("dve", lambda e, wT=wT, HN=HN, tmpN=tmpN, ddv=ddv, uST=uSTl, uVW=uVWl, P=P: e.reciprocal(out=ddv[P, 1:2], in_=ddv[P, 0:1]), reads=[R_dd], writes=[R_dd])
                    fw.op("dve", lambda e, wT=wT, HN=HN, tmpN=tmpN, ddv=ddv, uST=uSTl, uVW=uVWl, P=P, hh=hh, HM=HM: e.tensor_scalar(out=HM[P, hh * 256:(hh + 1) * 256], in0=HN[P, 0:256], scalar1=ddv[P, 1:2],
                                                                             scalar2=None, op0=ALU.mult),
                          reads=[R_HN, R_dd], writes=R_hm)
                    fw.op("dve", lambda e, wT=wT, HN=HN, tmpN=tmpN, ddv=ddv, uST=uSTl, uVW=uVWl, P=P, tt=tt, hh=hh, g1=gc(1): e.tensor_scalar(out=U(uVW)[P, 0:257], in0=mvA[P, tt * 4 + hh, :], scalar1=g1,
                                                                                      scalar2=None, op0=ALU.mult),
                          reads=Ru_mv + [R_gcol], writes=[R_u[uVWl]])
                    for dc in range(2):
                        j = hh * 2 + dc
                        um = uMKT + tt * 2 + (hh // 2)
                        co = (hh % 2) * 256 + dc * 128
                        fw.op("pe", lambda e, wT=wT, HN=HN, tmpN=tmpN, ddv=ddv, uST=uSTl, uVW=uVWl, dc=dc, um=um, co=co, P=P, po=po: e.matmul(
                            PS[3 + dc][:, 0:257], lhsT=U(um)[P, co:co + 128], rhs=U(uVW)[P, 0:257], start=True, stop=True,
                            tile_position=(po, 0)),
                            reads=[R_u[um], R_u[uVWl]], writes=[R_ps[3 + dc]])
                        fw.op("dve", lambda e, wT=wT, HN=HN, tmpN=tmpN, ddv=ddv, uST=uSTl, uVW=uVWl, j=j, dc=dc, hh=hh, c=c: e.scalar_tensor_tensor(
                            out=CT[:, j, :], in0=CT[:, j, :], scalar=GL[:, hh * 8 + c: hh * 8 + c + 1], in1=PS[3 + dc][:, 0:257],
                            op0=ALU.mult, op1=ALU.add),
                            reads=[R_CT[j], R_GL, R_ps[3 + dc]], writes=[R_CT[j]])
                        fw.op("act", lambda e, wT=wT, HN=HN, tmpN=tmpN, ddv=ddv, uST=uSTl, uVW=uVWl, j=j: e.activation(out=CTb[:, j, :], in_=CT[:, j, :], func=AF.Copy),
                              reads=[R_CT[j]], writes=[R_CTb[j]])
                if (c + 1) * L % 128 == 0 or c == nch - 1:
                    rows = min(128, ntok - tt * 128)
                    for hh in range(4):
                        fw.op("act", lambda e, hh=hh, rows=rows, HM=HM: e.activation(out=mlt[:rows, 768:1024], in_=HM[:rows, hh * 256:(hh + 1) * 256],
                                                                                    func=AF.Square, accum_out=ssml[:rows, hh:hh + 1]),
                              reads=R_hm, writes=[R_ssml])
                    fw.op("act", lambda e, rows=rows: e.activation(out=ssml[:rows, 4:8], in_=ssml[:rows, 0:4], func=AF.Sqrt, bias=EPS, scale=1.0 / 256),
                          reads=[R_ssml], writes=[R_ssml])
                    fw.op("dve", lambda e, rows=rows: e.reciprocal(out=ssml[:rows, 4:8], in_=ssml[:rows, 4:8]), reads=[R_ssml], writes=[R_ssml])
                    for hh in range(4):
                        fw.op("dve", lambda e, hh=hh, rows=rows, HM=HM: e.scalar_tensor_tensor(
                            out=HM[:rows, hh * 256:(hh + 1) * 256], in0=HM[:rows, hh * 256:(hh + 1) * 256], scalar=ssml[:rows, 4 + hh:5 + hh],
                            in1=gml_bc[:rows, hh * 256:(hh + 1) * 256], op0=ALU.mult, op1=ALU.mult),
                            reads=R_hm + [R_ssml, R_c], writes=R_hm)
                    for half in range(2):
                        u = uMO + tt * 2 + half
                        fw.op("dve", lambda e, u=u, half=half, rows=rows, HM=HM: e.tensor_tensor(
                            out=U(u)[:rows, :], in0=HM[:rows, half * 512:(half + 1) * 512], in1=U(u)[:rows, :], op=ALU.mult),
                            reads=R_hm + [R_u[u]], writes=[R_u[u]])
                    for g in range(2):
                        u = uMO + tt * 2 + g
                        for j in range(4):
                            fw.op("pe", lambda e, g=g, j=j, u=u, rows=rows: e.transpose(
                                out=PTB[g][:, j * 128: j * 128 + rows], in_=U(u)[:rows, j * 128:(j + 1) * 128], identity=identB[:rows, :rows]),
                                reads=[R_u[u], R_c], writes=[R_pt[g]])
                        for j in range(4):
                            k = 8 + g * 4 + j
                            o_ap = U(uCAT + k)[:, tt * 128: tt * 128 + rows]
                            i_ap = PTB[g][:, j * 128: j * 128 + rows]
                            if g == 0:
                                fw.op("dve", lambda e, o_ap=o_ap, i_ap=i_ap: e.tensor_copy(out=o_ap, in_=i_ap), reads=[R_pt[g]], writes=[R_u[uCAT + k]])
                            else:
                                fw.op("act", lambda e, o_ap=o_ap, i_ap=i_ap: e.activation(out=o_ap, in_=i_ap, func=AF.Copy), reads=[R_pt[g]], writes=[R_u[uCAT + k]])
            if dbg is not None:
                for k in range(8, 16):
                    fw.dma("pool", D_dbg, dbg[k, :, tok0:tok0 + ntok], U(uCAT + k)[:, 0:ntok], reads=[R_u[uCAT + k]])

        def block(seq, tok0, src0, ntok, xsrc, ysink, kout, vout):
            nt = (ntok + 127) // 128
            for tt in range(nt):
                rows = min(128, ntok - tt * 128)
                fw.dma("sp", D_x[tt], xres[:rows, tt, :], xsrc[src0 + tt * 128: src0 + tt * 128 + rows, :], writes=[R_x[tt]])
            norm_to_hT(0, ntok)
            if stop < 1:
                return
            uQT = 0; uKT = 8; uV = 16
            for half in range(2):
                slab = load_slab(w_in, 16, half * 512, 512)

                def cq(ch, m, pb, half=half):
                    h = half * 4 + ch
                    fw.op("act", lambda e, h=h, pb=pb: e.activation(out=U(uQT + h)[:, 0:ntok], in_=PS[pb][:, 0:ntok], func=AF.Copy),
                          reads=[R_ps[pb]], writes=[R_u[uQT + h]])
                proj_FM(slab, 16, 512, hT_get, R_hT, ntok, cq)
            if stop < 2:
                return
            for half in range(2):
                slab = load_slab(w_in, 16, 1024 + half * 512, 512)

                def ck_tm(tt, rows, pb, half=half):
                    f = next_f()
                    fw.op("act", lambda e, f=f, pb=pb, rows=rows: e.activation(out=ARF[:rows, f, :], in_=PS[pb][:rows, :], func=AF.Copy),
                          reads=[R_ps[pb]], writes=[R_f[f]])
                    fw.dma("sp", D_f[f], kout[src0 + tt * 128: src0 + tt * 128 + rows, half * 512:(half + 1) * 512], ARF[:rows, f, :],
                           reads=[R_f[f]])
                    ub = 24 + tt * 2 + half
                    fw.op("dve", lambda e, ub=ub, f=f, rows=rows: e.tensor_copy(out=U(ub)[:rows, :], in_=ARF[:rows, f, :]),
                          reads=[R_f[f]], writes=[R_u[ub]])
                    tm_to_fm(U(ub), [R_u[ub]], uKT + half * 4, tt, rows, tt % 2)
                proj_TM(slab, 16, 512, hT_get, R_hT, ntok, ck_tm)

                for ch in range(4):
                    h = half * 4 + ch
                    fw.dma("sp", D_unit(uKT + h), ktS[seq][h, :, tok0:tok0 + ntok], U(uKT + h)[:, 0:ntok], reads=[R_u[uKT + h]], writes=[R_ktS[seq][h]])
            if stop < 3:
                return
            for half in range(2):
                slab = load_slab(w_in, 16, 2048 + half * 512, 512)

                def cv_tm(tt, rows, pb, half=half):
                    f = next_f()
                    fw.op("act", lambda e, f=f, pb=pb, rows=rows: e.activation(out=ARF[:rows, f, :], in_=PS[pb][:rows, :], func=AF.Copy),
                          reads=[R_ps[pb]], writes=[R_f[f]])
                    fw.dma("sp", D_f[f], vout[src0 + tt * 128: src0 + tt * 128 + rows, half * 512:(half + 1) * 512], ARF[:rows, f, :],
                           reads=[R_f[f]])
                    u = uV + tt * 2 + half
                    fw.op("dve", lambda e, u=u, f=f, rows=rows: e.tensor_copy(out=U(u)[:rows, :], in_=ARF[:rows, f, :]),
                          reads=[R_f[f]], writes=[R_u[u]])
                    kt = (tok0 + tt * 128) // 128
                    for hh in range(4):
                        fw.dma("sp", D_unit(u), vS[seq][half * 4 + hh, 0:rows, kt, :], U(u)[:rows, hh * 128:(hh + 1) * 128], reads=[R_u[u]],
                               writes=[R_vS[seq][half * 4 + hh]])
                proj_TM(slab, 16, 512, hT_get, R_hT, ntok, cv_tm)
            if stop < 4:
                return
            attention(seq, tok0, ntok, diag=(seq == 0))
            if stop < 5:
                return
            mlstm(seq, tok0, ntok, 64 if seq == 0 else TS)
            if stop < 6:
                return
            proj_residual(w_out, lambda k: U(uCAT + k), R_u[uCAT:uCAT + 16], ntok, [(0, 16)])
            if stop < 7:
                dump_x(ntok, ysink, src0)
                return
            xattn(seq, ntok)
            if stop < 8:
                dump_x(ntok, ysink, src0)
                return
            ffn(ntok)
            final_norm(ntok, ysink, src0)

        def proj_residual(w, actT, actR, ntok, kparts):
            for cg in range(4):
                for pi, (k0, kc_n) in enumerate(kparts):
                    slab = load_slab(w, kc_n, cg * 512, 512, r0=k0 * 128)

                    def cons(tt, rows, pb, cg=cg):
                        fw.op("dve", lambda e, tt=tt, rows=rows, pb=pb, cg=cg: e.tensor_tensor(
                            out=xres[:rows, tt, cg * 512:(cg + 1) * 512], in0=xres[:rows, tt, cg * 512:(cg + 1) * 512],
                            in1=PS[pb][:rows, :], op=ALU.add), reads=[R_ps[pb], R_x[tt]], writes=[R_x[tt]])
                    proj_TM(slab, kc_n, 512, actT, actR, ntok, cons, k0=k0, acc=[0, 1, 2, 3],
                            first=(pi == 0), last=(pi == len(kparts) - 1))

        uXQ = 0; uMKT_ = 16; uMV_ = 24; uOT = 32; uXP = 48
        memKT = AR[:, uMKT_ * 512:(uMKT_ + 8) * 512].rearrange("p (j k) -> p j k", k=256)
        memV = AR[:, uMV_ * 512:(uMV_ + 8) * 512].rearrange("p (t c) -> p t c", c=2048)
        Ru_mkt = R_u[uMKT_:uMKT_ + 8]; Ru_mvv = R_u[uMV_:uMV_ + 8]

        def load_mem(seq):
            if seq == 0:
                fw.dma("sp", D_mem, AR[:, uMKT_ * 512:(uMKT_ + 8) * 512], mkS[0][:, :], reads=[R_mkS[0]], writes=Ru_mkt)
                fw.dma("sp", D_mem, AR[:, uMV_ * 512:(uMV_ + 8) * 512], mvS[0][:, :], reads=[R_mvS[0]], writes=Ru_mvv)
            else:
                for tt in range(2):
                    fw.dma("pool", D_memp, memV[:, tt, :], cmv[tt * 128:(tt + 1) * 128, :], writes=Ru_mvv)
                    ust = uOT + tt * 4
                    fw.dma("pool", D_memp, AR[:, ust * 512:(ust + 4) * 512], cmk[tt * 128:(tt + 1) * 128, :], writes=R_u[ust:ust + 4])
                    for g4 in range(4):
                        g = g4 % 2
                        for j in range(4):
                            jj = g4 * 4 + j
                            fw.op("pe", lambda e, g=g, j=j, jj=jj, ust=ust: e.transpose(
                                out=PTB[g][:, j * 128:(j + 1) * 128], in_=AR[:, ust * 512 + jj * 128: ust * 512 + (jj + 1) * 128],
                                identity=identB[:, :]), reads=R_u[ust:ust + 4] + [R_c], writes=[R_pt[g]])
                        for j in range(4):
                            jj = g4 * 4 + j
                            o_ap = memKT[:, jj, tt * 128:(tt + 1) * 128]
                            i_ap = PTB[g][:, j * 128:(j + 1) * 128]
                            if g == 0:
                                fw.op("dve", lambda e, o_ap=o_ap, i_ap=i_ap: e.tensor_copy(out=o_ap, in_=i_ap), reads=[R_pt[g]], writes=Ru_mkt)
                            else:
                                fw.op("act", lambda e, o_ap=o_ap, i_ap=i_ap: e.activation(out=o_ap, in_=i_ap, func=AF.Copy), reads=[R_pt[g]], writes=Ru_mkt)

        def xattn(seq, ntok):
            n = ntok
            norm_to_hT(1, ntok)
            load_mem(seq)
            for cg in range(4):
                slab = load_slab(w_mq, 16, cg * 512, 512)

                def c_q(ch, m, pb, cg=cg):
                    u = uXQ + cg * 4 + ch
                    fw.op("act", lambda e, u=u, pb=pb: e.activation(out=U(u)[:, 0:n], in_=PS[pb][:, 0:n], func=AF.Copy),
                          reads=[R_ps[pb]], writes=[R_u[u]])
                proj_FM(slab, 16, 512, hT_get, R_hT, ntok, c_q)
            rinv = ARF[:, 6, :]
            for hh in range(4):
                for kt in range(2):
                    sbk = kt
                    for dc in range(4):
                        j = hh * 4 + dc
                        fw.op("pe", lambda e, j=j, dc=dc, kt=kt, sbk=sbk: e.matmul(
                            PS[sbk][:, 0:n], lhsT=memKT[:, j, kt * 128:(kt + 1) * 128], rhs=U(uXQ + j)[:, 0:n],
                            start=(dc == 0), stop=(dc == 3)), reads=Ru_mkt + [R_u[uXQ + j]], writes=[R_ps[sbk]])
                    fw.op("act", lambda e, kt=kt, sbk=sbk: e.activation(out=U(uXP + kt)[:, 0:n], in_=PS[sbk][:, 0:n], func=AF.Exp,
                                                                       scale=512.0 ** -0.5), reads=[R_ps[sbk]], writes=[R_u[uXP + kt]])
                for kt in range(2):
                    fw.op("pe", lambda e, kt=kt: e.matmul(PS[4][:, 0:n], lhsT=onesB[:, :], rhs=U(uXP + kt)[:, 0:n], start=(kt == 0), stop=(kt == 1)),
                          reads=[R_c, R_u[uXP + kt]], writes=[R_ps[4]])
                fw.op("act", lambda e: e.activation(out=rinv[:, 0:n], in_=PS[4][:, 0:n], func=AF.Ln), reads=[R_ps[4]], writes=[R_f[6]])
                fw.op("act", lambda e: e.activation(out=rinv[:, 0:n], in_=rinv[:, 0:n], func=AF.Exp, scale=-1.0), reads=[R_f[6]], writes=[R_f[6]])
                for jv in range(4):
                    ob = 2 + (jv % 2)
                    for kt in range(2):
                        fw.op("pe", lambda e, jv=jv, kt=kt, ob=ob, hh=hh: e.matmul(
                            PS[ob][:, 0:n], lhsT=memV[:, kt, hh * 512 + jv * 128: hh * 512 + (jv + 1) * 128], rhs=U(uXP + kt)[:, 0:n],
                            start=(kt == 0), stop=(kt == 1)), reads=Ru_mvv + [R_u[uXP + kt]], writes=[R_ps[ob]])
                    u = uOT + hh * 4 + jv
                    fw.op("dve", lambda e, u=u, ob=ob: e.tensor_tensor(out=U(u)[:, 0:n], in0=PS[ob][:, 0:n], in1=rinv[:, 0:n], op=ALU.mult),
                          reads=[R_ps[ob], R_f[6]], writes=[R_u[u]])
            proj_residual(w_mo, lambda k: U(uOT + k), R_u[uOT:uOT + 16], ntok, [(0, 16)])

        def ffn(ntok):
            n = ntok
            norm_to_hT(2, ntok)
            nslab = 11
            for si in range(nslab):
                ncols = 512 if si < 10 else DFF - 5120
                slab_g = load_slab(w_gate, 16, si * 512, ncols)
                slab_u = load_slab(w_up, 16, si * 512, ncols)
                nchk = ncols // 128
                for ch in range(nchk):
                    s_, wv = slab_g
                    pg = next_ps()
                    for k in range(16):
                        fw.op("pe", lambda e, wv=wv, pg=pg, k=k, ch=ch: e.matmul(
                            PS[pg][:, 0:n], lhsT=wv[:, k, ch * 128:(ch + 1) * 128], rhs=hT[:, k, 0:n], start=(k == 0), stop=(k == 15)),
                            reads=[R_hT[k], R_w[s_]], writes=[R_ps[pg]])
                    fw.op("act", lambda e, ch=ch, pg=pg: e.activation(out=ARF[:, 4 + ch, 0:n], in_=PS[pg][:, 0:n], func=AF.Copy),
                          reads=[R_ps[pg]], writes=[R_f[4 + ch]])
                for ch in range(nchk):
                    s_, wv = slab_u
                    f_ = si * 4 + ch
                    pu = next_ps()
                    for k in range(16):
                        fw.op("pe", lambda e, wv=wv, pu=pu, k=k, ch=ch: e.matmul(
                            PS[pu][:, 0:n], lhsT=wv[:, k, ch * 128:(ch + 1) * 128], rhs=hT[:, k, 0:n], start=(k == 0), stop=(k == 15)),
                            reads=[R_hT[k], R_w[s_]], writes=[R_ps[pu]])
                    G = ARF[:, 4 + ch, :]; RG = [R_f[4 + ch]]
                    fa = next_f()
                    acc = ARF[:, fa, :]
                    cw = lambda j, f_=f_: convw_t[:, j, f_:f_ + 1]
                    fw.op("dve", lambda e, G=G, acc=acc, w2=cw(2), b=cw(3): e.tensor_scalar(out=acc[:, 0:n], in0=G[:, 0:n], scalar1=w2, scalar2=b,
                                                                                       op0=ALU.mult, op1=ALU.add),
                          reads=RG + [R_c], writes=[R_f[fa]])
                    fw.op("dve", lambda e, G=G, acc=acc, w1=cw(1): e.scalar_tensor_tensor(
                        out=acc[:, 1:n], in0=G[:, 0:n - 1], scalar=w1, in1=acc[:, 1:n], op0=ALU.mult, op1=ALU.add),
                        reads=RG + [R_c, R_f[fa]], writes=[R_f[fa]])
                    fw.op("dve", lambda e, G=G, acc=acc, w0=cw(0): e.scalar_tensor_tensor(
                        out=acc[:, 2:n], in0=G[:, 0:n - 2], scalar=w0, in1=acc[:, 2:n], op0=ALU.mult, op1=ALU.add),
                        reads=RG + [R_c, R_f[fa]], writes=[R_f[fa]])
                    fw.op("dve", lambda e, acc=acc, w1=cw(1), f_=f_: e.scalar_tensor_tensor(
                        out=acc[:, 0:1], in0=halo[:, 1, f_:f_ + 1], scalar=w1, in1=acc[:, 0:1], op0=ALU.mult, op1=ALU.add),
                        reads=[R_halo, R_c, R_f[fa]], writes=[R_f[fa]])
                    fw.op("dve", lambda e, acc=acc, w0=cw(0), f_=f_: e.scalar_tensor_tensor(
                        out=acc[:, 0:2], in0=halo[:, :, f_], scalar=w0, in1=acc[:, 0:2], op0=ALU.mult, op1=ALU.add),
                        reads=[R_halo, R_c, R_f[fa]], writes=[R_f[fa]])
                    fw.op("dve", lambda e, G=G, f_=f_: e.tensor_copy(out=halo[:, :, f_], in_=G[:, n - 2:n]), reads=RG, writes=[R_halo])
                    fw.op("act", lambda e, acc=acc: e.activation(out=acc[:, 0:n], in_=acc[:, 0:n], func=AF.Silu), reads=[R_f[fa]], writes=[R_f[fa]])
                    fw.op("dve", lambda e, acc=acc, pu=pu, f_=f_: e.tensor_tensor(out=U(f_)[:, 0:n], in0=acc[:, 0:n], in1=PS[pu][:, 0:n], op=ALU.mult),
                          reads=[R_f[fa], R_ps[pu]], writes=[R_u[f_]])
            proj_residual(w_down, lambda k: U(k), R_u[0:43], ntok, [(0, 16), (16, 16), (32, 11)])

        def final_norm(ntok, ysink, src0):
            nt = (ntok + 127) // 128
            for tt in range(nt):
                rows = min(128, ntok - tt * 128)
                c = 32 + 4 * tt
                fw.op("dve", lambda e, c=c: e.memset(stat[:, c:c + 2], 0.0), writes=[R_stat])
                fw.op("act", lambda e, tt=tt, rows=rows, c=c: e.activation(
                    out=AR[:rows, 60 * 512:64 * 512], in_=xres[:rows, tt, :], func=AF.Square, accum_out=stat[:rows, c:c + 1]),
                    reads=[R_x[tt]], writes=R_u[60:64] + [R_stat])
                fw.op("act", lambda e, rows=rows, c=c: e.activation(
                    out=stat[:rows, c + 1:c + 2], in_=stat[:rows, c:c + 1], func=AF.Sqrt, bias=EPS, scale=1.0 / D),
                    reads=[R_stat], writes=[R_stat])
                fw.op("dve", lambda e, rows=rows, c=c: e.reciprocal(out=stat[:rows, c + 2:c + 3], in_=stat[:rows, c + 1:c + 2]),
                      reads=[R_stat], writes=[R_stat])
                fw.op("dve", lambda e, tt=tt, rows=rows, c=c: e.scalar_tensor_tensor(
                    out=xres[:rows, tt, :], in0=xres[:rows, tt, :], scalar=stat[:rows, c + 2:c + 3], in1=gfinal_bc[:rows, :],
                    op0=ALU.mult, op1=ALU.mult), reads=[R_x[tt], R_stat, R_c], writes=[R_x[tt]])
                fw.dma("sp", D_x[tt], ysink[src0 + tt * 128: src0 + tt * 128 + rows, :], xres[:rows, tt, :], reads=[R_x[tt]])

        def dump_x(ntok, ysink, src0):
            for tt in range((ntok + 127) // 128):
                rows = min(128, ntok - tt * 128)
                fw.dma("sp", D_x[tt], ysink[src0 + tt * 128: src0 + tt * 128 + rows, :], xres[:rows, tt, :], reads=[R_x[tt]])

        def conv_state_out(outp):
            pb = next_ps()
            fw.op("pe", lambda e, pb=pb: e.transpose(out=PS[pb][0:86, 0:128], in_=halo[:].rearrange("p j f -> p (j f)"), identity=identF[:, :]),
                  reads=[R_halo, R_c], writes=[R_ps[pb]])
            f = next_f()
            fw.op("dve", lambda e, pb=pb, f=f: e.tensor_copy(out=ARF[0:86, f, 0:128], in_=PS[pb][0:86, 0:128]), reads=[R_ps[pb]], writes=[R_f[f]])
            fw.dma("sp", D_f[f], outp.rearrange("j (f p) -> (j f) p", p=128), ARF[0:86, f, 0:128], reads=[R_f[f]])

        def conv_state_in(inp_):
            f = next_f()
            fw.dma("sp", D_f[f], ARF[0:86, f, 0:128], inp_.rearrange("j (f p) -> (j f) p", p=128), writes=[R_f[f]])
            pb = next_ps()
            fw.op("pe", lambda e, pb=pb, f=f: e.transpose(out=PS[pb][:, 0:86], in_=ARF[0:86, f, 0:128], identity=identF[0:86, 0:86]),
                  reads=[R_f[f], R_c], writes=[R_ps[pb]])
            fw.op("dve", lambda e, pb=pb: e.tensor_copy(out=halo[:].rearrange("p j f -> p (j f)"), in_=PS[pb][:, 0:86]), reads=[R_ps[pb]], writes=[R_halo])

        def mem_block():
            for tt in range(2):
                fw.dma("sp", D_x[tt], xres[:, tt, :], mem[tt * 128:(tt + 1) * 128, :], writes=[R_x[tt]])
            norm_to_hT(3, MEM)
            for (w, outp) in ((w_mk, pmk), (w_mv, pmv)):
                for cg in range(4):
                    slab = load_slab(w, 16, cg * 512, 512)

                    def cm(tt, rows, pb, cg=cg, outp=outp):
                        f = next_f()
                        fw.op("act", lambda e, f=f, pb=pb, rows=rows: e.activation(out=ARF[:rows, f, :], in_=PS[pb][:rows, :], func=AF.Copy),
                              reads=[R_ps[pb]], writes=[R_f[f]])
                        fw.dma("sp", D_f[f], outp[tt * 128: tt * 128 + rows, cg * 512:(cg + 1) * 512], ARF[:rows, f, :],
                               reads=[R_f[f]])
                        if outp is pmv:
                            u = 36 + (tt * 4 + cg) % 4
                            fw.op("dve", lambda e, u=u, f=f: e.tensor_copy(out=U(u)[:, :], in_=ARF[:, f, :]), reads=[R_f[f]], writes=[R_u[u]])
                            fw.dma("sp", D_unit(u), mvS[0][:, tt * 2048 + cg * 512: tt * 2048 + (cg + 1) * 512], U(u)[:, :], reads=[R_u[u]],
                                   writes=[R_mvS[0]])
                    proj_TM(slab, 16, 512, hT_get, R_hT, MEM, cm)
                    if w is w_mk:
                        def cmkT(ch, m, pb, cg=cg):
                            j = cg * 4 + ch
                            u = 32 + (j % 4)
                            fw.op("dve", lambda e, u=u, pb=pb: e.tensor_copy(out=U(u)[:, 0:MEM], in_=PS[pb][:, 0:MEM]), reads=[R_ps[pb]], writes=[R_u[u]])
                            fw.dma("sp", D_unit(u), mkS[0][:, j * 256:(j + 1) * 256], U(u)[:, 0:MEM], reads=[R_u[u]], writes=[R_mkS[0]])
                        proj_FM(slab, 16, 512, hT_get, R_hT, MEM, cmkT)

        for nm_ in ("w_mk", "w_mv", "w_in", "w_out", "w_mq", "w_mo", "w_gate", "w_up", "w_down"):
            convert_weight(nm_)
        if do_mem:
            mem_block()
        ml_init_zero()
        fw.op("dve", lambda e: e.memset(halo[:], 0.0), writes=[R_halo])
        for b in range(nblk):
            block(0, b * 512, b * 512, 512, x, y, pk, pv)
        if stop >= 5:
            ml_out_state(pc, pn, pm)
        if stop >= 8:
            conv_state_out(pconv)
        if sample:
            if stop >= 8:
                conv_state_in(conv0)
            cache_prologue()
            if stop >= 5:
                ml_init_state()
            block(1, T, 0, TS, xs, ys, sk, sv)
            if stop >= 5:
                ml_out_state(sc, sn, sm)
            if stop >= 8:
                conv_state_out(sconv)
        fw.emit()
    return nc


def _prep_inputs(inp, b):
    f = np.float32
    g = lambda k: np.asarray(inp[k], dtype=f)
    d = {}
    d["x"] = np.ascontiguousarray(g("x_prompt")[b])
    d["xs"] = np.ascontiguousarray(g("x_sample")[b])
    d["ck"] = np.ascontiguousarray(g("cache_da_k")[0, b].reshape(T, 1024))
    d["cv"] = np.ascontiguousarray(g("cache_da_v")[0, b].reshape(T, 1024))
    d["c0"] = np.ascontiguousarray(g("state_ml_c")[0, b])
    d["n0"] = np.ascontiguousarray(g("state_ml_n")[0, b])
    d["m0"] = np.ascontiguousarray(g("state_ml_m")[0, b].reshape(4, 1))
    d["conv0"] = np.ascontiguousarray(g("state_ffn_conv")[0, b])
    d["cmk"] = np.ascontiguousarray(g("cache_mem_k")[0, b].reshape(MEM, D))
    d["cmv"] = np.ascontiguousarray(g("cache_mem_v")[0, b].reshape(MEM, D))
    d["mem"] = np.ascontiguousarray(g("mem_prompt")[b])
    for k in ("w_in", "w_out", "w_mq", "w_mk", "w_mv", "w_mo", "w_gate", "w_up", "w_down"):
        d[k] = np.ascontiguousarray(g(k)[0])
    gs = np.stack([g("g_mix")[0], g("g_xattn")[0], g("g_ffn")[0], g("g_mem")[0]], 0)
    d["gpk"] = np.ascontiguousarray(gs.reshape(4, 16, 128).transpose(2, 0, 1))
    d["lamv"] = np.concatenate([g("lambda_q1")[0], g("lambda_k1")[0], g("lambda_q2")[0], g("lambda_k2")[0]])[None, :].copy()
    d["gda"] = np.ascontiguousarray(g("g_da_sub")[0].reshape(128, 1))
    d["bgate"] = np.ascontiguousarray(np.stack([g("b_ig")[0], g("b_fg")[0]], 1))
    d["gml"] = np.ascontiguousarray(g("g_ml")[0])
    cw = np.concatenate([g("conv_w")[0], g("conv_b")], 0)
    d["convw"] = np.ascontiguousarray(cw.reshape(4, 43, 128).transpose(2, 0, 1))
    d["gfinal"] = np.ascontiguousarray(g("g_final"))
    d["identf"] = np.eye(128, dtype=f)
    kk = np.arange(128)[:, None, None] + 128 * np.arange(4)[None, :, None]
    qq = np.arange(512)[None, None, :]
    d["masks"] = np.ascontiguousarray(((kk // 64) <= (qq // 64)).astype(f))
    es_ = np.zeros((4, 4, 128), f)
    for hh in range(4):
        es_[hh, hh, :] = 1.0
    d["esel"] = es_.reshape(4, 512)
    pp = np.arange(128)[:, None] % 64
    tt_ = np.arange(64)[None, :]
    d["maskml"] = np.where(pp <= tt_, 0.0, -1e30).astype(f)
    return d


_NC_CACHE = {}


def kernel(**inp):
    cfg = ("full",)
    if cfg not in _NC_CACHE:
        _NC_CACHE[cfg] = build()
    nc = _NC_CACHE[cfg]
    in_maps = [_prep_inputs(inp, b) for b in range(8)]
    res = run_bass_kernel_spmd(nc, in_maps, core_ids=list(range(8)))
    r = res.results
    st = lambda k: np.stack([np.asarray(r[b][k], dtype=np.float32) for b in range(8)], 0)
    y_prompt = st("y")
    y_sample = st("ys")
    p_k = st("pk").reshape(1, 8, T, 8, 128)
    p_v = st("pv").reshape(1, 8, T, 8, 128)
    p_c = st("pc")[None]
    p_n = st("pn")[None]
    p_m = st("pm").reshape(1, 8, 4)
    p_conv = st("pconv")[None]
    p_mk = st("pmk").reshape(1, 8, MEM, 4, 512)
    p_mv = st("pmv").reshape(1, 8, MEM, 4, 512)
    s_k = st("sk").reshape(1, 8, TS, 8, 128)
    s_v = st("sv").reshape(1, 8, TS, 8, 128)
    s_c = st("sc")[None]
    s_n = st("sn")[None]
    s_m = st("sm").reshape(1, 8, 4)
    s_conv = st("sconv")[None]
    return (y_prompt, y_sample, p_k, p_v, p_c, p_n, p_m, p_conv, p_mk, p_mv,
            s_k, s_v, s_c, s_n, s_m, s_conv)
```
